# Optimizing a Trainium2 kernel written in Bass

```python
import math
import jax, jax.numpy as jnp
from jax import lax
import numpy as np

D_MODEL = 2048
BATCH = 1
SEQ = 16384
DEPTH = 2

HEAD_DIM = 128
N_FOX_HEADS = 8
N_DIL_HEADS = 8
DIL_PAIRS = ((128, 1), (512, 4), (2048, 16))
Q_BLOCK = 128
D_FF = 5632
S5_GROUP = 16
S5_WIDTH = 1024
S5_GROUPS = S5_WIDTH // S5_GROUP
S5_STATE = 64
N_EVEN = (DEPTH + 1) // 2
N_ODD = DEPTH // 2
ALPHA = (2.0 * DEPTH) ** 0.25
BETA = (8.0 * DEPTH) ** -0.25
LN_EPS = 1e-5
FOX_W = N_FOX_HEADS * HEAD_DIM
DIL_W = N_DIL_HEADS * HEAD_DIM
ATTN_IN = 3 * FOX_W + N_FOX_HEADS + 3 * DIL_W
ATTN_OUT = FOX_W + DIL_W

kernel_name = "hybrid_fox_dilated_s5_macaron_deepnorm"


def layer_norm(x, g, b):
    xf = x.astype(jnp.float32)
    mu = xf.mean(-1, keepdims=True)
    var = jnp.square(xf - mu).mean(-1, keepdims=True)
    return ((xf - mu) * lax.rsqrt(var + LN_EPS) * g + b).astype(x.dtype)


def swiglu(x, w_gate, w_up, w_down):
    return (jax.nn.silu(x @ w_gate) * (x @ w_up)) @ w_down


def forgetting_attention(q, k, v, log_f):
    B, S, H, Dh = q.shape
    nb = S // Q_BLOCK
    c = jnp.cumsum(log_f.astype(jnp.float32), axis=1).transpose(0, 2, 1)
    kpos = jnp.arange(S)
    scale = Dh ** -0.5

    def block(i):
        start = i * Q_BLOCK
        qb = lax.dynamic_slice_in_dim(q, start, Q_BLOCK, axis=1)
        cb = lax.dynamic_slice_in_dim(c, start, Q_BLOCK, axis=2)
        s = jnp.einsum('bqhd,bkhd->bhqk', qb, k).astype(jnp.float32) * scale
        s = s + cb[..., :, None] - c[..., None, :]
        qpos = start + jnp.arange(Q_BLOCK)
        causal = kpos[None, :] <= qpos[:, None]
        p = jax.nn.softmax(jnp.where(causal, s, -jnp.inf), axis=-1)
        return jnp.einsum('bhqk,bkhd->bqhd', p.astype(v.dtype), v)

    out = lax.map(block, jnp.arange(nb))
    return out.transpose(1, 0, 2, 3, 4).reshape(B, S, H, Dh)


def dilated_branch(q, k, v, window, dilation):
    B, S, H, Dh = q.shape
    span = window // dilation
    L = S // dilation
    blk = span
    nb = -(-L // blk)
    pad = nb * blk - L

    def to_blocks(t):
        t = t.reshape(B, L, dilation, H, Dh).transpose(0, 2, 1, 3, 4).reshape(B * dilation, L, H, Dh)
        t = jnp.pad(t, ((0, 0), (0, pad), (0, 0), (0, 0)))
        return t.reshape(B * dilation, nb, blk, H, Dh)

    qb, kb, vb = to_blocks(q), to_blocks(k), to_blocks(v)
    prev = lambda t: jnp.concatenate([jnp.zeros_like(t[:, :1]), t[:, :-1]], axis=1)
    kc = jnp.concatenate([prev(kb), kb], axis=2)
    vc = jnp.concatenate([prev(vb), vb], axis=2)
    j = jnp.arange(blk)[:, None]
    m = jnp.arange(2 * blk)[None, :]
    dist = blk + j - m
    band = (dist >= 0) & (dist <= span)
    has_prev = (jnp.arange(nb) > 0)[:, None, None] | (m >= blk)[None]
    mask = band[None] & has_prev
    s = jnp.einsum('gnqhd,gnkhd->gnhqk', qb, kc).astype(jnp.float32) * (Dh ** -0.5)
    s = jnp.where(mask[None, :, None], s, -jnp.inf)
    lse = jax.nn.logsumexp(s, axis=-1)
    p = jnp.exp(s - lse[..., None])
    o = jnp.einsum('gnhqk,gnkhd->gnqhd', p.astype(vc.dtype), vc)
    o = o.reshape(B * dilation, nb * blk, H, Dh)[:, :L]
    o = o.reshape(B, dilation, L, H, Dh).transpose(0, 2, 1, 3, 4).reshape(B, S, H, Dh)
    lse = lse.transpose(0, 1, 3, 2).reshape(B * dilation, nb * blk, H)[:, :L]
    lse = lse.reshape(B, dilation, L, H).transpose(0, 2, 1, 3).reshape(B, S, H)
    return o, lse


def dilated_attention(q, k, v):
    branches = [dilated_branch(q, k, v, w, d) for (w, d) in DIL_PAIRS]
    outs = jnp.stack([o for o, _ in branches], axis=0)
    lses = jnp.stack([l for _, l in branches], axis=0)
    wts = jax.nn.softmax(lses, axis=0)
    return jnp.einsum('gbsh,gbshd->bshd', wts.astype(outs.dtype), outs)


def attention_mixer(x, w_in, b_f, w_out):
    B, S, _ = x.shape
    proj = x @ w_in
    fox_qkv, f_logit, dil_qkv = jnp.split(proj, [3 * FOX_W, 3 * FOX_W + N_FOX_HEADS], axis=-1)
    qa, ka, va = [t.reshape(B, S, N_FOX_HEADS, HEAD_DIM) for t in jnp.split(fox_qkv, 3, axis=-1)]
    log_f = jax.nn.log_sigmoid((f_logit + b_f).astype(jnp.float32))
    ya = forgetting_attention(qa, ka, va, log_f)
    qb, kb, vb = [t.reshape(B, S, N_DIL_HEADS, HEAD_DIM) for t in jnp.split(dil_qkv, 3, axis=-1)]
    yb = dilated_attention(qb, kb, vb)
    y = jnp.concatenate([ya.reshape(B, S, FOX_W), yb.reshape(B, S, DIL_W)], axis=-1)
    return y @ w_out


def ssm_combine(e1, e2):
    a1r, a1i, b1r, b1i = e1
    a2r, a2i, b2r, b2i = e2
    return (a2r * a1r - a2i * a1i,
            a2r * a1i + a2i * a1r,
            a2r * b1r - a2i * b1i + b2r,
            a2r * b1i + a2i * b1r + b2i)


def s5_mixer(x, w_in, lam_re, lam_im, log_dt, b_re, b_im, c_re, c_im, d_skip, w_glu_out, w_glu_gate):
    B, S, _ = x.shape
    f32 = jnp.float32
    u = (x @ w_in).astype(f32).reshape(B, S, S5_GROUPS, S5_GROUP)
    lr, li = lam_re.astype(f32), lam_im.astype(f32)
    dt = jnp.exp(log_dt.astype(f32))[:, None]
    mag = jnp.exp(lr * dt)
    a_re, a_im = mag * jnp.cos(li * dt), mag * jnp.sin(li * dt)
    den = lr * lr + li * li
    coef_re = ((a_re - 1.0) * lr + a_im * li) / den
    coef_im = (a_im * lr - (a_re - 1.0) * li) / den
    br, bi = b_re.astype(f32), b_im.astype(f32)
    bb_re = coef_re[..., None] * br - coef_im[..., None] * bi
    bb_im = coef_re[..., None] * bi + coef_im[..., None] * br
    bu_re = jnp.einsum('bsgh,gph->bsgp', u, bb_re)
    bu_im = jnp.einsum('bsgh,gph->bsgp', u, bb_im)
    ar = jnp.broadcast_to(a_re, bu_re.shape)
    ai = jnp.broadcast_to(a_im, bu_re.shape)
    _, _, xr, xi = lax.associative_scan(ssm_combine, (ar, ai, bu_re, bu_im), axis=1)
    y = jnp.einsum('bsgp,ghp->bsgh', xr, c_re.astype(f32)) - jnp.einsum('bsgp,ghp->bsgh', xi, c_im.astype(f32))
    y = (y + d_skip.astype(f32).reshape(S5_GROUPS, S5_GROUP) * u).reshape(B, S, S5_WIDTH)
    z = jax.nn.gelu(y).astype(x.dtype)
    return (z @ w_glu_out) * jax.nn.sigmoid(z @ w_glu_gate)


def setup_inputs(seed: int = 0) -> dict:
    key = jax.random.key(seed)
    ks = jax.random.split(key, 24)
    nrm = lambda k, shape, scale: jax.random.normal(k, shape, jnp.float32) * scale
    n_idx = jnp.arange(S5_STATE, dtype=jnp.float32)
    return {
        "x": nrm(ks[0], (BATCH, SEQ, D_MODEL), 1.0),
        "ffn1_w_gate": nrm(ks[1], (DEPTH, D_MODEL, D_FF), D_MODEL ** -0.5),
        "ffn1_w_up": nrm(ks[2], (DEPTH, D_MODEL, D_FF), D_MODEL ** -0.5),
        "ffn1_w_down": nrm(ks[3], (DEPTH, D_FF, D_MODEL), BETA * D_FF ** -0.5),
        "ffn2_w_gate": nrm(ks[4], (DEPTH, D_MODEL, D_FF), D_MODEL ** -0.5),
        "ffn2_w_up": nrm(ks[5], (DEPTH, D_MODEL, D_FF), D_MODEL ** -0.5),
        "ffn2_w_down": nrm(ks[6], (DEPTH, D_FF, D_MODEL), BETA * D_FF ** -0.5),
        "ln_gain": 1.0 + nrm(ks[7], (DEPTH, 3, D_MODEL), 0.02),
        "ln_bias": nrm(ks[8], (DEPTH, 3, D_MODEL), 0.02),
        "attn_w_in": nrm(ks[9], (N_EVEN, D_MODEL, ATTN_IN), D_MODEL ** -0.5),
        "attn_b_f": jax.random.uniform(ks[10], (N_EVEN, N_FOX_HEADS), jnp.float32, 1.0, 5.0),
        "attn_w_out": nrm(ks[11], (N_EVEN, ATTN_OUT, D_MODEL), BETA * ATTN_OUT ** -0.5),
        "s5_w_in": nrm(ks[12], (N_ODD, D_MODEL, S5_WIDTH), D_MODEL ** -0.5),
        "s5_lambda_re": -0.5 + nrm(ks[13], (N_ODD, S5_GROUPS, S5_STATE), 0.01),
        "s5_lambda_im": math.pi * n_idx + nrm(ks[14], (N_ODD, S5_GROUPS, S5_STATE), 0.01),
        "s5_log_dt": jax.random.uniform(ks[15], (N_ODD, S5_GROUPS), jnp.float32, math.log(1e-3), math.log(1e-1)),
        "s5_b_re": nrm(ks[16], (N_ODD, S5_GROUPS, S5_STATE, S5_GROUP), (2 * S5_GROUP) ** -0.5),
        "s5_b_im": nrm(ks[17], (N_ODD, S5_GROUPS, S5_STATE, S5_GROUP), (2 * S5_GROUP) ** -0.5),
        "s5_c_re": nrm(ks[18], (N_ODD, S5_GROUPS, S5_GROUP, S5_STATE), S5_STATE ** -0.5),
        "s5_c_im": nrm(ks[19], (N_ODD, S5_GROUPS, S5_GROUP, S5_STATE), S5_STATE ** -0.5),
        "s5_d": nrm(ks[20], (N_ODD, S5_WIDTH), 1.0),
        "s5_w_glu_out": nrm(ks[21], (N_ODD, S5_WIDTH, D_MODEL), BETA * S5_WIDTH ** -0.5),
        "s5_w_glu_gate": nrm(ks[22], (N_ODD, S5_WIDTH, D_MODEL), S5_WIDTH ** -0.5),
    }


def reference(x, ffn1_w_gate, ffn1_w_up, ffn1_w_down, ffn2_w_gate, ffn2_w_up, ffn2_w_down,
              ln_gain, ln_bias, attn_w_in, attn_b_f, attn_w_out, s5_w_in, s5_lambda_re,
              s5_lambda_im, s5_log_dt, s5_b_re, s5_b_im, s5_c_re, s5_c_im, s5_d,
              s5_w_glu_out, s5_w_glu_gate):
    for i in range(DEPTH):
        x = layer_norm(ALPHA * x + 0.5 * swiglu(x, ffn1_w_gate[i], ffn1_w_up[i], ffn1_w_down[i]),
                       ln_gain[i, 0], ln_bias[i, 0])
        j = i // 2
        if i % 2 == 0:
            mix = attention_mixer(x, attn_w_in[j], attn_b_f[j], attn_w_out[j])
        else:
            mix = s5_mixer(x, s5_w_in[j], s5_lambda_re[j], s5_lambda_im[j], s5_log_dt[j],
                           s5_b_re[j], s5_b_im[j], s5_c_re[j], s5_c_im[j], s5_d[j],
                           s5_w_glu_out[j], s5_w_glu_gate[j])
        x = layer_norm(ALPHA * x + mix, ln_gain[i, 1], ln_bias[i, 1])
        x = layer_norm(ALPHA * x + 0.5 * swiglu(x, ffn2_w_gate[i], ffn2_w_up[i], ffn2_w_down[i]),
                       ln_gain[i, 2], ln_bias[i, 2])
    return x
```

```python
from contextlib import ExitStack
import math
import numpy as np
import ml_dtypes

import concourse.bass as bass
import concourse.mybir as mybir
from concourse.bass_utils import run_bass_kernel_spmd

F32 = mybir.dt.float32
BF16 = mybir.dt.bfloat16
ALU = mybir.AluOpType
AF = mybir.ActivationFunctionType

D = 2048
SEQ = 16384
NCORE = 8
TPC = SEQ // NCORE
DFF = 5632
NFC = DFF // 128
HD = 128
ATT_IN = 6152
S5W = 1024
ALPHA = 4.0 ** 0.25
LN_EPS = 1e-5
TT = 512
NST = TT // 128


class Prog:
    ENGS = ("pe", "act", "dve", "pool", "sp")

    def __init__(self, nc, es, ndma=6):
        self.nc = nc
        self.ops = {e: [] for e in self.ENGS}
        self.semh = {}
        self.cnt = {e: 0 for e in self.ENGS}
        for e in self.ENGS:
            self.semh["e:" + e] = es.enter_context(nc.semaphore("se_" + e))
        self.nd = ndma
        self.dcnt = {}
        self.dnext = {"sp": 0, "pool": 0}
        for q in ("sp", "pool"):
            for k in range(ndma):
                self.semh[f"d:{q}:{k}"] = es.enter_context(nc.semaphore(f"sd_{q}{k}"))
                self.dcnt[(q, k)] = 0
        self.known = {e: {} for e in self.ENGS}
        self.res = {}
        self.semh["e:cc"] = es.enter_context(nc.semaphore("se_cc"))
        self.ccn = 0

    def collective(self, groups, src, dst, reads=(), writes=()):
        waits = self._collect("pool", reads, writes)
        self.ccn += 1
        ev = ("e:cc", self.ccn)
        self.ops["pool"].append((waits, (lambda e, g=groups, a=src, b=dst: e.collective_compute(
            "AllGather", ALU.bypass, replica_groups=g, ins=[a], outs=[b])), "e:cc", 1))
        self._commit(ev, reads, writes)

    def barrier(self):
        allw = []
        for e in self.ENGS:
            if self.cnt[e]:
                allw.append(("e:" + e, self.cnt[e]))
        for (q, k), c in self.dcnt.items():
            if c:
                allw.append((f"d:{q}:{k}", c * 16))
        for e in self.ENGS:
            kn = self.known[e]
            waits = []
            for sk, v in allw:
                if sk == "e:" + e:
                    continue
                if kn.get(sk, 0) < v:
                    kn[sk] = v
                    waits.append((sk, v))
            self.ops[e].append((waits, None, None, 0))

    def _collect(self, eng, reads, writes):
        deps = {}
        own = "e:" + eng

        def add(ev, same_ok):
            if ev is None:
                return
            sk, v = ev
            if sk == own and not same_ok:
                return
            if deps.get(sk, 0) < v:
                deps[sk] = v

        for r in reads:
            st = self.res.get(r)
            if st is not None and st[0] is not None:
                add(st[0], True)
        for w in writes:
            st = self.res.get(w)
            if st is not None:
                add(st[0], False)
                for ev in st[1].values():
                    add(ev, False)
        waits = []
        kn = self.known[eng]
        for sk, v in deps.items():
            if kn.get(sk, 0) < v:
                kn[sk] = v
                waits.append((sk, v))
        return waits

    def _commit(self, ev, reads, writes):
        sk = ev[0]
        for r in reads:
            st = self.res.get(r)
            if st is None:
                st = [None, {}]
                self.res[r] = st
            old = st[1].get(sk)
            if old is None or old[1] < ev[1]:
                st[1][sk] = ev
        for w in writes:
            self.res[w] = [ev, {}]

    def op(self, eng, fn, reads=(), writes=(), inc=True):
        waits = self._collect(eng, reads, writes)
        if inc:
            self.cnt[eng] += 1
            ev = ("e:" + eng, self.cnt[eng])
        else:
            ev = ("e:" + eng, self.cnt[eng] + 1)
        self.ops[eng].append((waits, fn, ("e:" + eng) if inc else None, 1))
        self._commit(ev, reads, writes)

    def dma(self, q, out, in_, reads=(), writes=()):
        k = self.dnext[q] % self.nd
        self.dnext[q] += 1
        sk = f"d:{q}:{k}"
        waits = self._collect(q, reads, writes)
        prev = self.dcnt[(q, k)] * 16
        if prev and self.known[q].get(sk, 0) < prev:
            self.known[q][sk] = prev
            waits.append((sk, prev))
        self.dcnt[(q, k)] += 1
        ev = (sk, self.dcnt[(q, k)] * 16)
        self.ops[q].append((waits, (lambda e, o=out, i=in_: e.dma_start(out=o, in_=i)), sk, 16))
        self._commit(ev, reads, writes)

    def finish(self):
        waits = []
        for e in self.ENGS:
            if self.cnt[e]:
                waits.append(("e:" + e, self.cnt[e]))
        for (q, k), c in self.dcnt.items():
            if c:
                waits.append((f"d:{q}:{k}", c * 16))
        if self.ccn:
            waits.append(("e:cc", self.ccn))
        self.ops["sp"].append((waits, None, None, 0))

    def emit(self):
        nc = self.nc
        with nc.Block() as block:
            def mk(name):
                def body(e):
                    for waits, fn, sk, n in self.ops[name]:
                        for (wsk, v) in waits:
                            e.wait_ge(self.semh[wsk], v)
                        if fn is None:
                            continue
                        ins = fn(e)
                        if sk is not None:
                            ins.then_inc(self.semh[sk], n)
                return body
            block.tensor(mk("pe"))
            block.scalar(mk("act"))
            block.vector(mk("dve"))
            block.gpsimd(mk("pool"))
            block.sync(mk("sp"))
        self.ops = {e: [] for e in self.ENGS}


def mm(out, lhsT, rhs, start, stop):
    return lambda e: e.matmul(out, lhsT, rhs, start=start, stop=stop)


class TokPipe:
    def __init__(self, nc, es, P, ident_dram, tag=""):
        self.nc, self.P = nc, P
        sb = lambda n, s, d: es.enter_context(nc.sbuf_tensor(n + tag, s, d))
        ps = lambda n: es.enter_context(nc.psum_tensor(n + tag, [128, 512], F32))
        self.X = sb("X", [128, NST, D], F32)
        self.XT = sb("XT", [128, 16, TT], BF16)
        self.HT = sb("HT", [128, NFC, TT], BF16)
        self.WR = sb("WR", [128, 3, 11264], BF16)
        self.G = sb("G", [128, D], F32)
        self.B = sb("B", [128, D], F32)
        self.XB = sb("XB", [128, 2, D], BF16)
        self.SG = sb("SG", [128, 2, TT], F32)
        self.OT = sb("OT", [128, 2, TT], BF16)
        self.OF = sb("OF", [128, 2, TT], F32)
        self.stats = sb("stats", [128, NST, 24], F32)
        self.mv = sb("mv", [128, NST, 2], F32)
        self.rstd = sb("rstd", [128, NST, 1], F32)
        self.nmr = sb("nmr", [128, NST, 1], F32)
        self.ident = sb("ident_sb", [128, 128], BF16)
        self.PG = [ps("PG0"), ps("PG1")]
        self.PU = [ps("PU0"), ps("PU1")]
        self.PY = [ps("PY0"), ps("PY1")]
        self.PT = [ps("PT0"), ps("PT1")]
        self.ring = 0
        self.ot = 0
        P.dma("sp", self.ident[:, :], ident_dram, writes=[("ident",)])

    def ring_next(self):
        s = self.ring % 3
        self.ring += 1
        return s

    def load_x(self, x_rows):
        P = self.P
        P.dma("sp", self.X[:, :, :], x_rows.rearrange("(s p) d -> p s d", p=128),
              writes=[("X", st) for st in range(NST)])
        for st in range(NST):
            self.make_xt(st)

    def load_x_only(self, x_rows):
        P = self.P
        P.dma("sp", self.X[:, :, :], x_rows.rearrange("(s p) d -> p s d", p=128),
              writes=[("X", st) for st in range(NST)])

    def load_xt(self, xt_rows, nk):
        P = self.P
        P.dma("sp", self.XT[:, 0:nk, :], xt_rows.rearrange("(k p) t -> p k t", p=128),
              writes=[("XT", st) for st in range(NST)])

    def store_x(self, out_rows):
        P = self.P
        P.dma("sp", out_rows.rearrange("(s p) d -> p s d", p=128), self.X[:, :, :],
              reads=[("X", st) for st in range(NST)])

    def make_xt(self, st):
        P = self.P
        b = st % 2
        X, XB, XT, PT, ident = self.X, self.XB, self.XT, self.PT, self.ident
        P.op("act", lambda e: e.activation(out=XB[:, b, :], in_=X[:, st, :], func=AF.Copy),
             reads=[("X", st)], writes=[("XB", b)])
        for kg in range(4):
            pb = (st * 4 + kg) % 2
            for j in range(4):
                kc = kg * 4 + j
                P.op("pe", mm(PT[pb][:, j * 128:(j + 1) * 128], XB[:, b, kc * 128:(kc + 1) * 128],
                              ident[:, :], True, True),
                     reads=[("XB", b), ("ident",)], writes=[("ps", "T", pb)], inc=(j == 3))
            src = PT[pb][:, :].rearrange("p (j t) -> p j t", j=4)
            dst = XT[:, kg * 4:(kg + 1) * 4, st * 128:(st + 1) * 128]
            P.op("dve", lambda e, s=src, d=dst: e.tensor_copy(out=d, in_=s),
                 reads=[("ps", "T", pb)], writes=[("XT", st)])

    def load_ln(self, g_row, b_row):
        P = self.P
        P.dma("sp", self.G[:, :], g_row.partition_broadcast(128), writes=[("G",)])
        P.dma("sp", self.B[:, :], b_row.partition_broadcast(128), writes=[("B",)])

    def layernorm(self, eps, make_xt=True):
        P = self.P
        X, G, B = self.X, self.G, self.B
        stats, mv, rstd, nmr = self.stats, self.mv, self.rstd, self.nmr
        for st in range(NST):
            for q in range(4):
                P.op("dve", lambda e, st=st, q=q: e.bn_stats(out=stats[:, st, q * 6:(q + 1) * 6],
                                                             in_=X[:, st, q * 512:(q + 1) * 512]),
                     reads=[("X", st)], writes=[("stats", st)])
            P.op("dve", lambda e, st=st: e.bn_aggr(out=mv[:, st, :], in_=stats[:, st, :]),
                 reads=[("stats", st)], writes=[("mv", st)])
            P.op("dve", lambda e, st=st: e.tensor_scalar(out=rstd[:, st, :], in0=mv[:, st, 1:2],
                                                         scalar1=eps, scalar2=None, op0=ALU.add),
                 reads=[("mv", st)], writes=[("rstd", st)])
            P.op("act", lambda e, st=st: e.activation(out=rstd[:, st, :], in_=rstd[:, st, :], func=AF.Ln),
                 reads=[("rstd", st)], writes=[("rstd", st)])
            P.op("act", lambda e, st=st: e.activation(out=rstd[:, st, :], in_=rstd[:, st, :], func=AF.Exp,
                                                      scale=-0.5),
                 reads=[("rstd", st)], writes=[("rstd", st)])
            P.op("dve", lambda e, st=st: e.scalar_tensor_tensor(out=nmr[:, st, :], in0=mv[:, st, 0:1],
                                                                scalar=-1.0, in1=rstd[:, st, :],
                                                                op0=ALU.mult, op1=ALU.mult),
                 reads=[("mv", st), ("rstd", st)], writes=[("nmr", st)])
            P.op("act", lambda e, st=st: e.activation(out=X[:, st, :], in_=X[:, st, :], func=AF.Identity,
                                                      bias=nmr[:, st, :], scale=rstd[:, st, :]),
                 reads=[("X", st), ("rstd", st), ("nmr", st)], writes=[("X", st)])
            P.op("dve", lambda e, st=st: e.tensor_tensor(out=X[:, st, :], in0=X[:, st, :], in1=G[:, :],
                                                         op=ALU.mult),
                 reads=[("X", st), ("G",)], writes=[("X", st)])
            P.op("pool", lambda e, st=st: e.tensor_tensor(out=X[:, st, :], in0=X[:, st, :], in1=B[:, :],
                                                          op=ALU.add),
                 reads=[("X", st), ("B",)], writes=[("X", st)])
            if make_xt:
                self.make_xt(st)

    def ffn(self, wg, wu, wd, g_row, b_row):
        P = self.P
        XT, HT, WR, SG, X = self.XT, self.HT, self.WR, self.SG, self.X
        PG, PU, PY = self.PG, self.PU, self.PY
        self.load_ln(g_row, b_row)
        wgv = wg.rearrange("(k p) n -> p k n", p=128)
        wuv = wu.rearrange("(k p) n -> p k n", p=128)
        wdv = wd.rearrange("(k p) n -> p k n", p=128)
        xt_keys = [("XT", st) for st in range(NST)]
        for fg in range(NFC // 2):
            s = self.ring_next()
            gv = WR[:, s, 0:4096].rearrange("p (k n) -> p k n", k=16)
            uv = WR[:, s, 4096:8192].rearrange("p (k n) -> p k n", k=16)
            P.dma("pool", gv, wgv[:, :, fg * 256:(fg + 1) * 256], writes=[("WR", s, "a")])
            P.dma("pool", uv, wuv[:, :, fg * 256:(fg + 1) * 256], writes=[("WR", s, "b")])
            for fc in range(2):
                ch = fg * 2 + fc
                b = ch % 2
                for kc in range(16):
                    P.op("pe", mm(PG[b][:, :], gv[:, kc, fc * 128:(fc + 1) * 128], XT[:, kc, :],
                                  kc == 0, kc == 15),
                         reads=[("WR", s, "a")] + xt_keys, writes=[("ps", "G", b)], inc=(kc == 15))
                for kc in range(16):
                    P.op("pe", mm(PU[b][:, :], uv[:, kc, fc * 128:(fc + 1) * 128], XT[:, kc, :],
                                  kc == 0, kc == 15),
                         reads=[("WR", s, "b")] + xt_keys, writes=[("ps", "U", b)], inc=(kc == 15))
                P.op("act", lambda e, b=b: e.activation(out=SG[:, b, :], in_=PG[b][:, :], func=AF.Silu),
                     reads=[("ps", "G", b)], writes=[("SG", b)])
                P.op("dve", lambda e, b=b, ch=ch: e.tensor_tensor(out=HT[:, ch, :], in0=SG[:, b, :],
                                                                  in1=PU[b][:, :], op=ALU.mult),
                     reads=[("SG", b), ("ps", "U", b)], writes=[("HT", ch)])
        coef = 0.5 / ALPHA
        ht_keys = [("HT", ch) for ch in range(NFC)]
        it = 0
        for dp in range(D // 256):
            s = self.ring_next()
            dv = WR[:, s, 0:NFC * 256].rearrange("p (k n) -> p k n", k=NFC)
            P.dma("pool", dv, wdv[:, :, dp * 256:(dp + 1) * 256],
                  writes=[("WR", s, "a"), ("WR", s, "b")])
            for st in range(NST):
                b = it % 2
                it += 1
                for fc in range(NFC):
                    P.op("pe", mm(PY[b][:, 0:256], HT[:, fc, st * 128:(st + 1) * 128], dv[:, fc, :],
                                  fc == 0, fc == NFC - 1),
                         reads=[("WR", s, "a"), ("WR", s, "b")] + (ht_keys if fc == 0 else []),
                         writes=[("ps", "Y", b)], inc=(fc == NFC - 1))
                xs = X[:, st, dp * 256:(dp + 1) * 256]
                P.op("dve", lambda e, b=b, xs=xs: e.scalar_tensor_tensor(out=xs, in0=PY[b][:, 0:256],
                                                                         scalar=coef, in1=xs,
                                                                         op0=ALU.mult, op1=ALU.add),
                     reads=[("ps", "Y", b), ("X", st)], writes=[("X", st)])
        self.layernorm(LN_EPS / (ALPHA * ALPHA))

    def lin_tm(self, nk, w, ncols, consumer, extra_w=None):
        P = self.P
        XT, WR, PY, PG = self.XT, self.WR, self.PY, self.PG
        wv = w.rearrange("(k p) n -> p k n", p=128)
        wv2 = extra_w.rearrange("(k p) n -> p k n", p=128) if extra_w is not None else None
        xt_keys = [("XT", st) for st in range(NST)]
        it = 0
        for dp in range(ncols // 256):
            s = self.ring_next()
            dv = WR[:, s, 0:nk * 256].rearrange("p (k n) -> p k n", k=nk)
            P.dma("pool", dv, wv[:, :, dp * 256:(dp + 1) * 256], writes=[("WR", s, "a")])
            if wv2 is not None:
                dv2 = WR[:, s, 5632:5632 + nk * 256].rearrange("p (k n) -> p k n", k=nk)
                P.dma("pool", dv2, wv2[:, :, dp * 256:(dp + 1) * 256], writes=[("WR", s, "b")])
            for st in range(NST):
                b = it % 2
                it += 1
                for kc in range(nk):
                    P.op("pe", mm(PY[b][:, 0:256], XT[:, kc, st * 128:(st + 1) * 128], dv[:, kc, :],
                                  kc == 0, kc == nk - 1),
                         reads=[("WR", s, "a")] + xt_keys, writes=[("ps", "Y", b)], inc=(kc == nk - 1))
                if wv2 is not None:
                    for kc in range(nk):
                        P.op("pe", mm(PG[b][:, 0:256], XT[:, kc, st * 128:(st + 1) * 128], dv2[:, kc, :],
                                      kc == 0, kc == nk - 1),
                             reads=[("WR", s, "b")] + xt_keys, writes=[("ps", "G", b)],
                             inc=(kc == nk - 1))
                    consumer(st, dp * 256, b)
                else:
                    consumer(st, dp * 256, b)

    def lin_fm(self, w, pieces, consumer):
        P = self.P
        XT, WR, PG = self.XT, self.WR, self.PG
        wv = w.rearrange("(k p) n -> p k n", p=128)
        xt_keys = [("XT", st) for st in range(NST)]
        it = 0
        for (c0, ncol) in pieces:
            s = self.ring_next()
            dv = WR[:, s, 0:16 * ncol].rearrange("p (k n) -> p k n", k=16)
            P.dma("pool", dv, wv[:, :, c0:c0 + ncol], writes=[("WR", s, "a")])
            off = 0
            while off < ncol:
                wdt = min(128, ncol - off)
                b = it % 2
                it += 1
                for kc in range(16):
                    P.op("pe", mm(PG[b][0:wdt, :], dv[:, kc, off:off + wdt], XT[:, kc, :],
                                  kc == 0, kc == 15),
                         reads=[("WR", s, "a")] + xt_keys, writes=[("ps", "G", b)], inc=(kc == 15))
                consumer(c0 + off, wdt, b)
                off += wdt


def _new_nc():
    return bass.Bass("TRN2", target_bir_lowering=False)


def _ident_np():
    return np.eye(128, dtype=np.float32).astype(ml_dtypes.bfloat16)


def build_l1(ntok=TPC):
    nc = _new_nc()
    dt = lambda n, s, d, k: nc.dram_tensor(n, s, d, kind=k).ap()
    x = dt("x", [ntok, D], F32, "ExternalInput")
    wg = dt("wg", [D, DFF], F32, "ExternalInput")
    wu = dt("wu", [D, DFF], F32, "ExternalInput")
    wd = dt("wd", [DFF, D], F32, "ExternalInput")
    lng = dt("lng", [1, D], F32, "ExternalInput")
    lnb = dt("lnb", [1, D], F32, "ExternalInput")
    win = dt("win", [D, ATT_IN], F32, "ExternalInput")
    ident = dt("ident", [128, 128], BF16, "ExternalInput")
    x1 = dt("x1", [ntok, D], F32, "ExternalOutput")
    projT = dt("projT", [6144, ntok], BF16, "ExternalOutput")
    fT = dt("fT", [8, ntok], F32, "ExternalOutput")
    with ExitStack() as es:
        P = Prog(nc, es)
        T = TokPipe(nc, es, P, ident)
        pieces = [(c, 256) for c in range(0, 3072, 256)] + [(3072, 8)] + \
                 [(c, 256) for c in range(3080, 6152, 256)]
        for t in range(ntok // TT):
            t0 = t * TT
            T.load_x(x[t0:t0 + TT, :])
            T.ffn(wg, wu, wd, lng[0, :], lnb[0, :])
            T.store_x(x1[t0:t0 + TT, :])

            def cons(c0, wdt, b, t0=t0):
                k = T.ot % 2
                T.ot += 1
                if wdt == 8:
                    P.op("act", lambda e: e.activation(out=T.OF[0:8, k, :], in_=T.PG[b][0:8, :], func=AF.Copy),
                         reads=[("ps", "G", b)], writes=[("OF", k)])
                    P.dma("sp", fT[:, t0:t0 + TT], T.OF[0:8, k, :], reads=[("OF", k)])
                else:
                    r0 = c0 if c0 < 3072 else c0 - 8
                    P.op("act", lambda e: e.activation(out=T.OT[0:wdt, k, :], in_=T.PG[b][0:wdt, :],
                                                       func=AF.Copy),
                         reads=[("ps", "G", b)], writes=[("OT", k)])
                    P.dma("sp", projT[r0:r0 + wdt, t0:t0 + TT], T.OT[0:wdt, k, :], reads=[("OT", k)])
            T.lin_fm(win, pieces, cons)
        P.finish()
        P.emit()
    return nc


def build_l3(ntok=TPC):
    nc = _new_nc()
    dt = lambda n, s, d, k="ExternalInput": nc.dram_tensor(n, s, d, kind=k).ap()
    x1 = dt("x1", [ntok, D], F32)
    yT = dt("yT", [D, ntok], BF16)
    wo = dt("wo", [D, D], F32)
    w2 = [dt("w2g", [D, DFF], F32), dt("w2u", [D, DFF], F32), dt("w2d", [DFF, D], F32)]
    w3 = [dt("w3g", [D, DFF], F32), dt("w3u", [D, DFF], F32), dt("w3d", [DFF, D], F32)]
    lng = dt("lng", [3, D], F32)
    lnb = dt("lnb", [3, D], F32)
    wsi = dt("wsi", [D, S5W], F32)
    ident = dt("ident", [128, 128], BF16)
    x3 = dt("x3", [ntok, D], F32, "ExternalOutput")
    u = dt("u", [ntok, S5W], F32, "ExternalOutput")
    with ExitStack() as es:
        P = Prog(nc, es)
        T = TokPipe(nc, es, P, ident)
        for t in range(ntok // TT):
            t0 = t * TT
            T.load_x_only(x1[t0:t0 + TT, :])
            T.load_xt(yT[:, t0:t0 + TT], 16)
            T.load_ln(lng[0, :], lnb[0, :])

            def cons_res(st, c0, b):
                xs = T.X[:, st, c0:c0 + 256]
                P.op("dve", lambda e: e.scalar_tensor_tensor(out=xs, in0=T.PY[b][:, 0:256], scalar=1.0 / ALPHA,
                                                             in1=xs, op0=ALU.mult, op1=ALU.add),
                     reads=[("ps", "Y", b), ("X", st)], writes=[("X", st)])
            T.lin_tm(16, wo, D, cons_res)
            T.layernorm(LN_EPS / (ALPHA * ALPHA))
            T.ffn(w2[0], w2[1], w2[2], lng[1, :], lnb[1, :])
            T.ffn(w3[0], w3[1], w3[2], lng[2, :], lnb[2, :])
            T.store_x(x3[t0:t0 + TT, :])

            def cons_u(st, c0, b, t0=t0):
                k = T.ot % 2
                T.ot += 1
                P.op("act", lambda e: e.activation(out=T.OF[:, k, 0:256], in_=T.PY[b][:, 0:256], func=AF.Copy),
                     reads=[("ps", "Y", b)], writes=[("OF", k)])
                P.dma("sp", u[t0 + st * 128:t0 + (st + 1) * 128, c0:c0 + 256], T.OF[:, k, 0:256],
                      reads=[("OF", k)])
            T.lin_tm(16, wsi, S5W, cons_u)
        P.finish()
        P.emit()
    return nc


def build_l5(ntok=TPC):
    nc = _new_nc()
    dt = lambda n, s, d, k="ExternalInput": nc.dram_tensor(n, s, d, kind=k).ap()
    x3 = dt("x3", [ntok, D], F32)
    zT = dt("zT", [S5W, ntok], BF16)
    wgo = dt("wgo", [S5W, D], F32)
    wgg = dt("wgg", [S5W, D], F32)
    w4 = [dt("w4g", [D, DFF], F32), dt("w4u", [D, DFF], F32), dt("w4d", [DFF, D], F32)]
    lng = dt("lng", [2, D], F32)
    lnb = dt("lnb", [2, D], F32)
    ident = dt("ident", [128, 128], BF16)
    out = dt("out", [ntok, D], F32, "ExternalOutput")
    with ExitStack() as es:
        P = Prog(nc, es)
        T = TokPipe(nc, es, P, ident)
        for t in range(ntok // TT):
            t0 = t * TT
            T.load_x_only(x3[t0:t0 + TT, :])
            T.load_xt(zT[:, t0:t0 + TT], 8)
            T.load_ln(lng[0, :], lnb[0, :])

            def cons_glu(st, c0, b):
                k = T.ot % 2
                T.ot += 1
                sg = T.SG[:, k, 0:256]
                xs = T.X[:, st, c0:c0 + 256]
                P.op("act", lambda e: e.activation(out=sg, in_=T.PG[b][:, 0:256], func=AF.Sigmoid),
                     reads=[("ps", "G", b)], writes=[("SG", k)])
                P.op("dve", lambda e: e.tensor_tensor(out=sg, in0=sg, in1=T.PY[b][:, 0:256], op=ALU.mult),
                     reads=[("SG", k), ("ps", "Y", b)], writes=[("SG", k)])
                P.op("dve", lambda e: e.scalar_tensor_tensor(out=xs, in0=sg, scalar=1.0 / ALPHA, in1=xs,
                                                             op0=ALU.mult, op1=ALU.add),
                     reads=[("SG", k), ("X", st)], writes=[("X", st)])
            T.lin_tm(8, wgo, D, cons_glu, extra_w=wgg)
            T.layernorm(LN_EPS / (ALPHA * ALPHA))
            T.ffn(w4[0], w4[1], w4[2], lng[1, :], lnb[1, :])
            T.store_x(out[t0:t0 + TT, :])
        P.finish()
        P.emit()
    return nc


DIL = ((1, 16), (4, 4), (16, 1))
NEG = -30000.0


def l2_consts():
    p = np.arange(128)[:, None]
    q = np.arange(512)[None, :]
    fm = np.zeros((128, 4, 512), np.float32)
    for jj in range(4):
        fm[:, jj, :] = np.where(jj * 128 + p > q, NEG, 0.0)
    q1 = np.arange(128)[None, :]
    dm = np.zeros((128, 256), np.float32)
    dm[:, 0:128] = np.where(p < q1, NEG, 0.0)
    dm[:, 128:256] = np.where(p > q1, NEG, 0.0)
    uincl = (p <= q1).astype(np.float32)
    lstrict = (p < q1).astype(np.float32)
    bf = ml_dtypes.bfloat16
    return {
        "ident": _ident_np(), "ones_bf": np.ones((128, 128), bf), "fmask": fm.astype(bf),
        "dmask": dm.astype(bf), "uincl": uincl, "lstrict": lstrict, "ones_f": np.ones((128, 128), np.float32),
    }


def build_l2(seq=SEQ):
    nc = _new_nc()
    dt = lambda n, s, d, k="ExternalInput": nc.dram_tensor(n, s, d, kind=k).ap()
    nblk = seq // 128
    qf = dt("qf", [128, seq], BF16)
    kf = dt("kf", [128, seq], BF16)
    vf = dt("vf", [seq, 128], BF16)
    f2T = dt("f2T", [128, nblk], F32)
    nbf = dt("nbf", [128, 1], F32)
    qd = dt("qd", [128, seq], BF16)
    kd = dt("kd", [128, seq], BF16)
    vdp = [dt(f"vd{d}", [seq, 128], BF16) for (d, _) in DIL]
    c_ident = dt("ident", [128, 128], BF16)
    c_ones = dt("ones_bf", [128, 128], BF16)
    c_fm = dt("fmask", [128, 4, 512], BF16)
    c_dm = dt("dmask", [128, 256], BF16)
    c_ui = dt("uincl", [128, 128], F32)
    c_ls = dt("lstrict", [128, 128], F32)
    c_of = dt("ones_f", [128, 128], F32)
    yf = dt("yf", [128, seq], BF16, "ExternalOutput")
    yd = dt("yd", [128, seq], BF16, "ExternalOutput")
    scr = nc.dram_tensor("scr", [6, seq], BF16).ap()
    scale = HD ** -0.5
    with ExitStack() as es:
        P = Prog(nc, es)
        sb = lambda n, s, d: es.enter_context(nc.sbuf_tensor(n, s, d))
        ps = lambda n: es.enter_context(nc.psum_tensor(n, [128, 512], F32))
        BIG = [sb(f"BIG{i}", [128, seq], BF16) for i in range(4)]
        ident = sb("identb", [128, 128], BF16)
        ones = sb("onesb", [128, 128], BF16)
        FM = sb("FM", [128, 4, 512], BF16)
        DM = sb("DM", [128, 256], BF16)
        UI = sb("UI", [128, 128], F32)
        LS = sb("LS", [128, 128], F32)
        OFc = sb("OFc", [128, 128], F32)
        NB = sb("NB", [128, 1], F32)
        LF = sb("LF", [128, nblk], F32)
        RB = sb("RB", [128, 128], F32)
        C = sb("C", [128, 128], F32)
        R1 = sb("R1", [128, 128], F32)
        HI = sb("HI", [128, 6, 128], BF16)
        QF = sb("QF", [128, 2, 512], BF16)
        PTt = sb("PTt", [128, 3, 512], BF16)
        RL = sb("RL", [128, 512], F32)
        YO = sb("YO", [128, 2, 512], BF16)
        VU = sb("VU", [128, 4, 2, 128], BF16)
        ACC = sb("ACC", [128, 2, 2048], F32)
        YW = sb("YW", [128, 2048], BF16)
        PS = [ps("PS0"), ps("PS1")]
        PO = [ps("PO0"), ps("PO1")]
        PL = [ps("PL0"), ps("PL1")]
        for (t, src, key) in ((ident, c_ident, "ident"), (ones, c_ones, "ones"), (FM, c_fm, "FM"),
                              (DM, c_dm, "DM"), (UI, c_ui, "UI"), (LS, c_ls, "LS"), (OFc, c_of, "OFc"),
                              (NB, nbf, "NB"), (LF, f2T, "LF")):
            idx = (slice(None),) * len(t.shape)
            P.dma("sp", t[idx], src, writes=[(key,)])
        P.op("dve", lambda e: e.tensor_scalar(out=NB[:, :], in0=NB[:, :], scalar1=-1.0, scalar2=None, op0=ALU.mult),
             reads=[("NB",)], writes=[("NB",)])
        KT, VK, AK, AQ = BIG[0], BIG[1], BIG[2], BIG[3]
        P.dma("sp", KT[:, :], kf, writes=[("BIG", 0)])
        P.dma("sp", VK[:, :].rearrange("p (b d) -> p b d", d=128), vf.rearrange("(b p) d -> p b d", p=128),
              writes=[("BIG", 1)])
        P.op("act", lambda e: e.activation(out=LF[:, :], in_=LF[:, :], func=AF.Exp, bias=NB[:, :], scale=-1.0),
             reads=[("LF",), ("NB",)], writes=[("LF",)])
        P.op("act", lambda e: e.activation(out=LF[:, :], in_=LF[:, :], func=AF.Ln, bias=OFc[:, 0:1], scale=1.0),
             reads=[("LF",), ("OFc",)], writes=[("LF",)])
        P.op("dve", lambda e: e.tensor_scalar(out=LF[:, :], in0=LF[:, :], scalar1=-1.0, scalar2=None, op0=ALU.mult),
             reads=[("LF",)], writes=[("LF",)])
        P.op("pe", mm(PS[0][0:nblk, 0:128], LF[:, :], OFc[:, :], True, True),
             reads=[("LF",), ("OFc",)], writes=[("ps", "S", 0)])
        P.op("dve", lambda e: e.tensor_copy(out=RB[0:nblk, :], in_=PS[0][0:nblk, 0:128]),
             reads=[("ps", "S", 0)], writes=[("RB",)])
        P.op("pe", mm(PS[1][0:nblk, 0:128], LF[:, :], UI[:, :], True, False),
             reads=[("LF",), ("UI",)], writes=[("ps", "S", 1)], inc=False)
        P.op("pe", mm(PS[1][0:nblk, 0:128], LS[0:nblk, 0:nblk], RB[0:nblk, :], False, True),
             reads=[("RB",), ("LS",)], writes=[("ps", "S", 1)])
        P.op("dve", lambda e: e.tensor_scalar(out=C[0:nblk, :], in0=PS[1][0:nblk, 0:128], scalar1=1.0 / scale,
                                              scalar2=None, op0=ALU.mult),
             reads=[("ps", "S", 1)], writes=[("C",)])
        cur = C
        for i in range(3):
            P.op("dve", lambda e, i=i, cur=cur: e.tensor_copy(out=HI[0:nblk, 3 + i, :], in_=cur[0:nblk, :]),
                 reads=[("C",), ("R1",)], writes=[("HI", 3 + i)])
            P.op("dve", lambda e, i=i: e.tensor_scalar(out=HI[0:nblk, i, :], in0=HI[0:nblk, 3 + i, :],
                                                       scalar1=-1.0, scalar2=None, op0=ALU.mult),
                 reads=[("HI", 3 + i)], writes=[("HI", i)])
            if i < 2:
                P.op("dve", lambda e, i=i, cur=cur: e.tensor_tensor(out=R1[0:nblk, :], in0=cur[0:nblk, :],
                                                                    in1=HI[0:nblk, 3 + i, :], op=ALU.subtract),
                     reads=[("C",), ("R1",), ("HI", 3 + i)], writes=[("R1",)])
                cur = R1
        for i in range(6):
            P.dma("sp", scr[i, :].rearrange("(p j) -> p j", j=128), HI[0:nblk, i, :],
                  reads=[("HI", i)], writes=[("scr", i)])
        P.op("pool", lambda e: e.memset(AK[0:6, :], 1.0), writes=[("BIG", 2)])
        P.op("pool", lambda e: e.memset(AQ[0:6, :], 1.0), writes=[("BIG", 3)])
        P.dma("sp", AK[0:3, :], scr[0:3, :], reads=[("scr", i) for i in range(3)], writes=[("BIG", 2)])
        P.dma("sp", AQ[3:6, :], scr[3:6, :], reads=[("scr", 3 + i) for i in range(3)], writes=[("BIG", 3)])
        pt = 0
        for i in range(seq // 512):
            qb = i % 2
            P.dma("sp", QF[:, qb, :], qf[:, i * 512:(i + 1) * 512], writes=[("QF", qb)])
            nj = 4 * i + 4
            for j in range(nj):
                sbk = j % 2
                diag = j >= 4 * i
                P.op("pe", mm(PS[sbk][:, :], KT[:, j * 128:(j + 1) * 128], QF[:, qb, :], True, False),
                     reads=[("BIG", 0), ("QF", qb)], writes=[("ps", "S", sbk)], inc=False)
                P.op("pe", mm(PS[sbk][:, :], AK[0:6, j * 128:(j + 1) * 128], AQ[0:6, i * 512:(i + 1) * 512],
                              False, not diag),
                     reads=[("BIG", 2), ("BIG", 3)], writes=[("ps", "S", sbk)], inc=not diag)
                if diag:
                    P.op("pe", mm(PS[sbk][:, :], ident[:, :], FM[:, j - 4 * i, :], False, True),
                         reads=[("ident",), ("FM",)], writes=[("ps", "S", sbk)])
                pk = pt % 3
                pt += 1
                P.op("act", lambda e, sbk=sbk, pk=pk: e.activation(out=PTt[:, pk, :], in_=PS[sbk][:, :],
                                                                   func=AF.Exp, scale=scale),
                     reads=[("ps", "S", sbk)], writes=[("PT", pk)])
                P.op("pe", mm(PO[qb][:, :], VK[:, j * 128:(j + 1) * 128], PTt[:, pk, :], j == 0, j == nj - 1),
                     reads=[("BIG", 1), ("PT", pk)], writes=[("ps", "O", qb)], inc=False)
                P.op("pe", mm(PL[qb][:, :], ones[:, :], PTt[:, pk, :], j == 0, j == nj - 1),
                     reads=[("ones",), ("PT", pk)], writes=[("ps", "L", qb)])
            P.op("dve", lambda e, qb=qb: e.reciprocal(out=RL[:, :], in_=PL[qb][:, :]),
                 reads=[("ps", "L", qb)], writes=[("RL",)])
            P.op("dve", lambda e, qb=qb: e.tensor_tensor(out=YO[:, qb, :], in0=RL[:, :], in1=PO[qb][:, :],
                                                         op=ALU.mult),
                 reads=[("RL",), ("ps", "O", qb)], writes=[("YO", qb)])
            P.dma("sp", yf[:, i * 512:(i + 1) * 512], YO[:, qb, :], reads=[("YO", qb)])
        QD, KD = BIG[0], BIG[1]
        P.dma("sp", QD[:, :], qd, writes=[("BIG", 0)])
        P.dma("sp", KD[:, :], kd, writes=[("BIG", 1)])
        un = 0
        for w in range(seq // 2048):
            for bi, (d, nb) in enumerate(DIL):
                nbt = seq // (128 * d)
                qv = QD[:, :].rearrange("p (l r) -> p r l", r=d)
                kv = KD[:, :].rearrange("p (l r) -> p r l", r=d)
                ao = ACC[:, 0, :].rearrange("p (l r) -> p r l", r=d)
                al = ACC[:, 1, :].rearrange("p (l r) -> p r l", r=d)
                for r in range(d):
                    for bl in range(nb):
                        Bg = nb * w + bl
                        hp = Bg > 0
                        k = un % 4
                        sbk = un % 2
                        un += 1
                        row0 = (r * nbt + Bg) * 128
                        if hp:
                            P.dma("sp", VU[:, k, :, :],
                                  vdp[bi][row0 - 128:row0 + 128, :].rearrange("(t p) e -> p t e", p=128),
                                  writes=[("VU", k)])
                        else:
                            P.dma("sp", VU[:, k, 1, :], vdp[bi][row0:row0 + 128, :], writes=[("VU", k)])
                        qs = qv[:, r, Bg * 128:(Bg + 1) * 128]
                        lo = 0 if hp else 128
                        P.op("pe", mm(PS[sbk][:, lo:256], ident[:, :], DM[:, lo:256], True, False),
                             reads=[("ident",), ("DM",)], writes=[("ps", "S", sbk)], inc=False)
                        if hp:
                            P.op("pe", mm(PS[sbk][:, 0:128], kv[:, r, (Bg - 1) * 128:Bg * 128], qs, False, False),
                                 reads=[("BIG", 0), ("BIG", 1)], writes=[("ps", "S", sbk)], inc=False)
                        P.op("pe", mm(PS[sbk][:, 128:256], kv[:, r, Bg * 128:(Bg + 1) * 128], qs, False, True),
                             reads=[("BIG", 0), ("BIG", 1)], writes=[("ps", "S", sbk)])
                        pk = pt % 3
                        pt += 1
                        P.op("act", lambda e, sbk=sbk, pk=pk, lo=lo: e.activation(
                            out=PTt[:, pk, lo:256], in_=PS[sbk][:, lo:256], func=AF.Exp, scale=scale),
                             reads=[("ps", "S", sbk)], writes=[("PT", pk)])
                        if hp:
                            P.op("pe", mm(PO[sbk][:, 0:128], VU[:, k, 0, :], PTt[:, pk, 0:128], True, False),
                                 reads=[("VU", k), ("PT", pk)], writes=[("ps", "O", sbk)], inc=False)
                        P.op("pe", mm(PO[sbk][:, 0:128], VU[:, k, 1, :], PTt[:, pk, 128:256], not hp, True),
                             reads=[("VU", k), ("PT", pk)], writes=[("ps", "O", sbk)], inc=False)
                        if hp:
                            P.op("pe", mm(PL[sbk][:, 0:128], ones[:, :], PTt[:, pk, 0:128], True, False),
                                 reads=[("ones",), ("PT", pk)], writes=[("ps", "L", sbk)], inc=False)
                        P.op("pe", mm(PL[sbk][:, 0:128], ones[:, :], PTt[:, pk, 128:256], not hp, True),
                             reads=[("ones",), ("PT", pk)], writes=[("ps", "L", sbk)])
                        do = ao[:, r, bl * 128:(bl + 1) * 128]
                        dl = al[:, r, bl * 128:(bl + 1) * 128]
                        if bi == 0:
                            P.op("dve", lambda e, do=do, sbk=sbk: e.tensor_copy(out=do, in_=PO[sbk][:, 0:128]),
                                 reads=[("ps", "O", sbk)], writes=[("ACC",)])
                            P.op("dve", lambda e, dl=dl, sbk=sbk: e.tensor_copy(out=dl, in_=PL[sbk][:, 0:128]),
                                 reads=[("ps", "L", sbk)], writes=[("ACC",)])
                        else:
                            P.op("dve", lambda e, do=do, sbk=sbk: e.tensor_tensor(out=do, in0=do, in1=PO[sbk][:, 0:128],
                                                                                  op=ALU.add),
                                 reads=[("ps", "O", sbk), ("ACC",)], writes=[("ACC",)])
                            P.op("dve", lambda e, dl=dl, sbk=sbk: e.tensor_tensor(out=dl, in0=dl, in1=PL[sbk][:, 0:128],
                                                                                  op=ALU.add),
                                 reads=[("ps", "L", sbk), ("ACC",)], writes=[("ACC",)])
            P.op("dve", lambda e: e.reciprocal(out=ACC[:, 1, :], in_=ACC[:, 1, :]),
                 reads=[("ACC",)], writes=[("ACC",)])
            P.op("dve", lambda e: e.tensor_tensor(out=YW[:, :], in0=ACC[:, 0, :], in1=ACC[:, 1, :], op=ALU.mult),
                 reads=[("ACC",)], writes=[("YW",)])
            P.dma("sp", yd[:, w * 2048:(w + 1) * 2048], YW[:, :], reads=[("YW",)])
        P.finish()
        P.emit()
    return nc


def attn_phase(nc, P, es, A, scr, fsc, aug, GYF, GYD):
    seq = SEQ
    nblk = seq // 128
    qf, kf, vfT, qd, kd, vdT = scr
    nbf = A["nbf"]
    c_ident, c_ones, c_fm, c_dm = A["ident"], A["ones_bf"], A["fmask"], A["dmask"]
    c_ui, c_ls, c_of, c_idf = A["uincl"], A["lstrict"], A["ones_f"], A["ident_f"]
    scale = HD ** -0.5
    if True:
        sb = lambda n, s, d: es.enter_context(nc.sbuf_tensor(n + "_p2b", s, d))
        ps = lambda n: es.enter_context(nc.psum_tensor(n + "_p2b", [128, 512], F32))
        BIG = [sb(f"BIG{i}", [128, seq], BF16) for i in range(4)]
        ident = sb("identb", [128, 128], BF16)
        ones = sb("onesb", [128, 128], BF16)
        FM = sb("FM", [128, 4, 512], BF16)
        DM = sb("DM", [128, 256], BF16)
        UI = sb("UI", [128, 128], F32)
        LS = sb("LS", [128, 128], F32)
        OFc = sb("OFc", [128, 128], F32)
        NB = sb("NB", [128, 1], F32)
        LF = sb("LF", [128, nblk], F32)
        LFR = sb("LFR", [128, 128], F32)
        IDF = sb("IDF", [128, 128], F32)
        PVb = [ps("PV0"), ps("PV1")]
        RB = sb("RB", [128, 128], F32)
        C = sb("C", [128, 128], F32)
        R1 = sb("R1", [128, 128], F32)
        HI = sb("HI", [128, 6, 128], BF16)
        QF = sb("QF", [128, 2, 512], BF16)
        PTt = sb("PTt", [128, 3, 512], BF16)
        RL = sb("RL", [128, 512], F32)
        YO = sb("YO", [128, 2, 512], BF16)
        VU = sb("VU", [128, 4, 2, 128], BF16)
        ACC = sb("ACC", [128, 2, 2048], F32)
        YW = sb("YW", [128, 2048], BF16)
        PS = [ps("PS0"), ps("PS1")]
        PO = [ps("PO0"), ps("PO1")]
        PL = [ps("PL0"), ps("PL1")]
        for (t, src, key) in ((ident, c_ident, "ident"), (ones, c_ones, "ones"), (FM, c_fm, "FM"),
                              (DM, c_dm, "DM"), (UI, c_ui, "UI"), (LS, c_ls, "LS"), (OFc, c_of, "OFc"),
                              (NB, nbf, "NB"), (IDF, c_idf, "IDF"),
                              (LFR, fsc.rearrange("o (p j) -> (o p) j", j=128), "LFR")):
            idx = (slice(None),) * len(t.shape)
            P.dma("sp", t[idx], src, writes=[(key,)])
        P.op("dve", lambda e: e.tensor_scalar(out=NB[:, :], in0=NB[:, :], scalar1=-1.0, scalar2=None, op0=ALU.mult),
             reads=[("NB",)], writes=[("NB",)])
        KT, VK, AK, AQ = BIG[0], BIG[1], BIG[2], BIG[3]
        P.dma("sp", KT[:, :], kf, writes=[("BIG", 0)])
        P.dma("sp", AQ[:, :], vfT, writes=[("BIG", 3)])
        for bq in range(seq // 512):
            pv = bq % 2
            for j in range(4):
                blk = bq * 4 + j
                P.op("pe", mm(PVb[pv][:, j * 128:(j + 1) * 128], AQ[:, blk * 128:(blk + 1) * 128], ident[:, :],
                              True, True),
                     reads=[("BIG", 3), ("ident",)], writes=[("ps", "V", pv)], inc=(j == 3))
            P.op("act", lambda e, pv=pv, bq=bq: e.activation(out=VK[:, bq * 512:(bq + 1) * 512], in_=PVb[pv][:, :],
                                                            func=AF.Copy),
                 reads=[("ps", "V", pv)], writes=[("BIG", 1)])
        P.op("act", lambda e: e.activation(out=LFR[:, :], in_=LFR[:, :], func=AF.Exp, bias=NB[:, :], scale=-1.0),
             reads=[("LFR",), ("NB",)], writes=[("LFR",)])
        P.op("act", lambda e: e.activation(out=LFR[:, :], in_=LFR[:, :], func=AF.Ln, bias=OFc[:, 0:1], scale=1.0),
             reads=[("LFR",), ("OFc",)], writes=[("LFR",)])
        P.op("dve", lambda e: e.tensor_scalar(out=LFR[:, :], in0=LFR[:, :], scalar1=-1.0, scalar2=None, op0=ALU.mult),
             reads=[("LFR",)], writes=[("LFR",)])
        P.op("pe", mm(PS[0][:, 0:128], LFR[:, :], IDF[:, :], True, True),
             reads=[("LFR",), ("IDF",)], writes=[("ps", "S", 0)])
        P.op("dve", lambda e: e.tensor_copy(out=LF[:, :], in_=PS[0][:, 0:128]),
             reads=[("ps", "S", 0)], writes=[("LF",)])
        P.op("pe", mm(PS[0][0:nblk, 0:128], LF[:, :], OFc[:, :], True, True),
             reads=[("LF",), ("OFc",)], writes=[("ps", "S", 0)])
        P.op("dve", lambda e: e.tensor_copy(out=RB[0:nblk, :], in_=PS[0][0:nblk, 0:128]),
             reads=[("ps", "S", 0)], writes=[("RB",)])
        P.op("pe", mm(PS[1][0:nblk, 0:128], LF[:, :], UI[:, :], True, False),
             reads=[("LF",), ("UI",)], writes=[("ps", "S", 1)], inc=False)
        P.op("pe", mm(PS[1][0:nblk, 0:128], LS[0:nblk, 0:nblk], RB[0:nblk, :], False, True),
             reads=[("RB",), ("LS",)], writes=[("ps", "S", 1)])
        P.op("dve", lambda e: e.tensor_scalar(out=C[0:nblk, :], in0=PS[1][0:nblk, 0:128], scalar1=1.0 / scale,
                                              scalar2=None, op0=ALU.mult),
             reads=[("ps", "S", 1)], writes=[("C",)])
        cur = C
        for i in range(3):
            P.op("dve", lambda e, i=i, cur=cur: e.tensor_copy(out=HI[0:nblk, 3 + i, :], in_=cur[0:nblk, :]),
                 reads=[("C",), ("R1",)], writes=[("HI", 3 + i)])
            P.op("dve", lambda e, i=i: e.tensor_scalar(out=HI[0:nblk, i, :], in0=HI[0:nblk, 3 + i, :],
                                                       scalar1=-1.0, scalar2=None, op0=ALU.mult),
                 reads=[("HI", 3 + i)], writes=[("HI", i)])
            if i < 2:
                P.op("dve", lambda e, i=i, cur=cur: e.tensor_tensor(out=R1[0:nblk, :], in0=cur[0:nblk, :],
                                                                    in1=HI[0:nblk, 3 + i, :], op=ALU.subtract),
                     reads=[("C",), ("R1",), ("HI", 3 + i)], writes=[("R1",)])
                cur = R1
        for i in range(6):
            P.dma("sp", aug[i, :].rearrange("(p j) -> p j", j=128), HI[0:nblk, i, :],
                  reads=[("HI", i)], writes=[("scr", i)])
        P.op("pool", lambda e: e.memset(AK[0:6, :], 1.0), writes=[("BIG", 2)])
        P.op("pool", lambda e: e.memset(AQ[0:6, :], 1.0), writes=[("BIG", 3)])
        P.dma("sp", AK[0:3, :], aug[0:3, :], reads=[("scr", i) for i in range(3)], writes=[("BIG", 2)])
        P.dma("sp", AQ[3:6, :], aug[3:6, :], reads=[("scr", 3 + i) for i in range(3)], writes=[("BIG", 3)])
        pairs = [(i, j) for i in range(seq // 512) for j in range(4 * i + 4)]
        pt = 0

        def fox_scores(n):
            i, j = pairs[n]
            qb, sbk = i % 2, n % 2
            if j == 0:
                P.dma("sp", QF[:, qb, :], qf[:, i * 512:(i + 1) * 512], writes=[("QF", qb)])
            diag = j >= 4 * i
            P.op("pe", mm(PS[sbk][:, :], KT[:, j * 128:(j + 1) * 128], QF[:, qb, :], True, False),
                 reads=[("BIG", 0), ("QF", qb)], writes=[("ps", "S", sbk)], inc=False)
            P.op("pe", mm(PS[sbk][:, :], AK[0:6, j * 128:(j + 1) * 128], AQ[0:6, i * 512:(i + 1) * 512],
                          False, not diag),
                 reads=[("BIG", 2), ("BIG", 3)], writes=[("ps", "S", sbk)], inc=not diag)
            if diag:
                P.op("pe", mm(PS[sbk][:, :], ident[:, :], FM[:, j - 4 * i, :], False, True),
                     reads=[("ident",), ("FM",)], writes=[("ps", "S", sbk)])

        fox_scores(0)
        for n, (i, j) in enumerate(pairs):
            qb, sbk = i % 2, n % 2
            nj = 4 * i + 4
            if n + 1 < len(pairs):
                fox_scores(n + 1)
            pk = pt % 3
            pt += 1
            P.op("act", lambda e, sbk=sbk, pk=pk: e.activation(out=PTt[:, pk, :], in_=PS[sbk][:, :],
                                                               func=AF.Exp, scale=scale),
                 reads=[("ps", "S", sbk)], writes=[("PT", pk)])
            P.op("pe", mm(PO[qb][:, :], VK[:, j * 128:(j + 1) * 128], PTt[:, pk, :], j == 0, j == nj - 1),
                 reads=[("BIG", 1), ("PT", pk)], writes=[("ps", "O", qb)], inc=False)
            P.op("pe", mm(PL[qb][:, :], ones[:, :], PTt[:, pk, :], j == 0, j == nj - 1),
                 reads=[("ones",), ("PT", pk)], writes=[("ps", "L", qb)])
            if j == nj - 1:
                P.op("dve", lambda e, qb=qb: e.reciprocal(out=RL[:, :], in_=PL[qb][:, :]),
                     reads=[("ps", "L", qb)], writes=[("RL",)])
                P.op("dve", lambda e, qb=qb: e.tensor_tensor(out=YO[:, qb, :], in0=RL[:, :], in1=PO[qb][:, :],
                                                             op=ALU.mult),
                     reads=[("RL",), ("ps", "O", qb)], writes=[("YO", qb)])
                wdx = i // 4
                P.dma("sp", GYF.src[wdx][:, (i % 4) * 512:(i % 4 + 1) * 512], YO[:, qb, :], reads=[("YO", qb)],
                      writes=[("GYF", "s", wdx, i % 4)])
                if i % 4 == 3:
                    P.collective(G4, GYF.src[wdx], GYF.a[wdx], reads=[("GYF", "s", wdx, k4) for k4 in range(4)],
                                 writes=[("GYF", "a", wdx)])
                    if wdx > 0:
                        GYF.s2(wdx - 1)
        GYF.s2(seq // 2048 - 1)
        QD, KD, VD = BIG[0], BIG[1], BIG[2]
        P.dma("sp", QD[:, :], qd, writes=[("BIG", 0)])
        P.dma("sp", KD[:, :], kd, writes=[("BIG", 1)])
        P.dma("sp", VD[:, :], vdT, writes=[("BIG", 2)])
        units = []
        for w in range(seq // 2048):
            for bi, (d, nb) in enumerate(DIL):
                for r in range(d):
                    for bl in range(nb):
                        units.append((w, bi, d, nb, r, bl))

        def dil_front(un):
            w, bi, d, nb, r, bl = units[un]
            qv = QD[:, :].rearrange("p (l r) -> p r l", r=d)
            kv = KD[:, :].rearrange("p (l r) -> p r l", r=d)
            vv = VD[:, :].rearrange("p (l r) -> p r l", r=d)
            Bg = nb * w + bl
            hp = Bg > 0
            k, sbk = un % 4, un % 2
            if hp:
                P.op("pe", mm(PVb[sbk][:, 0:128], vv[:, r, (Bg - 1) * 128:Bg * 128], ident[:, :], True, True),
                     reads=[("BIG", 2), ("ident",)], writes=[("ps", "V", sbk)], inc=False)
            P.op("pe", mm(PVb[sbk][:, 128:256], vv[:, r, Bg * 128:(Bg + 1) * 128], ident[:, :], True, True),
                 reads=[("BIG", 2), ("ident",)], writes=[("ps", "V", sbk)])
            vlo = 0 if hp else 1
            P.op("act", lambda e, k=k, sbk=sbk, vlo=vlo: e.activation(
                out=VU[:, k, vlo:2, :], in_=PVb[sbk][:, vlo * 128:256].rearrange("p (t e) -> p t e", e=128),
                func=AF.Copy),
                 reads=[("ps", "V", sbk)], writes=[("VU", k)])
            qs = qv[:, r, Bg * 128:(Bg + 1) * 128]
            lo = 0 if hp else 128
            P.op("pe", mm(PS[sbk][:, lo:256], ident[:, :], DM[:, lo:256], True, False),
                 reads=[("ident",), ("DM",)], writes=[("ps", "S", sbk)], inc=False)
            if hp:
                P.op("pe", mm(PS[sbk][:, 0:128], kv[:, r, (Bg - 1) * 128:Bg * 128], qs, False, False),
                     reads=[("BIG", 0), ("BIG", 1)], writes=[("ps", "S", sbk)], inc=False)
            P.op("pe", mm(PS[sbk][:, 128:256], kv[:, r, Bg * 128:(Bg + 1) * 128], qs, False, True),
                 reads=[("BIG", 0), ("BIG", 1)], writes=[("ps", "S", sbk)])

        dil_front(0)
        for un, (w, bi, d, nb, r, bl) in enumerate(units):
            if un + 1 < len(units):
                dil_front(un + 1)
            Bg = nb * w + bl
            hp = Bg > 0
            k, sbk = un % 4, un % 2
            lo = 0 if hp else 128
            ao = ACC[:, 0, :].rearrange("p (l r) -> p r l", r=d)
            al = ACC[:, 1, :].rearrange("p (l r) -> p r l", r=d)
            pk = pt % 3
            pt += 1
            P.op("act", lambda e, sbk=sbk, pk=pk, lo=lo: e.activation(
                out=PTt[:, pk, lo:256], in_=PS[sbk][:, lo:256], func=AF.Exp, scale=scale),
                 reads=[("ps", "S", sbk)], writes=[("PT", pk)])
            if hp:
                P.op("pe", mm(PO[sbk][:, 0:128], VU[:, k, 0, :], PTt[:, pk, 0:128], True, False),
                     reads=[("VU", k), ("PT", pk)], writes=[("ps", "O", sbk)], inc=False)
            P.op("pe", mm(PO[sbk][:, 0:128], VU[:, k, 1, :], PTt[:, pk, 128:256], not hp, True),
                 reads=[("VU", k), ("PT", pk)], writes=[("ps", "O", sbk)], inc=False)
            if hp:
                P.op("pe", mm(PL[sbk][:, 0:128], ones[:, :], PTt[:, pk, 0:128], True, False),
                     reads=[("ones",), ("PT", pk)], writes=[("ps", "L", sbk)], inc=False)
            P.op("pe", mm(PL[sbk][:, 0:128], ones[:, :], PTt[:, pk, 128:256], not hp, True),
                 reads=[("ones",), ("PT", pk)], writes=[("ps", "L", sbk)])
            do = ao[:, r, bl * 128:(bl + 1) * 128]
            dl = al[:, r, bl * 128:(bl + 1) * 128]
            if bi == 0:
                P.op("dve", lambda e, do=do, sbk=sbk: e.tensor_copy(out=do, in_=PO[sbk][:, 0:128]),
                     reads=[("ps", "O", sbk)], writes=[("ACC",)])
                P.op("dve", lambda e, dl=dl, sbk=sbk: e.tensor_copy(out=dl, in_=PL[sbk][:, 0:128]),
                     reads=[("ps", "L", sbk)], writes=[("ACC",)])
            else:
                P.op("dve", lambda e, do=do, sbk=sbk: e.tensor_tensor(out=do, in0=do, in1=PO[sbk][:, 0:128],
                                                                      op=ALU.add),
                     reads=[("ps", "O", sbk), ("ACC",)], writes=[("ACC",)])
                P.op("dve", lambda e, dl=dl, sbk=sbk: e.tensor_tensor(out=dl, in0=dl, in1=PL[sbk][:, 0:128],
                                                                      op=ALU.add),
                     reads=[("ps", "L", sbk), ("ACC",)], writes=[("ACC",)])
            last_in_window = (un + 1 == len(units)) or (units[un + 1][0] != w)
            if last_in_window:
                P.op("dve", lambda e: e.reciprocal(out=ACC[:, 1, :], in_=ACC[:, 1, :]),
                     reads=[("ACC",)], writes=[("ACC",)])
                P.op("dve", lambda e: e.tensor_tensor(out=YW[:, :], in0=ACC[:, 0, :], in1=ACC[:, 1, :], op=ALU.mult),
                     reads=[("ACC",)], writes=[("YW",)])
                P.dma("sp", GYD.src[w], YW[:, :], reads=[("YW",)], writes=[("GYD", "s", w)])
                GYD.s1(w)
                if w > 0:
                    GYD.s2(w - 1)
        GYD.s2(seq // 2048 - 1)


def dil_perm(seq, d):
    nbt = seq // (128 * d)
    r = np.arange(d)[:, None, None]
    B = np.arange(nbt)[None, :, None]
    p = np.arange(128)[None, None, :]
    return (r + d * (128 * B + p)).reshape(-1)


NG = 8
TWO_PI = 2.0 * math.pi


def l4_consts():
    p = np.arange(128)[:, None]
    t = np.arange(128)[None, :]
    m0 = (np.arange(128) < 64).astype(np.float32)[:, None]
    mask4 = np.broadcast_to((t >= p).astype(np.float32)[:, None, :], (128, 4, 128)).copy()
    ti = np.broadcast_to(np.arange(130, dtype=np.float32)[None, :], (128, 130)).copy()
    tir = np.broadcast_to((127.0 - np.arange(128, dtype=np.float32))[None, :], (128, 128)).copy()
    sg = np.concatenate([m0, 1.0 - m0, -m0, -(1.0 - m0), np.full_like(m0, 0.5 * math.pi)], 1)
    return {"mask4": mask4, "ti": ti, "tir": tir, "sg": sg, "ident": _ident_np()}


def build_l4(seq=SEQ):
    nc = _new_nc()
    dt = lambda n, s, d, k="ExternalInput": nc.dram_tensor(n, s, d, kind=k).ap()
    nch = seq // 128
    uS = dt("uS", [128, nch, 128], F32)
    uG = dt("uG", [NG, 128, nch, 16], F32)
    lr_in = dt("lam_re2", [128, NG], F32)
    li_in = dt("lam_im2", [128, NG], F32)
    ldt_in = dt("logdt2", [128, NG], F32)
    br_in = dt("br2", [128, NG, 16], F32)
    bi_in = dt("bi2", [128, NG, 16], F32)
    cr_in = dt("cr2", [128, NG, 16], F32)
    ci_in = dt("ci2", [128, NG, 16], F32)
    d_in = dt("d2", [128, 128], F32)
    c_mask = dt("mask4", [128, 4, 128], F32)
    c_ti = dt("ti", [128, 130], F32)
    c_tir = dt("tir", [128, 128], F32)
    c_sg = dt("sg", [128, 5], F32)
    c_ident = dt("ident", [128, 128], BF16)
    zout = dt("zout", [NG, 128, nch, 16], BF16, "ExternalOutput")
    with ExitStack() as es:
        P = Prog(nc, es)
        sb = lambda n, s, d: es.enter_context(nc.sbuf_tensor(n, s, d))
        ps = lambda n: es.enter_context(nc.psum_tensor(n, [128, 512], F32))
        U16 = sb("U16", [128, nch, 128], BF16)
        UF = sb("UF", [128, 1, nch, 16], F32)
        TS = sb("TS", [128, 16, 16, 128], BF16)
        CF = sb("CF", [128, 1, 16, 129], BF16)
        BFc = sb("BFc", [128, 16, 128], BF16)
        WF = sb("WF", [128, 16, 128], BF16)
        M1 = sb("M1", [128, 16, 128], BF16)
        YS = sb("YS", [128, 1, nch, 16], F32)
        ZS = sb("ZS", [128, 1, nch, 16], BF16)
        G1 = sb("G1", [128, nch * 16], F32)
        VST = sb("VST", [128, NG, nch], F32)
        VI = sb("VI", [64, NG, nch], F32)
        SC = [[sb(f"SC{a}{b}", [64, NG, nch], F32) for b in range(2)] for a in range(2)]
        XP = sb("XP", [128, NG, nch], BF16)
        XPI = sb("XPI", [64, NG, nch], BF16)
        LR = sb("LR", [128, NG], F32)
        LI = sb("LI", [128, NG], F32)
        DTt = sb("DTt", [128, NG], F32)
        LDR = sb("LDR", [128, NG], F32)
        LDI = sb("LDI", [128, NG], F32)
        NLDR = sb("NLDR", [128, NG], F32)
        SM = sb("SM", [128, 12, NG], F32)
        BR = sb("BR", [128, NG, 16], F32)
        BI = sb("BI", [128, NG, 16], F32)
        CR = sb("CR", [128, NG, 16], F32)
        CI = sb("CI", [128, NG, 16], F32)
        BBR = sb("BBR", [128, NG, 16], F32)
        BBI = sb("BBI", [128, NG, 16], F32)
        CA = sb("CA", [128, NG, 16], F32)
        CB = sb("CB", [128, NG, 16], F32)
        BA = sb("BA", [128, NG, 16], F32)
        BB = sb("BB", [128, NG, 16], F32)
        D2 = sb("D2", [128, 128], F32)
        MASK = sb("MASK", [128, 4, 128], F32)
        TI = sb("TI", [128, 130], F32)
        TIR = sb("TIR", [128, 128], F32)
        SGN = sb("SGN", [128, 5], F32)
        ident = sb("identb", [128, 128], BF16)
        TB = sb("TB", [128, 6, 130], F32)
        AW = sb("AW", [128, 2, NG], F32)
        WW = sb("WW", [64, 2, 3, NG], F32)
        PA = [ps(f"PA{i}") for i in range(4)]
        PYb = [ps("PYa"), ps("PYb")]
        PV = ps("PV")
        loads = ((LR, lr_in, "LR"), (LI, li_in, "LI"), (DTt, ldt_in, "DT"), (BR, br_in, "BR"), (BI, bi_in, "BI"),
                 (CR, cr_in, "CR"), (CI, ci_in, "CI"), (D2, d_in, "D2"), (MASK, c_mask, "MASK"), (TI, c_ti, "TI"),
                 (TIR, c_tir, "TIR"), (SGN, c_sg, "SGN"), (ident, c_ident, "ident"))
        for (t, src, key) in loads:
            idx = (slice(None),) * len(t.shape)
            P.dma("sp", t[idx], src, writes=[(key,)])
        P.dma("pool", U16[:, :, :], uS, writes=[("U16",)])
        m0, m1, nm0, nm1 = SGN[:, 0:1], SGN[:, 1:2], SGN[:, 2:3], SGN[:, 3:4]

        def dve(fn, reads, writes):
            P.op("dve", fn, reads=reads, writes=writes)

        def tt(out, a, b, op, reads, writes, eng="dve"):
            P.op(eng, lambda e: e.tensor_tensor(out=out, in0=a, in1=b, op=op), reads=reads, writes=writes)

        def tsc(out, a, s1, op0, reads, writes, s2=None, op1=None, eng="dve"):
            if op1 is None:
                P.op(eng, lambda e: e.tensor_scalar(out=out, in0=a, scalar1=s1, scalar2=None, op0=op0),
                     reads=reads, writes=writes)
            else:
                P.op(eng, lambda e: e.tensor_scalar(out=out, in0=a, scalar1=s1, scalar2=s2, op0=op0, op1=op1),
                     reads=reads, writes=writes)

        def stt(out, a, s, b, op0, op1, reads, writes, eng="dve"):
            P.op(eng, lambda e: e.scalar_tensor_tensor(out=out, in0=a, scalar=s, in1=b, op0=op0, op1=op1),
                 reads=reads, writes=writes)

        def act(out, a, func, reads, writes, bias=None, scale=None):
            kw = {}
            if bias is not None:
                kw["bias"] = bias
            if scale is not None:
                kw["scale"] = scale
            P.op("act", lambda e: e.activation(out=out, in_=a, func=func, **kw), reads=reads, writes=writes)

        def cp(out, a, reads, writes, eng="dve"):
            P.op(eng, lambda e: e.tensor_copy(out=out, in_=a), reads=reads, writes=writes)

        RI = sb("RI", [128, 130], mybir.dt.int32)
        RF = sb("RF", [128, 2, 130], F32)

        def reduce_angle(arg, shift, out, rk, n):
            t, tf = RF[:, 0, 0:n], RF[:, 1, 0:n]
            ti = RI[:, 0:n]
            tsc(t, arg, 1.0 / TWO_PI, ALU.mult, rk, [("RF", 0)], s2=0.5 + shift, op1=ALU.add)
            cp(ti, t, [("RF", 0)], [("RI",)])
            cp(tf, ti, [("RI",)], [("RF", 1)])
            tt(t, t, tf, ALU.subtract, [("RF", 0), ("RF", 1)], [("RF", 0)])
            tsc(t, t, -0.5, ALU.add, [("RF", 0)], [("RF", 0)], s2=TWO_PI, op1=ALU.mult)
            tsc(tf, t, -math.pi, ALU.is_lt, [("RF", 0)], [("RF", 1)])
            stt(t, tf, TWO_PI, t, ALU.mult, ALU.add, [("RF", 0), ("RF", 1)], [("RF", 0)])
            tsc(tf, t, math.pi, ALU.is_gt, [("RF", 0)], [("RF", 1)])
            stt(out, tf, -TWO_PI, t, ALU.mult, ALU.add, [("RF", 0), ("RF", 1)], [("RF", 0)])

        def sincos(arg, sin_out, cos_out, tmp, rk, wk_s, wk_c, tk):
            n = arg.shape[-1]
            reduce_angle(arg, 0.0, RF[:, 0, 0:n], rk, n)
            act(sin_out, RF[:, 0, 0:n], AF.Sin, [("RF", 0)], wk_s)
            tsc(RF[:, 1, 0:n], RF[:, 0, 0:n], -1.0, ALU.mult, [("RF", 0)], [("RF", 1)])
            tt(RF[:, 1, 0:n], RF[:, 0, 0:n], RF[:, 1, 0:n], ALU.max, [("RF", 0), ("RF", 1)], [("RF", 1)])
            act(cos_out, RF[:, 1, 0:n], AF.Sin, [("RF", 1), ("SGN",)], wk_c, bias=SGN[:, 4:5], scale=-1.0)

        S = lambda i: SM[:, i, :]
        act(DTt[:, :], DTt[:, :], AF.Exp, [("DT",)], [("DT",)])
        tt(LDR[:, :], LR[:, :], DTt[:, :], ALU.mult, [("LR",), ("DT",)], [("LDR",)])
        tt(LDI[:, :], LI[:, :], DTt[:, :], ALU.mult, [("LI",), ("DT",)], [("LDI",)])
        tsc(NLDR[:, :], LDR[:, :], -1.0, ALU.mult, [("LDR",)], [("NLDR",)])
        tsc(S(5), LDI[:, :], TWO_PI, ALU.add, [("LDI",)], [("S", 5)])
        sincos(S(5), S(0), S(1), S(6), [("S", 5)], [("S", 0)], [("S", 1)], ("S", 6))
        act(S(2), LDR[:, :], AF.Exp, [("LDR",)], [("S", 2)])
        tt(S(3), S(2), S(1), ALU.mult, [("S", 2), ("S", 1)], [("S", 3)])
        tsc(S(3), S(3), -1.0, ALU.add, [("S", 3)], [("S", 3)])
        tt(S(4), S(2), S(0), ALU.mult, [("S", 2), ("S", 0)], [("S", 4)])
        tt(S(5), LR[:, :], LR[:, :], ALU.mult, [("LR",)], [("S", 5)])
        tt(S(6), LI[:, :], LI[:, :], ALU.mult, [("LI",)], [("S", 6)])
        tt(S(5), S(5), S(6), ALU.add, [("S", 5), ("S", 6)], [("S", 5)])
        dve(lambda e: e.reciprocal(out=S(5), in_=S(5)), [("S", 5)], [("S", 5)])
        tt(S(7), S(3), LR[:, :], ALU.mult, [("S", 3), ("LR",)], [("S", 7)])
        tt(S(6), S(4), LI[:, :], ALU.mult, [("S", 4), ("LI",)], [("S", 6)])
        tt(S(7), S(7), S(6), ALU.add, [("S", 7), ("S", 6)], [("S", 7)])
        tt(S(7), S(7), S(5), ALU.mult, [("S", 7), ("S", 5)], [("S", 7)])
        tt(S(8), S(4), LR[:, :], ALU.mult, [("S", 4), ("LR",)], [("S", 8)])
        tt(S(6), S(3), LI[:, :], ALU.mult, [("S", 3), ("LI",)], [("S", 6)])
        tt(S(8), S(8), S(6), ALU.subtract, [("S", 8), ("S", 6)], [("S", 8)])
        tt(S(8), S(8), S(5), ALU.mult, [("S", 8), ("S", 5)], [("S", 8)])
        tsc(S(9), S(8), -1.0, ALU.mult, [("S", 8)], [("S", 9)])
        for g in range(NG):
            tsc(BBR[:, g, :], BR[:, g, :], SM[:, 7, g:g + 1], ALU.mult, [("BR",), ("S", 7)], [("BBR", g)])
            stt(BBR[:, g, :], BI[:, g, :], SM[:, 9, g:g + 1], BBR[:, g, :], ALU.mult, ALU.add,
                [("BI",), ("S", 9), ("BBR", g)], [("BBR", g)])
            tsc(BBI[:, g, :], BI[:, g, :], SM[:, 7, g:g + 1], ALU.mult, [("BI",), ("S", 7)], [("BBI", g)])
            stt(BBI[:, g, :], BR[:, g, :], SM[:, 8, g:g + 1], BBI[:, g, :], ALU.mult, ALU.add,
                [("BR",), ("S", 8), ("BBI", g)], [("BBI", g)])
        tsc(CA[:, :, :], CR[:, :, :], m0, ALU.mult, [("CR",), ("SGN",)], [("CA",)])
        stt(CA[:, :, :], CI[:, :, :], nm1, CA[:, :, :], ALU.mult, ALU.add, [("CI",), ("SGN",), ("CA",)], [("CA",)])
        tsc(CB[:, :, :], CI[:, :, :], nm0, ALU.mult, [("CI",), ("SGN",)], [("CB",)])
        stt(CB[:, :, :], CR[:, :, :], nm1, CB[:, :, :], ALU.mult, ALU.add, [("CR",), ("SGN",), ("CB",)], [("CB",)])
        bbk = [("BBR", g) for g in range(NG)] + [("BBI", g) for g in range(NG)]
        tsc(BA[:, :, :], BBR[:, :, :], m0, ALU.mult, bbk + [("SGN",)], [("BA",)])
        stt(BA[:, :, :], BBI[:, :, :], m1, BA[:, :, :], ALU.mult, ALU.add, bbk + [("SGN",), ("BA",)], [("BA",)])
        tsc(BB[:, :, :], BBI[:, :, :], nm0, ALU.mult, bbk + [("SGN",)], [("BB",)])
        stt(BB[:, :, :], BBR[:, :, :], m1, BB[:, :, :], ALU.mult, ALU.add, bbk + [("SGN",), ("BB",)], [("BB",)])

        def table(g, tvec, n, neg):
            tk = [("TB", i) for i in range(6)]
            tsc(TB[:, 0, 0:n], tvec, LDI[:, g:g + 1], ALU.mult, [("LDI",), ("TI",), ("TIR",)], [tk[0]])
            tsc(TB[:, 0, 0:n], TB[:, 0, 0:n], TWO_PI, ALU.add, [tk[0]], [tk[0]])
            sincos(TB[:, 0, 0:n], TB[:, 1, 0:n], TB[:, 2, 0:n], TB[:, 3, 0:n], [tk[0]], [tk[1]], [tk[2]], tk[3])
            act(TB[:, 3, 0:n], tvec, AF.Exp, [("TI",), ("TIR",), ("LDR",), ("NLDR",), tk[3]], [tk[3]],
                scale=(NLDR if neg else LDR)[:, g:g + 1])
            tt(TB[:, 4, 0:n], TB[:, 3, 0:n], TB[:, 2, 0:n], ALU.mult, [tk[3], tk[2]], [tk[4]])
            if neg:
                stt(TB[:, 5, 0:n], TB[:, 3, 0:n], -1.0, TB[:, 1, 0:n], ALU.mult, ALU.mult, [tk[3], tk[1]], [tk[5]])
            else:
                tt(TB[:, 5, 0:n], TB[:, 3, 0:n], TB[:, 1, 0:n], ALU.mult, [tk[3], tk[1]], [tk[5]])

        pa = 0
        for g in range(NG):
            table(g, TI[:, 128:129], 1, False)
            tsc(AW[:, 0, g:g + 1], TB[:, 4, 0:1], 1.0, ALU.mult, [("TB", 4)], [("AW",)])
            tsc(AW[:, 1, g:g + 1], TB[:, 5, 0:1], 1.0, ALU.mult, [("TB", 5)], [("AW",)])
            table(g, TIR[:, :], 128, False)
            for hp in range(16):
                tsc(G1[:, 0:128], TB[:, 4, 0:128], BA[:, g, hp:hp + 1], ALU.mult, [("TB", 4), ("BA",)], [("G1",)])
                stt(WF[:, hp, :], TB[:, 5, 0:128], BB[:, g, hp:hp + 1], G1[:, 0:128], ALU.mult, ALU.add,
                    [("TB", 5), ("BB",), ("G1",)], [("WF",)])
            for q in range(4):
                bk = pa % 4
                pa += 1
                for j in range(4):
                    P.op("pe", mm(PA[bk][:, j * 128:(j + 1) * 128], WF[:, q * 4 + j, :], ident[:, :], True, True),
                         reads=[("WF",), ("ident",)], writes=[("ps", "A", bk)], inc=(j == 3))
                act(M1[:, q * 4:(q + 1) * 4, :], PA[bk][:, :].rearrange("p (j t) -> p j t", j=4), AF.Copy,
                    [("ps", "A", bk)], [("M1",)])
            for hp in range(16):
                P.op("pe", mm(PV[:, 0:nch], M1[:, hp, :], U16[:, :, g * 16 + hp], hp == 0, hp == 15),
                     reads=[("M1",), ("U16",)], writes=[("ps", "V")], inc=(hp == 15))
            act(VST[:, g, :], PV[:, 0:nch], AF.Copy, [("ps", "V")], [("VST",)])
        P.dma("sp", VI[:, :, :], VST[64:128, :, :], reads=[("VST",)], writes=[("VI",)])
        cp(SC[0][0][:, :, :], VST[0:64, :, :], [("VST",)], [("SC", 0, 0)])
        cp(SC[0][1][:, :, :], VI[:, :, :], [("VI",)], [("SC", 0, 1)], eng="pool")
        tsc(WW[:, 0, 0, :], AW[0:64, 0, :], 1.0, ALU.mult, [("AW",)], [("WW", 0)])
        tsc(WW[:, 0, 1, :], AW[0:64, 1, :], 1.0, ALU.mult, [("AW",)], [("WW", 0)])
        tsc(WW[:, 0, 2, :], AW[0:64, 1, :], -1.0, ALU.mult, [("AW",)], [("WW", 0)])
        cur = 0
        sh = 1
        while sh < nch:
            nx = 1 - cur
            re, im = SC[cur]
            nre, nim = SC[nx]
            for g in range(NG):
                wr, wi, nwi = WW[:, cur, 0, g:g + 1], WW[:, cur, 1, g:g + 1], WW[:, cur, 2, g:g + 1]
                rk = [("SC", cur, 0), ("SC", cur, 1), ("WW", cur)]
                cp(nre[:, g, 0:sh], re[:, g, 0:sh], rk, [("SC", nx, 0)])
                stt(nre[:, g, sh:nch], re[:, g, 0:nch - sh], wr, re[:, g, sh:nch], ALU.mult, ALU.add, rk, [("SC", nx, 0)])
                stt(nre[:, g, sh:nch], im[:, g, 0:nch - sh], nwi, nre[:, g, sh:nch], ALU.mult, ALU.add,
                    rk + [("SC", nx, 0)], [("SC", nx, 0)])
                cp(nim[:, g, 0:sh], im[:, g, 0:sh], rk, [("SC", nx, 1)], eng="pool")
                stt(nim[:, g, sh:nch], im[:, g, 0:nch - sh], wr, im[:, g, sh:nch], ALU.mult, ALU.add, rk,
                    [("SC", nx, 1)])
                stt(nim[:, g, sh:nch], re[:, g, 0:nch - sh], wi, nim[:, g, sh:nch], ALU.mult, ALU.add,
                    rk + [("SC", nx, 1)], [("SC", nx, 1)])
            wk = [("WW", cur)]
            tt(WW[:, nx, 0, :], WW[:, cur, 0, :], WW[:, cur, 0, :], ALU.mult, wk, [("WW", nx)])
            tt(WW[:, nx, 2, :], WW[:, cur, 1, :], WW[:, cur, 1, :], ALU.mult, wk, [("WW", nx)])
            tt(WW[:, nx, 0, :], WW[:, nx, 0, :], WW[:, nx, 2, :], ALU.subtract, [("WW", nx)], [("WW", nx)])
            tt(WW[:, nx, 1, :], WW[:, cur, 0, :], WW[:, cur, 1, :], ALU.mult, wk, [("WW", nx)])
            tsc(WW[:, nx, 1, :], WW[:, nx, 1, :], 2.0, ALU.mult, [("WW", nx)], [("WW", nx)])
            tsc(WW[:, nx, 2, :], WW[:, nx, 1, :], -1.0, ALU.mult, [("WW", nx)], [("WW", nx)])
            cur = nx
            sh *= 2
        re, im = SC[cur]
        P.op("pool", lambda e: e.memset(XP[:, :, 0:1], 0.0), writes=[("XP",)])
        P.op("pool", lambda e: e.memset(XPI[:, :, 0:1], 0.0), writes=[("XPI",)])
        if nch > 1:
            P.op("dve", lambda e: e.tensor_copy(out=XP[0:64, :, 1:nch], in_=re[:, :, 0:nch - 1]),
                 reads=[("SC", cur, 0)], writes=[("XP",)])
            P.op("dve", lambda e: e.tensor_copy(out=XPI[:, :, 1:nch], in_=im[:, :, 0:nch - 1]),
                 reads=[("SC", cur, 1)], writes=[("XPI",)])
        P.dma("sp", XP[64:128, :, :], XPI[:, :, :], reads=[("XPI",)], writes=[("XP",)])
        gsc = 2.0 * math.sqrt(2.0 / math.pi)
        py = 0
        for g in range(NG):
            ub = 0
            P.dma("sp", UF[:, ub, :, :], uG[g], writes=[("UF", ub)])
            table(g, TI[:, 0:129], 129, False)
            for h in range(16):
                tsc(G1[:, 0:129], TB[:, 4, 0:129], CA[:, g, h:h + 1], ALU.mult, [("TB", 4), ("CA",)], [("G1",)])
                stt(CF[:, 0, h, :], TB[:, 5, 0:129], CB[:, g, h:h + 1], G1[:, 0:129], ALU.mult, ALU.add,
                    [("TB", 5), ("CB",), ("G1",)], [("CF", 0)])
            table(g, TI[:, 0:128], 128, True)
            for hp in range(16):
                tsc(G1[:, 0:128], TB[:, 4, 0:128], BA[:, g, hp:hp + 1], ALU.mult, [("TB", 4), ("BA",)], [("G1",)])
                stt(BFc[:, hp, :], TB[:, 5, 0:128], BB[:, g, hp:hp + 1], G1[:, 0:128], ALU.mult, ALU.add,
                    [("TB", 5), ("BB",), ("G1",)], [("BFc",)])
            for hp in range(16):
                for q in range(4):
                    bk = pa % 4
                    pa += 1
                    P.op("pe", mm(PA[bk][:, :], BFc[:, hp, :], CF[:, 0, q * 4:(q + 1) * 4, 0:128], True, True),
                         reads=[("BFc",), ("CF", 0)], writes=[("ps", "A", bk)])
                    tt(TS[:, hp, q * 4:(q + 1) * 4, :], PA[bk][:, :].rearrange("p (j t) -> p j t", j=4),
                       MASK[:, :, :], ALU.mult, [("ps", "A", bk), ("MASK",)], [("TS",)])
            for q in range(4):
                yb = py % 2
                py += 1
                for j in range(4):
                    h = q * 4 + j
                    o = PYb[yb][:, j * 128:j * 128 + nch]
                    for hp in range(16):
                        P.op("pe", mm(o, TS[:, hp, h, :], U16[:, :, g * 16 + hp], hp == 0, False),
                             reads=[("TS",), ("U16",)], writes=[("ps", "Y", yb)], inc=False)
                    P.op("pe", mm(o, CF[:, 0, h, 1:129], XP[:, g, :], False, True),
                         reads=[("CF", 0), ("XP",)], writes=[("ps", "Y", yb)], inc=(j == 3))
                for j in range(4):
                    h = q * 4 + j
                    stt(YS[:, ub, :, h], UF[:, ub, :, h], D2[:, g * 16 + h:g * 16 + h + 1],
                        PYb[yb][:, j * 128:j * 128 + nch], ALU.mult, ALU.add,
                        [("UF", ub), ("D2",), ("ps", "Y", yb)], [("YS", ub)])
            yv = YS[:, ub, :, :].rearrange("p c h -> p (c h)")
            zv = ZS[:, ub, :, :].rearrange("p c h -> p (c h)")
            n = nch * 16
            tt(G1[:, 0:n], yv, yv, ALU.mult, [("YS", ub)], [("G1",)], eng="pool")
            tsc(G1[:, 0:n], G1[:, 0:n], 0.044715, ALU.mult, [("G1",)], [("G1",)], s2=1.0, op1=ALU.add, eng="pool")
            tt(G1[:, 0:n], G1[:, 0:n], yv, ALU.mult, [("G1",), ("YS", ub)], [("G1",)], eng="pool")
            act(G1[:, 0:n], G1[:, 0:n], AF.Sigmoid, [("G1",)], [("G1",)], scale=gsc)
            tt(zv, G1[:, 0:n], yv, ALU.mult, [("G1",), ("YS", ub)], [("ZS", ub)], eng="pool")
            P.dma("sp", zout[g], ZS[:, ub, :, :], reads=[("ZS", ub)])
        P.finish()
        P.emit()
    return nc


def s5_phase(nc, P, es, S, c_ident, uS16, uG, GZ):
    seq = SEQ
    nch = seq // 128
    lr_in, li_in, ldt_in = S["lam_re2"], S["lam_im2"], S["logdt2"]
    br_in, bi_in, cr_in, ci_in, d_in = S["br2"], S["bi2"], S["cr2"], S["ci2"], S["d2"]
    c_mask, c_ti, c_tir, c_sg = S["mask4"], S["ti"], S["tir"], S["sg"]
    if True:
        sb = lambda n, s, d: es.enter_context(nc.sbuf_tensor(n + "_p4b", s, d))
        ps = lambda n: es.enter_context(nc.psum_tensor(n + "_p4b", [128, 512], F32))
        U16 = sb("U16", [128, nch, 128], BF16)
        UF = sb("UF", [128, 1, nch, 16], F32)
        TS = sb("TS", [128, 16, 16, 128], BF16)
        CF = sb("CF", [128, 1, 16, 129], BF16)
        BFc = sb("BFc", [128, 16, 128], BF16)
        WF = BFc
        M1 = CF[:, 0, :, 0:128]
        YS = sb("YS", [128, 1, nch, 16], F32)
        ZALL = sb("ZALL", [128, nch, 128], BF16)
        ZT = sb("ZT", [128, 2, 512], BF16)
        G1 = sb("G1", [128, nch * 16], F32)
        VST = sb("VST", [128, NG, nch], F32)
        VI = sb("VI", [64, NG, nch], F32)
        SC = [[sb(f"SC{a}{b}", [64, NG, nch], F32) for b in range(2)] for a in range(2)]
        XP = sb("XP", [128, NG, nch], BF16)
        XPI = sb("XPI", [64, NG, nch], BF16)
        LR = sb("LR", [128, NG], F32)
        LI = sb("LI", [128, NG], F32)
        DTt = sb("DTt", [128, NG], F32)
        LDR = sb("LDR", [128, NG], F32)
        LDI = sb("LDI", [128, NG], F32)
        NLDR = sb("NLDR", [128, NG], F32)
        SM = sb("SM", [128, 12, NG], F32)
        BR = sb("BR", [128, NG, 16], F32)
        BI = sb("BI", [128, NG, 16], F32)
        CR = sb("CR", [128, NG, 16], F32)
        CI = sb("CI", [128, NG, 16], F32)
        BBR = sb("BBR", [128, NG, 16], F32)
        BBI = sb("BBI", [128, NG, 16], F32)
        CA = sb("CA", [128, NG, 16], F32)
        CB = sb("CB", [128, NG, 16], F32)
        BA = sb("BA", [128, NG, 16], F32)
        BB = sb("BB", [128, NG, 16], F32)
        D2 = sb("D2", [128, 128], F32)
        MASK = sb("MASK", [128, 4, 128], F32)
        TI = sb("TI", [128, 130], F32)
        TIR = sb("TIR", [128, 128], F32)
        SGN = sb("SGN", [128, 5], F32)
        ident = sb("identb", [128, 128], BF16)
        TB = sb("TB", [128, 6, 130], F32)
        AW = sb("AW", [128, 2, NG], F32)
        WW = sb("WW", [64, 2, 3, NG], F32)
        PA = [ps(f"PA{i}") for i in range(4)]
        PYb = [ps("PYa"), ps("PYb")]
        PV = ps("PV")
        loads = ((LR, lr_in, "LR"), (LI, li_in, "LI"), (DTt, ldt_in, "DT"), (BR, br_in, "BR"), (BI, bi_in, "BI"),
                 (CR, cr_in, "CR"), (CI, ci_in, "CI"), (D2, d_in, "D2"), (MASK, c_mask, "MASK"), (TI, c_ti, "TI"),
                 (TIR, c_tir, "TIR"), (SGN, c_sg, "SGN"), (ident, c_ident, "ident"))
        for (t, src, key) in loads:
            idx = (slice(None),) * len(t.shape)
            P.dma("sp", t[idx], src, writes=[(key,)])
        P.dma("sp", U16[:, :, :], uS16, writes=[("U16",)])
        m0, m1, nm0, nm1 = SGN[:, 0:1], SGN[:, 1:2], SGN[:, 2:3], SGN[:, 3:4]

        def dve(fn, reads, writes):
            P.op("dve", fn, reads=reads, writes=writes)

        def tt(out, a, b, op, reads, writes, eng="dve"):
            P.op(eng, lambda e: e.tensor_tensor(out=out, in0=a, in1=b, op=op), reads=reads, writes=writes)

        def tsc(out, a, s1, op0, reads, writes, s2=None, op1=None, eng="dve"):
            if op1 is None:
                P.op(eng, lambda e: e.tensor_scalar(out=out, in0=a, scalar1=s1, scalar2=None, op0=op0),
                     reads=reads, writes=writes)
            else:
                P.op(eng, lambda e: e.tensor_scalar(out=out, in0=a, scalar1=s1, scalar2=s2, op0=op0, op1=op1),
                     reads=reads, writes=writes)

        def stt(out, a, s, b, op0, op1, reads, writes, eng="dve"):
            P.op(eng, lambda e: e.scalar_tensor_tensor(out=out, in0=a, scalar=s, in1=b, op0=op0, op1=op1),
                 reads=reads, writes=writes)

        def act(out, a, func, reads, writes, bias=None, scale=None):
            kw = {}
            if bias is not None:
                kw["bias"] = bias
            if scale is not None:
                kw["scale"] = scale
            P.op("act", lambda e: e.activation(out=out, in_=a, func=func, **kw), reads=reads, writes=writes)

        def cp(out, a, reads, writes, eng="dve"):
            P.op(eng, lambda e: e.tensor_copy(out=out, in_=a), reads=reads, writes=writes)

        RI = sb("RI", [128, 130], mybir.dt.int32)
        RF = sb("RF", [128, 2, 130], F32)

        def reduce_angle(arg, shift, out, rk, n):
            t, tf = RF[:, 0, 0:n], RF[:, 1, 0:n]
            ti = RI[:, 0:n]
            tsc(t, arg, 1.0 / TWO_PI, ALU.mult, rk, [("RF", 0)], s2=0.5 + shift, op1=ALU.add)
            cp(ti, t, [("RF", 0)], [("RI",)])
            cp(tf, ti, [("RI",)], [("RF", 1)])
            tt(t, t, tf, ALU.subtract, [("RF", 0), ("RF", 1)], [("RF", 0)])
            tsc(t, t, -0.5, ALU.add, [("RF", 0)], [("RF", 0)], s2=TWO_PI, op1=ALU.mult)
            tsc(tf, t, -math.pi, ALU.is_lt, [("RF", 0)], [("RF", 1)])
            stt(t, tf, TWO_PI, t, ALU.mult, ALU.add, [("RF", 0), ("RF", 1)], [("RF", 0)])
            tsc(tf, t, math.pi, ALU.is_gt, [("RF", 0)], [("RF", 1)])
            stt(out, tf, -TWO_PI, t, ALU.mult, ALU.add, [("RF", 0), ("RF", 1)], [("RF", 0)])

        def sincos(arg, sin_out, cos_out, tmp, rk, wk_s, wk_c, tk):
            n = arg.shape[-1]
            reduce_angle(arg, 0.0, RF[:, 0, 0:n], rk, n)
            act(sin_out, RF[:, 0, 0:n], AF.Sin, [("RF", 0)], wk_s)
            tsc(RF[:, 1, 0:n], RF[:, 0, 0:n], -1.0, ALU.mult, [("RF", 0)], [("RF", 1)])
            tt(RF[:, 1, 0:n], RF[:, 0, 0:n], RF[:, 1, 0:n], ALU.max, [("RF", 0), ("RF", 1)], [("RF", 1)])
            act(cos_out, RF[:, 1, 0:n], AF.Sin, [("RF", 1), ("SGN",)], wk_c, bias=SGN[:, 4:5], scale=-1.0)

        S = lambda i: SM[:, i, :]
        act(DTt[:, :], DTt[:, :], AF.Exp, [("DT",)], [("DT",)])
        tt(LDR[:, :], LR[:, :], DTt[:, :], ALU.mult, [("LR",), ("DT",)], [("LDR",)])
        tt(LDI[:, :], LI[:, :], DTt[:, :], ALU.mult, [("LI",), ("DT",)], [("LDI",)])
        tsc(NLDR[:, :], LDR[:, :], -1.0, ALU.mult, [("LDR",)], [("NLDR",)])
        tsc(S(5), LDI[:, :], TWO_PI, ALU.add, [("LDI",)], [("S", 5)])
        sincos(S(5), S(0), S(1), S(6), [("S", 5)], [("S", 0)], [("S", 1)], ("S", 6))
        act(S(2), LDR[:, :], AF.Exp, [("LDR",)], [("S", 2)])
        tt(S(3), S(2), S(1), ALU.mult, [("S", 2), ("S", 1)], [("S", 3)])
        tsc(S(3), S(3), -1.0, ALU.add, [("S", 3)], [("S", 3)])
        tt(S(4), S(2), S(0), ALU.mult, [("S", 2), ("S", 0)], [("S", 4)])
        tt(S(5), LR[:, :], LR[:, :], ALU.mult, [("LR",)], [("S", 5)])
        tt(S(6), LI[:, :], LI[:, :], ALU.mult, [("LI",)], [("S", 6)])
        tt(S(5), S(5), S(6), ALU.add, [("S", 5), ("S", 6)], [("S", 5)])
        dve(lambda e: e.reciprocal(out=S(5), in_=S(5)), [("S", 5)], [("S", 5)])
        tt(S(7), S(3), LR[:, :], ALU.mult, [("S", 3), ("LR",)], [("S", 7)])
        tt(S(6), S(4), LI[:, :], ALU.mult, [("S", 4), ("LI",)], [("S", 6)])
        tt(S(7), S(7), S(6), ALU.add, [("S", 7), ("S", 6)], [("S", 7)])
        tt(S(7), S(7), S(5), ALU.mult, [("S", 7), ("S", 5)], [("S", 7)])
        tt(S(8), S(4), LR[:, :], ALU.mult, [("S", 4), ("LR",)], [("S", 8)])
        tt(S(6), S(3), LI[:, :], ALU.mult, [("S", 3), ("LI",)], [("S", 6)])
        tt(S(8), S(8), S(6), ALU.subtract, [("S", 8), ("S", 6)], [("S", 8)])
        tt(S(8), S(8), S(5), ALU.mult, [("S", 8), ("S", 5)], [("S", 8)])
        tsc(S(9), S(8), -1.0, ALU.mult, [("S", 8)], [("S", 9)])
        for g in range(NG):
            tsc(BBR[:, g, :], BR[:, g, :], SM[:, 7, g:g + 1], ALU.mult, [("BR",), ("S", 7)], [("BBR", g)])
            stt(BBR[:, g, :], BI[:, g, :], SM[:, 9, g:g + 1], BBR[:, g, :], ALU.mult, ALU.add,
                [("BI",), ("S", 9), ("BBR", g)], [("BBR", g)])
            tsc(BBI[:, g, :], BI[:, g, :], SM[:, 7, g:g + 1], ALU.mult, [("BI",), ("S", 7)], [("BBI", g)])
            stt(BBI[:, g, :], BR[:, g, :], SM[:, 8, g:g + 1], BBI[:, g, :], ALU.mult, ALU.add,
                [("BR",), ("S", 8), ("BBI", g)], [("BBI", g)])
        tsc(CA[:, :, :], CR[:, :, :], m0, ALU.mult, [("CR",), ("SGN",)], [("CA",)])
        stt(CA[:, :, :], CI[:, :, :], nm1, CA[:, :, :], ALU.mult, ALU.add, [("CI",), ("SGN",), ("CA",)], [("CA",)])
        tsc(CB[:, :, :], CI[:, :, :], nm0, ALU.mult, [("CI",), ("SGN",)], [("CB",)])
        stt(CB[:, :, :], CR[:, :, :], nm1, CB[:, :, :], ALU.mult, ALU.add, [("CR",), ("SGN",), ("CB",)], [("CB",)])
        bbk = [("BBR", g) for g in range(NG)] + [("BBI", g) for g in range(NG)]
        tsc(BA[:, :, :], BBR[:, :, :], m0, ALU.mult, bbk + [("SGN",)], [("BA",)])
        stt(BA[:, :, :], BBI[:, :, :], m1, BA[:, :, :], ALU.mult, ALU.add, bbk + [("SGN",), ("BA",)], [("BA",)])
        tsc(BB[:, :, :], BBI[:, :, :], nm0, ALU.mult, bbk + [("SGN",)], [("BB",)])
        stt(BB[:, :, :], BBR[:, :, :], m1, BB[:, :, :], ALU.mult, ALU.add, bbk + [("SGN",), ("BB",)], [("BB",)])

        def table(g, tvec, n, neg):
            tk = [("TB", i) for i in range(6)]
            tsc(TB[:, 0, 0:n], tvec, LDI[:, g:g + 1], ALU.mult, [("LDI",), ("TI",), ("TIR",)], [tk[0]])
            tsc(TB[:, 0, 0:n], TB[:, 0, 0:n], TWO_PI, ALU.add, [tk[0]], [tk[0]])
            sincos(TB[:, 0, 0:n], TB[:, 1, 0:n], TB[:, 2, 0:n], TB[:, 3, 0:n], [tk[0]], [tk[1]], [tk[2]], tk[3])
            act(TB[:, 3, 0:n], tvec, AF.Exp, [("TI",), ("TIR",), ("LDR",), ("NLDR",), tk[3]], [tk[3]],
                scale=(NLDR if neg else LDR)[:, g:g + 1])
            tt(TB[:, 4, 0:n], TB[:, 3, 0:n], TB[:, 2, 0:n], ALU.mult, [tk[3], tk[2]], [tk[4]])
            if neg:
                stt(TB[:, 5, 0:n], TB[:, 3, 0:n], -1.0, TB[:, 1, 0:n], ALU.mult, ALU.mult, [tk[3], tk[1]], [tk[5]])
            else:
                tt(TB[:, 5, 0:n], TB[:, 3, 0:n], TB[:, 1, 0:n], ALU.mult, [tk[3], tk[1]], [tk[5]])

        pa = 0
        for g in range(NG):
            table(g, TI[:, 128:129], 1, False)
            tsc(AW[:, 0, g:g + 1], TB[:, 4, 0:1], 1.0, ALU.mult, [("TB", 4)], [("AW",)])
            tsc(AW[:, 1, g:g + 1], TB[:, 5, 0:1], 1.0, ALU.mult, [("TB", 5)], [("AW",)])
            table(g, TIR[:, :], 128, False)
            for hp in range(16):
                tsc(G1[:, 0:128], TB[:, 4, 0:128], BA[:, g, hp:hp + 1], ALU.mult, [("TB", 4), ("BA",)], [("G1",)])
                stt(WF[:, hp, :], TB[:, 5, 0:128], BB[:, g, hp:hp + 1], G1[:, 0:128], ALU.mult, ALU.add,
                    [("TB", 5), ("BB",), ("G1",)], [("BFc",)])
            for q in range(4):
                bk = pa % 4
                pa += 1
                for j in range(4):
                    P.op("pe", mm(PA[bk][:, j * 128:(j + 1) * 128], WF[:, q * 4 + j, :], ident[:, :], True, True),
                         reads=[("BFc",), ("ident",)], writes=[("ps", "A", bk)], inc=(j == 3))
                act(M1[:, q * 4:(q + 1) * 4, :], PA[bk][:, :].rearrange("p (j t) -> p j t", j=4), AF.Copy,
                    [("ps", "A", bk)], [("CF", 0)])
            for hp in range(16):
                P.op("pe", mm(PV[:, 0:nch], M1[:, hp, :], U16[:, :, g * 16 + hp], hp == 0, hp == 15),
                     reads=[("CF", 0), ("U16",)], writes=[("ps", "V")], inc=(hp == 15))
            act(VST[:, g, :], PV[:, 0:nch], AF.Copy, [("ps", "V")], [("VST",)])
        P.dma("sp", VI[:, :, :], VST[64:128, :, :], reads=[("VST",)], writes=[("VI",)])
        cp(SC[0][0][:, :, :], VST[0:64, :, :], [("VST",)], [("SC", 0, 0)])
        cp(SC[0][1][:, :, :], VI[:, :, :], [("VI",)], [("SC", 0, 1)], eng="pool")
        tsc(WW[:, 0, 0, :], AW[0:64, 0, :], 1.0, ALU.mult, [("AW",)], [("WW", 0)])
        tsc(WW[:, 0, 1, :], AW[0:64, 1, :], 1.0, ALU.mult, [("AW",)], [("WW", 0)])
        tsc(WW[:, 0, 2, :], AW[0:64, 1, :], -1.0, ALU.mult, [("AW",)], [("WW", 0)])
        cur = 0
        sh = 1
        while sh < nch:
            nx = 1 - cur
            re, im = SC[cur]
            nre, nim = SC[nx]
            for g in range(NG):
                wr, wi, nwi = WW[:, cur, 0, g:g + 1], WW[:, cur, 1, g:g + 1], WW[:, cur, 2, g:g + 1]
                rk = [("SC", cur, 0), ("SC", cur, 1), ("WW", cur)]
                cp(nre[:, g, 0:sh], re[:, g, 0:sh], rk, [("SC", nx, 0)])
                stt(nre[:, g, sh:nch], re[:, g, 0:nch - sh], wr, re[:, g, sh:nch], ALU.mult, ALU.add, rk, [("SC", nx, 0)])
                stt(nre[:, g, sh:nch], im[:, g, 0:nch - sh], nwi, nre[:, g, sh:nch], ALU.mult, ALU.add,
                    rk + [("SC", nx, 0)], [("SC", nx, 0)])
                cp(nim[:, g, 0:sh], im[:, g, 0:sh], rk, [("SC", nx, 1)], eng="pool")
                stt(nim[:, g, sh:nch], im[:, g, 0:nch - sh], wr, im[:, g, sh:nch], ALU.mult, ALU.add, rk,
                    [("SC", nx, 1)])
                stt(nim[:, g, sh:nch], re[:, g, 0:nch - sh], wi, nim[:, g, sh:nch], ALU.mult, ALU.add,
                    rk + [("SC", nx, 1)], [("SC", nx, 1)])
            wk = [("WW", cur)]
            tt(WW[:, nx, 0, :], WW[:, cur, 0, :], WW[:, cur, 0, :], ALU.mult, wk, [("WW", nx)])
            tt(WW[:, nx, 2, :], WW[:, cur, 1, :], WW[:, cur, 1, :], ALU.mult, wk, [("WW", nx)])
            tt(WW[:, nx, 0, :], WW[:, nx, 0, :], WW[:, nx, 2, :], ALU.subtract, [("WW", nx)], [("WW", nx)])
            tt(WW[:, nx, 1, :], WW[:, cur, 0, :], WW[:, cur, 1, :], ALU.mult, wk, [("WW", nx)])
            tsc(WW[:, nx, 1, :], WW[:, nx, 1, :], 2.0, ALU.mult, [("WW", nx)], [("WW", nx)])
            tsc(WW[:, nx, 2, :], WW[:, nx, 1, :], -1.0, ALU.mult, [("WW", nx)], [("WW", nx)])
            cur = nx
            sh *= 2
        re, im = SC[cur]
        P.op("pool", lambda e: e.memset(XP[:, :, 0:1], 0.0), writes=[("XP",)])
        P.op("pool", lambda e: e.memset(XPI[:, :, 0:1], 0.0), writes=[("XPI",)])
        if nch > 1:
            P.op("dve", lambda e: e.tensor_copy(out=XP[0:64, :, 1:nch], in_=re[:, :, 0:nch - 1]),
                 reads=[("SC", cur, 0)], writes=[("XP",)])
            P.op("dve", lambda e: e.tensor_copy(out=XPI[:, :, 1:nch], in_=im[:, :, 0:nch - 1]),
                 reads=[("SC", cur, 1)], writes=[("XPI",)])
        P.dma("sp", XP[64:128, :, :], XPI[:, :, :], reads=[("XPI",)], writes=[("XP",)])
        gsc = 2.0 * math.sqrt(2.0 / math.pi)
        py = 0
        for g in range(NG):
            ub = 0
            P.dma("sp", UF[:, ub, :, :], uG[g * 128:(g + 1) * 128, :].rearrange("s (c h) -> s c h", h=16),
                  writes=[("UF", ub)])
            table(g, TI[:, 0:129], 129, False)
            for h in range(16):
                tsc(G1[:, 0:129], TB[:, 4, 0:129], CA[:, g, h:h + 1], ALU.mult, [("TB", 4), ("CA",)], [("G1",)])
                stt(CF[:, 0, h, :], TB[:, 5, 0:129], CB[:, g, h:h + 1], G1[:, 0:129], ALU.mult, ALU.add,
                    [("TB", 5), ("CB",), ("G1",)], [("CF", 0)])
            table(g, TI[:, 0:128], 128, True)
            for hp in range(16):
                tsc(G1[:, 0:128], TB[:, 4, 0:128], BA[:, g, hp:hp + 1], ALU.mult, [("TB", 4), ("BA",)], [("G1",)])
                stt(BFc[:, hp, :], TB[:, 5, 0:128], BB[:, g, hp:hp + 1], G1[:, 0:128], ALU.mult, ALU.add,
                    [("TB", 5), ("BB",), ("G1",)], [("BFc",)])
            for hp in range(16):
                for q in range(4):
                    bk = pa % 4
                    pa += 1
                    P.op("pe", mm(PA[bk][:, :], BFc[:, hp, :], CF[:, 0, q * 4:(q + 1) * 4, 0:128], True, True),
                         reads=[("BFc",), ("CF", 0)], writes=[("ps", "A", bk)])
                    tt(TS[:, hp, q * 4:(q + 1) * 4, :], PA[bk][:, :].rearrange("p (j t) -> p j t", j=4),
                       MASK[:, :, :], ALU.mult, [("ps", "A", bk), ("MASK",)], [("TS",)])
            for q in range(4):
                yb = py % 2
                py += 1
                for j in range(4):
                    h = q * 4 + j
                    o = PYb[yb][:, j * 128:j * 128 + nch]
                    for hp in range(16):
                        P.op("pe", mm(o, TS[:, hp, h, :], U16[:, :, g * 16 + hp], hp == 0, False),
                             reads=[("TS",), ("U16",)], writes=[("ps", "Y", yb)], inc=False)
                    P.op("pe", mm(o, CF[:, 0, h, 1:129], XP[:, g, :], False, True),
                         reads=[("CF", 0), ("XP",)], writes=[("ps", "Y", yb)], inc=(j == 3))
                for j in range(4):
                    h = q * 4 + j
                    stt(YS[:, ub, :, h], UF[:, ub, :, h], D2[:, g * 16 + h:g * 16 + h + 1],
                        PYb[yb][:, j * 128:j * 128 + nch], ALU.mult, ALU.add,
                        [("UF", ub), ("D2",), ("ps", "Y", yb)], [("YS", ub)])
            yv = YS[:, ub, :, :]
            g3 = G1[:, 0:nch * 16].rearrange("p (c h) -> p c h", h=16)
            zv = ZALL[:, :, g * 16:(g + 1) * 16]
            tt(g3, yv, yv, ALU.mult, [("YS", ub)], [("G1",)], eng="pool")
            tsc(g3, g3, 0.044715, ALU.mult, [("G1",)], [("G1",)], s2=1.0, op1=ALU.add, eng="pool")
            tt(g3, g3, yv, ALU.mult, [("G1",), ("YS", ub)], [("G1",)], eng="pool")
            act(g3, g3, AF.Sigmoid, [("G1",)], [("G1",)], scale=gsc)
            tt(zv, g3, yv, ALU.mult, [("G1",), ("YS", ub)], [("ZALL",)], eng="pool")
        for c4 in range(nch // 4):
            bk = pa % 4
            pa += 1
            zb = c4 % 2
            for j in range(4):
                c = c4 * 4 + j
                P.op("pe", mm(PA[bk][:, j * 128:(j + 1) * 128], ZALL[:, c, :], ident[:, :], True, True),
                     reads=[("ZALL",), ("ident",)], writes=[("ps", "A", bk)], inc=(j == 3))
            act(ZT[:, zb, :], PA[bk][:, :], AF.Copy, [("ps", "A", bk)], [("ZT", zb)])
            w = c4 // 4
            P.dma("sp", GZ.src[w][:, (c4 % 4) * 512:(c4 % 4 + 1) * 512], ZT[:, zb, :], reads=[("ZT", zb)],
                  writes=[("GZ", "s", w, c4 % 4)])
            if c4 % 4 == 3:
                P.collective(G4, GZ.src[w], GZ.a[w], reads=[("GZ", "s", w, k4) for k4 in range(4)],
                             writes=[("GZ", "a", w)])
                if w > 0:
                    GZ.s2(w - 1)
        GZ.s2(nch // 16 - 1)


def l4_inputs(u_c, inp, core, seq=SEQ):
    nch = seq // 128
    gs = slice(core * NG, (core + 1) * NG)
    two = lambda a: np.ascontiguousarray(np.concatenate([a, a], 0).astype(np.float32))
    m = dict(l4_consts())
    u3 = u_c.reshape(nch, 128, 128)
    m["uS"] = np.ascontiguousarray(u3.transpose(1, 0, 2))
    m["uG"] = np.ascontiguousarray(u3.reshape(nch, 128, NG, 16).transpose(2, 1, 0, 3))
    m["lam_re2"] = two(inp["s5_lambda_re"][0][gs].T)
    m["lam_im2"] = two(inp["s5_lambda_im"][0][gs].T)
    m["logdt2"] = np.ascontiguousarray(np.broadcast_to(inp["s5_log_dt"][0][gs][None, :], (128, NG)).astype(np.float32))
    m["br2"] = two(inp["s5_b_re"][0][gs].transpose(1, 0, 2))
    m["bi2"] = two(inp["s5_b_im"][0][gs].transpose(1, 0, 2))
    m["cr2"] = two(inp["s5_c_re"][0][gs].transpose(2, 0, 1))
    m["ci2"] = two(inp["s5_c_im"][0][gs].transpose(2, 0, 1))
    m["d2"] = np.ascontiguousarray(np.broadcast_to(inp["s5_d"][0][core * 128:(core + 1) * 128][None, :], (128, 128)).astype(np.float32))
    return m


def l4_unpack(zout, seq=SEQ):
    nch = seq // 128
    return np.asarray(zout).transpose(2, 1, 0, 3).reshape(seq, 128)


_CACHE = {}


def _get(name, builder):
    if name not in _CACHE:
        _CACHE[name] = builder()
    return _CACHE[name]


def _run(nc, in_maps):
    res = run_bass_kernel_spmd(nc, in_maps, core_ids=list(range(NCORE)))
    return res.results


def kernel_unfused(**inputs):
    inp = {k: np.asarray(v) for k, v in inputs.items()}
    x = inp["x"][0]
    ident = _ident_np()
    cs = lambda c: slice(c * TPC, (c + 1) * TPC)
    ca = np.ascontiguousarray
    maps = [{"x": ca(x[cs(c)]), "wg": inp["ffn1_w_gate"][0], "wu": inp["ffn1_w_up"][0], "wd": inp["ffn1_w_down"][0],
             "lng": ca(inp["ln_gain"][0, 0][None]), "lnb": ca(inp["ln_bias"][0, 0][None]),
             "win": inp["attn_w_in"][0], "ident": ident} for c in range(NCORE)]
    r1 = _run(_get("l1", build_l1), maps)
    x1 = [r["x1"] for r in r1]
    projT = np.concatenate([np.asarray(r["projT"]) for r in r1], axis=1)
    fT = np.concatenate([np.asarray(r["fT"]) for r in r1], axis=1)
    del r1, maps
    consts = l2_consts()
    perms = {d: dil_perm(SEQ, d) for (d, _) in DIL}
    maps = []
    for c in range(NCORE):
        m = dict(consts)
        hs = slice(128 * c, 128 * (c + 1))
        m["qf"] = ca(projT[0:1024][hs])
        m["kf"] = ca(projT[1024:2048][hs])
        m["vf"] = ca(projT[2048:3072][hs].T)
        m["f2T"] = ca(fT[c].reshape(SEQ // 128, 128).T)
        m["nbf"] = np.full((128, 1), inp["attn_b_f"][0, c], np.float32)
        m["qd"] = ca(projT[3072:4096][hs])
        m["kd"] = ca(projT[4096:5120][hs])
        vdt = ca(projT[5120:6144][hs].T)
        for (d, _) in DIL:
            m[f"vd{d}"] = ca(vdt[perms[d]])
        maps.append(m)
    r2 = _run(_get("l2", build_l2), maps)
    yT = np.concatenate([np.asarray(r["yf"]) for r in r2] + [np.asarray(r["yd"]) for r in r2], axis=0)
    del r2, maps, projT
    lng3 = ca(np.stack([inp["ln_gain"][0, 1], inp["ln_gain"][0, 2], inp["ln_gain"][1, 0]]))
    lnb3 = ca(np.stack([inp["ln_bias"][0, 1], inp["ln_bias"][0, 2], inp["ln_bias"][1, 0]]))
    maps = [{"x1": x1[c], "yT": ca(yT[:, cs(c)]), "wo": inp["attn_w_out"][0],
             "w2g": inp["ffn2_w_gate"][0], "w2u": inp["ffn2_w_up"][0], "w2d": inp["ffn2_w_down"][0],
             "w3g": inp["ffn1_w_gate"][1], "w3u": inp["ffn1_w_up"][1], "w3d": inp["ffn1_w_down"][1],
             "lng": lng3, "lnb": lnb3, "wsi": inp["s5_w_in"][0], "ident": ident} for c in range(NCORE)]
    r3 = _run(_get("l3", build_l3), maps)
    x3 = [r["x3"] for r in r3]
    u = np.concatenate([np.asarray(r["u"]) for r in r3], axis=0)
    del r3, maps, x1, yT
    maps = [l4_inputs(ca(u[:, 128 * c:128 * (c + 1)]), inp, c) for c in range(NCORE)]
    r4 = _run(_get("l4", build_l4), maps)
    z = np.concatenate([l4_unpack(r["zout"]) for r in r4], axis=1)
    zT = ca(z.T)
    del r4, maps, u, z
    lng5 = ca(np.stack([inp["ln_gain"][1, 1], inp["ln_gain"][1, 2]]))
    lnb5 = ca(np.stack([inp["ln_bias"][1, 1], inp["ln_bias"][1, 2]]))
    maps = [{"x3": x3[c], "zT": ca(zT[:, cs(c)]), "wgo": inp["s5_w_glu_out"][0], "wgg": inp["s5_w_glu_gate"][0],
             "w4g": inp["ffn2_w_gate"][1], "w4u": inp["ffn2_w_up"][1], "w4d": inp["ffn2_w_down"][1],
             "lng": lng5, "lnb": lnb5, "ident": ident} for c in range(NCORE)]
    r5 = _run(_get("l5", build_l5), maps)
    out = np.concatenate([np.asarray(r["out"]) for r in r5], axis=0)
    return out.reshape(1, SEQ, D).astype(np.float32)


G4 = [[0, 1, 2, 3], [4, 5, 6, 7]]
G2 = [[0, 4], [1, 5], [2, 6], [3, 7]]


class Gather:
    def __init__(self, nc, P, name, rows, cols, dtype, n):
        self.P, self.name = P, name
        self.src = nc.dram_tensor(name + "_s", [n, rows, cols], dtype).ap()
        self.a = nc.dram_tensor(name + "_a", [n, 4 * rows, cols], dtype).ap()
        self.b = nc.dram_tensor(name + "_b", [n, 8 * rows, cols], dtype).ap()

    def s1(self, i):
        self.P.collective(G4, self.src[i], self.a[i], reads=[(self.name, "s", i)], writes=[(self.name, "a", i)])

    def s2(self, i):
        self.P.collective(G2, self.a[i], self.b[i], reads=[(self.name, "a", i)], writes=[(self.name, "b", i)])


def _xt_to_gather(P, T, GXo, t):
    for q in range(4):
        i = t * 4 + q
        P.dma("sp", GXo.src[i].rearrange("(j p) t -> p j t", p=128), T.XT[:, 4 * q:4 * q + 4, :],
              reads=[("XT", st) for st in range(NST)], writes=[(GXo.name, "s", i)])
        GXo.s1(i)
    if t > 0:
        for q in range(4):
            GXo.s2((t - 1) * 4 + q)
    if t == TPC // TT - 1:
        for q in range(4):
            GXo.s2(t * 4 + q)


def _select_window(P, T, OH, nk, loader):
    xk = [("XT", st) for st in range(NST)]
    for w in range(NCORE):
        s0 = (w % 2) * 16
        keys = [("HT", s0 + k) for k in range(nk)]
        loader(w, T.HT[:, s0:s0 + nk, :], keys)
        src = T.HT[:, s0:s0 + nk, :]
        dst = T.XT[:, 0:nk, :]
        if w == 0:
            P.op("dve", lambda e, src=src, dst=dst, w=w: e.tensor_scalar(out=dst, in0=src, scalar1=OH[:, w:w + 1],
                                                                      scalar2=None, op0=ALU.mult),
                 reads=keys + [("OH",)], writes=xk)
        else:
            P.op("dve", lambda e, src=src, dst=dst, w=w: e.scalar_tensor_tensor(out=dst, in0=src, scalar=OH[:, w:w + 1],
                                                                             in1=dst, op0=ALU.mult, op1=ALU.add),
                 reads=keys + [("OH",)] + xk, writes=xk)


def build_fused(nph=7, dbg=False):
    nc = _new_nc()
    dt = lambda n, s, d, k="ExternalInput": nc.dram_tensor(n, s, d, kind=k).ap()
    it_ = lambda n, s, d: nc.dram_tensor(n, s, d).ap()
    x = dt("x", [TPC, D], F32)
    nff = 1 if nph < 4 else (3 if nph < 7 else 4)
    ffw = [[dt(f"w{i}g", [D, DFF], F32), dt(f"w{i}u", [D, DFF], F32), dt(f"w{i}d", [DFF, D], F32)] for i in range(nff)]
    lng = dt("lng", [6, D], F32)
    lnb = dt("lnb", [6, D], F32)
    winc = dt("winc", [D, 769], F32)
    wo = dt("wo", [D, D], F32)
    wsic = dt("wsic", [D, 128], F32)
    wgo = dt("wgo", [S5W, D], F32)
    wgg = dt("wgg", [S5W, D], F32)
    onehot = dt("onehot", [128, NCORE], F32)
    A = {"nbf": dt("nbf", [128, 1], F32)}
    for n_, shp, d_ in (("ident", [128, 128], BF16), ("ones_bf", [128, 128], BF16), ("fmask", [128, 4, 512], BF16),
                        ("dmask", [128, 256], BF16), ("uincl", [128, 128], F32), ("lstrict", [128, 128], F32),
                        ("ones_f", [128, 128], F32), ("ident_f", [128, 128], F32)):
        A[n_] = dt(n_, shp, d_)
    S = {}
    for n_, shp in (("lam_re2", [128, NG]), ("lam_im2", [128, NG]), ("logdt2", [128, NG]), ("br2", [128, NG, 16]),
                    ("bi2", [128, NG, 16]), ("cr2", [128, NG, 16]), ("ci2", [128, NG, 16]), ("d2", [128, 128]),
                    ("mask4", [128, 4, 128]), ("ti", [128, 130]), ("tir", [128, 128]), ("sg", [128, 5])):
        S[n_] = dt(n_, shp, F32)
    out = dt("out", [TPC, D], F32, "ExternalOutput")
    ident = A["ident"]
    if dbg:
        it_ = lambda n, s, d: nc.dram_tensor(n, s, d, kind="ExternalOutput").ap()
    x1s = it_("x1s", [TPC, D], F32)
    x3s = it_("x3s", [TPC, D], F32)
    scr = [it_(f"scr{i}", [128, SEQ], BF16) for i in range(6)]
    fsc = it_("fsc", [1, SEQ], F32)
    dbg_g = it_("dbg_g", [8 * 512, TT], BF16)
    dbg_y = it_("dbg_y", [2, 8 * 128, 2048], BF16)
    dbg_z = it_("dbg_z", [8 * 128, 2048], BF16)
    it_ = lambda n, s, d: nc.dram_tensor(n, s, d).ap()
    aug = it_("augscr", [6, SEQ], BF16)
    uS16 = it_("uS16", [128, SEQ // 128, 128], BF16)
    uG = it_("uG", [NG * 128, (SEQ // 128) * 16], F32)
    nch = SEQ // 128
    eps2 = LN_EPS / (ALPHA * ALPHA)
    with ExitStack() as es0:
        P = Prog(nc, es0)
        GX = Gather(nc, P, "GX", 512, TT, BF16, 16)
        GYF = Gather(nc, P, "GYF", 128, 2048, BF16, 8)
        GYD = Gather(nc, P, "GYD", 128, 2048, BF16, 8)
        GZ = Gather(nc, P, "GZ", 128, 2048, BF16, 8)
        with ExitStack() as es:
            T = TokPipe(nc, es, P, ident, tag="_p1")
            for t in range(TPC // TT):
                t0 = t * TT
                T.load_x(x[t0:t0 + TT, :])
                T.ffn(ffw[0][0], ffw[0][1], ffw[0][2], lng[0, :], lnb[0, :])
                T.store_x(x1s[t0:t0 + TT, :])
                _xt_to_gather(P, T, GX, t)
            if nph == 1:
                P.dma("sp", dbg_g, GX.b[5], reads=[("GX", "b", 5)])
                P.finish()
                P.emit()
                return nc
            P.barrier()
            P.emit()
        with ExitStack() as es:
            sb = lambda n, s, d: es.enter_context(nc.sbuf_tensor(n + "_p2a", s, d))
            ps = lambda n: es.enter_context(nc.psum_tensor(n + "_p2a", [128, 512], F32))
            XT2 = sb("XT2", [128, 2, 16, TT], BF16)
            W = sb("Wip", [128, 16, 769], BF16)
            OT = sb("OT", [128, 4, TT], BF16)
            OF = sb("OF", [1, 2, TT], F32)
            PG = [ps(f"PG{i}") for i in range(4)]
            P.dma("pool", W[:, :, :], winc.rearrange("(k p) n -> p k n", p=128), writes=[("Wip",)])
            it = 0
            oc = 0
            for t in range(TPC // TT):
                for r in range(NCORE):
                    b = it % 2
                    it += 1
                    for q in range(4):
                        i = t * 4 + q
                        P.dma("sp", XT2[:, b, 4 * q:4 * q + 4, :],
                              GX.b[i][r * 512:(r + 1) * 512, :].rearrange("(j p) t -> p j t", p=128),
                              reads=[("GX", "b", i)], writes=[("XT2", b, q)])
                    tok0 = r * TPC + t * TT
                    xk = [("XT2", b, q) for q in range(4)]
                    for ci in range(7):
                        wdt = 128 if ci < 6 else 1
                        k = oc % 4
                        oc += 1
                        for kc in range(16):
                            P.op("pe", mm(PG[k][0:wdt, :], W[:, kc, ci * 128:ci * 128 + wdt], XT2[:, b, kc, :],
                                          kc == 0, kc == 15),
                                 reads=[("Wip",)] + xk, writes=[("ps", "Gp", k)], inc=(kc == 15))
                        if ci < 6:
                            P.op("act", lambda e, k=k: e.activation(out=OT[:, k, :], in_=PG[k][:, :], func=AF.Copy),
                                 reads=[("ps", "Gp", k)], writes=[("OTp", k)])
                            P.dma("sp", scr[ci][:, tok0:tok0 + TT], OT[:, k, :], reads=[("OTp", k)])
                        else:
                            k2 = k % 2
                            P.op("act", lambda e, k=k, k2=k2: e.activation(out=OF[0:1, k2, :], in_=PG[k][0:1, :],
                                                                          func=AF.Copy),
                                 reads=[("ps", "Gp", k)], writes=[("OFp", k2)])
                            P.dma("sp", fsc[0:1, tok0:tok0 + TT], OF[0:1, k2, :], reads=[("OFp", k2)])
            if nph == 2:
                P.finish()
                P.emit()
                return nc
            P.barrier()
            P.emit()
        with ExitStack() as es:
            attn_phase(nc, P, es, A, scr, fsc, aug, GYF, GYD)
            if nph == 3:
                P.dma("sp", dbg_y[0], GYF.b[0], reads=[("GYF", "b", 0)])
                P.dma("sp", dbg_y[1], GYD.b[0], reads=[("GYD", "b", 0)])
                P.finish()
                P.emit()
                return nc
            P.barrier()
            P.emit()
        with ExitStack() as es:
            T = TokPipe(nc, es, P, ident, tag="_p3")
            OH = es.enter_context(nc.sbuf_tensor("OH_p3", [128, NCORE], F32))
            P.dma("sp", OH[:, :], onehot, writes=[("OH",)])
            for t in range(TPC // TT):
                t0 = t * TT
                T.load_x_only(x1s[t0:t0 + TT, :])

                def ld_y(w, dst, keys, t0=t0):
                    P.dma("sp", dst[:, 0:8, :], GYF.b[w][:, t0:t0 + TT].rearrange("(h p) t -> p h t", p=128),
                          reads=[("GYF", "b", w)], writes=keys[0:8])
                    P.dma("sp", dst[:, 8:16, :], GYD.b[w][:, t0:t0 + TT].rearrange("(h p) t -> p h t", p=128),
                          reads=[("GYD", "b", w)], writes=keys[8:16])
                _select_window(P, T, OH, 16, ld_y)
                T.load_ln(lng[1, :], lnb[1, :])

                def cons_res(st, c0, b):
                    xs = T.X[:, st, c0:c0 + 256]
                    P.op("dve", lambda e: e.scalar_tensor_tensor(out=xs, in0=T.PY[b][:, 0:256], scalar=1.0 / ALPHA,
                                                                 in1=xs, op0=ALU.mult, op1=ALU.add),
                         reads=[("ps", "Y", b), ("X", st)], writes=[("X", st)])
                T.lin_tm(16, wo, D, cons_res)
                T.layernorm(eps2)
                T.ffn(ffw[1][0], ffw[1][1], ffw[1][2], lng[2, :], lnb[2, :])
                T.ffn(ffw[2][0], ffw[2][1], ffw[2][2], lng[3, :], lnb[3, :])
                T.store_x(x3s[t0:t0 + TT, :])
                _xt_to_gather(P, T, GX, t)
            if nph == 4:
                P.finish()
                P.emit()
                return nc
            P.barrier()
            P.emit()
        with ExitStack() as es:
            sb = lambda n, s, d: es.enter_context(nc.sbuf_tensor(n + "_p4a", s, d))
            ps = lambda n: es.enter_context(nc.psum_tensor(n + "_p4a", [128, 512], F32))
            XT2 = sb("XT2", [128, 2, 16, TT], BF16)
            W = sb("Wsi", [128, 16, 128], BF16)
            UB = sb("UB", [128, 2, 4, 128], BF16)
            UF = sb("UF", [128, 2, 4, 128], F32)
            PG = [ps(f"PG{i}") for i in range(2)]
            P.dma("pool", W[:, :, :], wsic.rearrange("(k p) n -> p k n", p=128), writes=[("Wsi",)])
            it = 0
            for t in range(TPC // TT):
                for r in range(NCORE):
                    b = it % 2
                    it += 1
                    for q in range(4):
                        i = t * 4 + q
                        P.dma("sp", XT2[:, b, 4 * q:4 * q + 4, :],
                              GX.b[i][r * 512:(r + 1) * 512, :].rearrange("(j p) t -> p j t", p=128),
                              reads=[("GX", "b", i)], writes=[("XT2", b, q)])
                    c0 = (r * TPC + t * TT) // 128
                    xk = [("XT2", b, q) for q in range(4)]
                    for st in range(4):
                        for kc in range(16):
                            P.op("pe", mm(PG[b][:, st * 128:(st + 1) * 128], XT2[:, b, kc, st * 128:(st + 1) * 128],
                                          W[:, kc, :], kc == 0, kc == 15),
                                 reads=[("Wsi",)] + xk, writes=[("ps", "Gu", b)], inc=(st == 3 and kc == 15))
                    pv = PG[b][:, :].rearrange("p (c h) -> p c h", c=4)
                    P.op("dve", lambda e, b=b, pv=pv: e.tensor_copy(out=UF[:, b, :, :], in_=pv),
                         reads=[("ps", "Gu", b)], writes=[("UFs", b)])
                    P.op("act", lambda e, b=b: e.activation(out=UB[:, b, :, :], in_=UF[:, b, :, :], func=AF.Copy),
                         reads=[("UFs", b)], writes=[("UB", b)])
                    P.dma("sp", uS16[:, c0:c0 + 4, :], UB[:, b, :, :], reads=[("UB", b)])
                    for g in range(NG):
                        P.dma("sp", uG[g * 128:(g + 1) * 128, c0 * 16:(c0 + 4) * 16].rearrange("s (c h) -> s c h", h=16),
                              UF[:, b, :, g * 16:(g + 1) * 16], reads=[("UFs", b)])
            if nph == 5:
                P.finish()
                P.emit()
                return nc
            P.barrier()
            P.emit()
        with ExitStack() as es:
            s5_phase(nc, P, es, S, ident, uS16, uG, GZ)
            if nph == 6:
                P.dma("sp", dbg_z, GZ.b[3], reads=[("GZ", "b", 3)])
                P.finish()
                P.emit()
                return nc
            P.barrier()
            P.emit()
        with ExitStack() as es:
            T = TokPipe(nc, es, P, ident, tag="_p5")
            OH = es.enter_context(nc.sbuf_tensor("OH_p5", [128, NCORE], F32))
            P.dma("sp", OH[:, :], onehot, writes=[("OH",)])
            for t in range(TPC // TT):
                t0 = t * TT
                T.load_x_only(x3s[t0:t0 + TT, :])

                def ld_z(w, dst, keys, t0=t0):
                    P.dma("sp", dst[:, 0:8, :], GZ.b[w][:, t0:t0 + TT].rearrange("(h p) t -> p h t", p=128),
                          reads=[("GZ", "b", w)], writes=keys)
                _select_window(P, T, OH, 8, ld_z)
                T.load_ln(lng[4, :], lnb[4, :])

                def cons_glu(st, c0, b):
                    k = T.ot % 2
                    T.ot += 1
                    sg = T.SG[:, k, 0:256]
                    xs = T.X[:, st, c0:c0 + 256]
                    P.op("act", lambda e: e.activation(out=sg, in_=T.PG[b][:, 0:256], func=AF.Sigmoid),
                         reads=[("ps", "G", b)], writes=[("SG", k)])
                    P.op("dve", lambda e: e.tensor_tensor(out=sg, in0=sg, in1=T.PY[b][:, 0:256], op=ALU.mult),
                         reads=[("SG", k), ("ps", "Y", b)], writes=[("SG", k)])
                    P.op("dve", lambda e: e.scalar_tensor_tensor(out=xs, in0=sg, scalar=1.0 / ALPHA, in1=xs,
                                                                 op0=ALU.mult, op1=ALU.add),
                         reads=[("SG", k), ("X", st)], writes=[("X", st)])
                T.lin_tm(8, wgo, D, cons_glu, extra_w=wgg)
                T.layernorm(eps2)
                T.ffn(ffw[3][0], ffw[3][1], ffw[3][2], lng[5, :], lnb[5, :])
                T.store_x(out[t0:t0 + TT, :])
            P.finish()
            P.emit()
    return nc


def l4_params(inp, core):
    gs = slice(core * NG, (core + 1) * NG)
    two = lambda a: np.ascontiguousarray(np.concatenate([a, a], 0).astype(np.float32))
    m = dict(l4_consts())
    del m["ident"]
    m["lam_re2"] = two(inp["s5_lambda_re"][0][gs].T)
    m["lam_im2"] = two(inp["s5_lambda_im"][0][gs].T)
    m["logdt2"] = np.ascontiguousarray(np.broadcast_to(inp["s5_log_dt"][0][gs][None, :], (128, NG)).astype(np.float32))
    m["br2"] = two(inp["s5_b_re"][0][gs].transpose(1, 0, 2))
    m["bi2"] = two(inp["s5_b_im"][0][gs].transpose(1, 0, 2))
    m["cr2"] = two(inp["s5_c_re"][0][gs].transpose(2, 0, 1))
    m["ci2"] = two(inp["s5_c_im"][0][gs].transpose(2, 0, 1))
    m["d2"] = np.ascontiguousarray(np.broadcast_to(inp["s5_d"][0][core * 128:(core + 1) * 128][None, :], (128, 128)).astype(np.float32))
    return m


def fused_inputs(inp):
    ca = np.ascontiguousarray
    x = inp["x"][0]
    consts = l2_consts()
    consts["ident_f"] = np.eye(128, dtype=np.float32)
    shared = {
        "w0g": inp["ffn1_w_gate"][0], "w0u": inp["ffn1_w_up"][0], "w0d": inp["ffn1_w_down"][0],
        "w1g": inp["ffn2_w_gate"][0], "w1u": inp["ffn2_w_up"][0], "w1d": inp["ffn2_w_down"][0],
        "w2g": inp["ffn1_w_gate"][1], "w2u": inp["ffn1_w_up"][1], "w2d": inp["ffn1_w_down"][1],
        "w3g": inp["ffn2_w_gate"][1], "w3u": inp["ffn2_w_up"][1], "w3d": inp["ffn2_w_down"][1],
        "lng": ca(inp["ln_gain"].reshape(6, D)), "lnb": ca(inp["ln_bias"].reshape(6, D)),
        "wo": inp["attn_w_out"][0], "wgo": inp["s5_w_glu_out"][0], "wgg": inp["s5_w_glu_gate"][0],
    }
    shared.update(consts)
    win = inp["attn_w_in"][0]
    maps = []
    for c in range(NCORE):
        m = dict(shared)
        m["x"] = ca(x[c * TPC:(c + 1) * TPC])
        cols = np.concatenate([np.arange(128) + off + 128 * c for off in (0, 1024, 2048, 3080, 4104, 5128)]
                              + [np.array([3072 + c])])
        m["winc"] = ca(win[:, cols])
        m["wsic"] = ca(inp["s5_w_in"][0][:, 128 * c:128 * (c + 1)])
        oh = np.zeros((128, NCORE), np.float32)
        oh[:, c] = 1.0
        m["onehot"] = oh
        m["nbf"] = np.full((128, 1), inp["attn_b_f"][0, c], np.float32)
        m.update(l4_params(inp, c))
        maps.append(m)
    return maps


def kernel(**inputs):
    inp = {k: np.asarray(v) for k, v in inputs.items()}
    maps = fused_inputs(inp)
    res = _run(_get("fused", build_fused), maps)
    out = np.concatenate([np.asarray(r["out"]) for r in res], axis=0)
    return out.reshape(1, SEQ, D).astype(np.float32)
```

```python
from contextlib import ExitStack
import math
import numpy as np
import ml_dtypes

import concourse.bass as bass
import concourse.mybir as mybir
from concourse.bass_utils import run_bass_kernel_spmd

F32 = mybir.dt.float32
BF16 = mybir.dt.bfloat16
ALU = mybir.AluOpType
AF = mybir.ActivationFunctionType

D = 2048
SEQ = 16384
NCORE = 8
TPC = SEQ // NCORE
DFF = 5632
NFC = DFF // 128
HD = 128
ATT_IN = 6152
S5W = 1024
ALPHA = 4.0 ** 0.25
LN_EPS = 1e-5
TT = 512
NST = TT // 128


class Prog:
    ENGS = ("pe", "act", "dve", "pool", "sp")

    def __init__(self, nc, es, ndma=6):
        self.nc = nc
        self.ops = {e: [] for e in self.ENGS}
        self.semh = {}
        self.cnt = {e: 0 for e in self.ENGS}
        for e in self.ENGS:
            self.semh["e:" + e] = es.enter_context(nc.semaphore("se_" + e))
        self.nd = ndma
        self.dcnt = {}
        self.dnext = {"sp": 0, "pool": 0}
        for q in ("sp", "pool"):
            for k in range(ndma):
                self.semh[f"d:{q}:{k}"] = es.enter_context(nc.semaphore(f"sd_{q}{k}"))
                self.dcnt[(q, k)] = 0
        self.known = {e: {} for e in self.ENGS}
        self.res = {}
        self.semh["e:cc"] = es.enter_context(nc.semaphore("se_cc"))
        self.ccn = 0

    def collective(self, groups, src, dst, reads=(), writes=()):
        waits = self._collect("pool", reads, writes)
        self.ccn += 1
        ev = ("e:cc", self.ccn)
        self.ops["pool"].append((waits, (lambda e, g=groups, a=src, b=dst: e.collective_compute(
            "AllGather", ALU.bypass, replica_groups=g, ins=[a], outs=[b])), "e:cc", 1))
        self._commit(ev, reads, writes)

    def barrier(self):
        allw = []
        for e in self.ENGS:
            if self.cnt[e]:
                allw.append(("e:" + e, self.cnt[e]))
        for (q, k), c in self.dcnt.items():
            if c:
                allw.append((f"d:{q}:{k}", c * 16))
        for e in self.ENGS:
            kn = self.known[e]
            waits = []
            for sk, v in allw:
                if sk == "e:" + e:
                    continue
                if kn.get(sk, 0) < v:
                    kn[sk] = v
                    waits.append((sk, v))
            self.ops[e].append((waits, None, None, 0))

    def _collect(self, eng, reads, writes):
        deps = {}
        own = "e:" + eng

        def add(ev, same_ok):
            if ev is None:
                return
            sk, v = ev
            if sk == own and not same_ok:
                return
            if deps.get(sk, 0) < v:
                deps[sk] = v

        for r in reads:
            st = self.res.get(r)
            if st is not None and st[0] is not None:
                add(st[0], True)
        for w in writes:
            st = self.res.get(w)
            if st is not None:
                add(st[0], False)
                for ev in st[1].values():
                    add(ev, False)
        waits = []
        kn = self.known[eng]
        for sk, v in deps.items():
            if kn.get(sk, 0) < v:
                kn[sk] = v
                waits.append((sk, v))
        return waits

    def _commit(self, ev, reads, writes):
        sk = ev[0]
        for r in reads:
            st = self.res.get(r)
            if st is None:
                st = [None, {}]
                self.res[r] = st
            old = st[1].get(sk)
            if old is None or old[1] < ev[1]:
                st[1][sk] = ev
        for w in writes:
            self.res[w] = [ev, {}]

    def op(self, eng, fn, reads=(), writes=(), inc=True):
        waits = self._collect(eng, reads, writes)
        if inc:
            self.cnt[eng] += 1
            ev = ("e:" + eng, self.cnt[eng])
        else:
            ev = ("e:" + eng, self.cnt[eng] + 1)
        self.ops[eng].append((waits, fn, ("e:" + eng) if inc else None, 1))
        self._commit(ev, reads, writes)

    def dma(self, q, out, in_, reads=(), writes=()):
        k = self.dnext[q] % self.nd
        self.dnext[q] += 1
        sk = f"d:{q}:{k}"
        waits = self._collect(q, reads, writes)
        prev = self.dcnt[(q, k)] * 16
        if prev and self.known[q].get(sk, 0) < prev:
            self.known[q][sk] = prev
            waits.append((sk, prev))
        self.dcnt[(q, k)] += 1
        ev = (sk, self.dcnt[(q, k)] * 16)
        self.ops[q].append((waits, (lambda e, o=out, i=in_: e.dma_start(out=o, in_=i)), sk, 16))
        self._commit(ev, reads, writes)

    def dma_fn(self, q, fn, reads=(), writes=()):
        k = self.dnext[q] % self.nd
        self.dnext[q] += 1
        sk = f"d:{q}:{k}"
        waits = self._collect(q, reads, writes)
        prev = self.dcnt[(q, k)] * 16
        if prev and self.known[q].get(sk, 0) < prev:
            self.known[q][sk] = prev
            waits.append((sk, prev))
        self.dcnt[(q, k)] += 1
        ev = (sk, self.dcnt[(q, k)] * 16)
        self.ops[q].append((waits, fn, sk, 16))
        self._commit(ev, reads, writes)

    def finish(self):
        waits = []
        for e in self.ENGS:
            if self.cnt[e]:
                waits.append(("e:" + e, self.cnt[e]))
        for (q, k), c in self.dcnt.items():
            if c:
                waits.append((f"d:{q}:{k}", c * 16))
        if self.ccn:
            waits.append(("e:cc", self.ccn))
        self.ops["sp"].append((waits, None, None, 0))

    def emit(self):
        nc = self.nc
        with nc.Block() as block:
            def mk(name):
                def body(e):
                    for waits, fn, sk, n in self.ops[name]:
                        for (wsk, v) in waits:
                            e.wait_ge(self.semh[wsk], v)
                        if fn is None:
                            continue
                        ins = fn(e)
                        if sk is not None:
                            ins.then_inc(self.semh[sk], n)
                return body
            block.tensor(mk("pe"))
            block.scalar(mk("act"))
            block.vector(mk("dve"))
            block.gpsimd(mk("pool"))
            block.sync(mk("sp"))
        self.ops = {e: [] for e in self.ENGS}


def mm(out, lhsT, rhs, start, stop):
    return lambda e: e.matmul(out, lhsT, rhs, start=start, stop=stop)


class TokPipe:
    def __init__(self, nc, es, P, ident_dram, tag=""):
        self.nc, self.P = nc, P
        sb = lambda n, s, d: es.enter_context(nc.sbuf_tensor(n + tag, s, d))
        ps = lambda n: es.enter_context(nc.psum_tensor(n + tag, [128, 512], F32))
        self.X = sb("X", [128, NST, D], F32)
        self.XT = sb("XT", [128, 16, TT], BF16)
        self.HT = sb("HT", [128, NFC, TT], BF16)
        self.WR = sb("WR", [128, 3, 11264], BF16)
        self.G = sb("G", [128, D], F32)
        self.B = sb("B", [128, D], F32)
        self.XB = sb("XB", [128, 2, D], BF16)
        self.SG = sb("SG", [128, 2, TT], F32)
        self.OT = sb("OT", [128, 2, TT], BF16)
        self.OF = sb("OF", [128, 2, TT], F32)
        self.stats = sb("stats", [128, NST, 24], F32)
        self.mv = sb("mv", [128, NST, 2], F32)
        self.rstd = sb("rstd", [128, NST, 1], F32)
        self.nmr = sb("nmr", [128, NST, 1], F32)
        self.ident = sb("ident_sb", [128, 128], BF16)
        self.PG = [ps("PG0"), ps("PG1")]
        self.PU = [ps("PU0"), ps("PU1")]
        self.PY = [ps("PY0"), ps("PY1")]
        self.PT = [ps("PT0"), ps("PT1")]
        self.ring = 0
        self.ot = 0
        P.dma("sp", self.ident[:, :], ident_dram, writes=[("ident",)])

    def ring_next(self):
        s = self.ring % 3
        self.ring += 1
        return s

    def load_x(self, x_rows):
        P = self.P
        P.dma("sp", self.X[:, :, :], x_rows.rearrange("(s p) d -> p s d", p=128),
              writes=[("X", st) for st in range(NST)])
        for st in range(NST):
            self.make_xt(st)

    def load_x_only(self, x_rows):
        P = self.P
        P.dma("sp", self.X[:, :, :], x_rows.rearrange("(s p) d -> p s d", p=128),
              writes=[("X", st) for st in range(NST)])

    def load_xt(self, xt_rows, nk):
        P = self.P
        P.dma("sp", self.XT[:, 0:nk, :], xt_rows.rearrange("(k p) t -> p k t", p=128),
              writes=[("XT", st) for st in range(NST)])

    def store_x(self, out_rows):
        P = self.P
        P.dma("sp", out_rows.rearrange("(s p) d -> p s d", p=128), self.X[:, :, :],
              reads=[("X", st) for st in range(NST)])

    def make_xt(self, st):
        P = self.P
        b = st % 2
        X, XB, XT, PT, ident = self.X, self.XB, self.XT, self.PT, self.ident
        P.op("act", lambda e: e.activation(out=XB[:, b, :], in_=X[:, st, :], func=AF.Copy),
             reads=[("X", st)], writes=[("XB", b)])
        for kg in range(4):
            pb = (st * 4 + kg) % 2
            for j in range(4):
                kc = kg * 4 + j
                P.op("pe", mm(PT[pb][:, j * 128:(j + 1) * 128], XB[:, b, kc * 128:(kc + 1) * 128],
                              ident[:, :], True, True),
                     reads=[("XB", b), ("ident",)], writes=[("ps", "T", pb)], inc=(j == 3))
            src = PT[pb][:, :].rearrange("p (j t) -> p j t", j=4)
            dst = XT[:, kg * 4:(kg + 1) * 4, st * 128:(st + 1) * 128]
            P.op("dve", lambda e, s=src, d=dst: e.tensor_copy(out=d, in_=s),
                 reads=[("ps", "T", pb)], writes=[("XT", st)])

    def load_ln(self, g_row, b_row):
        P = self.P
        P.dma("sp", self.G[:, :], g_row.partition_broadcast(128), writes=[("G",)])
        P.dma("sp", self.B[:, :], b_row.partition_broadcast(128), writes=[("B",)])

    def layernorm(self, eps, make_xt=True):
        P = self.P
        X, G, B = self.X, self.G, self.B
        stats, mv, rstd, nmr = self.stats, self.mv, self.rstd, self.nmr
        for st in range(NST):
            for q in range(4):
                P.op("dve", lambda e, st=st, q=q: e.bn_stats(out=stats[:, st, q * 6:(q + 1) * 6],
                                                             in_=X[:, st, q * 512:(q + 1) * 512]),
                     reads=[("X", st)], writes=[("stats", st)])
            P.op("dve", lambda e, st=st: e.bn_aggr(out=mv[:, st, :], in_=stats[:, st, :]),
                 reads=[("stats", st)], writes=[("mv", st)])
            P.op("dve", lambda e, st=st: e.tensor_scalar(out=rstd[:, st, :], in0=mv[:, st, 1:2],
                                                         scalar1=eps, scalar2=None, op0=ALU.add),
                 reads=[("mv", st)], writes=[("rstd", st)])
            P.op("act", lambda e, st=st: e.activation(out=rstd[:, st, :], in_=rstd[:, st, :], func=AF.Ln),
                 reads=[("rstd", st)], writes=[("rstd", st)])
            P.op("act", lambda e, st=st: e.activation(out=rstd[:, st, :], in_=rstd[:, st, :], func=AF.Exp,
                                                      scale=-0.5),
                 reads=[("rstd", st)], writes=[("rstd", st)])
            P.op("dve", lambda e, st=st: e.scalar_tensor_tensor(out=nmr[:, st, :], in0=mv[:, st, 0:1],
                                                                scalar=-1.0, in1=rstd[:, st, :],
                                                                op0=ALU.mult, op1=ALU.mult),
                 reads=[("mv", st), ("rstd", st)], writes=[("nmr", st)])
            P.op("act", lambda e, st=st: e.activation(out=X[:, st, :], in_=X[:, st, :], func=AF.Identity,
                                                      bias=nmr[:, st, :], scale=rstd[:, st, :]),
                 reads=[("X", st), ("rstd", st), ("nmr", st)], writes=[("X", st)])
            P.op("dve", lambda e, st=st: e.tensor_tensor(out=X[:, st, :], in0=X[:, st, :], in1=G[:, :],
                                                         op=ALU.mult),
                 reads=[("X", st), ("G",)], writes=[("X", st)])
            P.op("pool", lambda e, st=st: e.tensor_tensor(out=X[:, st, :], in0=X[:, st, :], in1=B[:, :],
                                                          op=ALU.add),
                 reads=[("X", st), ("B",)], writes=[("X", st)])
            if make_xt:
                self.make_xt(st)

    def ffn(self, wg, wu, wd, g_row, b_row):
        P = self.P
        XT, HT, WR, SG, X = self.XT, self.HT, self.WR, self.SG, self.X
        PG, PU, PY = self.PG, self.PU, self.PY
        self.load_ln(g_row, b_row)
        wgv = wg.rearrange("(k p) n -> p k n", p=128)
        wuv = wu.rearrange("(k p) n -> p k n", p=128)
        wdv = wd.rearrange("(k p) n -> p k n", p=128)
        xt_keys = [("XT", st) for st in range(NST)]
        for fg in range(NFC // 2):
            s = self.ring_next()
            gv = WR[:, s, 0:4096].rearrange("p (k n) -> p k n", k=16)
            uv = WR[:, s, 4096:8192].rearrange("p (k n) -> p k n", k=16)
            P.dma("pool", gv, wgv[:, :, fg * 256:(fg + 1) * 256], writes=[("WR", s, "a")])
            P.dma("pool", uv, wuv[:, :, fg * 256:(fg + 1) * 256], writes=[("WR", s, "b")])
            for fc in range(2):
                ch = fg * 2 + fc
                b = ch % 2
                for kc in range(16):
                    P.op("pe", mm(PG[b][:, :], gv[:, kc, fc * 128:(fc + 1) * 128], XT[:, kc, :],
                                  kc == 0, kc == 15),
                         reads=[("WR", s, "a")] + xt_keys, writes=[("ps", "G", b)], inc=(kc == 15))
                for kc in range(16):
                    P.op("pe", mm(PU[b][:, :], uv[:, kc, fc * 128:(fc + 1) * 128], XT[:, kc, :],
                                  kc == 0, kc == 15),
                         reads=[("WR", s, "b")] + xt_keys, writes=[("ps", "U", b)], inc=(kc == 15))
                P.op("act", lambda e, b=b: e.activation(out=SG[:, b, :], in_=PG[b][:, :], func=AF.Silu),
                     reads=[("ps", "G", b)], writes=[("SG", b)])
                P.op("dve", lambda e, b=b, ch=ch: e.tensor_tensor(out=HT[:, ch, :], in0=SG[:, b, :],
                                                                  in1=PU[b][:, :], op=ALU.mult),
                     reads=[("SG", b), ("ps", "U", b)], writes=[("HT", ch)])
        coef = 0.5 / ALPHA
        ht_keys = [("HT", ch) for ch in range(NFC)]
        it = 0
        for dp in range(D // 256):
            s = self.ring_next()
            dv = WR[:, s, 0:NFC * 256].rearrange("p (k n) -> p k n", k=NFC)
            P.dma("pool", dv, wdv[:, :, dp * 256:(dp + 1) * 256],
                  writes=[("WR", s, "a"), ("WR", s, "b")])
            for st in range(NST):
                b = it % 2
                it += 1
                for fc in range(NFC):
                    P.op("pe", mm(PY[b][:, 0:256], HT[:, fc, st * 128:(st + 1) * 128], dv[:, fc, :],
                                  fc == 0, fc == NFC - 1),
                         reads=[("WR", s, "a"), ("WR", s, "b")] + (ht_keys if fc == 0 else []),
                         writes=[("ps", "Y", b)], inc=(fc == NFC - 1))
                xs = X[:, st, dp * 256:(dp + 1) * 256]
                P.op("dve", lambda e, b=b, xs=xs: e.scalar_tensor_tensor(out=xs, in0=PY[b][:, 0:256],
                                                                         scalar=coef, in1=xs,
                                                                         op0=ALU.mult, op1=ALU.add),
                     reads=[("ps", "Y", b), ("X", st)], writes=[("X", st)])
        self.layernorm(LN_EPS / (ALPHA * ALPHA))

    def lin_tm(self, nk, w, ncols, consumer, extra_w=None):
        P = self.P
        XT, WR, PY, PG = self.XT, self.WR, self.PY, self.PG
        wv = w.rearrange("(k p) n -> p k n", p=128)
        wv2 = extra_w.rearrange("(k p) n -> p k n", p=128) if extra_w is not None else None
        xt_keys = [("XT", st) for st in range(NST)]
        it = 0
        for dp in range(ncols // 256):
            s = self.ring_next()
            dv = WR[:, s, 0:nk * 256].rearrange("p (k n) -> p k n", k=nk)
            P.dma("pool", dv, wv[:, :, dp * 256:(dp + 1) * 256], writes=[("WR", s, "a")])
            if wv2 is not None:
                dv2 = WR[:, s, 5632:5632 + nk * 256].rearrange("p (k n) -> p k n", k=nk)
                P.dma("pool", dv2, wv2[:, :, dp * 256:(dp + 1) * 256], writes=[("WR", s, "b")])
            for st in range(NST):
                b = it % 2
                it += 1
                for kc in range(nk):
                    P.op("pe", mm(PY[b][:, 0:256], XT[:, kc, st * 128:(st + 1) * 128], dv[:, kc, :],
                                  kc == 0, kc == nk - 1),
                         reads=[("WR", s, "a")] + xt_keys, writes=[("ps", "Y", b)], inc=(kc == nk - 1))
                if wv2 is not None:
                    for kc in range(nk):
                        P.op("pe", mm(PG[b][:, 0:256], XT[:, kc, st * 128:(st + 1) * 128], dv2[:, kc, :],
                                      kc == 0, kc == nk - 1),
                             reads=[("WR", s, "b")] + xt_keys, writes=[("ps", "G", b)],
                             inc=(kc == nk - 1))
                    consumer(st, dp * 256, b)
                else:
                    consumer(st, dp * 256, b)

    def lin_fm(self, w, pieces, consumer):
        P = self.P
        XT, WR, PG = self.XT, self.WR, self.PG
        wv = w.rearrange("(k p) n -> p k n", p=128)
        xt_keys = [("XT", st) for st in range(NST)]
        it = 0
        for (c0, ncol) in pieces:
            s = self.ring_next()
            dv = WR[:, s, 0:16 * ncol].rearrange("p (k n) -> p k n", k=16)
            P.dma("pool", dv, wv[:, :, c0:c0 + ncol], writes=[("WR", s, "a")])
            off = 0
            while off < ncol:
                wdt = min(128, ncol - off)
                b = it % 2
                it += 1
                for kc in range(16):
                    P.op("pe", mm(PG[b][0:wdt, :], dv[:, kc, off:off + wdt], XT[:, kc, :],
                                  kc == 0, kc == 15),
                         reads=[("WR", s, "a")] + xt_keys, writes=[("ps", "G", b)], inc=(kc == 15))
                consumer(c0 + off, wdt, b)
                off += wdt


def _new_nc():
    return bass.Bass("TRN2", target_bir_lowering=False)


def _ident_np():
    return np.eye(128, dtype=np.float32).astype(ml_dtypes.bfloat16)


def build_l1(ntok=TPC):
    nc = _new_nc()
    dt = lambda n, s, d, k: nc.dram_tensor(n, s, d, kind=k).ap()
    x = dt("x", [ntok, D], F32, "ExternalInput")
    wg = dt("wg", [D, DFF], F32, "ExternalInput")
    wu = dt("wu", [D, DFF], F32, "ExternalInput")
    wd = dt("wd", [DFF, D], F32, "ExternalInput")
    lng = dt("lng", [1, D], F32, "ExternalInput")
    lnb = dt("lnb", [1, D], F32, "ExternalInput")
    win = dt("win", [D, ATT_IN], F32, "ExternalInput")
    ident = dt("ident", [128, 128], BF16, "ExternalInput")
    x1 = dt("x1", [ntok, D], F32, "ExternalOutput")
    projT = dt("projT", [6144, ntok], BF16, "ExternalOutput")
    fT = dt("fT", [8, ntok], F32, "ExternalOutput")
    with ExitStack() as es:
        P = Prog(nc, es)
        T = TokPipe(nc, es, P, ident)
        pieces = [(c, 256) for c in range(0, 3072, 256)] + [(3072, 8)] + \
                 [(c, 256) for c in range(3080, 6152, 256)]
        for t in range(ntok // TT):
            t0 = t * TT
            T.load_x(x[t0:t0 + TT, :])
            T.ffn(wg, wu, wd, lng[0, :], lnb[0, :])
            T.store_x(x1[t0:t0 + TT, :])

            def cons(c0, wdt, b, t0=t0):
                k = T.ot % 2
                T.ot += 1
                if wdt == 8:
                    P.op("act", lambda e: e.activation(out=T.OF[0:8, k, :], in_=T.PG[b][0:8, :], func=AF.Copy),
                         reads=[("ps", "G", b)], writes=[("OF", k)])
                    P.dma("sp", fT[:, t0:t0 + TT], T.OF[0:8, k, :], reads=[("OF", k)])
                else:
                    r0 = c0 if c0 < 3072 else c0 - 8
                    P.op("act", lambda e: e.activation(out=T.OT[0:wdt, k, :], in_=T.PG[b][0:wdt, :],
                                                       func=AF.Copy),
                         reads=[("ps", "G", b)], writes=[("OT", k)])
                    P.dma("sp", projT[r0:r0 + wdt, t0:t0 + TT], T.OT[0:wdt, k, :], reads=[("OT", k)])
            T.lin_fm(win, pieces, cons)
        P.finish()
        P.emit()
    return nc


def build_l3(ntok=TPC):
    nc = _new_nc()
    dt = lambda n, s, d, k="ExternalInput": nc.dram_tensor(n, s, d, kind=k).ap()
    x1 = dt("x1", [ntok, D], F32)
    yT = dt("yT", [D, ntok], BF16)
    wo = dt("wo", [D, D], F32)
    w2 = [dt("w2g", [D, DFF], F32), dt("w2u", [D, DFF], F32), dt("w2d", [DFF, D], F32)]
    w3 = [dt("w3g", [D, DFF], F32), dt("w3u", [D, DFF], F32), dt("w3d", [DFF, D], F32)]
    lng = dt("lng", [3, D], F32)
    lnb = dt("lnb", [3, D], F32)
    wsi = dt("wsi", [D, S5W], F32)
    ident = dt("ident", [128, 128], BF16)
    x3 = dt("x3", [ntok, D], F32, "ExternalOutput")
    u = dt("u", [ntok, S5W], F32, "ExternalOutput")
    with ExitStack() as es:
        P = Prog(nc, es)
        T = TokPipe(nc, es, P, ident)
        for t in range(ntok // TT):
            t0 = t * TT
            T.load_x_only(x1[t0:t0 + TT, :])
            T.load_xt(yT[:, t0:t0 + TT], 16)
            T.load_ln(lng[0, :], lnb[0, :])

            def cons_res(st, c0, b):
                xs = T.X[:, st, c0:c0 + 256]
                P.op("dve", lambda e: e.scalar_tensor_tensor(out=xs, in0=T.PY[b][:, 0:256], scalar=1.0 / ALPHA,
                                                             in1=xs, op0=ALU.mult, op1=ALU.add),
                     reads=[("ps", "Y", b), ("X", st)], writes=[("X", st)])
            T.lin_tm(16, wo, D, cons_res)
            T.layernorm(LN_EPS / (ALPHA * ALPHA))
            T.ffn(w2[0], w2[1], w2[2], lng[1, :], lnb[1, :])
            T.ffn(w3[0], w3[1], w3[2], lng[2, :], lnb[2, :])
            T.store_x(x3[t0:t0 + TT, :])

            def cons_u(st, c0, b, t0=t0):
                k = T.ot % 2
                T.ot += 1
                P.op("act", lambda e: e.activation(out=T.OF[:, k, 0:256], in_=T.PY[b][:, 0:256], func=AF.Copy),
                     reads=[("ps", "Y", b)], writes=[("OF", k)])
                P.dma("sp", u[t0 + st * 128:t0 + (st + 1) * 128, c0:c0 + 256], T.OF[:, k, 0:256],
                      reads=[("OF", k)])
            T.lin_tm(16, wsi, S5W, cons_u)
        P.finish()
        P.emit()
    return nc


def build_l5(ntok=TPC):
    nc = _new_nc()
    dt = lambda n, s, d, k="ExternalInput": nc.dram_tensor(n, s, d, kind=k).ap()
    x3 = dt("x3", [ntok, D], F32)
    zT = dt("zT", [S5W, ntok], BF16)
    wgo = dt("wgo", [S5W, D], F32)
    wgg = dt("wgg", [S5W, D], F32)
    w4 = [dt("w4g", [D, DFF], F32), dt("w4u", [D, DFF], F32), dt("w4d", [DFF, D], F32)]
    lng = dt("lng", [2, D], F32)
    lnb = dt("lnb", [2, D], F32)
    ident = dt("ident", [128, 128], BF16)
    out = dt("out", [ntok, D], F32, "ExternalOutput")
    with ExitStack() as es:
        P = Prog(nc, es)
        T = TokPipe(nc, es, P, ident)
        for t in range(ntok // TT):
            t0 = t * TT
            T.load_x_only(x3[t0:t0 + TT, :])
            T.load_xt(zT[:, t0:t0 + TT], 8)
            T.load_ln(lng[0, :], lnb[0, :])

            def cons_glu(st, c0, b):
                k = T.ot % 2
                T.ot += 1
                sg = T.SG[:, k, 0:256]
                xs = T.X[:, st, c0:c0 + 256]
                P.op("act", lambda e: e.activation(out=sg, in_=T.PG[b][:, 0:256], func=AF.Sigmoid),
                     reads=[("ps", "G", b)], writes=[("SG", k)])
                P.op("dve", lambda e: e.tensor_tensor(out=sg, in0=sg, in1=T.PY[b][:, 0:256], op=ALU.mult),
                     reads=[("SG", k), ("ps", "Y", b)], writes=[("SG", k)])
                P.op("dve", lambda e: e.scalar_tensor_tensor(out=xs, in0=sg, scalar=1.0 / ALPHA, in1=xs,
                                                             op0=ALU.mult, op1=ALU.add),
                     reads=[("SG", k), ("X", st)], writes=[("X", st)])
            T.lin_tm(8, wgo, D, cons_glu, extra_w=wgg)
            T.layernorm(LN_EPS / (ALPHA * ALPHA))
            T.ffn(w4[0], w4[1], w4[2], lng[1, :], lnb[1, :])
            T.store_x(out[t0:t0 + TT, :])
        P.finish()
        P.emit()
    return nc


DIL = ((1, 16), (4, 4), (16, 1))
NEG = -30000.0


def l2_consts():
    p = np.arange(128)[:, None]
    q = np.arange(512)[None, :]
    fm = np.zeros((128, 4, 512), np.float32)
    for jj in range(4):
        fm[:, jj, :] = np.where(jj * 128 + p > q, NEG, 0.0)
    q1 = np.arange(128)[None, :]
    dm = np.zeros((128, 256), np.float32)
    dm[:, 0:128] = np.where(p < q1, NEG, 0.0)
    dm[:, 128:256] = np.where(p > q1, NEG, 0.0)
    uincl = (p <= q1).astype(np.float32)
    lstrict = (p < q1).astype(np.float32)
    bf = ml_dtypes.bfloat16
    return {
        "ident": _ident_np(), "ones_bf": np.ones((128, 128), bf), "fmask": fm.astype(bf),
        "dmask": dm.astype(bf), "uincl": uincl, "lstrict": lstrict, "ones_f": np.ones((128, 128), np.float32),
    }


def build_l2(seq=SEQ):
    nc = _new_nc()
    dt = lambda n, s, d, k="ExternalInput": nc.dram_tensor(n, s, d, kind=k).ap()
    nblk = seq // 128
    qf = dt("qf", [128, seq], BF16)
    kf = dt("kf", [128, seq], BF16)
    vf = dt("vf", [seq, 128], BF16)
    f2T = dt("f2T", [128, nblk], F32)
    nbf = dt("nbf", [128, 1], F32)
    qd = dt("qd", [128, seq], BF16)
    kd = dt("kd", [128, seq], BF16)
    vdp = [dt(f"vd{d}", [seq, 128], BF16) for (d, _) in DIL]
    c_ident = dt("ident", [128, 128], BF16)
    c_ones = dt("ones_bf", [128, 128], BF16)
    c_fm = dt("fmask", [128, 4, 512], BF16)
    c_dm = dt("dmask", [128, 256], BF16)
    c_ui = dt("uincl", [128, 128], F32)
    c_ls = dt("lstrict", [128, 128], F32)
    c_of = dt("ones_f", [128, 128], F32)
    yf = dt("yf", [128, seq], BF16, "ExternalOutput")
    yd = dt("yd", [128, seq], BF16, "ExternalOutput")
    scr = nc.dram_tensor("scr", [6, seq], BF16).ap()
    scale = HD ** -0.5
    with ExitStack() as es:
        P = Prog(nc, es)
        sb = lambda n, s, d: es.enter_context(nc.sbuf_tensor(n, s, d))
        ps = lambda n: es.enter_context(nc.psum_tensor(n, [128, 512], F32))
        BIG = [sb(f"BIG{i}", [128, seq], BF16) for i in range(4)]
        ident = sb("identb", [128, 128], BF16)
        ones = sb("onesb", [128, 128], BF16)
        FM = sb("FM", [128, 4, 512], BF16)
        DM = sb("DM", [128, 256], BF16)
        UI = sb("UI", [128, 128], F32)
        LS = sb("LS", [128, 128], F32)
        OFc = sb("OFc", [128, 128], F32)
        NB = sb("NB", [128, 1], F32)
        LF = sb("LF", [128, nblk], F32)
        RB = sb("RB", [128, 128], F32)
        C = sb("C", [128, 128], F32)
        R1 = sb("R1", [128, 128], F32)
        HI = sb("HI", [128, 6, 128], BF16)
        QF = sb("QF", [128, 2, 512], BF16)
        PTt = sb("PTt", [128, 3, 512], BF16)
        RL = sb("RL", [128, 512], F32)
        YO = sb("YO", [128, 2, 512], BF16)
        VU = sb("VU", [128, 4, 2, 128], BF16)
        ACC = sb("ACC", [128, 2, 2048], F32)
        YW = sb("YW", [128, 2048], BF16)
        PS = [ps("PS0"), ps("PS1")]
        PO = [ps("PO0"), ps("PO1")]
        PL = [ps("PL0"), ps("PL1")]
        for (t, src, key) in ((ident, c_ident, "ident"), (ones, c_ones, "ones"), (FM, c_fm, "FM"),
                              (DM, c_dm, "DM"), (UI, c_ui, "UI"), (LS, c_ls, "LS"), (OFc, c_of, "OFc"),
                              (NB, nbf, "NB"), (LF, f2T, "LF")):
            idx = (slice(None),) * len(t.shape)
            P.dma("sp", t[idx], src, writes=[(key,)])
        P.op("dve", lambda e: e.tensor_scalar(out=NB[:, :], in0=NB[:, :], scalar1=-1.0, scalar2=None, op0=ALU.mult),
             reads=[("NB",)], writes=[("NB",)])
        KT, VK, AK, AQ = BIG[0], BIG[1], BIG[2], BIG[3]
        P.dma("sp", KT[:, :], kf, writes=[("BIG", 0)])
        P.dma("sp", VK[:, :].rearrange("p (b d) -> p b d", d=128), vf.rearrange("(b p) d -> p b d", p=128),
              writes=[("BIG", 1)])
        P.op("act", lambda e: e.activation(out=LF[:, :], in_=LF[:, :], func=AF.Exp, bias=NB[:, :], scale=-1.0),
             reads=[("LF",), ("NB",)], writes=[("LF",)])
        P.op("act", lambda e: e.activation(out=LF[:, :], in_=LF[:, :], func=AF.Ln, bias=OFc[:, 0:1], scale=1.0),
             reads=[("LF",), ("OFc",)], writes=[("LF",)])
        P.op("dve", lambda e: e.tensor_scalar(out=LF[:, :], in0=LF[:, :], scalar1=-1.0, scalar2=None, op0=ALU.mult),
             reads=[("LF",)], writes=[("LF",)])
        P.op("pe", mm(PS[0][0:nblk, 0:128], LF[:, :], OFc[:, :], True, True),
             reads=[("LF",), ("OFc",)], writes=[("ps", "S", 0)])
        P.op("dve", lambda e: e.tensor_copy(out=RB[0:nblk, :], in_=PS[0][0:nblk, 0:128]),
             reads=[("ps", "S", 0)], writes=[("RB",)])
        P.op("pe", mm(PS[1][0:nblk, 0:128], LF[:, :], UI[:, :], True, False),
             reads=[("LF",), ("UI",)], writes=[("ps", "S", 1)], inc=False)
        P.op("pe", mm(PS[1][0:nblk, 0:128], LS[0:nblk, 0:nblk], RB[0:nblk, :], False, True),
             reads=[("RB",), ("LS",)], writes=[("ps", "S", 1)])
        P.op("dve", lambda e: e.tensor_scalar(out=C[0:nblk, :], in0=PS[1][0:nblk, 0:128], scalar1=1.0 / scale,
                                              scalar2=None, op0=ALU.mult),
             reads=[("ps", "S", 1)], writes=[("C",)])
        cur = C
        for i in range(3):
            P.op("dve", lambda e, i=i, cur=cur: e.tensor_copy(out=HI[0:nblk, 3 + i, :], in_=cur[0:nblk, :]),
                 reads=[("C",), ("R1",)], writes=[("HI", 3 + i)])
            P.op("dve", lambda e, i=i: e.tensor_scalar(out=HI[0:nblk, i, :], in0=HI[0:nblk, 3 + i, :],
                                                       scalar1=-1.0, scalar2=None, op0=ALU.mult),
                 reads=[("HI", 3 + i)], writes=[("HI", i)])
            if i < 2:
                P.op("dve", lambda e, i=i, cur=cur: e.tensor_tensor(out=R1[0:nblk, :], in0=cur[0:nblk, :],
                                                                    in1=HI[0:nblk, 3 + i, :], op=ALU.subtract),
                     reads=[("C",), ("R1",), ("HI", 3 + i)], writes=[("R1",)])
                cur = R1
        for i in range(6):
            P.dma("sp", scr[i, :].rearrange("(p j) -> p j", j=128), HI[0:nblk, i, :],
                  reads=[("HI", i)], writes=[("scr", i)])
        P.op("pool", lambda e: e.memset(AK[0:6, :], 1.0), writes=[("BIG", 2)])
        P.op("pool", lambda e: e.memset(AQ[0:6, :], 1.0), writes=[("BIG", 3)])
        P.dma("sp", AK[0:3, :], scr[0:3, :], reads=[("scr", i) for i in range(3)], writes=[("BIG", 2)])
        P.dma("sp", AQ[3:6, :], scr[3:6, :], reads=[("scr", 3 + i) for i in range(3)], writes=[("BIG", 3)])
        pt = 0
        for i in range(seq // 512):
            qb = i % 2
            P.dma("sp", QF[:, qb, :], qf[:, i * 512:(i + 1) * 512], writes=[("QF", qb)])
            nj = 4 * i + 4
            for j in range(nj):
                sbk = j % 2
                diag = j >= 4 * i
                P.op("pe", mm(PS[sbk][:, :], KT[:, j * 128:(j + 1) * 128], QF[:, qb, :], True, False),
                     reads=[("BIG", 0), ("QF", qb)], writes=[("ps", "S", sbk)], inc=False)
                P.op("pe", mm(PS[sbk][:, :], AK[0:6, j * 128:(j + 1) * 128], AQ[0:6, i * 512:(i + 1) * 512],
                              False, not diag),
                     reads=[("BIG", 2), ("BIG", 3)], writes=[("ps", "S", sbk)], inc=not diag)
                if diag:
                    P.op("pe", mm(PS[sbk][:, :], ident[:, :], FM[:, j - 4 * i, :], False, True),
                         reads=[("ident",), ("FM",)], writes=[("ps", "S", sbk)])
                pk = pt % 3
                pt += 1
                P.op("act", lambda e, sbk=sbk, pk=pk: e.activation(out=PTt[:, pk, :], in_=PS[sbk][:, :],
                                                                   func=AF.Exp, scale=scale),
                     reads=[("ps", "S", sbk)], writes=[("PT", pk)])
                P.op("pe", mm(PO[qb][:, :], VK[:, j * 128:(j + 1) * 128], PTt[:, pk, :], j == 0, j == nj - 1),
                     reads=[("BIG", 1), ("PT", pk)], writes=[("ps", "O", qb)], inc=False)
                P.op("pe", mm(PL[qb][:, :], ones[:, :], PTt[:, pk, :], j == 0, j == nj - 1),
                     reads=[("ones",), ("PT", pk)], writes=[("ps", "L", qb)])
            P.op("dve", lambda e, qb=qb: e.reciprocal(out=RL[:, :], in_=PL[qb][:, :]),
                 reads=[("ps", "L", qb)], writes=[("RL",)])
            P.op("dve", lambda e, qb=qb: e.tensor_tensor(out=YO[:, qb, :], in0=RL[:, :], in1=PO[qb][:, :],
                                                         op=ALU.mult),
                 reads=[("RL",), ("ps", "O", qb)], writes=[("YO", qb)])
            P.dma("sp", yf[:, i * 512:(i + 1) * 512], YO[:, qb, :], reads=[("YO", qb)])
        QD, KD = BIG[0], BIG[1]
        P.dma("sp", QD[:, :], qd, writes=[("BIG", 0)])
        P.dma("sp", KD[:, :], kd, writes=[("BIG", 1)])
        un = 0
        for w in range(seq // 2048):
            for bi, (d, nb) in enumerate(DIL):
                nbt = seq // (128 * d)
                qv = QD[:, :].rearrange("p (l r) -> p r l", r=d)
                kv = KD[:, :].rearrange("p (l r) -> p r l", r=d)
                ao = ACC[:, 0, :].rearrange("p (l r) -> p r l", r=d)
                al = ACC[:, 1, :].rearrange("p (l r) -> p r l", r=d)
                for r in range(d):
                    for bl in range(nb):
                        Bg = nb * w + bl
                        hp = Bg > 0
                        k = un % 4
                        sbk = un % 2
                        un += 1
                        row0 = (r * nbt + Bg) * 128
                        if hp:
                            P.dma("sp", VU[:, k, :, :],
                                  vdp[bi][row0 - 128:row0 + 128, :].rearrange("(t p) e -> p t e", p=128),
                                  writes=[("VU", k)])
                        else:
                            P.dma("sp", VU[:, k, 1, :], vdp[bi][row0:row0 + 128, :], writes=[("VU", k)])
                        qs = qv[:, r, Bg * 128:(Bg + 1) * 128]
                        lo = 0 if hp else 128
                        P.op("pe", mm(PS[sbk][:, lo:256], ident[:, :], DM[:, lo:256], True, False),
                             reads=[("ident",), ("DM",)], writes=[("ps", "S", sbk)], inc=False)
                        if hp:
                            P.op("pe", mm(PS[sbk][:, 0:128], kv[:, r, (Bg - 1) * 128:Bg * 128], qs, False, False),
                                 reads=[("BIG", 0), ("BIG", 1)], writes=[("ps", "S", sbk)], inc=False)
                        P.op("pe", mm(PS[sbk][:, 128:256], kv[:, r, Bg * 128:(Bg + 1) * 128], qs, False, True),
                             reads=[("BIG", 0), ("BIG", 1)], writes=[("ps", "S", sbk)])
                        pk = pt % 3
                        pt += 1
                        P.op("act", lambda e, sbk=sbk, pk=pk, lo=lo: e.activation(
                            out=PTt[:, pk, lo:256], in_=PS[sbk][:, lo:256], func=AF.Exp, scale=scale),
                             reads=[("ps", "S", sbk)], writes=[("PT", pk)])
                        if hp:
                            P.op("pe", mm(PO[sbk][:, 0:128], VU[:, k, 0, :], PTt[:, pk, 0:128], True, False),
                                 reads=[("VU", k), ("PT", pk)], writes=[("ps", "O", sbk)], inc=False)
                        P.op("pe", mm(PO[sbk][:, 0:128], VU[:, k, 1, :], PTt[:, pk, 128:256], not hp, True),
                             reads=[("VU", k), ("PT", pk)], writes=[("ps", "O", sbk)], inc=False)
                        if hp:
                            P.op("pe", mm(PL[sbk][:, 0:128], ones[:, :], PTt[:, pk, 0:128], True, False),
                                 reads=[("ones",), ("PT", pk)], writes=[("ps", "L", sbk)], inc=False)
                        P.op("pe", mm(PL[sbk][:, 0:128], ones[:, :], PTt[:, pk, 128:256], not hp, True),
                             reads=[("ones",), ("PT", pk)], writes=[("ps", "L", sbk)])
                        do = ao[:, r, bl * 128:(bl + 1) * 128]
                        dl = al[:, r, bl * 128:(bl + 1) * 128]
                        if bi == 0:
                            P.op("dve", lambda e, do=do, sbk=sbk: e.tensor_copy(out=do, in_=PO[sbk][:, 0:128]),
                                 reads=[("ps", "O", sbk)], writes=[("ACC",)])
                            P.op("dve", lambda e, dl=dl, sbk=sbk: e.tensor_copy(out=dl, in_=PL[sbk][:, 0:128]),
                                 reads=[("ps", "L", sbk)], writes=[("ACC",)])
                        else:
                            P.op("dve", lambda e, do=do, sbk=sbk: e.tensor_tensor(out=do, in0=do, in1=PO[sbk][:, 0:128],
                                                                                  op=ALU.add),
                                 reads=[("ps", "O", sbk), ("ACC",)], writes=[("ACC",)])
                            P.op("dve", lambda e, dl=dl, sbk=sbk: e.tensor_tensor(out=dl, in0=dl, in1=PL[sbk][:, 0:128],
                                                                                  op=ALU.add),
                                 reads=[("ps", "L", sbk), ("ACC",)], writes=[("ACC",)])
            P.op("dve", lambda e: e.reciprocal(out=ACC[:, 1, :], in_=ACC[:, 1, :]),
                 reads=[("ACC",)], writes=[("ACC",)])
            P.op("dve", lambda e: e.tensor_tensor(out=YW[:, :], in0=ACC[:, 0, :], in1=ACC[:, 1, :], op=ALU.mult),
                 reads=[("ACC",)], writes=[("YW",)])
            P.dma("sp", yd[:, w * 2048:(w + 1) * 2048], YW[:, :], reads=[("YW",)])
        P.finish()
        P.emit()
    return nc


def attn_phase(nc, P, es, A, scr, fsc, aug, GYF, GYD):
    seq = SEQ
    nblk = seq // 128
    qf, kf, vfT, qd, kd, vdT = scr
    nbf = A["nbf"]
    c_ident, c_ones, c_fm, c_dm = A["ident"], A["ones_bf"], A["fmask"], A["dmask"]
    c_ui, c_ls, c_of, c_idf = A["uincl"], A["lstrict"], A["ones_f"], A["ident_f"]
    scale = HD ** -0.5
    if True:
        sb = lambda n, s, d: es.enter_context(nc.sbuf_tensor(n + "_p2b", s, d))
        ps = lambda n: es.enter_context(nc.psum_tensor(n + "_p2b", [128, 512], F32))
        BIG = [sb(f"BIG{i}", [128, seq], BF16) for i in range(4)]
        ident = sb("identb", [128, 128], BF16)
        ones = sb("onesb", [128, 128], BF16)
        FM = sb("FM", [128, 4, 512], BF16)
        DM = sb("DM", [128, 256], BF16)
        UI = sb("UI", [128, 128], F32)
        LS = sb("LS", [128, 128], F32)
        OFc = sb("OFc", [128, 128], F32)
        NB = sb("NB", [128, 1], F32)
        LF = sb("LF", [128, nblk], F32)
        LFR = sb("LFR", [128, 128], F32)
        IDF = sb("IDF", [128, 128], F32)
        PVb = [ps("PV0"), ps("PV1")]
        RB = sb("RB", [128, 128], F32)
        C = sb("C", [128, 128], F32)
        R1 = sb("R1", [128, 128], F32)
        HI = sb("HI", [128, 6, 128], BF16)
        QF = sb("QF", [128, 2, 512], BF16)
        PTt = sb("PTt", [128, 3, 512], BF16)
        RL = sb("RL", [128, 512], F32)
        YO = sb("YO", [128, 2, 512], BF16)
        VU = sb("VU", [128, 4, 2, 128], BF16)
        ACC = sb("ACC", [128, 2, 2048], F32)
        YW = sb("YW", [128, 2048], BF16)
        PS = [ps("PS0"), ps("PS1")]
        PO = [ps("PO0"), ps("PO1")]
        PL = [ps("PL0"), ps("PL1")]
        for (t, src, key) in ((ident, c_ident, "ident"), (ones, c_ones, "ones"), (FM, c_fm, "FM"),
                              (DM, c_dm, "DM"), (UI, c_ui, "UI"), (LS, c_ls, "LS"), (OFc, c_of, "OFc"),
                              (NB, nbf, "NB"), (IDF, c_idf, "IDF"),
                              (LFR, fsc.rearrange("o (p j) -> (o p) j", j=128), "LFR")):
            idx = (slice(None),) * len(t.shape)
            P.dma("sp", t[idx], src, writes=[(key,)])
        P.op("dve", lambda e: e.tensor_scalar(out=NB[:, :], in0=NB[:, :], scalar1=-1.0, scalar2=None, op0=ALU.mult),
             reads=[("NB",)], writes=[("NB",)])
        KT, VK, AK, AQ = BIG[0], BIG[1], BIG[2], BIG[3]
        P.dma("sp", KT[:, :], kf, writes=[("BIG", 0)])
        P.dma("sp", AQ[:, :], vfT, writes=[("BIG", 3)])
        for bq in range(seq // 512):
            pv = bq % 2
            for j in range(4):
                blk = bq * 4 + j
                P.op("pe", mm(PVb[pv][:, j * 128:(j + 1) * 128], AQ[:, blk * 128:(blk + 1) * 128], ident[:, :],
                              True, True),
                     reads=[("BIG", 3), ("ident",)], writes=[("ps", "V", pv)], inc=(j == 3))
            P.op("act", lambda e, pv=pv, bq=bq: e.activation(out=VK[:, bq * 512:(bq + 1) * 512], in_=PVb[pv][:, :],
                                                            func=AF.Copy),
                 reads=[("ps", "V", pv)], writes=[("BIG", 1)])
        P.op("act", lambda e: e.activation(out=LFR[:, :], in_=LFR[:, :], func=AF.Exp, bias=NB[:, :], scale=-1.0),
             reads=[("LFR",), ("NB",)], writes=[("LFR",)])
        P.op("act", lambda e: e.activation(out=LFR[:, :], in_=LFR[:, :], func=AF.Ln, bias=OFc[:, 0:1], scale=1.0),
             reads=[("LFR",), ("OFc",)], writes=[("LFR",)])
        P.op("dve", lambda e: e.tensor_scalar(out=LFR[:, :], in0=LFR[:, :], scalar1=-1.0, scalar2=None, op0=ALU.mult),
             reads=[("LFR",)], writes=[("LFR",)])
        P.op("pe", mm(PS[0][:, 0:128], LFR[:, :], IDF[:, :], True, True),
             reads=[("LFR",), ("IDF",)], writes=[("ps", "S", 0)])
        P.op("dve", lambda e: e.tensor_copy(out=LF[:, :], in_=PS[0][:, 0:128]),
             reads=[("ps", "S", 0)], writes=[("LF",)])
        P.op("pe", mm(PS[0][0:nblk, 0:128], LF[:, :], OFc[:, :], True, True),
             reads=[("LF",), ("OFc",)], writes=[("ps", "S", 0)])
        P.op("dve", lambda e: e.tensor_copy(out=RB[0:nblk, :], in_=PS[0][0:nblk, 0:128]),
             reads=[("ps", "S", 0)], writes=[("RB",)])
        P.op("pe", mm(PS[1][0:nblk, 0:128], LF[:, :], UI[:, :], True, False),
             reads=[("LF",), ("UI",)], writes=[("ps", "S", 1)], inc=False)
        P.op("pe", mm(PS[1][0:nblk, 0:128], LS[0:nblk, 0:nblk], RB[0:nblk, :], False, True),
             reads=[("RB",), ("LS",)], writes=[("ps", "S", 1)])
        P.op("dve", lambda e: e.tensor_scalar(out=C[0:nblk, :], in0=PS[1][0:nblk, 0:128], scalar1=1.0 / scale,
                                              scalar2=None, op0=ALU.mult),
             reads=[("ps", "S", 1)], writes=[("C",)])
        cur = C
        for i in range(3):
            P.op("dve", lambda e, i=i, cur=cur: e.tensor_copy(out=HI[0:nblk, 3 + i, :], in_=cur[0:nblk, :]),
                 reads=[("C",), ("R1",)], writes=[("HI", 3 + i)])
            P.op("dve", lambda e, i=i: e.tensor_scalar(out=HI[0:nblk, i, :], in0=HI[0:nblk, 3 + i, :],
                                                       scalar1=-1.0, scalar2=None, op0=ALU.mult),
                 reads=[("HI", 3 + i)], writes=[("HI", i)])
            if i < 2:
                P.op("dve", lambda e, i=i, cur=cur: e.tensor_tensor(out=R1[0:nblk, :], in0=cur[0:nblk, :],
                                                                    in1=HI[0:nblk, 3 + i, :], op=ALU.subtract),
                     reads=[("C",), ("R1",), ("HI", 3 + i)], writes=[("R1",)])
                cur = R1
        for i in range(6):
            P.dma("sp", aug[i, :].rearrange("(p j) -> p j", j=128), HI[0:nblk, i, :],
                  reads=[("HI", i)], writes=[("scr", i)])
        P.op("pool", lambda e: e.memset(AK[0:6, :], 1.0), writes=[("BIG", 2)])
        P.op("pool", lambda e: e.memset(AQ[0:6, :], 1.0), writes=[("BIG", 3)])
        P.dma("sp", AK[0:3, :], aug[0:3, :], reads=[("scr", i) for i in range(3)], writes=[("BIG", 2)])
        P.dma("sp", AQ[3:6, :], aug[3:6, :], reads=[("scr", 3 + i) for i in range(3)], writes=[("BIG", 3)])
        pairs = [(i, j) for i in range(seq // 512) for j in range(4 * i + 4)]
        pt = 0

        def fox_scores(n):
            i, j = pairs[n]
            qb, sbk = i % 2, n % 2
            if j == 0:
                P.dma("sp", QF[:, qb, :], qf[:, i * 512:(i + 1) * 512], writes=[("QF", qb)])
            diag = j >= 4 * i
            P.op("pe", mm(PS[sbk][:, :], KT[:, j * 128:(j + 1) * 128], QF[:, qb, :], True, False),
                 reads=[("BIG", 0), ("QF", qb)], writes=[("ps", "S", sbk)], inc=False)
            P.op("pe", mm(PS[sbk][:, :], AK[0:6, j * 128:(j + 1) * 128], AQ[0:6, i * 512:(i + 1) * 512],
                          False, not diag),
                 reads=[("BIG", 2), ("BIG", 3)], writes=[("ps", "S", sbk)], inc=not diag)
            if diag:
                P.op("pe", mm(PS[sbk][:, :], ident[:, :], FM[:, j - 4 * i, :], False, True),
                     reads=[("ident",), ("FM",)], writes=[("ps", "S", sbk)])

        fox_scores(0)
        for n, (i, j) in enumerate(pairs):
            qb, sbk = i % 2, n % 2
            nj = 4 * i + 4
            if n + 1 < len(pairs):
                fox_scores(n + 1)
            pk = pt % 3
            pt += 1
            P.op("act", lambda e, sbk=sbk, pk=pk: e.activation(out=PTt[:, pk, :], in_=PS[sbk][:, :],
                                                               func=AF.Exp, scale=scale),
                 reads=[("ps", "S", sbk)], writes=[("PT", pk)])
            P.op("pe", mm(PO[qb][:, :], VK[:, j * 128:(j + 1) * 128], PTt[:, pk, :], j == 0, j == nj - 1),
                 reads=[("BIG", 1), ("PT", pk)], writes=[("ps", "O", qb)], inc=False)
            P.op("pe", mm(PL[qb][:, :], ones[:, :], PTt[:, pk, :], j == 0, j == nj - 1),
                 reads=[("ones",), ("PT", pk)], writes=[("ps", "L", qb)])
            if j == nj - 1:
                P.op("dve", lambda e, qb=qb: e.reciprocal(out=RL[:, :], in_=PL[qb][:, :]),
                     reads=[("ps", "L", qb)], writes=[("RL",)])
                P.op("dve", lambda e, qb=qb: e.tensor_tensor(out=YO[:, qb, :], in0=RL[:, :], in1=PO[qb][:, :],
                                                             op=ALU.mult),
                     reads=[("RL",), ("ps", "O", qb)], writes=[("YO", qb)])
                wdx = i // 4
                P.dma("sp", GYF.src[wdx][:, (i % 4) * 512:(i % 4 + 1) * 512], YO[:, qb, :], reads=[("YO", qb)],
                      writes=[("GYF", "s", wdx, i % 4)])
                if i % 4 == 3:
                    P.collective(G4, GYF.src[wdx], GYF.a[wdx], reads=[("GYF", "s", wdx, k4) for k4 in range(4)],
                                 writes=[("GYF", "a", wdx)])
                    if wdx > 0:
                        GYF.s2(wdx - 1)
        GYF.s2(seq // 2048 - 1)
        QD, KD, VD = BIG[0], BIG[1], BIG[2]
        P.dma("sp", QD[:, :], qd, writes=[("BIG", 0)])
        P.dma("sp", KD[:, :], kd, writes=[("BIG", 1)])
        P.dma("sp", VD[:, :], vdT, writes=[("BIG", 2)])
        units = []
        for w in range(seq // 2048):
            for bi, (d, nb) in enumerate(DIL):
                for r in range(d):
                    for bl in range(nb):
                        units.append((w, bi, d, nb, r, bl))

        def dil_front(un):
            w, bi, d, nb, r, bl = units[un]
            qv = QD[:, :].rearrange("p (l r) -> p r l", r=d)
            kv = KD[:, :].rearrange("p (l r) -> p r l", r=d)
            vv = VD[:, :].rearrange("p (l r) -> p r l", r=d)
            Bg = nb * w + bl
            hp = Bg > 0
            k, sbk = un % 4, un % 2
            if hp:
                P.op("pe", mm(PVb[sbk][:, 0:128], vv[:, r, (Bg - 1) * 128:Bg * 128], ident[:, :], True, True),
                     reads=[("BIG", 2), ("ident",)], writes=[("ps", "V", sbk)], inc=False)
            P.op("pe", mm(PVb[sbk][:, 128:256], vv[:, r, Bg * 128:(Bg + 1) * 128], ident[:, :], True, True),
                 reads=[("BIG", 2), ("ident",)], writes=[("ps", "V", sbk)])
            vlo = 0 if hp else 1
            P.op("act", lambda e, k=k, sbk=sbk, vlo=vlo: e.activation(
                out=VU[:, k, vlo:2, :], in_=PVb[sbk][:, vlo * 128:256].rearrange("p (t e) -> p t e", e=128),
                func=AF.Copy),
                 reads=[("ps", "V", sbk)], writes=[("VU", k)])
            qs = qv[:, r, Bg * 128:(Bg + 1) * 128]
            lo = 0 if hp else 128
            P.op("pe", mm(PS[sbk][:, lo:256], ident[:, :], DM[:, lo:256], True, False),
                 reads=[("ident",), ("DM",)], writes=[("ps", "S", sbk)], inc=False)
            if hp:
                P.op("pe", mm(PS[sbk][:, 0:128], kv[:, r, (Bg - 1) * 128:Bg * 128], qs, False, False),
                     reads=[("BIG", 0), ("BIG", 1)], writes=[("ps", "S", sbk)], inc=False)
            P.op("pe", mm(PS[sbk][:, 128:256], kv[:, r, Bg * 128:(Bg + 1) * 128], qs, False, True),
                 reads=[("BIG", 0), ("BIG", 1)], writes=[("ps", "S", sbk)])

        dil_front(0)
        for un, (w, bi, d, nb, r, bl) in enumerate(units):
            if un + 1 < len(units):
                dil_front(un + 1)
            Bg = nb * w + bl
            hp = Bg > 0
            k, sbk = un % 4, un % 2
            lo = 0 if hp else 128
            ao = ACC[:, 0, :].rearrange("p (l r) -> p r l", r=d)
            al = ACC[:, 1, :].rearrange("p (l r) -> p r l", r=d)
            pk = pt % 3
            pt += 1
            P.op("act", lambda e, sbk=sbk, pk=pk, lo=lo: e.activation(
                out=PTt[:, pk, lo:256], in_=PS[sbk][:, lo:256], func=AF.Exp, scale=scale),
                 reads=[("ps", "S", sbk)], writes=[("PT", pk)])
            if hp:
                P.op("pe", mm(PO[sbk][:, 0:128], VU[:, k, 0, :], PTt[:, pk, 0:128], True, False),
                     reads=[("VU", k), ("PT", pk)], writes=[("ps", "O", sbk)], inc=False)
            P.op("pe", mm(PO[sbk][:, 0:128], VU[:, k, 1, :], PTt[:, pk, 128:256], not hp, True),
                 reads=[("VU", k), ("PT", pk)], writes=[("ps", "O", sbk)], inc=False)
            if hp:
                P.op("pe", mm(PL[sbk][:, 0:128], ones[:, :], PTt[:, pk, 0:128], True, False),
                     reads=[("ones",), ("PT", pk)], writes=[("ps", "L", sbk)], inc=False)
            P.op("pe", mm(PL[sbk][:, 0:128], ones[:, :], PTt[:, pk, 128:256], not hp, True),
                 reads=[("ones",), ("PT", pk)], writes=[("ps", "L", sbk)])
            do = ao[:, r, bl * 128:(bl + 1) * 128]
            dl = al[:, r, bl * 128:(bl + 1) * 128]
            if bi == 0:
                P.op("dve", lambda e, do=do, sbk=sbk: e.tensor_copy(out=do, in_=PO[sbk][:, 0:128]),
                     reads=[("ps", "O", sbk)], writes=[("ACC",)])
                P.op("dve", lambda e, dl=dl, sbk=sbk: e.tensor_copy(out=dl, in_=PL[sbk][:, 0:128]),
                     reads=[("ps", "L", sbk)], writes=[("ACC",)])
            else:
                P.op("dve", lambda e, do=do, sbk=sbk: e.tensor_tensor(out=do, in0=do, in1=PO[sbk][:, 0:128],
                                                                      op=ALU.add),
                     reads=[("ps", "O", sbk), ("ACC",)], writes=[("ACC",)])
                P.op("dve", lambda e, dl=dl, sbk=sbk: e.tensor_tensor(out=dl, in0=dl, in1=PL[sbk][:, 0:128],
                                                                      op=ALU.add),
                     reads=[("ps", "L", sbk), ("ACC",)], writes=[("ACC",)])
            last_in_window = (un + 1 == len(units)) or (units[un + 1][0] != w)
            if last_in_window:
                P.op("dve", lambda e: e.reciprocal(out=ACC[:, 1, :], in_=ACC[:, 1, :]),
                     reads=[("ACC",)], writes=[("ACC",)])
                P.op("dve", lambda e: e.tensor_tensor(out=YW[:, :], in0=ACC[:, 0, :], in1=ACC[:, 1, :], op=ALU.mult),
                     reads=[("ACC",)], writes=[("YW",)])
                P.dma("sp", GYD.src[w], YW[:, :], reads=[("YW",)], writes=[("GYD", "s", w)])
                GYD.s1(w)
                if w > 0:
                    GYD.s2(w - 1)
        GYD.s2(seq // 2048 - 1)


def dil_perm(seq, d):
    nbt = seq // (128 * d)
    r = np.arange(d)[:, None, None]
    B = np.arange(nbt)[None, :, None]
    p = np.arange(128)[None, None, :]
    return (r + d * (128 * B + p)).reshape(-1)


NG = 8
TWO_PI = 2.0 * math.pi


def l4_consts():
    p = np.arange(128)[:, None]
    t = np.arange(128)[None, :]
    m0 = (np.arange(128) < 64).astype(np.float32)[:, None]
    mask4 = np.broadcast_to((t >= p).astype(np.float32)[:, None, :], (128, 4, 128)).copy()
    ti = np.broadcast_to(np.arange(130, dtype=np.float32)[None, :], (128, 130)).copy()
    tir = np.broadcast_to((127.0 - np.arange(128, dtype=np.float32))[None, :], (128, 128)).copy()
    sg = np.concatenate([m0, 1.0 - m0, -m0, -(1.0 - m0), np.full_like(m0, 0.5 * math.pi)], 1)
    return {"mask4": mask4, "ti": ti, "tir": tir, "sg": sg, "ident": _ident_np()}


def build_l4(seq=SEQ):
    nc = _new_nc()
    dt = lambda n, s, d, k="ExternalInput": nc.dram_tensor(n, s, d, kind=k).ap()
    nch = seq // 128
    uS = dt("uS", [128, nch, 128], F32)
    uG = dt("uG", [NG, 128, nch, 16], F32)
    lr_in = dt("lam_re2", [128, NG], F32)
    li_in = dt("lam_im2", [128, NG], F32)
    ldt_in = dt("logdt2", [128, NG], F32)
    br_in = dt("br2", [128, NG, 16], F32)
    bi_in = dt("bi2", [128, NG, 16], F32)
    cr_in = dt("cr2", [128, NG, 16], F32)
    ci_in = dt("ci2", [128, NG, 16], F32)
    d_in = dt("d2", [128, 128], F32)
    c_mask = dt("mask4", [128, 4, 128], F32)
    c_ti = dt("ti", [128, 130], F32)
    c_tir = dt("tir", [128, 128], F32)
    c_sg = dt("sg", [128, 5], F32)
    c_ident = dt("ident", [128, 128], BF16)
    zout = dt("zout", [NG, 128, nch, 16], BF16, "ExternalOutput")
    with ExitStack() as es:
        P = Prog(nc, es)
        sb = lambda n, s, d: es.enter_context(nc.sbuf_tensor(n, s, d))
        ps = lambda n: es.enter_context(nc.psum_tensor(n, [128, 512], F32))
        U16 = sb("U16", [128, nch, 128], BF16)
        UF = sb("UF", [128, 1, nch, 16], F32)
        TS = sb("TS", [128, 16, 16, 128], BF16)
        CF = sb("CF", [128, 1, 16, 129], BF16)
        BFc = sb("BFc", [128, 16, 128], BF16)
        WF = sb("WF", [128, 16, 128], BF16)
        M1 = sb("M1", [128, 16, 128], BF16)
        YS = sb("YS", [128, 1, nch, 16], F32)
        ZS = sb("ZS", [128, 1, nch, 16], BF16)
        G1 = sb("G1", [128, nch * 16], F32)
        VST = sb("VST", [128, NG, nch], F32)
        VI = sb("VI", [64, NG, nch], F32)
        SC = [[sb(f"SC{a}{b}", [64, NG, nch], F32) for b in range(2)] for a in range(2)]
        XP = sb("XP", [128, NG, nch], BF16)
        XPI = sb("XPI", [64, NG, nch], BF16)
        LR = sb("LR", [128, NG], F32)
        LI = sb("LI", [128, NG], F32)
        DTt = sb("DTt", [128, NG], F32)
        LDR = sb("LDR", [128, NG], F32)
        LDI = sb("LDI", [128, NG], F32)
        NLDR = sb("NLDR", [128, NG], F32)
        SM = sb("SM", [128, 12, NG], F32)
        BR = sb("BR", [128, NG, 16], F32)
        BI = sb("BI", [128, NG, 16], F32)
        CR = sb("CR", [128, NG, 16], F32)
        CI = sb("CI", [128, NG, 16], F32)
        BBR = sb("BBR", [128, NG, 16], F32)
        BBI = sb("BBI", [128, NG, 16], F32)
        CA = sb("CA", [128, NG, 16], F32)
        CB = sb("CB", [128, NG, 16], F32)
        BA = sb("BA", [128, NG, 16], F32)
        BB = sb("BB", [128, NG, 16], F32)
        D2 = sb("D2", [128, 128], F32)
        MASK = sb("MASK", [128, 4, 128], F32)
        TI = sb("TI", [128, 130], F32)
        TIR = sb("TIR", [128, 128], F32)
        SGN = sb("SGN", [128, 5], F32)
        ident = sb("identb", [128, 128], BF16)
        TB = sb("TB", [128, 6, 130], F32)
        AW = sb("AW", [128, 2, NG], F32)
        WW = sb("WW", [64, 2, 3, NG], F32)
        PA = [ps(f"PA{i}") for i in range(4)]
        PYb = [ps("PYa"), ps("PYb")]
        PV = ps("PV")
        loads = ((LR, lr_in, "LR"), (LI, li_in, "LI"), (DTt, ldt_in, "DT"), (BR, br_in, "BR"), (BI, bi_in, "BI"),
                 (CR, cr_in, "CR"), (CI, ci_in, "CI"), (D2, d_in, "D2"), (MASK, c_mask, "MASK"), (TI, c_ti, "TI"),
                 (TIR, c_tir, "TIR"), (SGN, c_sg, "SGN"), (ident, c_ident, "ident"))
        for (t, src, key) in loads:
            idx = (slice(None),) * len(t.shape)
            P.dma("sp", t[idx], src, writes=[(key,)])
        P.dma("pool", U16[:, :, :], uS, writes=[("U16",)])
        m0, m1, nm0, nm1 = SGN[:, 0:1], SGN[:, 1:2], SGN[:, 2:3], SGN[:, 3:4]

        def dve(fn, reads, writes):
            P.op("dve", fn, reads=reads, writes=writes)

        def tt(out, a, b, op, reads, writes, eng="dve"):
            P.op(eng, lambda e: e.tensor_tensor(out=out, in0=a, in1=b, op=op), reads=reads, writes=writes)

        def tsc(out, a, s1, op0, reads, writes, s2=None, op1=None, eng="dve"):
            if op1 is None:
                P.op(eng, lambda e: e.tensor_scalar(out=out, in0=a, scalar1=s1, scalar2=None, op0=op0),
                     reads=reads, writes=writes)
            else:
                P.op(eng, lambda e: e.tensor_scalar(out=out, in0=a, scalar1=s1, scalar2=s2, op0=op0, op1=op1),
                     reads=reads, writes=writes)

        def stt(out, a, s, b, op0, op1, reads, writes, eng="dve"):
            P.op(eng, lambda e: e.scalar_tensor_tensor(out=out, in0=a, scalar=s, in1=b, op0=op0, op1=op1),
                 reads=reads, writes=writes)

        def act(out, a, func, reads, writes, bias=None, scale=None):
            kw = {}
            if bias is not None:
                kw["bias"] = bias
            if scale is not None:
                kw["scale"] = scale
            P.op("act", lambda e: e.activation(out=out, in_=a, func=func, **kw), reads=reads, writes=writes)

        def cp(out, a, reads, writes, eng="dve"):
            P.op(eng, lambda e: e.tensor_copy(out=out, in_=a), reads=reads, writes=writes)

        RI = sb("RI", [128, 130], mybir.dt.int32)
        RF = sb("RF", [128, 2, 130], F32)

        def reduce_angle(arg, shift, out, rk, n):
            t, tf = RF[:, 0, 0:n], RF[:, 1, 0:n]
            ti = RI[:, 0:n]
            tsc(t, arg, 1.0 / TWO_PI, ALU.mult, rk, [("RF", 0)], s2=0.5 + shift, op1=ALU.add)
            cp(ti, t, [("RF", 0)], [("RI",)])
            cp(tf, ti, [("RI",)], [("RF", 1)])
            tt(t, t, tf, ALU.subtract, [("RF", 0), ("RF", 1)], [("RF", 0)])
            tsc(t, t, -0.5, ALU.add, [("RF", 0)], [("RF", 0)], s2=TWO_PI, op1=ALU.mult)
            tsc(tf, t, -math.pi, ALU.is_lt, [("RF", 0)], [("RF", 1)])
            stt(t, tf, TWO_PI, t, ALU.mult, ALU.add, [("RF", 0), ("RF", 1)], [("RF", 0)])
            tsc(tf, t, math.pi, ALU.is_gt, [("RF", 0)], [("RF", 1)])
            stt(out, tf, -TWO_PI, t, ALU.mult, ALU.add, [("RF", 0), ("RF", 1)], [("RF", 0)])

        def sincos(arg, sin_out, cos_out, tmp, rk, wk_s, wk_c, tk):
            n = arg.shape[-1]
            reduce_angle(arg, 0.0, RF[:, 0, 0:n], rk, n)
            act(sin_out, RF[:, 0, 0:n], AF.Sin, [("RF", 0)], wk_s)
            tsc(RF[:, 1, 0:n], RF[:, 0, 0:n], -1.0, ALU.mult, [("RF", 0)], [("RF", 1)])
            tt(RF[:, 1, 0:n], RF[:, 0, 0:n], RF[:, 1, 0:n], ALU.max, [("RF", 0), ("RF", 1)], [("RF", 1)])
            act(cos_out, RF[:, 1, 0:n], AF.Sin, [("RF", 1), ("SGN",)], wk_c, bias=SGN[:, 4:5], scale=-1.0)

        S = lambda i: SM[:, i, :]
        act(DTt[:, :], DTt[:, :], AF.Exp, [("DT",)], [("DT",)])
        tt(LDR[:, :], LR[:, :], DTt[:, :], ALU.mult, [("LR",), ("DT",)], [("LDR",)])
        tt(LDI[:, :], LI[:, :], DTt[:, :], ALU.mult, [("LI",), ("DT",)], [("LDI",)])
        tsc(NLDR[:, :], LDR[:, :], -1.0, ALU.mult, [("LDR",)], [("NLDR",)])
        tsc(S(5), LDI[:, :], TWO_PI, ALU.add, [("LDI",)], [("S", 5)])
        sincos(S(5), S(0), S(1), S(6), [("S", 5)], [("S", 0)], [("S", 1)], ("S", 6))
        act(S(2), LDR[:, :], AF.Exp, [("LDR",)], [("S", 2)])
        tt(S(3), S(2), S(1), ALU.mult, [("S", 2), ("S", 1)], [("S", 3)])
        tsc(S(3), S(3), -1.0, ALU.add, [("S", 3)], [("S", 3)])
        tt(S(4), S(2), S(0), ALU.mult, [("S", 2), ("S", 0)], [("S", 4)])
        tt(S(5), LR[:, :], LR[:, :], ALU.mult, [("LR",)], [("S", 5)])
        tt(S(6), LI[:, :], LI[:, :], ALU.mult, [("LI",)], [("S", 6)])
        tt(S(5), S(5), S(6), ALU.add, [("S", 5), ("S", 6)], [("S", 5)])
        dve(lambda e: e.reciprocal(out=S(5), in_=S(5)), [("S", 5)], [("S", 5)])
        tt(S(7), S(3), LR[:, :], ALU.mult, [("S", 3), ("LR",)], [("S", 7)])
        tt(S(6), S(4), LI[:, :], ALU.mult, [("S", 4), ("LI",)], [("S", 6)])
        tt(S(7), S(7), S(6), ALU.add, [("S", 7), ("S", 6)], [("S", 7)])
        tt(S(7), S(7), S(5), ALU.mult, [("S", 7), ("S", 5)], [("S", 7)])
        tt(S(8), S(4), LR[:, :], ALU.mult, [("S", 4), ("LR",)], [("S", 8)])
        tt(S(6), S(3), LI[:, :], ALU.mult, [("S", 3), ("LI",)], [("S", 6)])
        tt(S(8), S(8), S(6), ALU.subtract, [("S", 8), ("S", 6)], [("S", 8)])
        tt(S(8), S(8), S(5), ALU.mult, [("S", 8), ("S", 5)], [("S", 8)])
        tsc(S(9), S(8), -1.0, ALU.mult, [("S", 8)], [("S", 9)])
        for g in range(NG):
            tsc(BBR[:, g, :], BR[:, g, :], SM[:, 7, g:g + 1], ALU.mult, [("BR",), ("S", 7)], [("BBR", g)])
            stt(BBR[:, g, :], BI[:, g, :], SM[:, 9, g:g + 1], BBR[:, g, :], ALU.mult, ALU.add,
                [("BI",), ("S", 9), ("BBR", g)], [("BBR", g)])
            tsc(BBI[:, g, :], BI[:, g, :], SM[:, 7, g:g + 1], ALU.mult, [("BI",), ("S", 7)], [("BBI", g)])
            stt(BBI[:, g, :], BR[:, g, :], SM[:, 8, g:g + 1], BBI[:, g, :], ALU.mult, ALU.add,
                [("BR",), ("S", 8), ("BBI", g)], [("BBI", g)])
        tsc(CA[:, :, :], CR[:, :, :], m0, ALU.mult, [("CR",), ("SGN",)], [("CA",)])
        stt(CA[:, :, :], CI[:, :, :], nm1, CA[:, :, :], ALU.mult, ALU.add, [("CI",), ("SGN",), ("CA",)], [("CA",)])
        tsc(CB[:, :, :], CI[:, :, :], nm0, ALU.mult, [("CI",), ("SGN",)], [("CB",)])
        stt(CB[:, :, :], CR[:, :, :], nm1, CB[:, :, :], ALU.mult, ALU.add, [("CR",), ("SGN",), ("CB",)], [("CB",)])
        bbk = [("BBR", g) for g in range(NG)] + [("BBI", g) for g in range(NG)]
        tsc(BA[:, :, :], BBR[:, :, :], m0, ALU.mult, bbk + [("SGN",)], [("BA",)])
        stt(BA[:, :, :], BBI[:, :, :], m1, BA[:, :, :], ALU.mult, ALU.add, bbk + [("SGN",), ("BA",)], [("BA",)])
        tsc(BB[:, :, :], BBI[:, :, :], nm0, ALU.mult, bbk + [("SGN",)], [("BB",)])
        stt(BB[:, :, :], BBR[:, :, :], m1, BB[:, :, :], ALU.mult, ALU.add, bbk + [("SGN",), ("BB",)], [("BB",)])

        def table(g, tvec, n, neg):
            tk = [("TB", i) for i in range(6)]
            tsc(TB[:, 0, 0:n], tvec, LDI[:, g:g + 1], ALU.mult, [("LDI",), ("TI",), ("TIR",)], [tk[0]])
            tsc(TB[:, 0, 0:n], TB[:, 0, 0:n], TWO_PI, ALU.add, [tk[0]], [tk[0]])
            sincos(TB[:, 0, 0:n], TB[:, 1, 0:n], TB[:, 2, 0:n], TB[:, 3, 0:n], [tk[0]], [tk[1]], [tk[2]], tk[3])
            act(TB[:, 3, 0:n], tvec, AF.Exp, [("TI",), ("TIR",), ("LDR",), ("NLDR",), tk[3]], [tk[3]],
                scale=(NLDR if neg else LDR)[:, g:g + 1])
            tt(TB[:, 4, 0:n], TB[:, 3, 0:n], TB[:, 2, 0:n], ALU.mult, [tk[3], tk[2]], [tk[4]])
            if neg:
                stt(TB[:, 5, 0:n], TB[:, 3, 0:n], -1.0, TB[:, 1, 0:n], ALU.mult, ALU.mult, [tk[3], tk[1]], [tk[5]])
            else:
                tt(TB[:, 5, 0:n], TB[:, 3, 0:n], TB[:, 1, 0:n], ALU.mult, [tk[3], tk[1]], [tk[5]])

        pa = 0
        for g in range(NG):
            table(g, TI[:, 128:129], 1, False)
            tsc(AW[:, 0, g:g + 1], TB[:, 4, 0:1], 1.0, ALU.mult, [("TB", 4)], [("AW",)])
            tsc(AW[:, 1, g:g + 1], TB[:, 5, 0:1], 1.0, ALU.mult, [("TB", 5)], [("AW",)])
            table(g, TIR[:, :], 128, False)
            for hp in range(16):
                tsc(G1[:, 0:128], TB[:, 4, 0:128], BA[:, g, hp:hp + 1], ALU.mult, [("TB", 4), ("BA",)], [("G1",)])
                stt(WF[:, hp, :], TB[:, 5, 0:128], BB[:, g, hp:hp + 1], G1[:, 0:128], ALU.mult, ALU.add,
                    [("TB", 5), ("BB",), ("G1",)], [("WF",)])
            for q in range(4):
                bk = pa % 4
                pa += 1
                for j in range(4):
                    P.op("pe", mm(PA[bk][:, j * 128:(j + 1) * 128], WF[:, q * 4 + j, :], ident[:, :], True, True),
                         reads=[("WF",), ("ident",)], writes=[("ps", "A", bk)], inc=(j == 3))
                act(M1[:, q * 4:(q + 1) * 4, :], PA[bk][:, :].rearrange("p (j t) -> p j t", j=4), AF.Copy,
                    [("ps", "A", bk)], [("M1",)])
            for hp in range(16):
                P.op("pe", mm(PV[:, 0:nch], M1[:, hp, :], U16[:, :, g * 16 + hp], hp == 0, hp == 15),
                     reads=[("M1",), ("U16",)], writes=[("ps", "V")], inc=(hp == 15))
            act(VST[:, g, :], PV[:, 0:nch], AF.Copy, [("ps", "V")], [("VST",)])
        P.dma("sp", VI[:, :, :], VST[64:128, :, :], reads=[("VST",)], writes=[("VI",)])
        cp(SC[0][0][:, :, :], VST[0:64, :, :], [("VST",)], [("SC", 0, 0)])
        cp(SC[0][1][:, :, :], VI[:, :, :], [("VI",)], [("SC", 0, 1)], eng="pool")
        tsc(WW[:, 0, 0, :], AW[0:64, 0, :], 1.0, ALU.mult, [("AW",)], [("WW", 0)])
        tsc(WW[:, 0, 1, :], AW[0:64, 1, :], 1.0, ALU.mult, [("AW",)], [("WW", 0)])
        tsc(WW[:, 0, 2, :], AW[0:64, 1, :], -1.0, ALU.mult, [("AW",)], [("WW", 0)])
        cur = 0
        sh = 1
        while sh < nch:
            nx = 1 - cur
            re, im = SC[cur]
            nre, nim = SC[nx]
            for g in range(NG):
                wr, wi, nwi = WW[:, cur, 0, g:g + 1], WW[:, cur, 1, g:g + 1], WW[:, cur, 2, g:g + 1]
                rk = [("SC", cur, 0), ("SC", cur, 1), ("WW", cur)]
                cp(nre[:, g, 0:sh], re[:, g, 0:sh], rk, [("SC", nx, 0)])
                stt(nre[:, g, sh:nch], re[:, g, 0:nch - sh], wr, re[:, g, sh:nch], ALU.mult, ALU.add, rk, [("SC", nx, 0)])
                stt(nre[:, g, sh:nch], im[:, g, 0:nch - sh], nwi, nre[:, g, sh:nch], ALU.mult, ALU.add,
                    rk + [("SC", nx, 0)], [("SC", nx, 0)])
                cp(nim[:, g, 0:sh], im[:, g, 0:sh], rk, [("SC", nx, 1)], eng="pool")
                stt(nim[:, g, sh:nch], im[:, g, 0:nch - sh], wr, im[:, g, sh:nch], ALU.mult, ALU.add, rk,
                    [("SC", nx, 1)])
                stt(nim[:, g, sh:nch], re[:, g, 0:nch - sh], wi, nim[:, g, sh:nch], ALU.mult, ALU.add,
                    rk + [("SC", nx, 1)], [("SC", nx, 1)])
            wk = [("WW", cur)]
            tt(WW[:, nx, 0, :], WW[:, cur, 0, :], WW[:, cur, 0, :], ALU.mult, wk, [("WW", nx)])
            tt(WW[:, nx, 2, :], WW[:, cur, 1, :], WW[:, cur, 1, :], ALU.mult, wk, [("WW", nx)])
            tt(WW[:, nx, 0, :], WW[:, nx, 0, :], WW[:, nx, 2, :], ALU.subtract, [("WW", nx)], [("WW", nx)])
            tt(WW[:, nx, 1, :], WW[:, cur, 0, :], WW[:, cur, 1, :], ALU.mult, wk, [("WW", nx)])
            tsc(WW[:, nx, 1, :], WW[:, nx, 1, :], 2.0, ALU.mult, [("WW", nx)], [("WW", nx)])
            tsc(WW[:, nx, 2, :], WW[:, nx, 1, :], -1.0, ALU.mult, [("WW", nx)], [("WW", nx)])
            cur = nx
            sh *= 2
        re, im = SC[cur]
        P.op("pool", lambda e: e.memset(XP[:, :, 0:1], 0.0), writes=[("XP",)])
        P.op("pool", lambda e: e.memset(XPI[:, :, 0:1], 0.0), writes=[("XPI",)])
        if nch > 1:
            P.op("dve", lambda e: e.tensor_copy(out=XP[0:64, :, 1:nch], in_=re[:, :, 0:nch - 1]),
                 reads=[("SC", cur, 0)], writes=[("XP",)])
            P.op("dve", lambda e: e.tensor_copy(out=XPI[:, :, 1:nch], in_=im[:, :, 0:nch - 1]),
                 reads=[("SC", cur, 1)], writes=[("XPI",)])
        P.dma("sp", XP[64:128, :, :], XPI[:, :, :], reads=[("XPI",)], writes=[("XP",)])
        gsc = 2.0 * math.sqrt(2.0 / math.pi)
        py = 0
        for g in range(NG):
            ub = 0
            P.dma("sp", UF[:, ub, :, :], uG[g], writes=[("UF", ub)])
            table(g, TI[:, 0:129], 129, False)
            for h in range(16):
                tsc(G1[:, 0:129], TB[:, 4, 0:129], CA[:, g, h:h + 1], ALU.mult, [("TB", 4), ("CA",)], [("G1",)])
                stt(CF[:, 0, h, :], TB[:, 5, 0:129], CB[:, g, h:h + 1], G1[:, 0:129], ALU.mult, ALU.add,
                    [("TB", 5), ("CB",), ("G1",)], [("CF", 0)])
            table(g, TI[:, 0:128], 128, True)
            for hp in range(16):
                tsc(G1[:, 0:128], TB[:, 4, 0:128], BA[:, g, hp:hp + 1], ALU.mult, [("TB", 4), ("BA",)], [("G1",)])
                stt(BFc[:, hp, :], TB[:, 5, 0:128], BB[:, g, hp:hp + 1], G1[:, 0:128], ALU.mult, ALU.add,
                    [("TB", 5), ("BB",), ("G1",)], [("BFc",)])
            for hp in range(16):
                for q in range(4):
                    bk = pa % 4
                    pa += 1
                    P.op("pe", mm(PA[bk][:, :], BFc[:, hp, :], CF[:, 0, q * 4:(q + 1) * 4, 0:128], True, True),
                         reads=[("BFc",), ("CF", 0)], writes=[("ps", "A", bk)])
                    tt(TS[:, hp, q * 4:(q + 1) * 4, :], PA[bk][:, :].rearrange("p (j t) -> p j t", j=4),
                       MASK[:, :, :], ALU.mult, [("ps", "A", bk), ("MASK",)], [("TS",)])
            for q in range(4):
                yb = py % 2
                py += 1
                for j in range(4):
                    h = q * 4 + j
                    o = PYb[yb][:, j * 128:j * 128 + nch]
                    for hp in range(16):
                        P.op("pe", mm(o, TS[:, hp, h, :], U16[:, :, g * 16 + hp], hp == 0, False),
                             reads=[("TS",), ("U16",)], writes=[("ps", "Y", yb)], inc=False)
                    P.op("pe", mm(o, CF[:, 0, h, 1:129], XP[:, g, :], False, True),
                         reads=[("CF", 0), ("XP",)], writes=[("ps", "Y", yb)], inc=(j == 3))
                for j in range(4):
                    h = q * 4 + j
                    stt(YS[:, ub, :, h], UF[:, ub, :, h], D2[:, g * 16 + h:g * 16 + h + 1],
                        PYb[yb][:, j * 128:j * 128 + nch], ALU.mult, ALU.add,
                        [("UF", ub), ("D2",), ("ps", "Y", yb)], [("YS", ub)])
            yv = YS[:, ub, :, :].rearrange("p c h -> p (c h)")
            zv = ZS[:, ub, :, :].rearrange("p c h -> p (c h)")
            n = nch * 16
            tt(G1[:, 0:n], yv, yv, ALU.mult, [("YS", ub)], [("G1",)], eng="pool")
            tsc(G1[:, 0:n], G1[:, 0:n], 0.044715, ALU.mult, [("G1",)], [("G1",)], s2=1.0, op1=ALU.add, eng="pool")
            tt(G1[:, 0:n], G1[:, 0:n], yv, ALU.mult, [("G1",), ("YS", ub)], [("G1",)], eng="pool")
            act(G1[:, 0:n], G1[:, 0:n], AF.Sigmoid, [("G1",)], [("G1",)], scale=gsc)
            tt(zv, G1[:, 0:n], yv, ALU.mult, [("G1",), ("YS", ub)], [("ZS", ub)], eng="pool")
            P.dma("sp", zout[g], ZS[:, ub, :, :], reads=[("ZS", ub)])
        P.finish()
        P.emit()
    return nc


def s5_phase(nc, P, es, S, c_ident, uS16, uG, GZ):
    seq = SEQ
    nch = seq // 128
    lr_in, li_in, ldt_in = S["lam_re2"], S["lam_im2"], S["logdt2"]
    br_in, bi_in, cr_in, ci_in, d_in = S["br2"], S["bi2"], S["cr2"], S["ci2"], S["d2"]
    c_mask, c_ti, c_tir, c_sg = S["mask4"], S["ti"], S["tir"], S["sg"]
    if True:
        sb = lambda n, s, d: es.enter_context(nc.sbuf_tensor(n + "_p4b", s, d))
        ps = lambda n: es.enter_context(nc.psum_tensor(n + "_p4b", [128, 512], F32))
        U16 = sb("U16", [128, nch, 128], BF16)
        UF = sb("UF", [128, 1, nch, 16], F32)
        TS = sb("TS", [128, 16, 16, 128], BF16)
        CF = sb("CF", [128, 1, 16, 129], BF16)
        BFc = sb("BFc", [128, 16, 128], BF16)
        WF = BFc
        M1 = CF[:, 0, :, 0:128]
        YS = sb("YS", [128, 1, nch, 16], F32)
        ZALL = sb("ZALL", [128, nch, 128], BF16)
        ZT = sb("ZT", [128, 2, 512], BF16)
        G1 = sb("G1", [128, nch * 16], F32)
        VST = sb("VST", [128, NG, nch], F32)
        VI = sb("VI", [64, NG, nch], F32)
        SC = [[sb(f"SC{a}{b}", [64, NG, nch], F32) for b in range(2)] for a in range(2)]
        XP = sb("XP", [128, NG, nch], BF16)
        XPI = sb("XPI", [64, NG, nch], BF16)
        LR = sb("LR", [128, NG], F32)
        LI = sb("LI", [128, NG], F32)
        DTt = sb("DTt", [128, NG], F32)
        LDR = sb("LDR", [128, NG], F32)
        LDI = sb("LDI", [128, NG], F32)
        NLDR = sb("NLDR", [128, NG], F32)
        SM = sb("SM", [128, 12, NG], F32)
        BR = sb("BR", [128, NG, 16], F32)
        BI = sb("BI", [128, NG, 16], F32)
        CR = sb("CR", [128, NG, 16], F32)
        CI = sb("CI", [128, NG, 16], F32)
        BBR = sb("BBR", [128, NG, 16], F32)
        BBI = sb("BBI", [128, NG, 16], F32)
        CA = sb("CA", [128, NG, 16], F32)
        CB = sb("CB", [128, NG, 16], F32)
        BA = sb("BA", [128, NG, 16], F32)
        BB = sb("BB", [128, NG, 16], F32)
        D2 = sb("D2", [128, 128], F32)
        MASK = sb("MASK", [128, 4, 128], F32)
        TI = sb("TI", [128, 130], F32)
        TIR = sb("TIR", [128, 128], F32)
        SGN = sb("SGN", [128, 5], F32)
        ident = sb("identb", [128, 128], BF16)
        TB = sb("TB", [128, 6, 130], F32)
        AW = sb("AW", [128, 2, NG], F32)
        WW = sb("WW", [64, 2, 3, NG], F32)
        PA = [ps(f"PA{i}") for i in range(4)]
        PYb = [ps("PYa"), ps("PYb")]
        PV = ps("PV")
        loads = ((LR, lr_in, "LR"), (LI, li_in, "LI"), (DTt, ldt_in, "DT"), (BR, br_in, "BR"), (BI, bi_in, "BI"),
                 (CR, cr_in, "CR"), (CI, ci_in, "CI"), (D2, d_in, "D2"), (MASK, c_mask, "MASK"), (TI, c_ti, "TI"),
                 (TIR, c_tir, "TIR"), (SGN, c_sg, "SGN"), (ident, c_ident, "ident"))
        for (t, src, key) in loads:
            idx = (slice(None),) * len(t.shape)
            P.dma("sp", t[idx], src, writes=[(key,)])
        P.dma("sp", U16[:, :, :], uS16, writes=[("U16",)])
        m0, m1, nm0, nm1 = SGN[:, 0:1], SGN[:, 1:2], SGN[:, 2:3], SGN[:, 3:4]

        def dve(fn, reads, writes):
            P.op("dve", fn, reads=reads, writes=writes)

        def tt(out, a, b, op, reads, writes, eng="dve"):
            P.op(eng, lambda e: e.tensor_tensor(out=out, in0=a, in1=b, op=op), reads=reads, writes=writes)

        def tsc(out, a, s1, op0, reads, writes, s2=None, op1=None, eng="dve"):
            if op1 is None:
                P.op(eng, lambda e: e.tensor_scalar(out=out, in0=a, scalar1=s1, scalar2=None, op0=op0),
                     reads=reads, writes=writes)
            else:
                P.op(eng, lambda e: e.tensor_scalar(out=out, in0=a, scalar1=s1, scalar2=s2, op0=op0, op1=op1),
                     reads=reads, writes=writes)

        def stt(out, a, s, b, op0, op1, reads, writes, eng="dve"):
            P.op(eng, lambda e: e.scalar_tensor_tensor(out=out, in0=a, scalar=s, in1=b, op0=op0, op1=op1),
                 reads=reads, writes=writes)

        def act(out, a, func, reads, writes, bias=None, scale=None):
            kw = {}
            if bias is not None:
                kw["bias"] = bias
            if scale is not None:
                kw["scale"] = scale
            P.op("act", lambda e: e.activation(out=out, in_=a, func=func, **kw), reads=reads, writes=writes)

        def cp(out, a, reads, writes, eng="dve"):
            P.op(eng, lambda e: e.tensor_copy(out=out, in_=a), reads=reads, writes=writes)

        RI = sb("RI", [128, 130], mybir.dt.int32)
        RF = sb("RF", [128, 2, 130], F32)

        def reduce_angle(arg, shift, out, rk, n):
            t, tf = RF[:, 0, 0:n], RF[:, 1, 0:n]
            ti = RI[:, 0:n]
            tsc(t, arg, 1.0 / TWO_PI, ALU.mult, rk, [("RF", 0)], s2=0.5 + shift, op1=ALU.add)
            cp(ti, t, [("RF", 0)], [("RI",)])
            cp(tf, ti, [("RI",)], [("RF", 1)])
            tt(t, t, tf, ALU.subtract, [("RF", 0), ("RF", 1)], [("RF", 0)])
            tsc(t, t, -0.5, ALU.add, [("RF", 0)], [("RF", 0)], s2=TWO_PI, op1=ALU.mult)
            tsc(tf, t, -math.pi, ALU.is_lt, [("RF", 0)], [("RF", 1)])
            stt(t, tf, TWO_PI, t, ALU.mult, ALU.add, [("RF", 0), ("RF", 1)], [("RF", 0)])
            tsc(tf, t, math.pi, ALU.is_gt, [("RF", 0)], [("RF", 1)])
            stt(out, tf, -TWO_PI, t, ALU.mult, ALU.add, [("RF", 0), ("RF", 1)], [("RF", 0)])

        def sincos(arg, sin_out, cos_out, tmp, rk, wk_s, wk_c, tk):
            n = arg.shape[-1]
            reduce_angle(arg, 0.0, RF[:, 0, 0:n], rk, n)
            act(sin_out, RF[:, 0, 0:n], AF.Sin, [("RF", 0)], wk_s)
            tsc(RF[:, 1, 0:n], RF[:, 0, 0:n], -1.0, ALU.mult, [("RF", 0)], [("RF", 1)])
            tt(RF[:, 1, 0:n], RF[:, 0, 0:n], RF[:, 1, 0:n], ALU.max, [("RF", 0), ("RF", 1)], [("RF", 1)])
            act(cos_out, RF[:, 1, 0:n], AF.Sin, [("RF", 1), ("SGN",)], wk_c, bias=SGN[:, 4:5], scale=-1.0)

        S = lambda i: SM[:, i, :]
        act(DTt[:, :], DTt[:, :], AF.Exp, [("DT",)], [("DT",)])
        tt(LDR[:, :], LR[:, :], DTt[:, :], ALU.mult, [("LR",), ("DT",)], [("LDR",)])
        tt(LDI[:, :], LI[:, :], DTt[:, :], ALU.mult, [("LI",), ("DT",)], [("LDI",)])
        tsc(NLDR[:, :], LDR[:, :], -1.0, ALU.mult, [("LDR",)], [("NLDR",)])
        tsc(S(5), LDI[:, :], TWO_PI, ALU.add, [("LDI",)], [("S", 5)])
        sincos(S(5), S(0), S(1), S(6), [("S", 5)], [("S", 0)], [("S", 1)], ("S", 6))
        act(S(2), LDR[:, :], AF.Exp, [("LDR",)], [("S", 2)])
        tt(S(3), S(2), S(1), ALU.mult, [("S", 2), ("S", 1)], [("S", 3)])
        tsc(S(3), S(3), -1.0, ALU.add, [("S", 3)], [("S", 3)])
        tt(S(4), S(2), S(0), ALU.mult, [("S", 2), ("S", 0)], [("S", 4)])
        tt(S(5), LR[:, :], LR[:, :], ALU.mult, [("LR",)], [("S", 5)])
        tt(S(6), LI[:, :], LI[:, :], ALU.mult, [("LI",)], [("S", 6)])
        tt(S(5), S(5), S(6), ALU.add, [("S", 5), ("S", 6)], [("S", 5)])
        dve(lambda e: e.reciprocal(out=S(5), in_=S(5)), [("S", 5)], [("S", 5)])
        tt(S(7), S(3), LR[:, :], ALU.mult, [("S", 3), ("LR",)], [("S", 7)])
        tt(S(6), S(4), LI[:, :], ALU.mult, [("S", 4), ("LI",)], [("S", 6)])
        tt(S(7), S(7), S(6), ALU.add, [("S", 7), ("S", 6)], [("S", 7)])
        tt(S(7), S(7), S(5), ALU.mult, [("S", 7), ("S", 5)], [("S", 7)])
        tt(S(8), S(4), LR[:, :], ALU.mult, [("S", 4), ("LR",)], [("S", 8)])
        tt(S(6), S(3), LI[:, :], ALU.mult, [("S", 3), ("LI",)], [("S", 6)])
        tt(S(8), S(8), S(6), ALU.subtract, [("S", 8), ("S", 6)], [("S", 8)])
        tt(S(8), S(8), S(5), ALU.mult, [("S", 8), ("S", 5)], [("S", 8)])
        tsc(S(9), S(8), -1.0, ALU.mult, [("S", 8)], [("S", 9)])
        for g in range(NG):
            tsc(BBR[:, g, :], BR[:, g, :], SM[:, 7, g:g + 1], ALU.mult, [("BR",), ("S", 7)], [("BBR", g)])
            stt(BBR[:, g, :], BI[:, g, :], SM[:, 9, g:g + 1], BBR[:, g, :], ALU.mult, ALU.add,
                [("BI",), ("S", 9), ("BBR", g)], [("BBR", g)])
            tsc(BBI[:, g, :], BI[:, g, :], SM[:, 7, g:g + 1], ALU.mult, [("BI",), ("S", 7)], [("BBI", g)])
            stt(BBI[:, g, :], BR[:, g, :], SM[:, 8, g:g + 1], BBI[:, g, :], ALU.mult, ALU.add,
                [("BR",), ("S", 8), ("BBI", g)], [("BBI", g)])
        tsc(CA[:, :, :], CR[:, :, :], m0, ALU.mult, [("CR",), ("SGN",)], [("CA",)])
        stt(CA[:, :, :], CI[:, :, :], nm1, CA[:, :, :], ALU.mult, ALU.add, [("CI",), ("SGN",), ("CA",)], [("CA",)])
        tsc(CB[:, :, :], CI[:, :, :], nm0, ALU.mult, [("CI",), ("SGN",)], [("CB",)])
        stt(CB[:, :, :], CR[:, :, :], nm1, CB[:, :, :], ALU.mult, ALU.add, [("CR",), ("SGN",), ("CB",)], [("CB",)])
        bbk = [("BBR", g) for g in range(NG)] + [("BBI", g) for g in range(NG)]
        tsc(BA[:, :, :], BBR[:, :, :], m0, ALU.mult, bbk + [("SGN",)], [("BA",)])
        stt(BA[:, :, :], BBI[:, :, :], m1, BA[:, :, :], ALU.mult, ALU.add, bbk + [("SGN",), ("BA",)], [("BA",)])
        tsc(BB[:, :, :], BBI[:, :, :], nm0, ALU.mult, bbk + [("SGN",)], [("BB",)])
        stt(BB[:, :, :], BBR[:, :, :], m1, BB[:, :, :], ALU.mult, ALU.add, bbk + [("SGN",), ("BB",)], [("BB",)])

        def table(g, tvec, n, neg):
            tk = [("TB", i) for i in range(6)]
            tsc(TB[:, 0, 0:n], tvec, LDI[:, g:g + 1], ALU.mult, [("LDI",), ("TI",), ("TIR",)], [tk[0]])
            tsc(TB[:, 0, 0:n], TB[:, 0, 0:n], TWO_PI, ALU.add, [tk[0]], [tk[0]])
            sincos(TB[:, 0, 0:n], TB[:, 1, 0:n], TB[:, 2, 0:n], TB[:, 3, 0:n], [tk[0]], [tk[1]], [tk[2]], tk[3])
            act(TB[:, 3, 0:n], tvec, AF.Exp, [("TI",), ("TIR",), ("LDR",), ("NLDR",), tk[3]], [tk[3]],
                scale=(NLDR if neg else LDR)[:, g:g + 1])
            tt(TB[:, 4, 0:n], TB[:, 3, 0:n], TB[:, 2, 0:n], ALU.mult, [tk[3], tk[2]], [tk[4]])
            if neg:
                stt(TB[:, 5, 0:n], TB[:, 3, 0:n], -1.0, TB[:, 1, 0:n], ALU.mult, ALU.mult, [tk[3], tk[1]], [tk[5]])
            else:
                tt(TB[:, 5, 0:n], TB[:, 3, 0:n], TB[:, 1, 0:n], ALU.mult, [tk[3], tk[1]], [tk[5]])

        pa = 0
        for g in range(NG):
            table(g, TI[:, 128:129], 1, False)
            tsc(AW[:, 0, g:g + 1], TB[:, 4, 0:1], 1.0, ALU.mult, [("TB", 4)], [("AW",)])
            tsc(AW[:, 1, g:g + 1], TB[:, 5, 0:1], 1.0, ALU.mult, [("TB", 5)], [("AW",)])
            table(g, TIR[:, :], 128, False)
            for hp in range(16):
                tsc(G1[:, 0:128], TB[:, 4, 0:128], BA[:, g, hp:hp + 1], ALU.mult, [("TB", 4), ("BA",)], [("G1",)])
                stt(WF[:, hp, :], TB[:, 5, 0:128], BB[:, g, hp:hp + 1], G1[:, 0:128], ALU.mult, ALU.add,
                    [("TB", 5), ("BB",), ("G1",)], [("BFc",)])
            for q in range(4):
                bk = pa % 4
                pa += 1
                for j in range(4):
                    P.op("pe", mm(PA[bk][:, j * 128:(j + 1) * 128], WF[:, q * 4 + j, :], ident[:, :], True, True),
                         reads=[("BFc",), ("ident",)], writes=[("ps", "A", bk)], inc=(j == 3))
                act(M1[:, q * 4:(q + 1) * 4, :], PA[bk][:, :].rearrange("p (j t) -> p j t", j=4), AF.Copy,
                    [("ps", "A", bk)], [("CF", 0)])
            for hp in range(16):
                P.op("pe", mm(PV[:, 0:nch], M1[:, hp, :], U16[:, :, g * 16 + hp], hp == 0, hp == 15),
                     reads=[("CF", 0), ("U16",)], writes=[("ps", "V")], inc=(hp == 15))
            act(VST[:, g, :], PV[:, 0:nch], AF.Copy, [("ps", "V")], [("VST",)])
        P.dma("sp", VI[:, :, :], VST[64:128, :, :], reads=[("VST",)], writes=[("VI",)])
        cp(SC[0][0][:, :, :], VST[0:64, :, :], [("VST",)], [("SC", 0, 0)])
        cp(SC[0][1][:, :, :], VI[:, :, :], [("VI",)], [("SC", 0, 1)], eng="pool")
        tsc(WW[:, 0, 0, :], AW[0:64, 0, :], 1.0, ALU.mult, [("AW",)], [("WW", 0)])
        tsc(WW[:, 0, 1, :], AW[0:64, 1, :], 1.0, ALU.mult, [("AW",)], [("WW", 0)])
        tsc(WW[:, 0, 2, :], AW[0:64, 1, :], -1.0, ALU.mult, [("AW",)], [("WW", 0)])
        cur = 0
        sh = 1
        while sh < nch:
            nx = 1 - cur
            re, im = SC[cur]
            nre, nim = SC[nx]
            for g in range(NG):
                wr, wi, nwi = WW[:, cur, 0, g:g + 1], WW[:, cur, 1, g:g + 1], WW[:, cur, 2, g:g + 1]
                rk = [("SC", cur, 0), ("SC", cur, 1), ("WW", cur)]
                cp(nre[:, g, 0:sh], re[:, g, 0:sh], rk, [("SC", nx, 0)])
                stt(nre[:, g, sh:nch], re[:, g, 0:nch - sh], wr, re[:, g, sh:nch], ALU.mult, ALU.add, rk, [("SC", nx, 0)])
                stt(nre[:, g, sh:nch], im[:, g, 0:nch - sh], nwi, nre[:, g, sh:nch], ALU.mult, ALU.add,
                    rk + [("SC", nx, 0)], [("SC", nx, 0)])
                cp(nim[:, g, 0:sh], im[:, g, 0:sh], rk, [("SC", nx, 1)], eng="pool")
                stt(nim[:, g, sh:nch], im[:, g, 0:nch - sh], wr, im[:, g, sh:nch], ALU.mult, ALU.add, rk,
                    [("SC", nx, 1)])
                stt(nim[:, g, sh:nch], re[:, g, 0:nch - sh], wi, nim[:, g, sh:nch], ALU.mult, ALU.add,
                    rk + [("SC", nx, 1)], [("SC", nx, 1)])
            wk = [("WW", cur)]
            tt(WW[:, nx, 0, :], WW[:, cur, 0, :], WW[:, cur, 0, :], ALU.mult, wk, [("WW", nx)])
            tt(WW[:, nx, 2, :], WW[:, cur, 1, :], WW[:, cur, 1, :], ALU.mult, wk, [("WW", nx)])
            tt(WW[:, nx, 0, :], WW[:, nx, 0, :], WW[:, nx, 2, :], ALU.subtract, [("WW", nx)], [("WW", nx)])
            tt(WW[:, nx, 1, :], WW[:, cur, 0, :], WW[:, cur, 1, :], ALU.mult, wk, [("WW", nx)])
            tsc(WW[:, nx, 1, :], WW[:, nx, 1, :], 2.0, ALU.mult, [("WW", nx)], [("WW", nx)])
            tsc(WW[:, nx, 2, :], WW[:, nx, 1, :], -1.0, ALU.mult, [("WW", nx)], [("WW", nx)])
            cur = nx
            sh *= 2
        re, im = SC[cur]
        P.op("pool", lambda e: e.memset(XP[:, :, 0:1], 0.0), writes=[("XP",)])
        P.op("pool", lambda e: e.memset(XPI[:, :, 0:1], 0.0), writes=[("XPI",)])
        if nch > 1:
            P.op("dve", lambda e: e.tensor_copy(out=XP[0:64, :, 1:nch], in_=re[:, :, 0:nch - 1]),
                 reads=[("SC", cur, 0)], writes=[("XP",)])
            P.op("dve", lambda e: e.tensor_copy(out=XPI[:, :, 1:nch], in_=im[:, :, 0:nch - 1]),
                 reads=[("SC", cur, 1)], writes=[("XPI",)])
        P.dma("sp", XP[64:128, :, :], XPI[:, :, :], reads=[("XPI",)], writes=[("XP",)])
        gsc = 2.0 * math.sqrt(2.0 / math.pi)
        py = 0
        for g in range(NG):
            ub = 0
            P.dma("sp", UF[:, ub, :, :], uG[g * 128:(g + 1) * 128, :].rearrange("s (c h) -> s c h", h=16),
                  writes=[("UF", ub)])
            table(g, TI[:, 0:129], 129, False)
            for h in range(16):
                tsc(G1[:, 0:129], TB[:, 4, 0:129], CA[:, g, h:h + 1], ALU.mult, [("TB", 4), ("CA",)], [("G1",)])
                stt(CF[:, 0, h, :], TB[:, 5, 0:129], CB[:, g, h:h + 1], G1[:, 0:129], ALU.mult, ALU.add,
                    [("TB", 5), ("CB",), ("G1",)], [("CF", 0)])
            table(g, TI[:, 0:128], 128, True)
            for hp in range(16):
                tsc(G1[:, 0:128], TB[:, 4, 0:128], BA[:, g, hp:hp + 1], ALU.mult, [("TB", 4), ("BA",)], [("G1",)])
                stt(BFc[:, hp, :], TB[:, 5, 0:128], BB[:, g, hp:hp + 1], G1[:, 0:128], ALU.mult, ALU.add,
                    [("TB", 5), ("BB",), ("G1",)], [("BFc",)])
            for hp in range(16):
                for q in range(4):
                    bk = pa % 4
                    pa += 1
                    P.op("pe", mm(PA[bk][:, :], BFc[:, hp, :], CF[:, 0, q * 4:(q + 1) * 4, 0:128], True, True),
                         reads=[("BFc",), ("CF", 0)], writes=[("ps", "A", bk)])
                    tt(TS[:, hp, q * 4:(q + 1) * 4, :], PA[bk][:, :].rearrange("p (j t) -> p j t", j=4),
                       MASK[:, :, :], ALU.mult, [("ps", "A", bk), ("MASK",)], [("TS",)])
            for q in range(4):
                yb = py % 2
                py += 1
                for j in range(4):
                    h = q * 4 + j
                    o = PYb[yb][:, j * 128:j * 128 + nch]
                    for hp in range(16):
                        P.op("pe", mm(o, TS[:, hp, h, :], U16[:, :, g * 16 + hp], hp == 0, False),
                             reads=[("TS",), ("U16",)], writes=[("ps", "Y", yb)], inc=False)
                    P.op("pe", mm(o, CF[:, 0, h, 1:129], XP[:, g, :], False, True),
                         reads=[("CF", 0), ("XP",)], writes=[("ps", "Y", yb)], inc=(j == 3))
                for j in range(4):
                    h = q * 4 + j
                    stt(YS[:, ub, :, h], UF[:, ub, :, h], D2[:, g * 16 + h:g * 16 + h + 1],
                        PYb[yb][:, j * 128:j * 128 + nch], ALU.mult, ALU.add,
                        [("UF", ub), ("D2",), ("ps", "Y", yb)], [("YS", ub)])
            yv = YS[:, ub, :, :]
            g3 = G1[:, 0:nch * 16].rearrange("p (c h) -> p c h", h=16)
            zv = ZALL[:, :, g * 16:(g + 1) * 16]
            tt(g3, yv, yv, ALU.mult, [("YS", ub)], [("G1",)], eng="pool")
            tsc(g3, g3, 0.044715, ALU.mult, [("G1",)], [("G1",)], s2=1.0, op1=ALU.add, eng="pool")
            tt(g3, g3, yv, ALU.mult, [("G1",), ("YS", ub)], [("G1",)], eng="pool")
            act(g3, g3, AF.Sigmoid, [("G1",)], [("G1",)], scale=gsc)
            tt(zv, g3, yv, ALU.mult, [("G1",), ("YS", ub)], [("ZALL",)], eng="pool")
        for c4 in range(nch // 4):
            bk = pa % 4
            pa += 1
            zb = c4 % 2
            for j in range(4):
                c = c4 * 4 + j
                P.op("pe", mm(PA[bk][:, j * 128:(j + 1) * 128], ZALL[:, c, :], ident[:, :], True, True),
                     reads=[("ZALL",), ("ident",)], writes=[("ps", "A", bk)], inc=(j == 3))
            act(ZT[:, zb, :], PA[bk][:, :], AF.Copy, [("ps", "A", bk)], [("ZT", zb)])
            w = c4 // 4
            P.dma("sp", GZ.src[w][:, (c4 % 4) * 512:(c4 % 4 + 1) * 512], ZT[:, zb, :], reads=[("ZT", zb)],
                  writes=[("GZ", "s", w, c4 % 4)])
            if c4 % 4 == 3:
                P.collective(G4, GZ.src[w], GZ.a[w], reads=[("GZ", "s", w, k4) for k4 in range(4)],
                             writes=[("GZ", "a", w)])
                if w > 0:
                    GZ.s2(w - 1)
        GZ.s2(nch // 16 - 1)


def l4_inputs(u_c, inp, core, seq=SEQ):
    nch = seq // 128
    gs = slice(core * NG, (core + 1) * NG)
    two = lambda a: np.ascontiguousarray(np.concatenate([a, a], 0).astype(np.float32))
    m = dict(l4_consts())
    u3 = u_c.reshape(nch, 128, 128)
    m["uS"] = np.ascontiguousarray(u3.transpose(1, 0, 2))
    m["uG"] = np.ascontiguousarray(u3.reshape(nch, 128, NG, 16).transpose(2, 1, 0, 3))
    m["lam_re2"] = two(inp["s5_lambda_re"][0][gs].T)
    m["lam_im2"] = two(inp["s5_lambda_im"][0][gs].T)
    m["logdt2"] = np.ascontiguousarray(np.broadcast_to(inp["s5_log_dt"][0][gs][None, :], (128, NG)).astype(np.float32))
    m["br2"] = two(inp["s5_b_re"][0][gs].transpose(1, 0, 2))
    m["bi2"] = two(inp["s5_b_im"][0][gs].transpose(1, 0, 2))
    m["cr2"] = two(inp["s5_c_re"][0][gs].transpose(2, 0, 1))
    m["ci2"] = two(inp["s5_c_im"][0][gs].transpose(2, 0, 1))
    m["d2"] = np.ascontiguousarray(np.broadcast_to(inp["s5_d"][0][core * 128:(core + 1) * 128][None, :], (128, 128)).astype(np.float32))
    return m


def l4_unpack(zout, seq=SEQ):
    nch = seq // 128
    return np.asarray(zout).transpose(2, 1, 0, 3).reshape(seq, 128)


_CACHE = {}


def _get(name, builder):
    if name not in _CACHE:
        _CACHE[name] = builder()
    return _CACHE[name]


def _run(nc, in_maps):
    res = run_bass_kernel_spmd(nc, in_maps, core_ids=list(range(NCORE)))
    return res.results


def kernel_unfused(**inputs):
    inp = {k: np.asarray(v) for k, v in inputs.items()}
    x = inp["x"][0]
    ident = _ident_np()
    cs = lambda c: slice(c * TPC, (c + 1) * TPC)
    ca = np.ascontiguousarray
    maps = [{"x": ca(x[cs(c)]), "wg": inp["ffn1_w_gate"][0], "wu": inp["ffn1_w_up"][0], "wd": inp["ffn1_w_down"][0],
             "lng": ca(inp["ln_gain"][0, 0][None]), "lnb": ca(inp["ln_bias"][0, 0][None]),
             "win": inp["attn_w_in"][0], "ident": ident} for c in range(NCORE)]
    r1 = _run(_get("l1", build_l1), maps)
    x1 = [r["x1"] for r in r1]
    projT = np.concatenate([np.asarray(r["projT"]) for r in r1], axis=1)
    fT = np.concatenate([np.asarray(r["fT"]) for r in r1], axis=1)
    del r1, maps
    consts = l2_consts()
    perms = {d: dil_perm(SEQ, d) for (d, _) in DIL}
    maps = []
    for c in range(NCORE):
        m = dict(consts)
        hs = slice(128 * c, 128 * (c + 1))
        m["qf"] = ca(projT[0:1024][hs])
        m["kf"] = ca(projT[1024:2048][hs])
        m["vf"] = ca(projT[2048:3072][hs].T)
        m["f2T"] = ca(fT[c].reshape(SEQ // 128, 128).T)
        m["nbf"] = np.full((128, 1), inp["attn_b_f"][0, c], np.float32)
        m["qd"] = ca(projT[3072:4096][hs])
        m["kd"] = ca(projT[4096:5120][hs])
        vdt = ca(projT[5120:6144][hs].T)
        for (d, _) in DIL:
            m[f"vd{d}"] = ca(vdt[perms[d]])
        maps.append(m)
    r2 = _run(_get("l2", build_l2), maps)
    yT = np.concatenate([np.asarray(r["yf"]) for r in r2] + [np.asarray(r["yd"]) for r in r2], axis=0)
    del r2, maps, projT
    lng3 = ca(np.stack([inp["ln_gain"][0, 1], inp["ln_gain"][0, 2], inp["ln_gain"][1, 0]]))
    lnb3 = ca(np.stack([inp["ln_bias"][0, 1], inp["ln_bias"][0, 2], inp["ln_bias"][1, 0]]))
    maps = [{"x1": x1[c], "yT": ca(yT[:, cs(c)]), "wo": inp["attn_w_out"][0],
             "w2g": inp["ffn2_w_gate"][0], "w2u": inp["ffn2_w_up"][0], "w2d": inp["ffn2_w_down"][0],
             "w3g": inp["ffn1_w_gate"][1], "w3u": inp["ffn1_w_up"][1], "w3d": inp["ffn1_w_down"][1],
             "lng": lng3, "lnb": lnb3, "wsi": inp["s5_w_in"][0], "ident": ident} for c in range(NCORE)]
    r3 = _run(_get("l3", build_l3), maps)
    x3 = [r["x3"] for r in r3]
    u = np.concatenate([np.asarray(r["u"]) for r in r3], axis=0)
    del r3, maps, x1, yT
    maps = [l4_inputs(ca(u[:, 128 * c:128 * (c + 1)]), inp, c) for c in range(NCORE)]
    r4 = _run(_get("l4", build_l4), maps)
    z = np.concatenate([l4_unpack(r["zout"]) for r in r4], axis=1)
    zT = ca(z.T)
    del r4, maps, u, z
    lng5 = ca(np.stack([inp["ln_gain"][1, 1], inp["ln_gain"][1, 2]]))
    lnb5 = ca(np.stack([inp["ln_bias"][1, 1], inp["ln_bias"][1, 2]]))
    maps = [{"x3": x3[c], "zT": ca(zT[:, cs(c)]), "wgo": inp["s5_w_glu_out"][0], "wgg": inp["s5_w_glu_gate"][0],
             "w4g": inp["ffn2_w_gate"][1], "w4u": inp["ffn2_w_up"][1], "w4d": inp["ffn2_w_down"][1],
             "lng": lng5, "lnb": lnb5, "ident": ident} for c in range(NCORE)]
    r5 = _run(_get("l5", build_l5), maps)
    out = np.concatenate([np.asarray(r["out"]) for r in r5], axis=0)
    return out.reshape(1, SEQ, D).astype(np.float32)


G4 = [[0, 1, 2, 3], [4, 5, 6, 7]]
G2 = [[0, 4], [1, 5], [2, 6], [3, 7]]


class Gather:
    def __init__(self, nc, P, name, rows, cols, dtype, n):
        self.P, self.name = P, name
        self.src = nc.dram_tensor(name + "_s", [n, rows, cols], dtype).ap()
        self.a = nc.dram_tensor(name + "_a", [n, 4 * rows, cols], dtype).ap()
        self.b = nc.dram_tensor(name + "_b", [n, 8 * rows, cols], dtype).ap()

    def s1(self, i):
        self.P.collective(G4, self.src[i], self.a[i], reads=[(self.name, "s", i)], writes=[(self.name, "a", i)])

    def s2(self, i):
        self.P.collective(G2, self.a[i], self.b[i], reads=[(self.name, "a", i)], writes=[(self.name, "b", i)])


def _xt_to_gather(P, T, GXo, t):
    for q in range(4):
        i = t * 4 + q
        P.dma("sp", GXo.src[i].rearrange("(j p) t -> p j t", p=128), T.XT[:, 4 * q:4 * q + 4, :],
              reads=[("XT", st) for st in range(NST)], writes=[(GXo.name, "s", i)])
        GXo.s1(i)
    if t > 0:
        for q in range(4):
            GXo.s2((t - 1) * 4 + q)
    if t == TPC // TT - 1:
        for q in range(4):
            GXo.s2(t * 4 + q)


def _gather_window(P, T, IX, t, srcs):
    xk = [("XT", st) for st in range(NST)]
    for k, (G, name) in enumerate(srcs):
        src2d = G.b.rearrange("w r (q c) -> (w r q) c", c=TT)
        for h in range(8):
            dst = T.XT[:, 8 * k + h, :]
            off = IX[:, t, h:h + 1]
            P.dma_fn("pool", (lambda e, dst=dst, src2d=src2d, off=off: e.indirect_dma_start(
                out=dst, out_offset=None, in_=src2d, in_offset=bass.IndirectOffsetOnAxis(ap=off, axis=0))),
                reads=[(name, "b", w) for w in range(NCORE)] + [("IX",)], writes=xk)


def build_fused(nph=7, dbg=False):
    nc = _new_nc()
    dt = lambda n, s, d, k="ExternalInput": nc.dram_tensor(n, s, d, kind=k).ap()
    it_ = lambda n, s, d: nc.dram_tensor(n, s, d).ap()
    x = dt("x", [TPC, D], F32)
    nff = 1 if nph < 4 else (3 if nph < 7 else 4)
    ffw = [[dt(f"w{i}g", [D, DFF], F32), dt(f"w{i}u", [D, DFF], F32), dt(f"w{i}d", [DFF, D], F32)] for i in range(nff)]
    lng = dt("lng", [6, D], F32)
    lnb = dt("lnb", [6, D], F32)
    winc = dt("winc", [D, 769], F32)
    wo = dt("wo", [D, D], F32)
    wsic = dt("wsic", [D, 128], F32)
    wgo = dt("wgo", [S5W, D], F32)
    wgg = dt("wgg", [S5W, D], F32)
    widx = dt("widx", [128, 4, 8], mybir.dt.int32)
    A = {"nbf": dt("nbf", [128, 1], F32)}
    for n_, shp, d_ in (("ident", [128, 128], BF16), ("ones_bf", [128, 128], BF16), ("fmask", [128, 4, 512], BF16),
                        ("dmask", [128, 256], BF16), ("uincl", [128, 128], F32), ("lstrict", [128, 128], F32),
                        ("ones_f", [128, 128], F32), ("ident_f", [128, 128], F32)):
        A[n_] = dt(n_, shp, d_)
    S = {}
    for n_, shp in (("lam_re2", [128, NG]), ("lam_im2", [128, NG]), ("logdt2", [128, NG]), ("br2", [128, NG, 16]),
                    ("bi2", [128, NG, 16]), ("cr2", [128, NG, 16]), ("ci2", [128, NG, 16]), ("d2", [128, 128]),
                    ("mask4", [128, 4, 128]), ("ti", [128, 130]), ("tir", [128, 128]), ("sg", [128, 5])):
        S[n_] = dt(n_, shp, F32)
    out = dt("out", [TPC, D], F32, "ExternalOutput")
    ident = A["ident"]
    if dbg:
        it_ = lambda n, s, d: nc.dram_tensor(n, s, d, kind="ExternalOutput").ap()
    x1s = it_("x1s", [TPC, D], F32)
    x3s = it_("x3s", [TPC, D], F32)
    scr = [it_(f"scr{i}", [128, SEQ], BF16) for i in range(6)]
    fsc = it_("fsc", [1, SEQ], F32)
    dbg_g = it_("dbg_g", [8 * 512, TT], BF16)
    dbg_y = it_("dbg_y", [2, 8 * 128, 2048], BF16)
    dbg_z = it_("dbg_z", [8 * 128, 2048], BF16)
    it_ = lambda n, s, d: nc.dram_tensor(n, s, d).ap()
    aug = it_("augscr", [6, SEQ], BF16)
    uS16 = it_("uS16", [128, SEQ // 128, 128], BF16)
    uG = it_("uG", [NG * 128, (SEQ // 128) * 16], F32)
    nch = SEQ // 128
    eps2 = LN_EPS / (ALPHA * ALPHA)
    with ExitStack() as es0:
        P = Prog(nc, es0)
        GX = Gather(nc, P, "GX", 512, TT, BF16, 16)
        GYF = Gather(nc, P, "GYF", 128, 2048, BF16, 8)
        GYD = Gather(nc, P, "GYD", 128, 2048, BF16, 8)
        GZ = Gather(nc, P, "GZ", 128, 2048, BF16, 8)
        with ExitStack() as es:
            T = TokPipe(nc, es, P, ident, tag="_p1")
            for t in range(TPC // TT):
                t0 = t * TT
                T.load_x(x[t0:t0 + TT, :])
                T.ffn(ffw[0][0], ffw[0][1], ffw[0][2], lng[0, :], lnb[0, :])
                T.store_x(x1s[t0:t0 + TT, :])
                _xt_to_gather(P, T, GX, t)
            if nph == 1:
                P.dma("sp", dbg_g, GX.b[5], reads=[("GX", "b", 5)])
                P.finish()
                P.emit()
                return nc
            P.barrier()
            P.emit()
        with ExitStack() as es:
            sb = lambda n, s, d: es.enter_context(nc.sbuf_tensor(n + "_p2a", s, d))
            ps = lambda n: es.enter_context(nc.psum_tensor(n + "_p2a", [128, 512], F32))
            XT2 = sb("XT2", [128, 2, 16, TT], BF16)
            W = sb("Wip", [128, 16, 769], BF16)
            OT = sb("OT", [128, 4, TT], BF16)
            OF = sb("OF", [1, 2, TT], F32)
            PG = [ps(f"PG{i}") for i in range(4)]
            P.dma("pool", W[:, :, :], winc.rearrange("(k p) n -> p k n", p=128), writes=[("Wip",)])
            it = 0
            oc = 0
            for t in range(TPC // TT):
                for r in range(NCORE):
                    b = it % 2
                    it += 1
                    for q in range(4):
                        i = t * 4 + q
                        P.dma("sp", XT2[:, b, 4 * q:4 * q + 4, :],
                              GX.b[i][r * 512:(r + 1) * 512, :].rearrange("(j p) t -> p j t", p=128),
                              reads=[("GX", "b", i)], writes=[("XT2", b, q)])
                    tok0 = r * TPC + t * TT
                    xk = [("XT2", b, q) for q in range(4)]
                    for ci in range(7):
                        wdt = 128 if ci < 6 else 1
                        k = oc % 4
                        oc += 1
                        for kc in range(16):
                            P.op("pe", mm(PG[k][0:wdt, :], W[:, kc, ci * 128:ci * 128 + wdt], XT2[:, b, kc, :],
                                          kc == 0, kc == 15),
                                 reads=[("Wip",)] + xk, writes=[("ps", "Gp", k)], inc=(kc == 15))
                        if ci < 6:
                            P.op("act", lambda e, k=k: e.activation(out=OT[:, k, :], in_=PG[k][:, :], func=AF.Copy),
                                 reads=[("ps", "Gp", k)], writes=[("OTp", k)])
                            P.dma("sp", scr[ci][:, tok0:tok0 + TT], OT[:, k, :], reads=[("OTp", k)])
                        else:
                            k2 = k % 2
                            P.op("act", lambda e, k=k, k2=k2: e.activation(out=OF[0:1, k2, :], in_=PG[k][0:1, :],
                                                                          func=AF.Copy),
                                 reads=[("ps", "Gp", k)], writes=[("OFp", k2)])
                            P.dma("sp", fsc[0:1, tok0:tok0 + TT], OF[0:1, k2, :], reads=[("OFp", k2)])
            if nph == 2:
                P.finish()
                P.emit()
                return nc
            P.barrier()
            P.emit()
        with ExitStack() as es:
            attn_phase(nc, P, es, A, scr, fsc, aug, GYF, GYD)
            if nph == 3:
                P.dma("sp", dbg_y[0], GYF.b[0], reads=[("GYF", "b", 0)])
                P.dma("sp", dbg_y[1], GYD.b[0], reads=[("GYD", "b", 0)])
                P.finish()
                P.emit()
                return nc
            P.barrier()
            P.emit()
        with ExitStack() as es:
            T = TokPipe(nc, es, P, ident, tag="_p3")
            IX = es.enter_context(nc.sbuf_tensor("IX_p3", [128, 4, 8], mybir.dt.int32))
            P.dma("sp", IX[:, :, :], widx, writes=[("IX",)])
            for t in range(TPC // TT):
                t0 = t * TT
                T.load_x_only(x1s[t0:t0 + TT, :])
                _gather_window(P, T, IX, t, [(GYF, "GYF"), (GYD, "GYD")])
                T.load_ln(lng[1, :], lnb[1, :])

                def cons_res(st, c0, b):
                    xs = T.X[:, st, c0:c0 + 256]
                    P.op("dve", lambda e: e.scalar_tensor_tensor(out=xs, in0=T.PY[b][:, 0:256], scalar=1.0 / ALPHA,
                                                                 in1=xs, op0=ALU.mult, op1=ALU.add),
                         reads=[("ps", "Y", b), ("X", st)], writes=[("X", st)])
                T.lin_tm(16, wo, D, cons_res)
                T.layernorm(eps2)
                T.ffn(ffw[1][0], ffw[1][1], ffw[1][2], lng[2, :], lnb[2, :])
                T.ffn(ffw[2][0], ffw[2][1], ffw[2][2], lng[3, :], lnb[3, :])
                T.store_x(x3s[t0:t0 + TT, :])
                _xt_to_gather(P, T, GX, t)
            if nph == 4:
                P.finish()
                P.emit()
                return nc
            P.barrier()
            P.emit()
        with ExitStack() as es:
            sb = lambda n, s, d: es.enter_context(nc.sbuf_tensor(n + "_p4a", s, d))
            ps = lambda n: es.enter_context(nc.psum_tensor(n + "_p4a", [128, 512], F32))
            XT2 = sb("XT2", [128, 2, 16, TT], BF16)
            W = sb("Wsi", [128, 16, 128], BF16)
            UB = sb("UB", [128, 2, 4, 128], BF16)
            UF = sb("UF", [128, 2, 4, 128], F32)
            PG = [ps(f"PG{i}") for i in range(2)]
            P.dma("pool", W[:, :, :], wsic.rearrange("(k p) n -> p k n", p=128), writes=[("Wsi",)])
            it = 0
            for t in range(TPC // TT):
                for r in range(NCORE):
                    b = it % 2
                    it += 1
                    for q in range(4):
                        i = t * 4 + q
                        P.dma("sp", XT2[:, b, 4 * q:4 * q + 4, :],
                              GX.b[i][r * 512:(r + 1) * 512, :].rearrange("(j p) t -> p j t", p=128),
                              reads=[("GX", "b", i)], writes=[("XT2", b, q)])
                    c0 = (r * TPC + t * TT) // 128
                    xk = [("XT2", b, q) for q in range(4)]
                    for st in range(4):
                        for kc in range(16):
                            P.op("pe", mm(PG[b][:, st * 128:(st + 1) * 128], XT2[:, b, kc, st * 128:(st + 1) * 128],
                                          W[:, kc, :], kc == 0, kc == 15),
                                 reads=[("Wsi",)] + xk, writes=[("ps", "Gu", b)], inc=(st == 3 and kc == 15))
                    pv = PG[b][:, :].rearrange("p (c h) -> p c h", c=4)
                    P.op("dve", lambda e, b=b, pv=pv: e.tensor_copy(out=UF[:, b, :, :], in_=pv),
                         reads=[("ps", "Gu", b)], writes=[("UFs", b)])
                    P.op("act", lambda e, b=b: e.activation(out=UB[:, b, :, :], in_=UF[:, b, :, :], func=AF.Copy),
                         reads=[("UFs", b)], writes=[("UB", b)])
                    P.dma("sp", uS16[:, c0:c0 + 4, :], UB[:, b, :, :], reads=[("UB", b)])
                    for g in range(NG):
                        P.dma("sp", uG[g * 128:(g + 1) * 128, c0 * 16:(c0 + 4) * 16].rearrange("s (c h) -> s c h", h=16),
                              UF[:, b, :, g * 16:(g + 1) * 16], reads=[("UFs", b)])
            if nph == 5:
                P.finish()
                P.emit()
                return nc
            P.barrier()
            P.emit()
        with ExitStack() as es:
            s5_phase(nc, P, es, S, ident, uS16, uG, GZ)
            if nph == 6:
                P.dma("sp", dbg_z, GZ.b[3], reads=[("GZ", "b", 3)])
                P.finish()
                P.emit()
                return nc
            P.barrier()
            P.emit()
        with ExitStack() as es:
            T = TokPipe(nc, es, P, ident, tag="_p5")
            IX = es.enter_context(nc.sbuf_tensor("IX_p5", [128, 4, 8], mybir.dt.int32))
            P.dma("sp", IX[:, :, :], widx, writes=[("IX",)])
            for t in range(TPC // TT):
                t0 = t * TT
                T.load_x_only(x3s[t0:t0 + TT, :])
                _gather_window(P, T, IX, t, [(GZ, "GZ")])
                T.load_ln(lng[4, :], lnb[4, :])

                def cons_glu(st, c0, b):
                    k = T.ot % 2
                    T.ot += 1
                    sg = T.SG[:, k, 0:256]
                    xs = T.X[:, st, c0:c0 + 256]
                    P.op("act", lambda e: e.activation(out=sg, in_=T.PG[b][:, 0:256], func=AF.Sigmoid),
                         reads=[("ps", "G", b)], writes=[("SG", k)])
                    P.op("dve", lambda e: e.tensor_tensor(out=sg, in0=sg, in1=T.PY[b][:, 0:256], op=ALU.mult),
                         reads=[("SG", k), ("ps", "Y", b)], writes=[("SG", k)])
                    P.op("dve", lambda e: e.scalar_tensor_tensor(out=xs, in0=sg, scalar=1.0 / ALPHA, in1=xs,
                                                                 op0=ALU.mult, op1=ALU.add),
                         reads=[("SG", k), ("X", st)], writes=[("X", st)])
                T.lin_tm(8, wgo, D, cons_glu, extra_w=wgg)
                T.layernorm(eps2)
                T.ffn(ffw[3][0], ffw[3][1], ffw[3][2], lng[5, :], lnb[5, :])
                T.store_x(out[t0:t0 + TT, :])
            P.finish()
            P.emit()
    return nc


def l4_params(inp, core):
    gs = slice(core * NG, (core + 1) * NG)
    two = lambda a: np.ascontiguousarray(np.concatenate([a, a], 0).astype(np.float32))
    m = dict(l4_consts())
    del m["ident"]
    m["lam_re2"] = two(inp["s5_lambda_re"][0][gs].T)
    m["lam_im2"] = two(inp["s5_lambda_im"][0][gs].T)
    m["logdt2"] = np.ascontiguousarray(np.broadcast_to(inp["s5_log_dt"][0][gs][None, :], (128, NG)).astype(np.float32))
    m["br2"] = two(inp["s5_b_re"][0][gs].transpose(1, 0, 2))
    m["bi2"] = two(inp["s5_b_im"][0][gs].transpose(1, 0, 2))
    m["cr2"] = two(inp["s5_c_re"][0][gs].transpose(2, 0, 1))
    m["ci2"] = two(inp["s5_c_im"][0][gs].transpose(2, 0, 1))
    m["d2"] = np.ascontiguousarray(np.broadcast_to(inp["s5_d"][0][core * 128:(core + 1) * 128][None, :], (128, 128)).astype(np.float32))
    return m


def fused_inputs(inp):
    ca = np.ascontiguousarray
    x = inp["x"][0]
    consts = l2_consts()
    consts["ident_f"] = np.eye(128, dtype=np.float32)
    shared = {
        "w0g": inp["ffn1_w_gate"][0], "w0u": inp["ffn1_w_up"][0], "w0d": inp["ffn1_w_down"][0],
        "w1g": inp["ffn2_w_gate"][0], "w1u": inp["ffn2_w_up"][0], "w1d": inp["ffn2_w_down"][0],
        "w2g": inp["ffn1_w_gate"][1], "w2u": inp["ffn1_w_up"][1], "w2d": inp["ffn1_w_down"][1],
        "w3g": inp["ffn2_w_gate"][1], "w3u": inp["ffn2_w_up"][1], "w3d": inp["ffn2_w_down"][1],
        "lng": ca(inp["ln_gain"].reshape(6, D)), "lnb": ca(inp["ln_bias"].reshape(6, D)),
        "wo": inp["attn_w_out"][0], "wgo": inp["s5_w_glu_out"][0], "wgg": inp["s5_w_glu_gate"][0],
    }
    shared.update(consts)
    win = inp["attn_w_in"][0]
    maps = []
    for c in range(NCORE):
        m = dict(shared)
        m["x"] = ca(x[c * TPC:(c + 1) * TPC])
        cols = np.concatenate([np.arange(128) + off + 128 * c for off in (0, 1024, 2048, 3080, 4104, 5128)]
                              + [np.array([3072 + c])])
        m["winc"] = ca(win[:, cols])
        m["wsic"] = ca(inp["s5_w_in"][0][:, 128 * c:128 * (c + 1)])
        pp = np.arange(128)[:, None, None]
        tt_ = np.arange(4)[None, :, None]
        hh = np.arange(8)[None, None, :]
        m["widx"] = np.ascontiguousarray(((c * 1024 + hh * 128 + pp) * 4 + tt_).astype(np.int32))
        m["nbf"] = np.full((128, 1), inp["attn_b_f"][0, c], np.float32)
        m.update(l4_params(inp, c))
        maps.append(m)
    return maps


def kernel(**inputs):
    inp = {k: np.asarray(v) for k, v in inputs.items()}
    maps = fused_inputs(inp)
    res = _run(_get("fused", build_fused), maps)
    out = np.concatenate([np.asarray(r["out"]) for r in res], axis=0)
    return out.reshape(1, SEQ, D).astype(np.float32)
```

```python
from contextlib import ExitStack
import math
import numpy as np
import ml_dtypes

import concourse.bass as bass
import concourse.mybir as mybir
from concourse.bass_utils import run_bass_kernel_spmd

F32 = mybir.dt.float32
BF16 = mybir.dt.bfloat16
ALU = mybir.AluOpType
AF = mybir.ActivationFunctionType

D = 2048
SEQ = 16384
NCORE = 8
TPC = SEQ // NCORE
DFF = 5632
NFC = DFF // 128
HD = 128
ATT_IN = 6152
S5W = 1024
ALPHA = 4.0 ** 0.25
LN_EPS = 1e-5
TT = 512
NST = TT // 128


class Prog:
    ENGS = ("pe", "act", "dve", "pool", "sp")

    def __init__(self, nc, es, ndma=6):
        self.nc = nc
        self.ops = {e: [] for e in self.ENGS}
        self.semh = {}
        self.cnt = {e: 0 for e in self.ENGS}
        for e in self.ENGS:
            self.semh["e:" + e] = es.enter_context(nc.semaphore("se_" + e))
        self.nd = ndma
        self.dcnt = {}
        self.dnext = {"sp": 0, "pool": 0}
        for q in ("sp", "pool"):
            for k in range(ndma):
                self.semh[f"d:{q}:{k}"] = es.enter_context(nc.semaphore(f"sd_{q}{k}"))
                self.dcnt[(q, k)] = 0
        self.known = {e: {} for e in self.ENGS}
        self.res = {}
        self.semh["e:cc"] = es.enter_context(nc.semaphore("se_cc"))
        self.ccn = 0

    def collective(self, groups, src, dst, reads=(), writes=()):
        waits = self._collect("pool", reads, writes)
        self.ccn += 1
        ev = ("e:cc", self.ccn)
        self.ops["pool"].append((waits, (lambda e, g=groups, a=src, b=dst: e.collective_compute(
            "AllGather", ALU.bypass, replica_groups=g, ins=[a], outs=[b])), "e:cc", 1))
        self._commit(ev, reads, writes)

    def barrier(self):
        allw = []
        for e in self.ENGS:
            if self.cnt[e]:
                allw.append(("e:" + e, self.cnt[e]))
        for (q, k), c in self.dcnt.items():
            if c:
                allw.append((f"d:{q}:{k}", c * 16))
        for e in self.ENGS:
            kn = self.known[e]
            waits = []
            for sk, v in allw:
                if sk == "e:" + e:
                    continue
                if kn.get(sk, 0) < v:
                    kn[sk] = v
                    waits.append((sk, v))
            self.ops[e].append((waits, None, None, 0))

    def _collect(self, eng, reads, writes):
        deps = {}
        own = "e:" + eng

        def add(ev, same_ok):
            if ev is None:
                return
            sk, v = ev
            if sk == own and not same_ok:
                return
            if deps.get(sk, 0) < v:
                deps[sk] = v

        for r in reads:
            st = self.res.get(r)
            if st is not None and st[0] is not None:
                add(st[0], True)
        for w in writes:
            st = self.res.get(w)
            if st is not None:
                add(st[0], False)
                for ev in st[1].values():
                    add(ev, False)
        waits = []
        kn = self.known[eng]
        for sk, v in deps.items():
            if kn.get(sk, 0) < v:
                kn[sk] = v
                waits.append((sk, v))
        return waits

    def _commit(self, ev, reads, writes):
        sk = ev[0]
        for r in reads:
            st = self.res.get(r)
            if st is None:
                st = [None, {}]
                self.res[r] = st
            old = st[1].get(sk)
            if old is None or old[1] < ev[1]:
                st[1][sk] = ev
        for w in writes:
            self.res[w] = [ev, {}]

    def op(self, eng, fn, reads=(), writes=(), inc=True):
        waits = self._collect(eng, reads, writes)
        if inc:
            self.cnt[eng] += 1
            ev = ("e:" + eng, self.cnt[eng])
        else:
            ev = ("e:" + eng, self.cnt[eng] + 1)
        self.ops[eng].append((waits, fn, ("e:" + eng) if inc else None, 1))
        self._commit(ev, reads, writes)

    def dma(self, q, out, in_, reads=(), writes=()):
        k = self.dnext[q] % self.nd
        self.dnext[q] += 1
        sk = f"d:{q}:{k}"
        waits = self._collect(q, reads, writes)
        prev = self.dcnt[(q, k)] * 16
        if prev and self.known[q].get(sk, 0) < prev:
            self.known[q][sk] = prev
            waits.append((sk, prev))
        self.dcnt[(q, k)] += 1
        ev = (sk, self.dcnt[(q, k)] * 16)
        self.ops[q].append((waits, (lambda e, o=out, i=in_: e.dma_start(out=o, in_=i)), sk, 16))
        self._commit(ev, reads, writes)

    def finish(self):
        waits = []
        for e in self.ENGS:
            if self.cnt[e]:
                waits.append(("e:" + e, self.cnt[e]))
        for (q, k), c in self.dcnt.items():
            if c:
                waits.append((f"d:{q}:{k}", c * 16))
        if self.ccn:
            waits.append(("e:cc", self.ccn))
        self.ops["sp"].append((waits, None, None, 0))

    def emit(self):
        nc = self.nc
        with nc.Block() as block:
            def mk(name):
                def body(e):
                    for waits, fn, sk, n in self.ops[name]:
                        for (wsk, v) in waits:
                            e.wait_ge(self.semh[wsk], v)
                        if fn is None:
                            continue
                        ins = fn(e)
                        if sk is not None:
                            ins.then_inc(self.semh[sk], n)
                return body
            block.tensor(mk("pe"))
            block.scalar(mk("act"))
            block.vector(mk("dve"))
            block.gpsimd(mk("pool"))
            block.sync(mk("sp"))
        self.ops = {e: [] for e in self.ENGS}


def mm(out, lhsT, rhs, start, stop):
    return lambda e: e.matmul(out, lhsT, rhs, start=start, stop=stop)


class TokPipe:
    def __init__(self, nc, es, P, ident_dram, tag=""):
        self.nc, self.P = nc, P
        sb = lambda n, s, d: es.enter_context(nc.sbuf_tensor(n + tag, s, d))
        ps = lambda n: es.enter_context(nc.psum_tensor(n + tag, [128, 512], F32))
        self.X = sb("X", [128, NST, D], F32)
        self.XT = sb("XT", [128, 16, TT], BF16)
        self.HT = sb("HT", [128, NFC, TT], BF16)
        self.WR = sb("WR", [128, 3, 11264], BF16)
        self.G = sb("G", [128, D], F32)
        self.B = sb("B", [128, D], F32)
        self.XB = sb("XB", [128, 2, D], BF16)
        self.SG = sb("SG", [128, 2, TT], F32)
        self.OT = sb("OT", [128, 2, TT], BF16)
        self.OF = sb("OF", [128, 2, TT], F32)
        self.stats = sb("stats", [128, NST, 24], F32)
        self.mv = sb("mv", [128, NST, 2], F32)
        self.rstd = sb("rstd", [128, NST, 1], F32)
        self.nmr = sb("nmr", [128, NST, 1], F32)
        self.ident = sb("ident_sb", [128, 128], BF16)
        self.PG = [ps("PG0"), ps("PG1")]
        self.PU = [ps("PU0"), ps("PU1")]
        self.PY = [ps("PY0"), ps("PY1")]
        self.PT = [ps("PT0"), ps("PT1")]
        self.ring = 0
        self.ot = 0
        P.dma("sp", self.ident[:, :], ident_dram, writes=[("ident",)])

    def ring_next(self):
        s = self.ring % 3
        self.ring += 1
        return s

    def load_x(self, x_rows):
        P = self.P
        P.dma("sp", self.X[:, :, :], x_rows.rearrange("(s p) d -> p s d", p=128),
              writes=[("X", st) for st in range(NST)])
        for st in range(NST):
            self.make_xt(st)

    def load_x_only(self, x_rows):
        P = self.P
        P.dma("sp", self.X[:, :, :], x_rows.rearrange("(s p) d -> p s d", p=128),
              writes=[("X", st) for st in range(NST)])

    def load_xt(self, xt_rows, nk):
        P = self.P
        P.dma("sp", self.XT[:, 0:nk, :], xt_rows.rearrange("(k p) t -> p k t", p=128),
              writes=[("XT", st) for st in range(NST)])

    def store_x(self, out_rows):
        P = self.P
        P.dma("sp", out_rows.rearrange("(s p) d -> p s d", p=128), self.X[:, :, :],
              reads=[("X", st) for st in range(NST)])

    def make_xt(self, st):
        P = self.P
        b = st % 2
        X, XB, XT, PT, ident = self.X, self.XB, self.XT, self.PT, self.ident
        P.op("act", lambda e: e.activation(out=XB[:, b, :], in_=X[:, st, :], func=AF.Copy),
             reads=[("X", st)], writes=[("XB", b)])
        for kg in range(4):
            pb = (st * 4 + kg) % 2
            for j in range(4):
                kc = kg * 4 + j
                P.op("pe", mm(PT[pb][:, j * 128:(j + 1) * 128], XB[:, b, kc * 128:(kc + 1) * 128],
                              ident[:, :], True, True),
                     reads=[("XB", b), ("ident",)], writes=[("ps", "T", pb)], inc=(j == 3))
            src = PT[pb][:, :].rearrange("p (j t) -> p j t", j=4)
            dst = XT[:, kg * 4:(kg + 1) * 4, st * 128:(st + 1) * 128]
            P.op("dve", lambda e, s=src, d=dst: e.tensor_copy(out=d, in_=s),
                 reads=[("ps", "T", pb)], writes=[("XT", st)])

    def load_ln(self, g_row, b_row):
        P = self.P
        P.dma("sp", self.G[:, :], g_row.partition_broadcast(128), writes=[("G",)])
        P.dma("sp", self.B[:, :], b_row.partition_broadcast(128), writes=[("B",)])

    def layernorm(self, eps, make_xt=True):
        P = self.P
        X, G, B = self.X, self.G, self.B
        stats, mv, rstd, nmr = self.stats, self.mv, self.rstd, self.nmr
        for st in range(NST):
            for q in range(4):
                P.op("dve", lambda e, st=st, q=q: e.bn_stats(out=stats[:, st, q * 6:(q + 1) * 6],
                                                             in_=X[:, st, q * 512:(q + 1) * 512]),
                     reads=[("X", st)], writes=[("stats", st)])
            P.op("dve", lambda e, st=st: e.bn_aggr(out=mv[:, st, :], in_=stats[:, st, :]),
                 reads=[("stats", st)], writes=[("mv", st)])
            P.op("dve", lambda e, st=st: e.tensor_scalar(out=rstd[:, st, :], in0=mv[:, st, 1:2],
                                                         scalar1=eps, scalar2=None, op0=ALU.add),
                 reads=[("mv", st)], writes=[("rstd", st)])
            P.op("act", lambda e, st=st: e.activation(out=rstd[:, st, :], in_=rstd[:, st, :], func=AF.Ln),
                 reads=[("rstd", st)], writes=[("rstd", st)])
            P.op("act", lambda e, st=st: e.activation(out=rstd[:, st, :], in_=rstd[:, st, :], func=AF.Exp,
                                                      scale=-0.5),
                 reads=[("rstd", st)], writes=[("rstd", st)])
            P.op("dve", lambda e, st=st: e.scalar_tensor_tensor(out=nmr[:, st, :], in0=mv[:, st, 0:1],
                                                                scalar=-1.0, in1=rstd[:, st, :],
                                                                op0=ALU.mult, op1=ALU.mult),
                 reads=[("mv", st), ("rstd", st)], writes=[("nmr", st)])
            P.op("act", lambda e, st=st: e.activation(out=X[:, st, :], in_=X[:, st, :], func=AF.Identity,
                                                      bias=nmr[:, st, :], scale=rstd[:, st, :]),
                 reads=[("X", st), ("rstd", st), ("nmr", st)], writes=[("X", st)])
            P.op("dve", lambda e, st=st: e.tensor_tensor(out=X[:, st, :], in0=X[:, st, :], in1=G[:, :],
                                                         op=ALU.mult),
                 reads=[("X", st), ("G",)], writes=[("X", st)])
            P.op("pool", lambda e, st=st: e.tensor_tensor(out=X[:, st, :], in0=X[:, st, :], in1=B[:, :],
                                                          op=ALU.add),
                 reads=[("X", st), ("B",)], writes=[("X", st)])
            if make_xt:
                self.make_xt(st)

    def ffn(self, wg, wu, wd, g_row, b_row):
        P = self.P
        XT, HT, WR, SG, X = self.XT, self.HT, self.WR, self.SG, self.X
        PG, PU, PY = self.PG, self.PU, self.PY
        self.load_ln(g_row, b_row)
        wgv = wg.rearrange("(k p) n -> p k n", p=128)
        wuv = wu.rearrange("(k p) n -> p k n", p=128)
        wdv = wd.rearrange("(k p) n -> p k n", p=128)
        xt_keys = [("XT", st) for st in range(NST)]
        for fg in range(NFC // 2):
            s = self.ring_next()
            gv = WR[:, s, 0:4096].rearrange("p (k n) -> p k n", k=16)
            uv = WR[:, s, 4096:8192].rearrange("p (k n) -> p k n", k=16)
            P.dma("pool", gv, wgv[:, :, fg * 256:(fg + 1) * 256], writes=[("WR", s, "a")])
            P.dma("pool", uv, wuv[:, :, fg * 256:(fg + 1) * 256], writes=[("WR", s, "b")])
            for fc in range(2):
                ch = fg * 2 + fc
                b = ch % 2
                for kc in range(16):
                    P.op("pe", mm(PG[b][:, :], gv[:, kc, fc * 128:(fc + 1) * 128], XT[:, kc, :],
                                  kc == 0, kc == 15),
                         reads=[("WR", s, "a")] + xt_keys, writes=[("ps", "G", b)], inc=(kc == 15))
                for kc in range(16):
                    P.op("pe", mm(PU[b][:, :], uv[:, kc, fc * 128:(fc + 1) * 128], XT[:, kc, :],
                                  kc == 0, kc == 15),
                         reads=[("WR", s, "b")] + xt_keys, writes=[("ps", "U", b)], inc=(kc == 15))
                P.op("act", lambda e, b=b: e.activation(out=SG[:, b, :], in_=PG[b][:, :], func=AF.Silu),
                     reads=[("ps", "G", b)], writes=[("SG", b)])
                P.op("dve", lambda e, b=b, ch=ch: e.tensor_tensor(out=HT[:, ch, :], in0=SG[:, b, :],
                                                                  in1=PU[b][:, :], op=ALU.mult),
                     reads=[("SG", b), ("ps", "U", b)], writes=[("HT", ch)])
        coef = 0.5 / ALPHA
        ht_keys = [("HT", ch) for ch in range(NFC)]
        it = 0
        for dp in range(D // 256):
            s = self.ring_next()
            dv = WR[:, s, 0:NFC * 256].rearrange("p (k n) -> p k n", k=NFC)
            P.dma("pool", dv, wdv[:, :, dp * 256:(dp + 1) * 256],
                  writes=[("WR", s, "a"), ("WR", s, "b")])
            for st in range(NST):
                b = it % 2
                it += 1
                for fc in range(NFC):
                    P.op("pe", mm(PY[b][:, 0:256], HT[:, fc, st * 128:(st + 1) * 128], dv[:, fc, :],
                                  fc == 0, fc == NFC - 1),
                         reads=[("WR", s, "a"), ("WR", s, "b")] + (ht_keys if fc == 0 else []),
                         writes=[("ps", "Y", b)], inc=(fc == NFC - 1))
                xs = X[:, st, dp * 256:(dp + 1) * 256]
                P.op("dve", lambda e, b=b, xs=xs: e.scalar_tensor_tensor(out=xs, in0=PY[b][:, 0:256],
                                                                         scalar=coef, in1=xs,
                                                                         op0=ALU.mult, op1=ALU.add),
                     reads=[("ps", "Y", b), ("X", st)], writes=[("X", st)])
        self.layernorm(LN_EPS / (ALPHA * ALPHA))

    def lin_tm(self, nk, w, ncols, consumer, extra_w=None):
        P = self.P
        XT, WR, PY, PG = self.XT, self.WR, self.PY, self.PG
        wv = w.rearrange("(k p) n -> p k n", p=128)
        wv2 = extra_w.rearrange("(k p) n -> p k n", p=128) if extra_w is not None else None
        xt_keys = [("XT", st) for st in range(NST)]
        it = 0
        for dp in range(ncols // 256):
            s = self.ring_next()
            dv = WR[:, s, 0:nk * 256].rearrange("p (k n) -> p k n", k=nk)
            P.dma("pool", dv, wv[:, :, dp * 256:(dp + 1) * 256], writes=[("WR", s, "a")])
            if wv2 is not None:
                dv2 = WR[:, s, 5632:5632 + nk * 256].rearrange("p (k n) -> p k n", k=nk)
                P.dma("pool", dv2, wv2[:, :, dp * 256:(dp + 1) * 256], writes=[("WR", s, "b")])
            for st in range(NST):
                b = it % 2
                it += 1
                for kc in range(nk):
                    P.op("pe", mm(PY[b][:, 0:256], XT[:, kc, st * 128:(st + 1) * 128], dv[:, kc, :],
                                  kc == 0, kc == nk - 1),
                         reads=[("WR", s, "a")] + xt_keys, writes=[("ps", "Y", b)], inc=(kc == nk - 1))
                if wv2 is not None:
                    for kc in range(nk):
                        P.op("pe", mm(PG[b][:, 0:256], XT[:, kc, st * 128:(st + 1) * 128], dv2[:, kc, :],
                                      kc == 0, kc == nk - 1),
                             reads=[("WR", s, "b")] + xt_keys, writes=[("ps", "G", b)],
                             inc=(kc == nk - 1))
                    consumer(st, dp * 256, b)
                else:
                    consumer(st, dp * 256, b)

    def lin_fm(self, w, pieces, consumer):
        P = self.P
        XT, WR, PG = self.XT, self.WR, self.PG
        wv = w.rearrange("(k p) n -> p k n", p=128)
        xt_keys = [("XT", st) for st in range(NST)]
        it = 0
        for (c0, ncol) in pieces:
            s = self.ring_next()
            dv = WR[:, s, 0:16 * ncol].rearrange("p (k n) -> p k n", k=16)
            P.dma("pool", dv, wv[:, :, c0:c0 + ncol], writes=[("WR", s, "a")])
            off = 0
            while off < ncol:
                wdt = min(128, ncol - off)
                b = it % 2
                it += 1
                for kc in range(16):
                    P.op("pe", mm(PG[b][0:wdt, :], dv[:, kc, off:off + wdt], XT[:, kc, :],
                                  kc == 0, kc == 15),
                         reads=[("WR", s, "a")] + xt_keys, writes=[("ps", "G", b)], inc=(kc == 15))
                consumer(c0 + off, wdt, b)
                off += wdt


def _new_nc():
    return bass.Bass("TRN2", target_bir_lowering=False)


def _ident_np():
    return np.eye(128, dtype=np.float32).astype(ml_dtypes.bfloat16)


def build_l1(ntok=TPC):
    nc = _new_nc()
    dt = lambda n, s, d, k: nc.dram_tensor(n, s, d, kind=k).ap()
    x = dt("x", [ntok, D], F32, "ExternalInput")
    wg = dt("wg", [D, DFF], F32, "ExternalInput")
    wu = dt("wu", [D, DFF], F32, "ExternalInput")
    wd = dt("wd", [DFF, D], F32, "ExternalInput")
    lng = dt("lng", [1, D], F32, "ExternalInput")
    lnb = dt("lnb", [1, D], F32, "ExternalInput")
    win = dt("win", [D, ATT_IN], F32, "ExternalInput")
    ident = dt("ident", [128, 128], BF16, "ExternalInput")
    x1 = dt("x1", [ntok, D], F32, "ExternalOutput")
    projT = dt("projT", [6144, ntok], BF16, "ExternalOutput")
    fT = dt("fT", [8, ntok], F32, "ExternalOutput")
    with ExitStack() as es:
        P = Prog(nc, es)
        T = TokPipe(nc, es, P, ident)
        pieces = [(c, 256) for c in range(0, 3072, 256)] + [(3072, 8)] + \
                 [(c, 256) for c in range(3080, 6152, 256)]
        for t in range(ntok // TT):
            t0 = t * TT
            T.load_x(x[t0:t0 + TT, :])
            T.ffn(wg, wu, wd, lng[0, :], lnb[0, :])
            T.store_x(x1[t0:t0 + TT, :])

            def cons(c0, wdt, b, t0=t0):
                k = T.ot % 2
                T.ot += 1
                if wdt == 8:
                    P.op("act", lambda e: e.activation(out=T.OF[0:8, k, :], in_=T.PG[b][0:8, :], func=AF.Copy),
                         reads=[("ps", "G", b)], writes=[("OF", k)])
                    P.dma("sp", fT[:, t0:t0 + TT], T.OF[0:8, k, :], reads=[("OF", k)])
                else:
                    r0 = c0 if c0 < 3072 else c0 - 8
                    P.op("act", lambda e: e.activation(out=T.OT[0:wdt, k, :], in_=T.PG[b][0:wdt, :],
                                                       func=AF.Copy),
                         reads=[("ps", "G", b)], writes=[("OT", k)])
                    P.dma("sp", projT[r0:r0 + wdt, t0:t0 + TT], T.OT[0:wdt, k, :], reads=[("OT", k)])
            T.lin_fm(win, pieces, cons)
        P.finish()
        P.emit()
    return nc


def build_l3(ntok=TPC):
    nc = _new_nc()
    dt = lambda n, s, d, k="ExternalInput": nc.dram_tensor(n, s, d, kind=k).ap()
    x1 = dt("x1", [ntok, D], F32)
    yT = dt("yT", [D, ntok], BF16)
    wo = dt("wo", [D, D], F32)
    w2 = [dt("w2g", [D, DFF], F32), dt("w2u", [D, DFF], F32), dt("w2d", [DFF, D], F32)]
    w3 = [dt("w3g", [D, DFF], F32), dt("w3u", [D, DFF], F32), dt("w3d", [DFF, D], F32)]
    lng = dt("lng", [3, D], F32)
    lnb = dt("lnb", [3, D], F32)
    wsi = dt("wsi", [D, S5W], F32)
    ident = dt("ident", [128, 128], BF16)
    x3 = dt("x3", [ntok, D], F32, "ExternalOutput")
    u = dt("u", [ntok, S5W], F32, "ExternalOutput")
    with ExitStack() as es:
        P = Prog(nc, es)
        T = TokPipe(nc, es, P, ident)
        for t in range(ntok // TT):
            t0 = t * TT
            T.load_x_only(x1[t0:t0 + TT, :])
            T.load_xt(yT[:, t0:t0 + TT], 16)
            T.load_ln(lng[0, :], lnb[0, :])

            def cons_res(st, c0, b):
                xs = T.X[:, st, c0:c0 + 256]
                P.op("dve", lambda e: e.scalar_tensor_tensor(out=xs, in0=T.PY[b][:, 0:256], scalar=1.0 / ALPHA,
                                                             in1=xs, op0=ALU.mult, op1=ALU.add),
                     reads=[("ps", "Y", b), ("X", st)], writes=[("X", st)])
            T.lin_tm(16, wo, D, cons_res)
            T.layernorm(LN_EPS / (ALPHA * ALPHA))
            T.ffn(w2[0], w2[1], w2[2], lng[1, :], lnb[1, :])
            T.ffn(w3[0], w3[1], w3[2], lng[2, :], lnb[2, :])
            T.store_x(x3[t0:t0 + TT, :])

            def cons_u(st, c0, b, t0=t0):
                k = T.ot % 2
                T.ot += 1
                P.op("act", lambda e: e.activation(out=T.OF[:, k, 0:256], in_=T.PY[b][:, 0:256], func=AF.Copy),
                     reads=[("ps", "Y", b)], writes=[("OF", k)])
                P.dma("sp", u[t0 + st * 128:t0 + (st + 1) * 128, c0:c0 + 256], T.OF[:, k, 0:256],
                      reads=[("OF", k)])
            T.lin_tm(16, wsi, S5W, cons_u)
        P.finish()
        P.emit()
    return nc


def build_l5(ntok=TPC):
    nc = _new_nc()
    dt = lambda n, s, d, k="ExternalInput": nc.dram_tensor(n, s, d, kind=k).ap()
    x3 = dt("x3", [ntok, D], F32)
    zT = dt("zT", [S5W, ntok], BF16)
    wgo = dt("wgo", [S5W, D], F32)
    wgg = dt("wgg", [S5W, D], F32)
    w4 = [dt("w4g", [D, DFF], F32), dt("w4u", [D, DFF], F32), dt("w4d", [DFF, D], F32)]
    lng = dt("lng", [2, D], F32)
    lnb = dt("lnb", [2, D], F32)
    ident = dt("ident", [128, 128], BF16)
    out = dt("out", [ntok, D], F32, "ExternalOutput")
    with ExitStack() as es:
        P = Prog(nc, es)
        T = TokPipe(nc, es, P, ident)
        for t in range(ntok // TT):
            t0 = t * TT
            T.load_x_only(x3[t0:t0 + TT, :])
            T.load_xt(zT[:, t0:t0 + TT], 8)
            T.load_ln(lng[0, :], lnb[0, :])

            def cons_glu(st, c0, b):
                k = T.ot % 2
                T.ot += 1
                sg = T.SG[:, k, 0:256]
                xs = T.X[:, st, c0:c0 + 256]
                P.op("act", lambda e: e.activation(out=sg, in_=T.PG[b][:, 0:256], func=AF.Sigmoid),
                     reads=[("ps", "G", b)], writes=[("SG", k)])
                P.op("dve", lambda e: e.tensor_tensor(out=sg, in0=sg, in1=T.PY[b][:, 0:256], op=ALU.mult),
                     reads=[("SG", k), ("ps", "Y", b)], writes=[("SG", k)])
                P.op("dve", lambda e: e.scalar_tensor_tensor(out=xs, in0=sg, scalar=1.0 / ALPHA, in1=xs,
                                                             op0=ALU.mult, op1=ALU.add),
                     reads=[("SG", k), ("X", st)], writes=[("X", st)])
            T.lin_tm(8, wgo, D, cons_glu, extra_w=wgg)
            T.layernorm(LN_EPS / (ALPHA * ALPHA))
            T.ffn(w4[0], w4[1], w4[2], lng[1, :], lnb[1, :])
            T.store_x(out[t0:t0 + TT, :])
        P.finish()
        P.emit()
    return nc


DIL = ((1, 16), (4, 4), (16, 1))
NEG = -30000.0


def l2_consts():
    p = np.arange(128)[:, None]
    q = np.arange(512)[None, :]
    fm = np.zeros((128, 4, 512), np.float32)
    for jj in range(4):
        fm[:, jj, :] = np.where(jj * 128 + p > q, NEG, 0.0)
    q1 = np.arange(128)[None, :]
    dm = np.zeros((128, 256), np.float32)
    dm[:, 0:128] = np.where(p < q1, NEG, 0.0)
    dm[:, 128:256] = np.where(p > q1, NEG, 0.0)
    uincl = (p <= q1).astype(np.float32)
    lstrict = (p < q1).astype(np.float32)
    bf = ml_dtypes.bfloat16
    return {
        "ident": _ident_np(), "ones_bf": np.ones((128, 128), bf), "fmask": fm.astype(bf),
        "dmask": dm.astype(bf), "uincl": uincl, "lstrict": lstrict, "ones_f": np.ones((128, 128), np.float32),
    }


def build_l2(seq=SEQ):
    nc = _new_nc()
    dt = lambda n, s, d, k="ExternalInput": nc.dram_tensor(n, s, d, kind=k).ap()
    nblk = seq // 128
    qf = dt("qf", [128, seq], BF16)
    kf = dt("kf", [128, seq], BF16)
    vf = dt("vf", [seq, 128], BF16)
    f2T = dt("f2T", [128, nblk], F32)
    nbf = dt("nbf", [128, 1], F32)
    qd = dt("qd", [128, seq], BF16)
    kd = dt("kd", [128, seq], BF16)
    vdp = [dt(f"vd{d}", [seq, 128], BF16) for (d, _) in DIL]
    c_ident = dt("ident", [128, 128], BF16)
    c_ones = dt("ones_bf", [128, 128], BF16)
    c_fm = dt("fmask", [128, 4, 512], BF16)
    c_dm = dt("dmask", [128, 256], BF16)
    c_ui = dt("uincl", [128, 128], F32)
    c_ls = dt("lstrict", [128, 128], F32)
    c_of = dt("ones_f", [128, 128], F32)
    yf = dt("yf", [128, seq], BF16, "ExternalOutput")
    yd = dt("yd", [128, seq], BF16, "ExternalOutput")
    scr = nc.dram_tensor("scr", [6, seq], BF16).ap()
    scale = HD ** -0.5
    with ExitStack() as es:
        P = Prog(nc, es)
        sb = lambda n, s, d: es.enter_context(nc.sbuf_tensor(n, s, d))
        ps = lambda n: es.enter_context(nc.psum_tensor(n, [128, 512], F32))
        BIG = [sb(f"BIG{i}", [128, seq], BF16) for i in range(4)]
        ident = sb("identb", [128, 128], BF16)
        ones = sb("onesb", [128, 128], BF16)
        FM = sb("FM", [128, 4, 512], BF16)
        DM = sb("DM", [128, 256], BF16)
        UI = sb("UI", [128, 128], F32)
        LS = sb("LS", [128, 128], F32)
        OFc = sb("OFc", [128, 128], F32)
        NB = sb("NB", [128, 1], F32)
        LF = sb("LF", [128, nblk], F32)
        RB = sb("RB", [128, 128], F32)
        C = sb("C", [128, 128], F32)
        R1 = sb("R1", [128, 128], F32)
        HI = sb("HI", [128, 6, 128], BF16)
        QF = sb("QF", [128, 2, 512], BF16)
        PTt = sb("PTt", [128, 3, 512], BF16)
        RL = sb("RL", [128, 512], F32)
        YO = sb("YO", [128, 2, 512], BF16)
        VU = sb("VU", [128, 4, 2, 128], BF16)
        ACC = sb("ACC", [128, 2, 2048], F32)
        YW = sb("YW", [128, 2048], BF16)
        PS = [ps("PS0"), ps("PS1")]
        PO = [ps("PO0"), ps("PO1")]
        PL = [ps("PL0"), ps("PL1")]
        for (t, src, key) in ((ident, c_ident, "ident"), (ones, c_ones, "ones"), (FM, c_fm, "FM"),
                              (DM, c_dm, "DM"), (UI, c_ui, "UI"), (LS, c_ls, "LS"), (OFc, c_of, "OFc"),
                              (NB, nbf, "NB"), (LF, f2T, "LF")):
            idx = (slice(None),) * len(t.shape)
            P.dma("sp", t[idx], src, writes=[(key,)])
        P.op("dve", lambda e: e.tensor_scalar(out=NB[:, :], in0=NB[:, :], scalar1=-1.0, scalar2=None, op0=ALU.mult),
             reads=[("NB",)], writes=[("NB",)])
        KT, VK, AK, AQ = BIG[0], BIG[1], BIG[2], BIG[3]
        P.dma("sp", KT[:, :], kf, writes=[("BIG", 0)])
        P.dma("sp", VK[:, :].rearrange("p (b d) -> p b d", d=128), vf.rearrange("(b p) d -> p b d", p=128),
              writes=[("BIG", 1)])
        P.op("act", lambda e: e.activation(out=LF[:, :], in_=LF[:, :], func=AF.Exp, bias=NB[:, :], scale=-1.0),
             reads=[("LF",), ("NB",)], writes=[("LF",)])
        P.op("act", lambda e: e.activation(out=LF[:, :], in_=LF[:, :], func=AF.Ln, bias=OFc[:, 0:1], scale=1.0),
             reads=[("LF",), ("OFc",)], writes=[("LF",)])
        P.op("dve", lambda e: e.tensor_scalar(out=LF[:, :], in0=LF[:, :], scalar1=-1.0, scalar2=None, op0=ALU.mult),
             reads=[("LF",)], writes=[("LF",)])
        P.op("pe", mm(PS[0][0:nblk, 0:128], LF[:, :], OFc[:, :], True, True),
             reads=[("LF",), ("OFc",)], writes=[("ps", "S", 0)])
        P.op("dve", lambda e: e.tensor_copy(out=RB[0:nblk, :], in_=PS[0][0:nblk, 0:128]),
             reads=[("ps", "S", 0)], writes=[("RB",)])
        P.op("pe", mm(PS[1][0:nblk, 0:128], LF[:, :], UI[:, :], True, False),
             reads=[("LF",), ("UI",)], writes=[("ps", "S", 1)], inc=False)
        P.op("pe", mm(PS[1][0:nblk, 0:128], LS[0:nblk, 0:nblk], RB[0:nblk, :], False, True),
             reads=[("RB",), ("LS",)], writes=[("ps", "S", 1)])
        P.op("dve", lambda e: e.tensor_scalar(out=C[0:nblk, :], in0=PS[1][0:nblk, 0:128], scalar1=1.0 / scale,
                                              scalar2=None, op0=ALU.mult),
             reads=[("ps", "S", 1)], writes=[("C",)])
        cur = C
        for i in range(3):
            P.op("dve", lambda e, i=i, cur=cur: e.tensor_copy(out=HI[0:nblk, 3 + i, :], in_=cur[0:nblk, :]),
                 reads=[("C",), ("R1",)], writes=[("HI", 3 + i)])
            P.op("dve", lambda e, i=i: e.tensor_scalar(out=HI[0:nblk, i, :], in0=HI[0:nblk, 3 + i, :],
                                                       scalar1=-1.0, scalar2=None, op0=ALU.mult),
                 reads=[("HI", 3 + i)], writes=[("HI", i)])
            if i < 2:
                P.op("dve", lambda e, i=i, cur=cur: e.tensor_tensor(out=R1[0:nblk, :], in0=cur[0:nblk, :],
                                                                    in1=HI[0:nblk, 3 + i, :], op=ALU.subtract),
                     reads=[("C",), ("R1",), ("HI", 3 + i)], writes=[("R1",)])
                cur = R1
        for i in range(6):
            P.dma("sp", scr[i, :].rearrange("(p j) -> p j", j=128), HI[0:nblk, i, :],
                  reads=[("HI", i)], writes=[("scr", i)])
        P.op("pool", lambda e: e.memset(AK[0:6, :], 1.0), writes=[("BIG", 2)])
        P.op("pool", lambda e: e.memset(AQ[0:6, :], 1.0), writes=[("BIG", 3)])
        P.dma("sp", AK[0:3, :], scr[0:3, :], reads=[("scr", i) for i in range(3)], writes=[("BIG", 2)])
        P.dma("sp", AQ[3:6, :], scr[3:6, :], reads=[("scr", 3 + i) for i in range(3)], writes=[("BIG", 3)])
        pt = 0
        for i in range(seq // 512):
            qb = i % 2
            P.dma("sp", QF[:, qb, :], qf[:, i * 512:(i + 1) * 512], writes=[("QF", qb)])
            nj = 4 * i + 4
            for j in range(nj):
                sbk = j % 2
                diag = j >= 4 * i
                P.op("pe", mm(PS[sbk][:, :], KT[:, j * 128:(j + 1) * 128], QF[:, qb, :], True, False),
                     reads=[("BIG", 0), ("QF", qb)], writes=[("ps", "S", sbk)], inc=False)
                P.op("pe", mm(PS[sbk][:, :], AK[0:6, j * 128:(j + 1) * 128], AQ[0:6, i * 512:(i + 1) * 512],
                              False, not diag),
                     reads=[("BIG", 2), ("BIG", 3)], writes=[("ps", "S", sbk)], inc=not diag)
                if diag:
                    P.op("pe", mm(PS[sbk][:, :], ident[:, :], FM[:, j - 4 * i, :], False, True),
                         reads=[("ident",), ("FM",)], writes=[("ps", "S", sbk)])
                pk = pt % 3
                pt += 1
                P.op("act", lambda e, sbk=sbk, pk=pk: e.activation(out=PTt[:, pk, :], in_=PS[sbk][:, :],
                                                                   func=AF.Exp, scale=scale),
                     reads=[("ps", "S", sbk)], writes=[("PT", pk)])
                P.op("pe", mm(PO[qb][:, :], VK[:, j * 128:(j + 1) * 128], PTt[:, pk, :], j == 0, j == nj - 1),
                     reads=[("BIG", 1), ("PT", pk)], writes=[("ps", "O", qb)], inc=False)
                P.op("pe", mm(PL[qb][:, :], ones[:, :], PTt[:, pk, :], j == 0, j == nj - 1),
                     reads=[("ones",), ("PT", pk)], writes=[("ps", "L", qb)])
            P.op("dve", lambda e, qb=qb: e.reciprocal(out=RL[:, :], in_=PL[qb][:, :]),
                 reads=[("ps", "L", qb)], writes=[("RL",)])
            P.op("dve", lambda e, qb=qb: e.tensor_tensor(out=YO[:, qb, :], in0=RL[:, :], in1=PO[qb][:, :],
                                                         op=ALU.mult),
                 reads=[("RL",), ("ps", "O", qb)], writes=[("YO", qb)])
            P.dma("sp", yf[:, i * 512:(i + 1) * 512], YO[:, qb, :], reads=[("YO", qb)])
        QD, KD = BIG[0], BIG[1]
        P.dma("sp", QD[:, :], qd, writes=[("BIG", 0)])
        P.dma("sp", KD[:, :], kd, writes=[("BIG", 1)])
        un = 0
        for w in range(seq // 2048):
            for bi, (d, nb) in enumerate(DIL):
                nbt = seq // (128 * d)
                qv = QD[:, :].rearrange("p (l r) -> p r l", r=d)
                kv = KD[:, :].rearrange("p (l r) -> p r l", r=d)
                ao = ACC[:, 0, :].rearrange("p (l r) -> p r l", r=d)
                al = ACC[:, 1, :].rearrange("p (l r) -> p r l", r=d)
                for r in range(d):
                    for bl in range(nb):
                        Bg = nb * w + bl
                        hp = Bg > 0
                        k = un % 4
                        sbk = un % 2
                        un += 1
                        row0 = (r * nbt + Bg) * 128
                        if hp:
                            P.dma("sp", VU[:, k, :, :],
                                  vdp[bi][row0 - 128:row0 + 128, :].rearrange("(t p) e -> p t e", p=128),
                                  writes=[("VU", k)])
                        else:
                            P.dma("sp", VU[:, k, 1, :], vdp[bi][row0:row0 + 128, :], writes=[("VU", k)])
                        qs = qv[:, r, Bg * 128:(Bg + 1) * 128]
                        lo = 0 if hp else 128
                        P.op("pe", mm(PS[sbk][:, lo:256], ident[:, :], DM[:, lo:256], True, False),
                             reads=[("ident",), ("DM",)], writes=[("ps", "S", sbk)], inc=False)
                        if hp:
                            P.op("pe", mm(PS[sbk][:, 0:128], kv[:, r, (Bg - 1) * 128:Bg * 128], qs, False, False),
                                 reads=[("BIG", 0), ("BIG", 1)], writes=[("ps", "S", sbk)], inc=False)
                        P.op("pe", mm(PS[sbk][:, 128:256], kv[:, r, Bg * 128:(Bg + 1) * 128], qs, False, True),
                             reads=[("BIG", 0), ("BIG", 1)], writes=[("ps", "S", sbk)])
                        pk = pt % 3
                        pt += 1
                        P.op("act", lambda e, sbk=sbk, pk=pk, lo=lo: e.activation(
                            out=PTt[:, pk, lo:256], in_=PS[sbk][:, lo:256], func=AF.Exp, scale=scale),
                             reads=[("ps", "S", sbk)], writes=[("PT", pk)])
                        if hp:
                            P.op("pe", mm(PO[sbk][:, 0:128], VU[:, k, 0, :], PTt[:, pk, 0:128], True, False),
                                 reads=[("VU", k), ("PT", pk)], writes=[("ps", "O", sbk)], inc=False)
                        P.op("pe", mm(PO[sbk][:, 0:128], VU[:, k, 1, :], PTt[:, pk, 128:256], not hp, True),
                             reads=[("VU", k), ("PT", pk)], writes=[("ps", "O", sbk)], inc=False)
                        if hp:
                            P.op("pe", mm(PL[sbk][:, 0:128], ones[:, :], PTt[:, pk, 0:128], True, False),
                                 reads=[("ones",), ("PT", pk)], writes=[("ps", "L", sbk)], inc=False)
                        P.op("pe", mm(PL[sbk][:, 0:128], ones[:, :], PTt[:, pk, 128:256], not hp, True),
                             reads=[("ones",), ("PT", pk)], writes=[("ps", "L", sbk)])
                        do = ao[:, r, bl * 128:(bl + 1) * 128]
                        dl = al[:, r, bl * 128:(bl + 1) * 128]
                        if bi == 0:
                            P.op("dve", lambda e, do=do, sbk=sbk: e.tensor_copy(out=do, in_=PO[sbk][:, 0:128]),
                                 reads=[("ps", "O", sbk)], writes=[("ACC",)])
                            P.op("dve", lambda e, dl=dl, sbk=sbk: e.tensor_copy(out=dl, in_=PL[sbk][:, 0:128]),
                                 reads=[("ps", "L", sbk)], writes=[("ACC",)])
                        else:
                            P.op("dve", lambda e, do=do, sbk=sbk: e.tensor_tensor(out=do, in0=do, in1=PO[sbk][:, 0:128],
                                                                                  op=ALU.add),
                                 reads=[("ps", "O", sbk), ("ACC",)], writes=[("ACC",)])
                            P.op("dve", lambda e, dl=dl, sbk=sbk: e.tensor_tensor(out=dl, in0=dl, in1=PL[sbk][:, 0:128],
                                                                                  op=ALU.add),
                                 reads=[("ps", "L", sbk), ("ACC",)], writes=[("ACC",)])
            P.op("dve", lambda e: e.reciprocal(out=ACC[:, 1, :], in_=ACC[:, 1, :]),
                 reads=[("ACC",)], writes=[("ACC",)])
            P.op("dve", lambda e: e.tensor_tensor(out=YW[:, :], in0=ACC[:, 0, :], in1=ACC[:, 1, :], op=ALU.mult),
                 reads=[("ACC",)], writes=[("YW",)])
            P.dma("sp", yd[:, w * 2048:(w + 1) * 2048], YW[:, :], reads=[("YW",)])
        P.finish()
        P.emit()
    return nc


def attn_phase(nc, P, es, A, scr, fsc, aug, GYF, GYD):
    seq = SEQ
    nblk = seq // 128
    qf, kf, vfT, qd, kd, vdT = scr
    nbf = A["nbf"]
    c_ident, c_ones, c_fm, c_dm = A["ident"], A["ones_bf"], A["fmask"], A["dmask"]
    c_ui, c_ls, c_of, c_idf = A["uincl"], A["lstrict"], A["ones_f"], A["ident_f"]
    scale = HD ** -0.5
    if True:
        sb = lambda n, s, d: es.enter_context(nc.sbuf_tensor(n + "_p2b", s, d))
        ps = lambda n: es.enter_context(nc.psum_tensor(n + "_p2b", [128, 512], F32))
        BIG = [sb(f"BIG{i}", [128, seq], BF16) for i in range(4)]
        ident = sb("identb", [128, 128], BF16)
        ones = sb("onesb", [128, 128], BF16)
        FM = sb("FM", [128, 4, 512], BF16)
        DM = sb("DM", [128, 256], BF16)
        UI = sb("UI", [128, 128], F32)
        LS = sb("LS", [128, 128], F32)
        OFc = sb("OFc", [128, 128], F32)
        NB = sb("NB", [128, 1], F32)
        LF = sb("LF", [128, nblk], F32)
        LFR = sb("LFR", [128, 128], F32)
        IDF = sb("IDF", [128, 128], F32)
        PVb = [ps("PV0"), ps("PV1")]
        RB = sb("RB", [128, 128], F32)
        C = sb("C", [128, 128], F32)
        R1 = sb("R1", [128, 128], F32)
        HI = sb("HI", [128, 6, 128], BF16)
        QF = sb("QF", [128, 2, 512], BF16)
        PTt = sb("PTt", [128, 3, 512], BF16)
        RL = sb("RL", [128, 512], F32)
        YO = sb("YO", [128, 2, 512], BF16)
        VU = sb("VU", [128, 4, 2, 128], BF16)
        ACC = sb("ACC", [128, 2, 2048], F32)
        YW = sb("YW", [128, 2048], BF16)
        PS = [ps("PS0"), ps("PS1")]
        PO = [ps("PO0"), ps("PO1")]
        PL = [ps("PL0"), ps("PL1")]
        for (t, src, key) in ((ident, c_ident, "ident"), (ones, c_ones, "ones"), (FM, c_fm, "FM"),
                              (DM, c_dm, "DM"), (UI, c_ui, "UI"), (LS, c_ls, "LS"), (OFc, c_of, "OFc"),
                              (NB, nbf, "NB"), (IDF, c_idf, "IDF"),
                              (LFR, fsc.rearrange("o (p j) -> (o p) j", j=128), "LFR")):
            idx = (slice(None),) * len(t.shape)
            P.dma("sp", t[idx], src, writes=[(key,)])
        P.op("dve", lambda e: e.tensor_scalar(out=NB[:, :], in0=NB[:, :], scalar1=-1.0, scalar2=None, op0=ALU.mult),
             reads=[("NB",)], writes=[("NB",)])
        KT, VK, AK, AQ = BIG[0], BIG[1], BIG[2], BIG[3]
        P.dma("sp", KT[:, :], kf, writes=[("BIG", 0)])
        P.dma("sp", AQ[:, :], vfT, writes=[("BIG", 3)])
        for bq in range(seq // 512):
            pv = bq % 2
            for j in range(4):
                blk = bq * 4 + j
                P.op("pe", mm(PVb[pv][:, j * 128:(j + 1) * 128], AQ[:, blk * 128:(blk + 1) * 128], ident[:, :],
                              True, True),
                     reads=[("BIG", 3), ("ident",)], writes=[("ps", "V", pv)], inc=(j == 3))
            P.op("act", lambda e, pv=pv, bq=bq: e.activation(out=VK[:, bq * 512:(bq + 1) * 512], in_=PVb[pv][:, :],
                                                            func=AF.Copy),
                 reads=[("ps", "V", pv)], writes=[("BIG", 1)])
        P.op("act", lambda e: e.activation(out=LFR[:, :], in_=LFR[:, :], func=AF.Exp, bias=NB[:, :], scale=-1.0),
             reads=[("LFR",), ("NB",)], writes=[("LFR",)])
        P.op("act", lambda e: e.activation(out=LFR[:, :], in_=LFR[:, :], func=AF.Ln, bias=OFc[:, 0:1], scale=1.0),
             reads=[("LFR",), ("OFc",)], writes=[("LFR",)])
        P.op("dve", lambda e: e.tensor_scalar(out=LFR[:, :], in0=LFR[:, :], scalar1=-1.0, scalar2=None, op0=ALU.mult),
             reads=[("LFR",)], writes=[("LFR",)])
        P.op("pe", mm(PS[0][:, 0:128], LFR[:, :], IDF[:, :], True, True),
             reads=[("LFR",), ("IDF",)], writes=[("ps", "S", 0)])
        P.op("dve", lambda e: e.tensor_copy(out=LF[:, :], in_=PS[0][:, 0:128]),
             reads=[("ps", "S", 0)], writes=[("LF",)])
        P.op("pe", mm(PS[0][0:nblk, 0:128], LF[:, :], OFc[:, :], True, True),
             reads=[("LF",), ("OFc",)], writes=[("ps", "S", 0)])
        P.op("dve", lambda e: e.tensor_copy(out=RB[0:nblk, :], in_=PS[0][0:nblk, 0:128]),
             reads=[("ps", "S", 0)], writes=[("RB",)])
        P.op("pe", mm(PS[1][0:nblk, 0:128], LF[:, :], UI[:, :], True, False),
             reads=[("LF",), ("UI",)], writes=[("ps", "S", 1)], inc=False)
        P.op("pe", mm(PS[1][0:nblk, 0:128], LS[0:nblk, 0:nblk], RB[0:nblk, :], False, True),
             reads=[("RB",), ("LS",)], writes=[("ps", "S", 1)])
        P.op("dve", lambda e: e.tensor_scalar(out=C[0:nblk, :], in0=PS[1][0:nblk, 0:128], scalar1=1.0 / scale,
                                              scalar2=None, op0=ALU.mult),
             reads=[("ps", "S", 1)], writes=[("C",)])
        cur = C
        for i in range(3):
            P.op("dve", lambda e, i=i, cur=cur: e.tensor_copy(out=HI[0:nblk, 3 + i, :], in_=cur[0:nblk, :]),
                 reads=[("C",), ("R1",)], writes=[("HI", 3 + i)])
            P.op("dve", lambda e, i=i: e.tensor_scalar(out=HI[0:nblk, i, :], in0=HI[0:nblk, 3 + i, :],
                                                       scalar1=-1.0, scalar2=None, op0=ALU.mult),
                 reads=[("HI", 3 + i)], writes=[("HI", i)])
            if i < 2:
                P.op("dve", lambda e, i=i, cur=cur: e.tensor_tensor(out=R1[0:nblk, :], in0=cur[0:nblk, :],
                                                                    in1=HI[0:nblk, 3 + i, :], op=ALU.subtract),
                     reads=[("C",), ("R1",), ("HI", 3 + i)], writes=[("R1",)])
                cur = R1
        for i in range(6):
            P.dma("sp", aug[i, :].rearrange("(p j) -> p j", j=128), HI[0:nblk, i, :],
                  reads=[("HI", i)], writes=[("scr", i)])
        P.op("pool", lambda e: e.memset(AK[0:6, :], 1.0), writes=[("BIG", 2)])
        P.op("pool", lambda e: e.memset(AQ[0:6, :], 1.0), writes=[("BIG", 3)])
        P.dma("sp", AK[0:3, :], aug[0:3, :], reads=[("scr", i) for i in range(3)], writes=[("BIG", 2)])
        P.dma("sp", AQ[3:6, :], aug[3:6, :], reads=[("scr", 3 + i) for i in range(3)], writes=[("BIG", 3)])
        pairs = [(i, j) for i in range(seq // 512) for j in range(4 * i + 4)]
        pt = 0

        def fox_scores(n):
            i, j = pairs[n]
            qb, sbk = i % 2, n % 2
            if j == 0:
                P.dma("sp", QF[:, qb, :], qf[:, i * 512:(i + 1) * 512], writes=[("QF", qb)])
            diag = j >= 4 * i
            P.op("pe", mm(PS[sbk][:, :], KT[:, j * 128:(j + 1) * 128], QF[:, qb, :], True, False),
                 reads=[("BIG", 0), ("QF", qb)], writes=[("ps", "S", sbk)], inc=False)
            P.op("pe", mm(PS[sbk][:, :], AK[0:6, j * 128:(j + 1) * 128], AQ[0:6, i * 512:(i + 1) * 512],
                          False, not diag),
                 reads=[("BIG", 2), ("BIG", 3)], writes=[("ps", "S", sbk)], inc=not diag)
            if diag:
                P.op("pe", mm(PS[sbk][:, :], ident[:, :], FM[:, j - 4 * i, :], False, True),
                     reads=[("ident",), ("FM",)], writes=[("ps", "S", sbk)])

        fox_scores(0)
        for n, (i, j) in enumerate(pairs):
            qb, sbk = i % 2, n % 2
            nj = 4 * i + 4
            if n + 1 < len(pairs):
                fox_scores(n + 1)
            pk = pt % 3
            pt += 1
            P.op("act", lambda e, sbk=sbk, pk=pk: e.activation(out=PTt[:, pk, :], in_=PS[sbk][:, :],
                                                               func=AF.Exp, scale=scale),
                 reads=[("ps", "S", sbk)], writes=[("PT", pk)])
            P.op("pe", mm(PO[qb][:, :], VK[:, j * 128:(j + 1) * 128], PTt[:, pk, :], j == 0, j == nj - 1),
                 reads=[("BIG", 1), ("PT", pk)], writes=[("ps", "O", qb)], inc=False)
            P.op("pe", mm(PL[qb][:, :], ones[:, :], PTt[:, pk, :], j == 0, j == nj - 1),
                 reads=[("ones",), ("PT", pk)], writes=[("ps", "L", qb)])
            if j == nj - 1:
                P.op("dve", lambda e, qb=qb: e.reciprocal(out=RL[:, :], in_=PL[qb][:, :]),
                     reads=[("ps", "L", qb)], writes=[("RL",)])
                P.op("dve", lambda e, qb=qb: e.tensor_tensor(out=YO[:, qb, :], in0=RL[:, :], in1=PO[qb][:, :],
                                                             op=ALU.mult),
                     reads=[("RL",), ("ps", "O", qb)], writes=[("YO", qb)])
                wdx = i // 4
                P.dma("sp", GYF.src[wdx][:, (i % 4) * 512:(i % 4 + 1) * 512], YO[:, qb, :], reads=[("YO", qb)],
                      writes=[("GYF", "s", wdx, i % 4)])
                if i % 4 == 3:
                    P.collective(G4, GYF.src[wdx], GYF.a[wdx], reads=[("GYF", "s", wdx, k4) for k4 in range(4)],
                                 writes=[("GYF", "a", wdx)])
                    if wdx > 0:
                        GYF.s2(wdx - 1)
        GYF.s2(seq // 2048 - 1)
        QD, KD, VD = BIG[0], BIG[1], BIG[2]
        P.dma("sp", QD[:, :], qd, writes=[("BIG", 0)])
        P.dma("sp", KD[:, :], kd, writes=[("BIG", 1)])
        P.dma("sp", VD[:, :], vdT, writes=[("BIG", 2)])
        units = []
        for w in range(seq // 2048):
            for bi, (d, nb) in enumerate(DIL):
                for r in range(d):
                    for bl in range(nb):
                        units.append((w, bi, d, nb, r, bl))

        def dil_front(un):
            w, bi, d, nb, r, bl = units[un]
            qv = QD[:, :].rearrange("p (l r) -> p r l", r=d)
            kv = KD[:, :].rearrange("p (l r) -> p r l", r=d)
            vv = VD[:, :].rearrange("p (l r) -> p r l", r=d)
            Bg = nb * w + bl
            hp = Bg > 0
            k, sbk = un % 4, un % 2
            if hp:
                P.op("pe", mm(PVb[sbk][:, 0:128], vv[:, r, (Bg - 1) * 128:Bg * 128], ident[:, :], True, True),
                     reads=[("BIG", 2), ("ident",)], writes=[("ps", "V", sbk)], inc=False)
            P.op("pe", mm(PVb[sbk][:, 128:256], vv[:, r, Bg * 128:(Bg + 1) * 128], ident[:, :], True, True),
                 reads=[("BIG", 2), ("ident",)], writes=[("ps", "V", sbk)])
            vlo = 0 if hp else 1
            P.op("act", lambda e, k=k, sbk=sbk, vlo=vlo: e.activation(
                out=VU[:, k, vlo:2, :], in_=PVb[sbk][:, vlo * 128:256].rearrange("p (t e) -> p t e", e=128),
                func=AF.Copy),
                 reads=[("ps", "V", sbk)], writes=[("VU", k)])
            qs = qv[:, r, Bg * 128:(Bg + 1) * 128]
            lo = 0 if hp else 128
            P.op("pe", mm(PS[sbk][:, lo:256], ident[:, :], DM[:, lo:256], True, False),
                 reads=[("ident",), ("DM",)], writes=[("ps", "S", sbk)], inc=False)
            if hp:
                P.op("pe", mm(PS[sbk][:, 0:128], kv[:, r, (Bg - 1) * 128:Bg * 128], qs, False, False),
                     reads=[("BIG", 0), ("BIG", 1)], writes=[("ps", "S", sbk)], inc=False)
            P.op("pe", mm(PS[sbk][:, 128:256], kv[:, r, Bg * 128:(Bg + 1) * 128], qs, False, True),
                 reads=[("BIG", 0), ("BIG", 1)], writes=[("ps", "S", sbk)])

        dil_front(0)
        for un, (w, bi, d, nb, r, bl) in enumerate(units):
            if un + 1 < len(units):
                dil_front(un + 1)
            Bg = nb * w + bl
            hp = Bg > 0
            k, sbk = un % 4, un % 2
            lo = 0 if hp else 128
            ao = ACC[:, 0, :].rearrange("p (l r) -> p r l", r=d)
            al = ACC[:, 1, :].rearrange("p (l r) -> p r l", r=d)
            pk = pt % 3
            pt += 1
            P.op("act", lambda e, sbk=sbk, pk=pk, lo=lo: e.activation(
                out=PTt[:, pk, lo:256], in_=PS[sbk][:, lo:256], func=AF.Exp, scale=scale),
                 reads=[("ps", "S", sbk)], writes=[("PT", pk)])
            if hp:
                P.op("pe", mm(PO[sbk][:, 0:128], VU[:, k, 0, :], PTt[:, pk, 0:128], True, False),
                     reads=[("VU", k), ("PT", pk)], writes=[("ps", "O", sbk)], inc=False)
            P.op("pe", mm(PO[sbk][:, 0:128], VU[:, k, 1, :], PTt[:, pk, 128:256], not hp, True),
                 reads=[("VU", k), ("PT", pk)], writes=[("ps", "O", sbk)], inc=False)
            if hp:
                P.op("pe", mm(PL[sbk][:, 0:128], ones[:, :], PTt[:, pk, 0:128], True, False),
                     reads=[("ones",), ("PT", pk)], writes=[("ps", "L", sbk)], inc=False)
            P.op("pe", mm(PL[sbk][:, 0:128], ones[:, :], PTt[:, pk, 128:256], not hp, True),
                 reads=[("ones",), ("PT", pk)], writes=[("ps", "L", sbk)])
            do = ao[:, r, bl * 128:(bl + 1) * 128]
            dl = al[:, r, bl * 128:(bl + 1) * 128]
            if bi == 0:
                P.op("dve", lambda e, do=do, sbk=sbk: e.tensor_copy(out=do, in_=PO[sbk][:, 0:128]),
                     reads=[("ps", "O", sbk)], writes=[("ACC",)])
                P.op("dve", lambda e, dl=dl, sbk=sbk: e.tensor_copy(out=dl, in_=PL[sbk][:, 0:128]),
                     reads=[("ps", "L", sbk)], writes=[("ACC",)])
            else:
                P.op("dve", lambda e, do=do, sbk=sbk: e.tensor_tensor(out=do, in0=do, in1=PO[sbk][:, 0:128],
                                                                      op=ALU.add),
                     reads=[("ps", "O", sbk), ("ACC",)], writes=[("ACC",)])
                P.op("dve", lambda e, dl=dl, sbk=sbk: e.tensor_tensor(out=dl, in0=dl, in1=PL[sbk][:, 0:128],
                                                                      op=ALU.add),
                     reads=[("ps", "L", sbk), ("ACC",)], writes=[("ACC",)])
            last_in_window = (un + 1 == len(units)) or (units[un + 1][0] != w)
            if last_in_window:
                P.op("dve", lambda e: e.reciprocal(out=ACC[:, 1, :], in_=ACC[:, 1, :]),
                     reads=[("ACC",)], writes=[("ACC",)])
                P.op("dve", lambda e: e.tensor_tensor(out=YW[:, :], in0=ACC[:, 0, :], in1=ACC[:, 1, :], op=ALU.mult),
                     reads=[("ACC",)], writes=[("YW",)])
                P.dma("sp", GYD.src[w], YW[:, :], reads=[("YW",)], writes=[("GYD", "s", w)])
                GYD.s1(w)
                if w > 0:
                    GYD.s2(w - 1)
        GYD.s2(seq // 2048 - 1)


def dil_perm(seq, d):
    nbt = seq // (128 * d)
    r = np.arange(d)[:, None, None]
    B = np.arange(nbt)[None, :, None]
    p = np.arange(128)[None, None, :]
    return (r + d * (128 * B + p)).reshape(-1)


NG = 8
TWO_PI = 2.0 * math.pi


def l4_consts():
    p = np.arange(128)[:, None]
    t = np.arange(128)[None, :]
    m0 = (np.arange(128) < 64).astype(np.float32)[:, None]
    mask4 = np.broadcast_to((t >= p).astype(np.float32)[:, None, :], (128, 4, 128)).copy()
    ti = np.broadcast_to(np.arange(130, dtype=np.float32)[None, :], (128, 130)).copy()
    tir = np.broadcast_to((127.0 - np.arange(128, dtype=np.float32))[None, :], (128, 128)).copy()
    sg = np.concatenate([m0, 1.0 - m0, -m0, -(1.0 - m0), np.full_like(m0, 0.5 * math.pi)], 1)
    return {"mask4": mask4, "ti": ti, "tir": tir, "sg": sg, "ident": _ident_np()}


def build_l4(seq=SEQ):
    nc = _new_nc()
    dt = lambda n, s, d, k="ExternalInput": nc.dram_tensor(n, s, d, kind=k).ap()
    nch = seq // 128
    uS = dt("uS", [128, nch, 128], F32)
    uG = dt("uG", [NG, 128, nch, 16], F32)
    lr_in = dt("lam_re2", [128, NG], F32)
    li_in = dt("lam_im2", [128, NG], F32)
    ldt_in = dt("logdt2", [128, NG], F32)
    br_in = dt("br2", [128, NG, 16], F32)
    bi_in = dt("bi2", [128, NG, 16], F32)
    cr_in = dt("cr2", [128, NG, 16], F32)
    ci_in = dt("ci2", [128, NG, 16], F32)
    d_in = dt("d2", [128, 128], F32)
    c_mask = dt("mask4", [128, 4, 128], F32)
    c_ti = dt("ti", [128, 130], F32)
    c_tir = dt("tir", [128, 128], F32)
    c_sg = dt("sg", [128, 5], F32)
    c_ident = dt("ident", [128, 128], BF16)
    zout = dt("zout", [NG, 128, nch, 16], BF16, "ExternalOutput")
    with ExitStack() as es:
        P = Prog(nc, es)
        sb = lambda n, s, d: es.enter_context(nc.sbuf_tensor(n, s, d))
        ps = lambda n: es.enter_context(nc.psum_tensor(n, [128, 512], F32))
        U16 = sb("U16", [128, nch, 128], BF16)
        UF = sb("UF", [128, 1, nch, 16], F32)
        TS = sb("TS", [128, 16, 16, 128], BF16)
        CF = sb("CF", [128, 1, 16, 129], BF16)
        BFc = sb("BFc", [128, 16, 128], BF16)
        WF = sb("WF", [128, 16, 128], BF16)
        M1 = sb("M1", [128, 16, 128], BF16)
        YS = sb("YS", [128, 1, nch, 16], F32)
        ZS = sb("ZS", [128, 1, nch, 16], BF16)
        G1 = sb("G1", [128, nch * 16], F32)
        VST = sb("VST", [128, NG, nch], F32)
        VI = sb("VI", [64, NG, nch], F32)
        SC = [[sb(f"SC{a}{b}", [64, NG, nch], F32) for b in range(2)] for a in range(2)]
        XP = sb("XP", [128, NG, nch], BF16)
        XPI = sb("XPI", [64, NG, nch], BF16)
        LR = sb("LR", [128, NG], F32)
        LI = sb("LI", [128, NG], F32)
        DTt = sb("DTt", [128, NG], F32)
        LDR = sb("LDR", [128, NG], F32)
        LDI = sb("LDI", [128, NG], F32)
        NLDR = sb("NLDR", [128, NG], F32)
        SM = sb("SM", [128, 12, NG], F32)
        BR = sb("BR", [128, NG, 16], F32)
        BI = sb("BI", [128, NG, 16], F32)
        CR = sb("CR", [128, NG, 16], F32)
        CI = sb("CI", [128, NG, 16], F32)
        BBR = sb("BBR", [128, NG, 16], F32)
        BBI = sb("BBI", [128, NG, 16], F32)
        CA = sb("CA", [128, NG, 16], F32)
        CB = sb("CB", [128, NG, 16], F32)
        BA = sb("BA", [128, NG, 16], F32)
        BB = sb("BB", [128, NG, 16], F32)
        D2 = sb("D2", [128, 128], F32)
        MASK = sb("MASK", [128, 4, 128], F32)
        TI = sb("TI", [128, 130], F32)
        TIR = sb("TIR", [128, 128], F32)
        SGN = sb("SGN", [128, 5], F32)
        ident = sb("identb", [128, 128], BF16)
        TB = sb("TB", [128, 6, 130], F32)
        AW = sb("AW", [128, 2, NG], F32)
        WW = sb("WW", [64, 2, 3, NG], F32)
        PA = [ps(f"PA{i}") for i in range(4)]
        PYb = [ps("PYa"), ps("PYb")]
        PV = ps("PV")
        loads = ((LR, lr_in, "LR"), (LI, li_in, "LI"), (DTt, ldt_in, "DT"), (BR, br_in, "BR"), (BI, bi_in, "BI"),
                 (CR, cr_in, "CR"), (CI, ci_in, "CI"), (D2, d_in, "D2"), (MASK, c_mask, "MASK"), (TI, c_ti, "TI"),
                 (TIR, c_tir, "TIR"), (SGN, c_sg, "SGN"), (ident, c_ident, "ident"))
        for (t, src, key) in loads:
            idx = (slice(None),) * len(t.shape)
            P.dma("sp", t[idx], src, writes=[(key,)])
        P.dma("pool", U16[:, :, :], uS, writes=[("U16",)])
        m0, m1, nm0, nm1 = SGN[:, 0:1], SGN[:, 1:2], SGN[:, 2:3], SGN[:, 3:4]

        def dve(fn, reads, writes):
            P.op("dve", fn, reads=reads, writes=writes)

        def tt(out, a, b, op, reads, writes, eng="dve"):
            P.op(eng, lambda e: e.tensor_tensor(out=out, in0=a, in1=b, op=op), reads=reads, writes=writes)

        def tsc(out, a, s1, op0, reads, writes, s2=None, op1=None, eng="dve"):
            if op1 is None:
                P.op(eng, lambda e: e.tensor_scalar(out=out, in0=a, scalar1=s1, scalar2=None, op0=op0),
                     reads=reads, writes=writes)
            else:
                P.op(eng, lambda e: e.tensor_scalar(out=out, in0=a, scalar1=s1, scalar2=s2, op0=op0, op1=op1),
                     reads=reads, writes=writes)

        def stt(out, a, s, b, op0, op1, reads, writes, eng="dve"):
            P.op(eng, lambda e: e.scalar_tensor_tensor(out=out, in0=a, scalar=s, in1=b, op0=op0, op1=op1),
                 reads=reads, writes=writes)

        def act(out, a, func, reads, writes, bias=None, scale=None):
            kw = {}
            if bias is not None:
                kw["bias"] = bias
            if scale is not None:
                kw["scale"] = scale
            P.op("act", lambda e: e.activation(out=out, in_=a, func=func, **kw), reads=reads, writes=writes)

        def cp(out, a, reads, writes, eng="dve"):
            P.op(eng, lambda e: e.tensor_copy(out=out, in_=a), reads=reads, writes=writes)

        RI = sb("RI", [128, 130], mybir.dt.int32)
        RF = sb("RF", [128, 2, 130], F32)

        def reduce_angle(arg, shift, out, rk, n):
            t, tf = RF[:, 0, 0:n], RF[:, 1, 0:n]
            ti = RI[:, 0:n]
            tsc(t, arg, 1.0 / TWO_PI, ALU.mult, rk, [("RF", 0)], s2=0.5 + shift, op1=ALU.add)
            cp(ti, t, [("RF", 0)], [("RI",)])
            cp(tf, ti, [("RI",)], [("RF", 1)])
            tt(t, t, tf, ALU.subtract, [("RF", 0), ("RF", 1)], [("RF", 0)])
            tsc(t, t, -0.5, ALU.add, [("RF", 0)], [("RF", 0)], s2=TWO_PI, op1=ALU.mult)
            tsc(tf, t, -math.pi, ALU.is_lt, [("RF", 0)], [("RF", 1)])
            stt(t, tf, TWO_PI, t, ALU.mult, ALU.add, [("RF", 0), ("RF", 1)], [("RF", 0)])
            tsc(tf, t, math.pi, ALU.is_gt, [("RF", 0)], [("RF", 1)])
            stt(out, tf, -TWO_PI, t, ALU.mult, ALU.add, [("RF", 0), ("RF", 1)], [("RF", 0)])

        def sincos(arg, sin_out, cos_out, tmp, rk, wk_s, wk_c, tk):
            n = arg.shape[-1]
            reduce_angle(arg, 0.0, RF[:, 0, 0:n], rk, n)
            act(sin_out, RF[:, 0, 0:n], AF.Sin, [("RF", 0)], wk_s)
            tsc(RF[:, 1, 0:n], RF[:, 0, 0:n], -1.0, ALU.mult, [("RF", 0)], [("RF", 1)])
            tt(RF[:, 1, 0:n], RF[:, 0, 0:n], RF[:, 1, 0:n], ALU.max, [("RF", 0), ("RF", 1)], [("RF", 1)])
            act(cos_out, RF[:, 1, 0:n], AF.Sin, [("RF", 1), ("SGN",)], wk_c, bias=SGN[:, 4:5], scale=-1.0)

        S = lambda i: SM[:, i, :]
        act(DTt[:, :], DTt[:, :], AF.Exp, [("DT",)], [("DT",)])
        tt(LDR[:, :], LR[:, :], DTt[:, :], ALU.mult, [("LR",), ("DT",)], [("LDR",)])
        tt(LDI[:, :], LI[:, :], DTt[:, :], ALU.mult, [("LI",), ("DT",)], [("LDI",)])
        tsc(NLDR[:, :], LDR[:, :], -1.0, ALU.mult, [("LDR",)], [("NLDR",)])
        tsc(S(5), LDI[:, :], TWO_PI, ALU.add, [("LDI",)], [("S", 5)])
        sincos(S(5), S(0), S(1), S(6), [("S", 5)], [("S", 0)], [("S", 1)], ("S", 6))
        act(S(2), LDR[:, :], AF.Exp, [("LDR",)], [("S", 2)])
        tt(S(3), S(2), S(1), ALU.mult, [("S", 2), ("S", 1)], [("S", 3)])
        tsc(S(3), S(3), -1.0, ALU.add, [("S", 3)], [("S", 3)])
        tt(S(4), S(2), S(0), ALU.mult, [("S", 2), ("S", 0)], [("S", 4)])
        tt(S(5), LR[:, :], LR[:, :], ALU.mult, [("LR",)], [("S", 5)])
        tt(S(6), LI[:, :], LI[:, :], ALU.mult, [("LI",)], [("S", 6)])
        tt(S(5), S(5), S(6), ALU.add, [("S", 5), ("S", 6)], [("S", 5)])
        dve(lambda e: e.reciprocal(out=S(5), in_=S(5)), [("S", 5)], [("S", 5)])
        tt(S(7), S(3), LR[:, :], ALU.mult, [("S", 3), ("LR",)], [("S", 7)])
        tt(S(6), S(4), LI[:, :], ALU.mult, [("S", 4), ("LI",)], [("S", 6)])
        tt(S(7), S(7), S(6), ALU.add, [("S", 7), ("S", 6)], [("S", 7)])
        tt(S(7), S(7), S(5), ALU.mult, [("S", 7), ("S", 5)], [("S", 7)])
        tt(S(8), S(4), LR[:, :], ALU.mult, [("S", 4), ("LR",)], [("S", 8)])
        tt(S(6), S(3), LI[:, :], ALU.mult, [("S", 3), ("LI",)], [("S", 6)])
        tt(S(8), S(8), S(6), ALU.subtract, [("S", 8), ("S", 6)], [("S", 8)])
        tt(S(8), S(8), S(5), ALU.mult, [("S", 8), ("S", 5)], [("S", 8)])
        tsc(S(9), S(8), -1.0, ALU.mult, [("S", 8)], [("S", 9)])
        for g in range(NG):
            tsc(BBR[:, g, :], BR[:, g, :], SM[:, 7, g:g + 1], ALU.mult, [("BR",), ("S", 7)], [("BBR", g)])
            stt(BBR[:, g, :], BI[:, g, :], SM[:, 9, g:g + 1], BBR[:, g, :], ALU.mult, ALU.add,
                [("BI",), ("S", 9), ("BBR", g)], [("BBR", g)])
            tsc(BBI[:, g, :], BI[:, g, :], SM[:, 7, g:g + 1], ALU.mult, [("BI",), ("S", 7)], [("BBI", g)])
            stt(BBI[:, g, :], BR[:, g, :], SM[:, 8, g:g + 1], BBI[:, g, :], ALU.mult, ALU.add,
                [("BR",), ("S", 8), ("BBI", g)], [("BBI", g)])
        tsc(CA[:, :, :], CR[:, :, :], m0, ALU.mult, [("CR",), ("SGN",)], [("CA",)])
        stt(CA[:, :, :], CI[:, :, :], nm1, CA[:, :, :], ALU.mult, ALU.add, [("CI",), ("SGN",), ("CA",)], [("CA",)])
        tsc(CB[:, :, :], CI[:, :, :], nm0, ALU.mult, [("CI",), ("SGN",)], [("CB",)])
        stt(CB[:, :, :], CR[:, :, :], nm1, CB[:, :, :], ALU.mult, ALU.add, [("CR",), ("SGN",), ("CB",)], [("CB",)])
        bbk = [("BBR", g) for g in range(NG)] + [("BBI", g) for g in range(NG)]
        tsc(BA[:, :, :], BBR[:, :, :], m0, ALU.mult, bbk + [("SGN",)], [("BA",)])
        stt(BA[:, :, :], BBI[:, :, :], m1, BA[:, :, :], ALU.mult, ALU.add, bbk + [("SGN",), ("BA",)], [("BA",)])
        tsc(BB[:, :, :], BBI[:, :, :], nm0, ALU.mult, bbk + [("SGN",)], [("BB",)])
        stt(BB[:, :, :], BBR[:, :, :], m1, BB[:, :, :], ALU.mult, ALU.add, bbk + [("SGN",), ("BB",)], [("BB",)])

        def table(g, tvec, n, neg):
            tk = [("TB", i) for i in range(6)]
            tsc(TB[:, 0, 0:n], tvec, LDI[:, g:g + 1], ALU.mult, [("LDI",), ("TI",), ("TIR",)], [tk[0]])
            tsc(TB[:, 0, 0:n], TB[:, 0, 0:n], TWO_PI, ALU.add, [tk[0]], [tk[0]])
            sincos(TB[:, 0, 0:n], TB[:, 1, 0:n], TB[:, 2, 0:n], TB[:, 3, 0:n], [tk[0]], [tk[1]], [tk[2]], tk[3])
            act(TB[:, 3, 0:n], tvec, AF.Exp, [("TI",), ("TIR",), ("LDR",), ("NLDR",), tk[3]], [tk[3]],
                scale=(NLDR if neg else LDR)[:, g:g + 1])
            tt(TB[:, 4, 0:n], TB[:, 3, 0:n], TB[:, 2, 0:n], ALU.mult, [tk[3], tk[2]], [tk[4]])
            if neg:
                stt(TB[:, 5, 0:n], TB[:, 3, 0:n], -1.0, TB[:, 1, 0:n], ALU.mult, ALU.mult, [tk[3], tk[1]], [tk[5]])
            else:
                tt(TB[:, 5, 0:n], TB[:, 3, 0:n], TB[:, 1, 0:n], ALU.mult, [tk[3], tk[1]], [tk[5]])

        pa = 0
        for g in range(NG):
            table(g, TI[:, 128:129], 1, False)
            tsc(AW[:, 0, g:g + 1], TB[:, 4, 0:1], 1.0, ALU.mult, [("TB", 4)], [("AW",)])
            tsc(AW[:, 1, g:g + 1], TB[:, 5, 0:1], 1.0, ALU.mult, [("TB", 5)], [("AW",)])
            table(g, TIR[:, :], 128, False)
            for hp in range(16):
                tsc(G1[:, 0:128], TB[:, 4, 0:128], BA[:, g, hp:hp + 1], ALU.mult, [("TB", 4), ("BA",)], [("G1",)])
                stt(WF[:, hp, :], TB[:, 5, 0:128], BB[:, g, hp:hp + 1], G1[:, 0:128], ALU.mult, ALU.add,
                    [("TB", 5), ("BB",), ("G1",)], [("WF",)])
            for q in range(4):
                bk = pa % 4
                pa += 1
                for j in range(4):
                    P.op("pe", mm(PA[bk][:, j * 128:(j + 1) * 128], WF[:, q * 4 + j, :], ident[:, :], True, True),
                         reads=[("WF",), ("ident",)], writes=[("ps", "A", bk)], inc=(j == 3))
                act(M1[:, q * 4:(q + 1) * 4, :], PA[bk][:, :].rearrange("p (j t) -> p j t", j=4), AF.Copy,
                    [("ps", "A", bk)], [("M1",)])
            for hp in range(16):
                P.op("pe", mm(PV[:, 0:nch], M1[:, hp, :], U16[:, :, g * 16 + hp], hp == 0, hp == 15),
                     reads=[("M1",), ("U16",)], writes=[("ps", "V")], inc=(hp == 15))
            act(VST[:, g, :], PV[:, 0:nch], AF.Copy, [("ps", "V")], [("VST",)])
        P.dma("sp", VI[:, :, :], VST[64:128, :, :], reads=[("VST",)], writes=[("VI",)])
        cp(SC[0][0][:, :, :], VST[0:64, :, :], [("VST",)], [("SC", 0, 0)])
        cp(SC[0][1][:, :, :], VI[:, :, :], [("VI",)], [("SC", 0, 1)], eng="pool")
        tsc(WW[:, 0, 0, :], AW[0:64, 0, :], 1.0, ALU.mult, [("AW",)], [("WW", 0)])
        tsc(WW[:, 0, 1, :], AW[0:64, 1, :], 1.0, ALU.mult, [("AW",)], [("WW", 0)])
        tsc(WW[:, 0, 2, :], AW[0:64, 1, :], -1.0, ALU.mult, [("AW",)], [("WW", 0)])
        cur = 0
        sh = 1
        while sh < nch:
            nx = 1 - cur
            re, im = SC[cur]
            nre, nim = SC[nx]
            for g in range(NG):
                wr, wi, nwi = WW[:, cur, 0, g:g + 1], WW[:, cur, 1, g:g + 1], WW[:, cur, 2, g:g + 1]
                rk = [("SC", cur, 0), ("SC", cur, 1), ("WW", cur)]
                cp(nre[:, g, 0:sh], re[:, g, 0:sh], rk, [("SC", nx, 0)])
                stt(nre[:, g, sh:nch], re[:, g, 0:nch - sh], wr, re[:, g, sh:nch], ALU.mult, ALU.add, rk, [("SC", nx, 0)])
                stt(nre[:, g, sh:nch], im[:, g, 0:nch - sh], nwi, nre[:, g, sh:nch], ALU.mult, ALU.add,
                    rk + [("SC", nx, 0)], [("SC", nx, 0)])
                cp(nim[:, g, 0:sh], im[:, g, 0:sh], rk, [("SC", nx, 1)], eng="pool")
                stt(nim[:, g, sh:nch], im[:, g, 0:nch - sh], wr, im[:, g, sh:nch], ALU.mult, ALU.add, rk,
                    [("SC", nx, 1)])
                stt(nim[:, g, sh:nch], re[:, g, 0:nch - sh], wi, nim[:, g, sh:nch], ALU.mult, ALU.add,
                    rk + [("SC", nx, 1)], [("SC", nx, 1)])
            wk = [("WW", cur)]
            tt(WW[:, nx, 0, :], WW[:, cur, 0, :], WW[:, cur, 0, :], ALU.mult, wk, [("WW", nx)])
            tt(WW[:, nx, 2, :], WW[:, cur, 1, :], WW[:, cur, 1, :], ALU.mult, wk, [("WW", nx)])
            tt(WW[:, nx, 0, :], WW[:, nx, 0, :], WW[:, nx, 2, :], ALU.subtract, [("WW", nx)], [("WW", nx)])
            tt(WW[:, nx, 1, :], WW[:, cur, 0, :], WW[:, cur, 1, :], ALU.mult, wk, [("WW", nx)])
            tsc(WW[:, nx, 1, :], WW[:, nx, 1, :], 2.0, ALU.mult, [("WW", nx)], [("WW", nx)])
            tsc(WW[:, nx, 2, :], WW[:, nx, 1, :], -1.0, ALU.mult, [("WW", nx)], [("WW", nx)])
            cur = nx
            sh *= 2
        re, im = SC[cur]
        P.op("pool", lambda e: e.memset(XP[:, :, 0:1], 0.0), writes=[("XP",)])
        P.op("pool", lambda e: e.memset(XPI[:, :, 0:1], 0.0), writes=[("XPI",)])
        if nch > 1:
            P.op("dve", lambda e: e.tensor_copy(out=XP[0:64, :, 1:nch], in_=re[:, :, 0:nch - 1]),
                 reads=[("SC", cur, 0)], writes=[("XP",)])
            P.op("dve", lambda e: e.tensor_copy(out=XPI[:, :, 1:nch], in_=im[:, :, 0:nch - 1]),
                 reads=[("SC", cur, 1)], writes=[("XPI",)])
        P.dma("sp", XP[64:128, :, :], XPI[:, :, :], reads=[("XPI",)], writes=[("XP",)])
        gsc = 2.0 * math.sqrt(2.0 / math.pi)
        py = 0
        for g in range(NG):
            ub = 0
            P.dma("sp", UF[:, ub, :, :], uG[g], writes=[("UF", ub)])
            table(g, TI[:, 0:129], 129, False)
            for h in range(16):
                tsc(G1[:, 0:129], TB[:, 4, 0:129], CA[:, g, h:h + 1], ALU.mult, [("TB", 4), ("CA",)], [("G1",)])
                stt(CF[:, 0, h, :], TB[:, 5, 0:129], CB[:, g, h:h + 1], G1[:, 0:129], ALU.mult, ALU.add,
                    [("TB", 5), ("CB",), ("G1",)], [("CF", 0)])
            table(g, TI[:, 0:128], 128, True)
            for hp in range(16):
                tsc(G1[:, 0:128], TB[:, 4, 0:128], BA[:, g, hp:hp + 1], ALU.mult, [("TB", 4), ("BA",)], [("G1",)])
                stt(BFc[:, hp, :], TB[:, 5, 0:128], BB[:, g, hp:hp + 1], G1[:, 0:128], ALU.mult, ALU.add,
                    [("TB", 5), ("BB",), ("G1",)], [("BFc",)])
            for hp in range(16):
                for q in range(4):
                    bk = pa % 4
                    pa += 1
                    P.op("pe", mm(PA[bk][:, :], BFc[:, hp, :], CF[:, 0, q * 4:(q + 1) * 4, 0:128], True, True),
                         reads=[("BFc",), ("CF", 0)], writes=[("ps", "A", bk)])
                    tt(TS[:, hp, q * 4:(q + 1) * 4, :], PA[bk][:, :].rearrange("p (j t) -> p j t", j=4),
                       MASK[:, :, :], ALU.mult, [("ps", "A", bk), ("MASK",)], [("TS",)])
            for q in range(4):
                yb = py % 2
                py += 1
                for j in range(4):
                    h = q * 4 + j
                    o = PYb[yb][:, j * 128:j * 128 + nch]
                    for hp in range(16):
                        P.op("pe", mm(o, TS[:, hp, h, :], U16[:, :, g * 16 + hp], hp == 0, False),
                             reads=[("TS",), ("U16",)], writes=[("ps", "Y", yb)], inc=False)
                    P.op("pe", mm(o, CF[:, 0, h, 1:129], XP[:, g, :], False, True),
                         reads=[("CF", 0), ("XP",)], writes=[("ps", "Y", yb)], inc=(j == 3))
                for j in range(4):
                    h = q * 4 + j
                    stt(YS[:, ub, :, h], UF[:, ub, :, h], D2[:, g * 16 + h:g * 16 + h + 1],
                        PYb[yb][:, j * 128:j * 128 + nch], ALU.mult, ALU.add,
                        [("UF", ub), ("D2",), ("ps", "Y", yb)], [("YS", ub)])
            yv = YS[:, ub, :, :].rearrange("p c h -> p (c h)")
            zv = ZS[:, ub, :, :].rearrange("p c h -> p (c h)")
            n = nch * 16
            tt(G1[:, 0:n], yv, yv, ALU.mult, [("YS", ub)], [("G1",)], eng="pool")
            tsc(G1[:, 0:n], G1[:, 0:n], 0.044715, ALU.mult, [("G1",)], [("G1",)], s2=1.0, op1=ALU.add, eng="pool")
            tt(G1[:, 0:n], G1[:, 0:n], yv, ALU.mult, [("G1",), ("YS", ub)], [("G1",)], eng="pool")
            act(G1[:, 0:n], G1[:, 0:n], AF.Sigmoid, [("G1",)], [("G1",)], scale=gsc)
            tt(zv, G1[:, 0:n], yv, ALU.mult, [("G1",), ("YS", ub)], [("ZS", ub)], eng="pool")
            P.dma("sp", zout[g], ZS[:, ub, :, :], reads=[("ZS", ub)])
        P.finish()
        P.emit()
    return nc


def s5_phase(nc, P, es, S, c_ident, uS16, uG, GZ):
    seq = SEQ
    nch = seq // 128
    lr_in, li_in, ldt_in = S["lam_re2"], S["lam_im2"], S["logdt2"]
    br_in, bi_in, cr_in, ci_in, d_in = S["br2"], S["bi2"], S["cr2"], S["ci2"], S["d2"]
    c_mask, c_ti, c_tir, c_sg = S["mask4"], S["ti"], S["tir"], S["sg"]
    if True:
        sb = lambda n, s, d: es.enter_context(nc.sbuf_tensor(n + "_p4b", s, d))
        ps = lambda n: es.enter_context(nc.psum_tensor(n + "_p4b", [128, 512], F32))
        U16 = sb("U16", [128, nch, 128], BF16)
        UF = sb("UF", [128, 1, nch, 16], F32)
        TS = sb("TS", [128, 16, 16, 128], BF16)
        CF = sb("CF", [128, 1, 16, 129], BF16)
        BFc = sb("BFc", [128, 16, 128], BF16)
        WF = BFc
        M1 = CF[:, 0, :, 0:128]
        YS = sb("YS", [128, 1, nch, 16], F32)
        ZALL = sb("ZALL", [128, nch, 128], BF16)
        ZT = sb("ZT", [128, 2, 512], BF16)
        G1 = sb("G1", [128, nch * 16], F32)
        VST = sb("VST", [128, NG, nch], F32)
        VI = sb("VI", [64, NG, nch], F32)
        SC = [[sb(f"SC{a}{b}", [64, NG, nch], F32) for b in range(2)] for a in range(2)]
        XP = sb("XP", [128, NG, nch], BF16)
        XPI = sb("XPI", [64, NG, nch], BF16)
        LR = sb("LR", [128, NG], F32)
        LI = sb("LI", [128, NG], F32)
        DTt = sb("DTt", [128, NG], F32)
        LDR = sb("LDR", [128, NG], F32)
        LDI = sb("LDI", [128, NG], F32)
        NLDR = sb("NLDR", [128, NG], F32)
        SM = sb("SM", [128, 12, NG], F32)
        BR = sb("BR", [128, NG, 16], F32)
        BI = sb("BI", [128, NG, 16], F32)
        CR = sb("CR", [128, NG, 16], F32)
        CI = sb("CI", [128, NG, 16], F32)
        BBR = sb("BBR", [128, NG, 16], F32)
        BBI = sb("BBI", [128, NG, 16], F32)
        CA = sb("CA", [128, NG, 16], F32)
        CB = sb("CB", [128, NG, 16], F32)
        BA = sb("BA", [128, NG, 16], F32)
        BB = sb("BB", [128, NG, 16], F32)
        D2 = sb("D2", [128, 128], F32)
        MASK = sb("MASK", [128, 4, 128], F32)
        TI = sb("TI", [128, 130], F32)
        TIR = sb("TIR", [128, 128], F32)
        SGN = sb("SGN", [128, 5], F32)
        ident = sb("identb", [128, 128], BF16)
        TB = sb("TB", [128, 6, 130], F32)
        AW = sb("AW", [128, 2, NG], F32)
        WW = sb("WW", [64, 2, 3, NG], F32)
        PA = [ps(f"PA{i}") for i in range(4)]
        PYb = [ps("PYa"), ps("PYb")]
        PV = ps("PV")
        loads = ((LR, lr_in, "LR"), (LI, li_in, "LI"), (DTt, ldt_in, "DT"), (BR, br_in, "BR"), (BI, bi_in, "BI"),
                 (CR, cr_in, "CR"), (CI, ci_in, "CI"), (D2, d_in, "D2"), (MASK, c_mask, "MASK"), (TI, c_ti, "TI"),
                 (TIR, c_tir, "TIR"), (SGN, c_sg, "SGN"), (ident, c_ident, "ident"))
        for (t, src, key) in loads:
            idx = (slice(None),) * len(t.shape)
            P.dma("sp", t[idx], src, writes=[(key,)])
        P.dma("sp", U16[:, :, :], uS16, writes=[("U16",)])
        m0, m1, nm0, nm1 = SGN[:, 0:1], SGN[:, 1:2], SGN[:, 2:3], SGN[:, 3:4]

        def dve(fn, reads, writes):
            P.op("dve", fn, reads=reads, writes=writes)

        def tt(out, a, b, op, reads, writes, eng="dve"):
            P.op(eng, lambda e: e.tensor_tensor(out=out, in0=a, in1=b, op=op), reads=reads, writes=writes)

        def tsc(out, a, s1, op0, reads, writes, s2=None, op1=None, eng="dve"):
            if op1 is None:
                P.op(eng, lambda e: e.tensor_scalar(out=out, in0=a, scalar1=s1, scalar2=None, op0=op0),
                     reads=reads, writes=writes)
            else:
                P.op(eng, lambda e: e.tensor_scalar(out=out, in0=a, scalar1=s1, scalar2=s2, op0=op0, op1=op1),
                     reads=reads, writes=writes)

        def stt(out, a, s, b, op0, op1, reads, writes, eng="dve"):
            P.op(eng, lambda e: e.scalar_tensor_tensor(out=out, in0=a, scalar=s, in1=b, op0=op0, op1=op1),
                 reads=reads, writes=writes)

        def act(out, a, func, reads, writes, bias=None, scale=None):
            kw = {}
            if bias is not None:
                kw["bias"] = bias
            if scale is not None:
                kw["scale"] = scale
            P.op("act", lambda e: e.activation(out=out, in_=a, func=func, **kw), reads=reads, writes=writes)

        def cp(out, a, reads, writes, eng="dve"):
            P.op(eng, lambda e: e.tensor_copy(out=out, in_=a), reads=reads, writes=writes)

        RI = sb("RI", [128, 130], mybir.dt.int32)
        RF = sb("RF", [128, 2, 130], F32)

        def reduce_angle(arg, shift, out, rk, n):
            t, tf = RF[:, 0, 0:n], RF[:, 1, 0:n]
            ti = RI[:, 0:n]
            tsc(t, arg, 1.0 / TWO_PI, ALU.mult, rk, [("RF", 0)], s2=0.5 + shift, op1=ALU.add)
            cp(ti, t, [("RF", 0)], [("RI",)])
            cp(tf, ti, [("RI",)], [("RF", 1)])
            tt(t, t, tf, ALU.subtract, [("RF", 0), ("RF", 1)], [("RF", 0)])
            tsc(t, t, -0.5, ALU.add, [("RF", 0)], [("RF", 0)], s2=TWO_PI, op1=ALU.mult)
            tsc(tf, t, -math.pi, ALU.is_lt, [("RF", 0)], [("RF", 1)])
            stt(t, tf, TWO_PI, t, ALU.mult, ALU.add, [("RF", 0), ("RF", 1)], [("RF", 0)])
            tsc(tf, t, math.pi, ALU.is_gt, [("RF", 0)], [("RF", 1)])
            stt(out, tf, -TWO_PI, t, ALU.mult, ALU.add, [("RF", 0), ("RF", 1)], [("RF", 0)])

        def sincos(arg, sin_out, cos_out, tmp, rk, wk_s, wk_c, tk):
            n = arg.shape[-1]
            reduce_angle(arg, 0.0, RF[:, 0, 0:n], rk, n)
            act(sin_out, RF[:, 0, 0:n], AF.Sin, [("RF", 0)], wk_s)
            tsc(RF[:, 1, 0:n], RF[:, 0, 0:n], -1.0, ALU.mult, [("RF", 0)], [("RF", 1)])
            tt(RF[:, 1, 0:n], RF[:, 0, 0:n], RF[:, 1, 0:n], ALU.max, [("RF", 0), ("RF", 1)], [("RF", 1)])
            act(cos_out, RF[:, 1, 0:n], AF.Sin, [("RF", 1), ("SGN",)], wk_c, bias=SGN[:, 4:5], scale=-1.0)

        S = lambda i: SM[:, i, :]
        act(DTt[:, :], DTt[:, :], AF.Exp, [("DT",)], [("DT",)])
        tt(LDR[:, :], LR[:, :], DTt[:, :], ALU.mult, [("LR",), ("DT",)], [("LDR",)])
        tt(LDI[:, :], LI[:, :], DTt[:, :], ALU.mult, [("LI",), ("DT",)], [("LDI",)])
        tsc(NLDR[:, :], LDR[:, :], -1.0, ALU.mult, [("LDR",)], [("NLDR",)])
        tsc(S(5), LDI[:, :], TWO_PI, ALU.add, [("LDI",)], [("S", 5)])
        sincos(S(5), S(0), S(1), S(6), [("S", 5)], [("S", 0)], [("S", 1)], ("S", 6))
        act(S(2), LDR[:, :], AF.Exp, [("LDR",)], [("S", 2)])
        tt(S(3), S(2), S(1), ALU.mult, [("S", 2), ("S", 1)], [("S", 3)])
        tsc(S(3), S(3), -1.0, ALU.add, [("S", 3)], [("S", 3)])
        tt(S(4), S(2), S(0), ALU.mult, [("S", 2), ("S", 0)], [("S", 4)])
        tt(S(5), LR[:, :], LR[:, :], ALU.mult, [("LR",)], [("S", 5)])
        tt(S(6), LI[:, :], LI[:, :], ALU.mult, [("LI",)], [("S", 6)])
        tt(S(5), S(5), S(6), ALU.add, [("S", 5), ("S", 6)], [("S", 5)])
        dve(lambda e: e.reciprocal(out=S(5), in_=S(5)), [("S", 5)], [("S", 5)])
        tt(S(7), S(3), LR[:, :], ALU.mult, [("S", 3), ("LR",)], [("S", 7)])
        tt(S(6), S(4), LI[:, :], ALU.mult, [("S", 4), ("LI",)], [("S", 6)])
        tt(S(7), S(7), S(6), ALU.add, [("S", 7), ("S", 6)], [("S", 7)])
        tt(S(7), S(7), S(5), ALU.mult, [("S", 7), ("S", 5)], [("S", 7)])
        tt(S(8), S(4), LR[:, :], ALU.mult, [("S", 4), ("LR",)], [("S", 8)])
        tt(S(6), S(3), LI[:, :], ALU.mult, [("S", 3), ("LI",)], [("S", 6)])
        tt(S(8), S(8), S(6), ALU.subtract, [("S", 8), ("S", 6)], [("S", 8)])
        tt(S(8), S(8), S(5), ALU.mult, [("S", 8), ("S", 5)], [("S", 8)])
        tsc(S(9), S(8), -1.0, ALU.mult, [("S", 8)], [("S", 9)])
        for g in range(NG):
            tsc(BBR[:, g, :], BR[:, g, :], SM[:, 7, g:g + 1], ALU.mult, [("BR",), ("S", 7)], [("BBR", g)])
            stt(BBR[:, g, :], BI[:, g, :], SM[:, 9, g:g + 1], BBR[:, g, :], ALU.mult, ALU.add,
                [("BI",), ("S", 9), ("BBR", g)], [("BBR", g)])
            tsc(BBI[:, g, :], BI[:, g, :], SM[:, 7, g:g + 1], ALU.mult, [("BI",), ("S", 7)], [("BBI", g)])
            stt(BBI[:, g, :], BR[:, g, :], SM[:, 8, g:g + 1], BBI[:, g, :], ALU.mult, ALU.add,
                [("BR",), ("S", 8), ("BBI", g)], [("BBI", g)])
        tsc(CA[:, :, :], CR[:, :, :], m0, ALU.mult, [("CR",), ("SGN",)], [("CA",)])
        stt(CA[:, :, :], CI[:, :, :], nm1, CA[:, :, :], ALU.mult, ALU.add, [("CI",), ("SGN",), ("CA",)], [("CA",)])
        tsc(CB[:, :, :], CI[:, :, :], nm0, ALU.mult, [("CI",), ("SGN",)], [("CB",)])
        stt(CB[:, :, :], CR[:, :, :], nm1, CB[:, :, :], ALU.mult, ALU.add, [("CR",), ("SGN",), ("CB",)], [("CB",)])
        bbk = [("BBR", g) for g in range(NG)] + [("BBI", g) for g in range(NG)]
        tsc(BA[:, :, :], BBR[:, :, :], m0, ALU.mult, bbk + [("SGN",)], [("BA",)])
        stt(BA[:, :, :], BBI[:, :, :], m1, BA[:, :, :], ALU.mult, ALU.add, bbk + [("SGN",), ("BA",)], [("BA",)])
        tsc(BB[:, :, :], BBI[:, :, :], nm0, ALU.mult, bbk + [("SGN",)], [("BB",)])
        stt(BB[:, :, :], BBR[:, :, :], m1, BB[:, :, :], ALU.mult, ALU.add, bbk + [("SGN",), ("BB",)], [("BB",)])

        def table(g, tvec, n, neg):
            tk = [("TB", i) for i in range(6)]
            tsc(TB[:, 0, 0:n], tvec, LDI[:, g:g + 1], ALU.mult, [("LDI",), ("TI",), ("TIR",)], [tk[0]])
            tsc(TB[:, 0, 0:n], TB[:, 0, 0:n], TWO_PI, ALU.add, [tk[0]], [tk[0]])
            sincos(TB[:, 0, 0:n], TB[:, 1, 0:n], TB[:, 2, 0:n], TB[:, 3, 0:n], [tk[0]], [tk[1]], [tk[2]], tk[3])
            act(TB[:, 3, 0:n], tvec, AF.Exp, [("TI",), ("TIR",), ("LDR",), ("NLDR",), tk[3]], [tk[3]],
                scale=(NLDR if neg else LDR)[:, g:g + 1])
            tt(TB[:, 4, 0:n], TB[:, 3, 0:n], TB[:, 2, 0:n], ALU.mult, [tk[3], tk[2]], [tk[4]])
            if neg:
                stt(TB[:, 5, 0:n], TB[:, 3, 0:n], -1.0, TB[:, 1, 0:n], ALU.mult, ALU.mult, [tk[3], tk[1]], [tk[5]])
            else:
                tt(TB[:, 5, 0:n], TB[:, 3, 0:n], TB[:, 1, 0:n], ALU.mult, [tk[3], tk[1]], [tk[5]])

        pa = 0
        for g in range(NG):
            table(g, TI[:, 128:129], 1, False)
            tsc(AW[:, 0, g:g + 1], TB[:, 4, 0:1], 1.0, ALU.mult, [("TB", 4)], [("AW",)])
            tsc(AW[:, 1, g:g + 1], TB[:, 5, 0:1], 1.0, ALU.mult, [("TB", 5)], [("AW",)])
            table(g, TIR[:, :], 128, False)
            for hp in range(16):
                tsc(G1[:, 0:128], TB[:, 4, 0:128], BA[:, g, hp:hp + 1], ALU.mult, [("TB", 4), ("BA",)], [("G1",)])
                stt(WF[:, hp, :], TB[:, 5, 0:128], BB[:, g, hp:hp + 1], G1[:, 0:128], ALU.mult, ALU.add,
                    [("TB", 5), ("BB",), ("G1",)], [("BFc",)])
            for q in range(4):
                bk = pa % 4
                pa += 1
                for j in range(4):
                    P.op("pe", mm(PA[bk][:, j * 128:(j + 1) * 128], WF[:, q * 4 + j, :], ident[:, :], True, True),
                         reads=[("BFc",), ("ident",)], writes=[("ps", "A", bk)], inc=(j == 3))
                act(M1[:, q * 4:(q + 1) * 4, :], PA[bk][:, :].rearrange("p (j t) -> p j t", j=4), AF.Copy,
                    [("ps", "A", bk)], [("CF", 0)])
            for hp in range(16):
                P.op("pe", mm(PV[:, 0:nch], M1[:, hp, :], U16[:, :, g * 16 + hp], hp == 0, hp == 15),
                     reads=[("CF", 0), ("U16",)], writes=[("ps", "V")], inc=(hp == 15))
            act(VST[:, g, :], PV[:, 0:nch], AF.Copy, [("ps", "V")], [("VST",)])
        P.dma("sp", VI[:, :, :], VST[64:128, :, :], reads=[("VST",)], writes=[("VI",)])
        cp(SC[0][0][:, :, :], VST[0:64, :, :], [("VST",)], [("SC", 0, 0)])
        cp(SC[0][1][:, :, :], VI[:, :, :], [("VI",)], [("SC", 0, 1)], eng="pool")
        tsc(WW[:, 0, 0, :], AW[0:64, 0, :], 1.0, ALU.mult, [("AW",)], [("WW", 0)])
        tsc(WW[:, 0, 1, :], AW[0:64, 1, :], 1.0, ALU.mult, [("AW",)], [("WW", 0)])
        tsc(WW[:, 0, 2, :], AW[0:64, 1, :], -1.0, ALU.mult, [("AW",)], [("WW", 0)])
        cur = 0
        sh = 1
        while sh < nch:
            nx = 1 - cur
            re, im = SC[cur]
            nre, nim = SC[nx]
            for g in range(NG):
                wr, wi, nwi = WW[:, cur, 0, g:g + 1], WW[:, cur, 1, g:g + 1], WW[:, cur, 2, g:g + 1]
                rk = [("SC", cur, 0), ("SC", cur, 1), ("WW", cur)]
                cp(nre[:, g, 0:sh], re[:, g, 0:sh], rk, [("SC", nx, 0)])
                stt(nre[:, g, sh:nch], re[:, g, 0:nch - sh], wr, re[:, g, sh:nch], ALU.mult, ALU.add, rk, [("SC", nx, 0)])
                stt(nre[:, g, sh:nch], im[:, g, 0:nch - sh], nwi, nre[:, g, sh:nch], ALU.mult, ALU.add,
                    rk + [("SC", nx, 0)], [("SC", nx, 0)])
                cp(nim[:, g, 0:sh], im[:, g, 0:sh], rk, [("SC", nx, 1)], eng="pool")
                stt(nim[:, g, sh:nch], im[:, g, 0:nch - sh], wr, im[:, g, sh:nch], ALU.mult, ALU.add, rk,
                    [("SC", nx, 1)])
                stt(nim[:, g, sh:nch], re[:, g, 0:nch - sh], wi, nim[:, g, sh:nch], ALU.mult, ALU.add,
                    rk + [("SC", nx, 1)], [("SC", nx, 1)])
            wk = [("WW", cur)]
            tt(WW[:, nx, 0, :], WW[:, cur, 0, :], WW[:, cur, 0, :], ALU.mult, wk, [("WW", nx)])
            tt(WW[:, nx, 2, :], WW[:, cur, 1, :], WW[:, cur, 1, :], ALU.mult, wk, [("WW", nx)])
            tt(WW[:, nx, 0, :], WW[:, nx, 0, :], WW[:, nx, 2, :], ALU.subtract, [("WW", nx)], [("WW", nx)])
            tt(WW[:, nx, 1, :], WW[:, cur, 0, :], WW[:, cur, 1, :], ALU.mult, wk, [("WW", nx)])
            tsc(WW[:, nx, 1, :], WW[:, nx, 1, :], 2.0, ALU.mult, [("WW", nx)], [("WW", nx)])
            tsc(WW[:, nx, 2, :], WW[:, nx, 1, :], -1.0, ALU.mult, [("WW", nx)], [("WW", nx)])
            cur = nx
            sh *= 2
        re, im = SC[cur]
        P.op("pool", lambda e: e.memset(XP[:, :, 0:1], 0.0), writes=[("XP",)])
        P.op("pool", lambda e: e.memset(XPI[:, :, 0:1], 0.0), writes=[("XPI",)])
        if nch > 1:
            P.op("dve", lambda e: e.tensor_copy(out=XP[0:64, :, 1:nch], in_=re[:, :, 0:nch - 1]),
                 reads=[("SC", cur, 0)], writes=[("XP",)])
            P.op("dve", lambda e: e.tensor_copy(out=XPI[:, :, 1:nch], in_=im[:, :, 0:nch - 1]),
                 reads=[("SC", cur, 1)], writes=[("XPI",)])
        P.dma("sp", XP[64:128, :, :], XPI[:, :, :], reads=[("XPI",)], writes=[("XP",)])
        gsc = 2.0 * math.sqrt(2.0 / math.pi)
        py = 0
        for g in range(NG):
            ub = 0
            P.dma("sp", UF[:, ub, :, :], uG[g * 128:(g + 1) * 128, :].rearrange("s (c h) -> s c h", h=16),
                  writes=[("UF", ub)])
            table(g, TI[:, 0:129], 129, False)
            for h in range(16):
                tsc(G1[:, 0:129], TB[:, 4, 0:129], CA[:, g, h:h + 1], ALU.mult, [("TB", 4), ("CA",)], [("G1",)])
                stt(CF[:, 0, h, :], TB[:, 5, 0:129], CB[:, g, h:h + 1], G1[:, 0:129], ALU.mult, ALU.add,
                    [("TB", 5), ("CB",), ("G1",)], [("CF", 0)])
            table(g, TI[:, 0:128], 128, True)
            for hp in range(16):
                tsc(G1[:, 0:128], TB[:, 4, 0:128], BA[:, g, hp:hp + 1], ALU.mult, [("TB", 4), ("BA",)], [("G1",)])
                stt(BFc[:, hp, :], TB[:, 5, 0:128], BB[:, g, hp:hp + 1], G1[:, 0:128], ALU.mult, ALU.add,
                    [("TB", 5), ("BB",), ("G1",)], [("BFc",)])
            for hp in range(16):
                for q in range(4):
                    bk = pa % 4
                    pa += 1
                    P.op("pe", mm(PA[bk][:, :], BFc[:, hp, :], CF[:, 0, q * 4:(q + 1) * 4, 0:128], True, True),
                         reads=[("BFc",), ("CF", 0)], writes=[("ps", "A", bk)])
                    tt(TS[:, hp, q * 4:(q + 1) * 4, :], PA[bk][:, :].rearrange("p (j t) -> p j t", j=4),
                       MASK[:, :, :], ALU.mult, [("ps", "A", bk), ("MASK",)], [("TS",)])
            for q in range(4):
                yb = py % 2
                py += 1
                for j in range(4):
                    h = q * 4 + j
                    o = PYb[yb][:, j * 128:j * 128 + nch]
                    for hp in range(16):
                        P.op("pe", mm(o, TS[:, hp, h, :], U16[:, :, g * 16 + hp], hp == 0, False),
                             reads=[("TS",), ("U16",)], writes=[("ps", "Y", yb)], inc=False)
                    P.op("pe", mm(o, CF[:, 0, h, 1:129], XP[:, g, :], False, True),
                         reads=[("CF", 0), ("XP",)], writes=[("ps", "Y", yb)], inc=(j == 3))
                for j in range(4):
                    h = q * 4 + j
                    stt(YS[:, ub, :, h], UF[:, ub, :, h], D2[:, g * 16 + h:g * 16 + h + 1],
                        PYb[yb][:, j * 128:j * 128 + nch], ALU.mult, ALU.add,
                        [("UF", ub), ("D2",), ("ps", "Y", yb)], [("YS", ub)])
            yv = YS[:, ub, :, :]
            g3 = G1[:, 0:nch * 16].rearrange("p (c h) -> p c h", h=16)
            zv = ZALL[:, :, g * 16:(g + 1) * 16]
            tt(g3, yv, yv, ALU.mult, [("YS", ub)], [("G1",)], eng="pool")
            tsc(g3, g3, 0.044715, ALU.mult, [("G1",)], [("G1",)], s2=1.0, op1=ALU.add, eng="pool")
            tt(g3, g3, yv, ALU.mult, [("G1",), ("YS", ub)], [("G1",)], eng="pool")
            act(g3, g3, AF.Sigmoid, [("G1",)], [("G1",)], scale=gsc)
            tt(zv, g3, yv, ALU.mult, [("G1",), ("YS", ub)], [("ZALL",)], eng="pool")
        for c4 in range(nch // 4):
            bk = pa % 4
            pa += 1
            zb = c4 % 2
            for j in range(4):
                c = c4 * 4 + j
                P.op("pe", mm(PA[bk][:, j * 128:(j + 1) * 128], ZALL[:, c, :], ident[:, :], True, True),
                     reads=[("ZALL",), ("ident",)], writes=[("ps", "A", bk)], inc=(j == 3))
            act(ZT[:, zb, :], PA[bk][:, :], AF.Copy, [("ps", "A", bk)], [("ZT", zb)])
            w = c4 // 4
            P.dma("sp", GZ.src[w][:, (c4 % 4) * 512:(c4 % 4 + 1) * 512], ZT[:, zb, :], reads=[("ZT", zb)],
                  writes=[("GZ", "s", w, c4 % 4)])
            if c4 % 4 == 3:
                P.collective(G4, GZ.src[w], GZ.a[w], reads=[("GZ", "s", w, k4) for k4 in range(4)],
                             writes=[("GZ", "a", w)])
                if w > 0:
                    GZ.s2(w - 1)
        GZ.s2(nch // 16 - 1)


def l4_inputs(u_c, inp, core, seq=SEQ):
    nch = seq // 128
    gs = slice(core * NG, (core + 1) * NG)
    two = lambda a: np.ascontiguousarray(np.concatenate([a, a], 0).astype(np.float32))
    m = dict(l4_consts())
    u3 = u_c.reshape(nch, 128, 128)
    m["uS"] = np.ascontiguousarray(u3.transpose(1, 0, 2))
    m["uG"] = np.ascontiguousarray(u3.reshape(nch, 128, NG, 16).transpose(2, 1, 0, 3))
    m["lam_re2"] = two(inp["s5_lambda_re"][0][gs].T)
    m["lam_im2"] = two(inp["s5_lambda_im"][0][gs].T)
    m["logdt2"] = np.ascontiguousarray(np.broadcast_to(inp["s5_log_dt"][0][gs][None, :], (128, NG)).astype(np.float32))
    m["br2"] = two(inp["s5_b_re"][0][gs].transpose(1, 0, 2))
    m["bi2"] = two(inp["s5_b_im"][0][gs].transpose(1, 0, 2))
    m["cr2"] = two(inp["s5_c_re"][0][gs].transpose(2, 0, 1))
    m["ci2"] = two(inp["s5_c_im"][0][gs].transpose(2, 0, 1))
    m["d2"] = np.ascontiguousarray(np.broadcast_to(inp["s5_d"][0][core * 128:(core + 1) * 128][None, :], (128, 128)).astype(np.float32))
    return m


def l4_unpack(zout, seq=SEQ):
    nch = seq // 128
    return np.asarray(zout).transpose(2, 1, 0, 3).reshape(seq, 128)


_CACHE = {}


def _get(name, builder):
    if name not in _CACHE:
        _CACHE[name] = builder()
    return _CACHE[name]


def _run(nc, in_maps):
    res = run_bass_kernel_spmd(nc, in_maps, core_ids=list(range(NCORE)))
    return res.results


def kernel_unfused(**inputs):
    inp = {k: np.asarray(v) for k, v in inputs.items()}
    x = inp["x"][0]
    ident = _ident_np()
    cs = lambda c: slice(c * TPC, (c + 1) * TPC)
    ca = np.ascontiguousarray
    maps = [{"x": ca(x[cs(c)]), "wg": inp["ffn1_w_gate"][0], "wu": inp["ffn1_w_up"][0], "wd": inp["ffn1_w_down"][0],
             "lng": ca(inp["ln_gain"][0, 0][None]), "lnb": ca(inp["ln_bias"][0, 0][None]),
             "win": inp["attn_w_in"][0], "ident": ident} for c in range(NCORE)]
    r1 = _run(_get("l1", build_l1), maps)
    x1 = [r["x1"] for r in r1]
    projT = np.concatenate([np.asarray(r["projT"]) for r in r1], axis=1)
    fT = np.concatenate([np.asarray(r["fT"]) for r in r1], axis=1)
    del r1, maps
    consts = l2_consts()
    perms = {d: dil_perm(SEQ, d) for (d, _) in DIL}
    maps = []
    for c in range(NCORE):
        m = dict(consts)
        hs = slice(128 * c, 128 * (c + 1))
        m["qf"] = ca(projT[0:1024][hs])
        m["kf"] = ca(projT[1024:2048][hs])
        m["vf"] = ca(projT[2048:3072][hs].T)
        m["f2T"] = ca(fT[c].reshape(SEQ // 128, 128).T)
        m["nbf"] = np.full((128, 1), inp["attn_b_f"][0, c], np.float32)
        m["qd"] = ca(projT[3072:4096][hs])
        m["kd"] = ca(projT[4096:5120][hs])
        vdt = ca(projT[5120:6144][hs].T)
        for (d, _) in DIL:
            m[f"vd{d}"] = ca(vdt[perms[d]])
        maps.append(m)
    r2 = _run(_get("l2", build_l2), maps)
    yT = np.concatenate([np.asarray(r["yf"]) for r in r2] + [np.asarray(r["yd"]) for r in r2], axis=0)
    del r2, maps, projT
    lng3 = ca(np.stack([inp["ln_gain"][0, 1], inp["ln_gain"][0, 2], inp["ln_gain"][1, 0]]))
    lnb3 = ca(np.stack([inp["ln_bias"][0, 1], inp["ln_bias"][0, 2], inp["ln_bias"][1, 0]]))
    maps = [{"x1": x1[c], "yT": ca(yT[:, cs(c)]), "wo": inp["attn_w_out"][0],
             "w2g": inp["ffn2_w_gate"][0], "w2u": inp["ffn2_w_up"][0], "w2d": inp["ffn2_w_down"][0],
             "w3g": inp["ffn1_w_gate"][1], "w3u": inp["ffn1_w_up"][1], "w3d": inp["ffn1_w_down"][1],
             "lng": lng3, "lnb": lnb3, "wsi": inp["s5_w_in"][0], "ident": ident} for c in range(NCORE)]
    r3 = _run(_get("l3", build_l3), maps)
    x3 = [r["x3"] for r in r3]
    u = np.concatenate([np.asarray(r["u"]) for r in r3], axis=0)
    del r3, maps, x1, yT
    maps = [l4_inputs(ca(u[:, 128 * c:128 * (c + 1)]), inp, c) for c in range(NCORE)]
    r4 = _run(_get("l4", build_l4), maps)
    z = np.concatenate([l4_unpack(r["zout"]) for r in r4], axis=1)
    zT = ca(z.T)
    del r4, maps, u, z
    lng5 = ca(np.stack([inp["ln_gain"][1, 1], inp["ln_gain"][1, 2]]))
    lnb5 = ca(np.stack([inp["ln_bias"][1, 1], inp["ln_bias"][1, 2]]))
    maps = [{"x3": x3[c], "zT": ca(zT[:, cs(c)]), "wgo": inp["s5_w_glu_out"][0], "wgg": inp["s5_w_glu_gate"][0],
             "w4g": inp["ffn2_w_gate"][1], "w4u": inp["ffn2_w_up"][1], "w4d": inp["ffn2_w_down"][1],
             "lng": lng5, "lnb": lnb5, "ident": ident} for c in range(NCORE)]
    r5 = _run(_get("l5", build_l5), maps)
    out = np.concatenate([np.asarray(r["out"]) for r in r5], axis=0)
    return out.reshape(1, SEQ, D).astype(np.float32)


G4 = [[0, 1, 2, 3], [4, 5, 6, 7]]
G2 = [[0, 4], [1, 5], [2, 6], [3, 7]]


class Gather:
    def __init__(self, nc, P, name, rows, cols, dtype, n):
        self.P, self.name = P, name
        self.src = nc.dram_tensor(name + "_s", [n, rows, cols], dtype).ap()
        self.a = nc.dram_tensor(name + "_a", [n, 4 * rows, cols], dtype).ap()
        self.b = nc.dram_tensor(name + "_b", [n, 8 * rows, cols], dtype).ap()

    def s1(self, i):
        self.P.collective(G4, self.src[i], self.a[i], reads=[(self.name, "s", i)], writes=[(self.name, "a", i)])

    def s2(self, i):
        self.P.collective(G2, self.a[i], self.b[i], reads=[(self.name, "a", i)], writes=[(self.name, "b", i)])


def _xt_to_gather(P, T, GXo, t):
    for q in range(4):
        i = t * 4 + q
        P.dma("sp", GXo.src[i].rearrange("(j p) t -> p j t", p=128), T.XT[:, 4 * q:4 * q + 4, :],
              reads=[("XT", st) for st in range(NST)], writes=[(GXo.name, "s", i)])
        GXo.s1(i)
    if t > 0:
        for q in range(4):
            GXo.s2((t - 1) * 4 + q)
    if t == TPC // TT - 1:
        for q in range(4):
            GXo.s2(t * 4 + q)


def _select_window(P, T, OH, nk, loader):
    xk = [("XT", st) for st in range(NST)]
    for w in range(NCORE):
        s0 = (w % 2) * 16
        keys = [("HT", s0 + k) for k in range(nk)]
        loader(w, T.HT[:, s0:s0 + nk, :], keys)
        src = T.HT[:, s0:s0 + nk, :]
        dst = T.XT[:, 0:nk, :]
        if w == 0:
            P.op("dve", lambda e, src=src, dst=dst, w=w: e.tensor_scalar(out=dst, in0=src, scalar1=OH[:, w:w + 1],
                                                                      scalar2=None, op0=ALU.mult),
                 reads=keys + [("OH",)], writes=xk)
        else:
            P.op("dve", lambda e, src=src, dst=dst, w=w: e.scalar_tensor_tensor(out=dst, in0=src, scalar=OH[:, w:w + 1],
                                                                             in1=dst, op0=ALU.mult, op1=ALU.add),
                 reads=keys + [("OH",)] + xk, writes=xk)


def build_fused(nph=7, dbg=False):
    nc = _new_nc()
    dt = lambda n, s, d, k="ExternalInput": nc.dram_tensor(n, s, d, kind=k).ap()
    it_ = lambda n, s, d: nc.dram_tensor(n, s, d).ap()
    x = dt("x", [TPC, D], F32)
    nff = 1 if nph < 4 else (3 if nph < 7 else 4)
    ffw = [[dt(f"w{i}g", [D, DFF], F32), dt(f"w{i}u", [D, DFF], F32), dt(f"w{i}d", [DFF, D], F32)] for i in range(nff)]
    lng = dt("lng", [6, D], F32)
    lnb = dt("lnb", [6, D], F32)
    winc = dt("winc", [D, 769], F32)
    wo = dt("wo", [D, D], F32)
    wsic = dt("wsic", [D, 128], F32)
    wgo = dt("wgo", [S5W, D], F32)
    wgg = dt("wgg", [S5W, D], F32)
    onehot = dt("onehot", [128, NCORE], F32)
    A = {"nbf": dt("nbf", [128, 1], F32)}
    for n_, shp, d_ in (("ident", [128, 128], BF16), ("ones_bf", [128, 128], BF16), ("fmask", [128, 4, 512], BF16),
                        ("dmask", [128, 256], BF16), ("uincl", [128, 128], F32), ("lstrict", [128, 128], F32),
                        ("ones_f", [128, 128], F32), ("ident_f", [128, 128], F32)):
        A[n_] = dt(n_, shp, d_)
    S = {}
    for n_, shp in (("lam_re2", [128, NG]), ("lam_im2", [128, NG]), ("logdt2", [128, NG]), ("br2", [128, NG, 16]),
                    ("bi2", [128, NG, 16]), ("cr2", [128, NG, 16]), ("ci2", [128, NG, 16]), ("d2", [128, 128]),
                    ("mask4", [128, 4, 128]), ("ti", [128, 130]), ("tir", [128, 128]), ("sg", [128, 5])):
        S[n_] = dt(n_, shp, F32)
    out = dt("out", [TPC, D], F32, "ExternalOutput")
    ident = A["ident"]
    if dbg:
        it_ = lambda n, s, d: nc.dram_tensor(n, s, d, kind="ExternalOutput").ap()
    x1s = it_("x1s", [TPC, D], F32)
    x3s = it_("x3s", [TPC, D], F32)
    scr = [it_(f"scr{i}", [128, SEQ], BF16) for i in range(6)]
    fsc = it_("fsc", [1, SEQ], F32)
    dbg_g = it_("dbg_g", [8 * 512, TT], BF16)
    dbg_y = it_("dbg_y", [2, 8 * 128, 2048], BF16)
    dbg_z = it_("dbg_z", [8 * 128, 2048], BF16)
    it_ = lambda n, s, d: nc.dram_tensor(n, s, d).ap()
    aug = it_("augscr", [6, SEQ], BF16)
    uS16 = it_("uS16", [128, SEQ // 128, 128], BF16)
    uG = it_("uG", [NG * 128, (SEQ // 128) * 16], F32)
    nch = SEQ // 128
    eps2 = LN_EPS / (ALPHA * ALPHA)
    with ExitStack() as es0:
        P = Prog(nc, es0)
        GX = Gather(nc, P, "GX", 512, TT, BF16, 16)
        GYF = Gather(nc, P, "GYF", 128, 2048, BF16, 8)
        GYD = Gather(nc, P, "GYD", 128, 2048, BF16, 8)
        GZ = Gather(nc, P, "GZ", 128, 2048, BF16, 8)
        with ExitStack() as es:
            T = TokPipe(nc, es, P, ident, tag="_p1")
            for t in range(TPC // TT):
                t0 = t * TT
                T.load_x(x[t0:t0 + TT, :])
                T.ffn(ffw[0][0], ffw[0][1], ffw[0][2], lng[0, :], lnb[0, :])
                T.store_x(x1s[t0:t0 + TT, :])
                _xt_to_gather(P, T, GX, t)
            if nph == 1:
                P.dma("sp", dbg_g, GX.b[5], reads=[("GX", "b", 5)])
                P.finish()
                P.emit()
                return nc
            P.barrier()
            P.emit()
        with ExitStack() as es:
            sb = lambda n, s, d: es.enter_context(nc.sbuf_tensor(n + "_p2a", s, d))
            ps = lambda n: es.enter_context(nc.psum_tensor(n + "_p2a", [128, 512], F32))
            XT2 = sb("XT2", [128, 2, 16, TT], BF16)
            W = sb("Wip", [128, 16, 769], BF16)
            OT = sb("OT", [128, 4, TT], BF16)
            OF = sb("OF", [1, 2, TT], F32)
            PG = [ps(f"PG{i}") for i in range(4)]
            P.dma("pool", W[:, :, :], winc.rearrange("(k p) n -> p k n", p=128), writes=[("Wip",)])
            it = 0
            oc = 0
            for t in range(TPC // TT):
                for r in range(NCORE):
                    b = it % 2
                    it += 1
                    for q in range(4):
                        i = t * 4 + q
                        P.dma("sp", XT2[:, b, 4 * q:4 * q + 4, :],
                              GX.b[i][r * 512:(r + 1) * 512, :].rearrange("(j p) t -> p j t", p=128),
                              reads=[("GX", "b", i)], writes=[("XT2", b, q)])
                    tok0 = r * TPC + t * TT
                    xk = [("XT2", b, q) for q in range(4)]
                    for ci in range(7):
                        wdt = 128 if ci < 6 else 1
                        k = oc % 4
                        oc += 1
                        for kc in range(16):
                            P.op("pe", mm(PG[k][0:wdt, :], W[:, kc, ci * 128:ci * 128 + wdt], XT2[:, b, kc, :],
                                          kc == 0, kc == 15),
                                 reads=[("Wip",)] + xk, writes=[("ps", "Gp", k)], inc=(kc == 15))
                        if ci < 6:
                            P.op("act", lambda e, k=k: e.activation(out=OT[:, k, :], in_=PG[k][:, :], func=AF.Copy),
                                 reads=[("ps", "Gp", k)], writes=[("OTp", k)])
                            P.dma("pool", scr[ci][:, tok0:tok0 + TT], OT[:, k, :], reads=[("OTp", k)])
                        else:
                            k2 = k % 2
                            P.op("act", lambda e, k=k, k2=k2: e.activation(out=OF[0:1, k2, :], in_=PG[k][0:1, :],
                                                                          func=AF.Copy),
                                 reads=[("ps", "Gp", k)], writes=[("OFp", k2)])
                            P.dma("pool", fsc[0:1, tok0:tok0 + TT], OF[0:1, k2, :], reads=[("OFp", k2)])
            if nph == 2:
                P.finish()
                P.emit()
                return nc
            P.barrier()
            P.emit()
        with ExitStack() as es:
            attn_phase(nc, P, es, A, scr, fsc, aug, GYF, GYD)
            if nph == 3:
                P.dma("sp", dbg_y[0], GYF.b[0], reads=[("GYF", "b", 0)])
                P.dma("sp", dbg_y[1], GYD.b[0], reads=[("GYD", "b", 0)])
                P.finish()
                P.emit()
                return nc
            P.barrier()
            P.emit()
        with ExitStack() as es:
            T = TokPipe(nc, es, P, ident, tag="_p3")
            OH = es.enter_context(nc.sbuf_tensor("OH_p3", [128, NCORE], F32))
            P.dma("sp", OH[:, :], onehot, writes=[("OH",)])
            for t in range(TPC // TT):
                t0 = t * TT
                T.load_x_only(x1s[t0:t0 + TT, :])

                def ld_y(w, dst, keys, t0=t0):
                    P.dma("sp", dst[:, 0:8, :], GYF.b[w][:, t0:t0 + TT].rearrange("(h p) t -> p h t", p=128),
                          reads=[("GYF", "b", w)], writes=keys[0:8])
                    P.dma("sp", dst[:, 8:16, :], GYD.b[w][:, t0:t0 + TT].rearrange("(h p) t -> p h t", p=128),
                          reads=[("GYD", "b", w)], writes=keys[8:16])
                _select_window(P, T, OH, 16, ld_y)
                T.load_ln(lng[1, :], lnb[1, :])

                def cons_res(st, c0, b):
                    xs = T.X[:, st, c0:c0 + 256]
                    P.op("dve", lambda e: e.scalar_tensor_tensor(out=xs, in0=T.PY[b][:, 0:256], scalar=1.0 / ALPHA,
                                                                 in1=xs, op0=ALU.mult, op1=ALU.add),
                         reads=[("ps", "Y", b), ("X", st)], writes=[("X", st)])
                T.lin_tm(16, wo, D, cons_res)
                T.layernorm(eps2)
                T.ffn(ffw[1][0], ffw[1][1], ffw[1][2], lng[2, :], lnb[2, :])
                T.ffn(ffw[2][0], ffw[2][1], ffw[2][2], lng[3, :], lnb[3, :])
                T.store_x(x3s[t0:t0 + TT, :])
                _xt_to_gather(P, T, GX, t)
            if nph == 4:
                P.finish()
                P.emit()
                return nc
            P.barrier()
            P.emit()
        with ExitStack() as es:
            sb = lambda n, s, d: es.enter_context(nc.sbuf_tensor(n + "_p4a", s, d))
            ps = lambda n: es.enter_context(nc.psum_tensor(n + "_p4a", [128, 512], F32))
            XT2 = sb("XT2", [128, 2, 16, TT], BF16)
            W = sb("Wsi", [128, 16, 128], BF16)
            UB = sb("UB", [128, 2, 4, 128], BF16)
            UF = sb("UF", [128, 2, 4, 128], F32)
            PG = [ps(f"PG{i}") for i in range(2)]
            P.dma("pool", W[:, :, :], wsic.rearrange("(k p) n -> p k n", p=128), writes=[("Wsi",)])
            it = 0
            for t in range(TPC // TT):
                for r in range(NCORE):
                    b = it % 2
                    it += 1
                    for q in range(4):
                        i = t * 4 + q
                        P.dma("sp", XT2[:, b, 4 * q:4 * q + 4, :],
                              GX.b[i][r * 512:(r + 1) * 512, :].rearrange("(j p) t -> p j t", p=128),
                              reads=[("GX", "b", i)], writes=[("XT2", b, q)])
                    c0 = (r * TPC + t * TT) // 128
                    xk = [("XT2", b, q) for q in range(4)]
                    for st in range(4):
                        for kc in range(16):
                            P.op("pe", mm(PG[b][:, st * 128:(st + 1) * 128], XT2[:, b, kc, st * 128:(st + 1) * 128],
                                          W[:, kc, :], kc == 0, kc == 15),
                                 reads=[("Wsi",)] + xk, writes=[("ps", "Gu", b)], inc=(st == 3 and kc == 15))
                    pv = PG[b][:, :].rearrange("p (c h) -> p c h", c=4)
                    P.op("dve", lambda e, b=b, pv=pv: e.tensor_copy(out=UF[:, b, :, :], in_=pv),
                         reads=[("ps", "Gu", b)], writes=[("UFs", b)])
                    P.op("act", lambda e, b=b: e.activation(out=UB[:, b, :, :], in_=UF[:, b, :, :], func=AF.Copy),
                         reads=[("UFs", b)], writes=[("UB", b)])
                    P.dma("pool", uS16[:, c0:c0 + 4, :], UB[:, b, :, :], reads=[("UB", b)])
                    for g in range(NG):
                        P.dma("pool", uG[g * 128:(g + 1) * 128, c0 * 16:(c0 + 4) * 16].rearrange("s (c h) -> s c h", h=16),
                              UF[:, b, :, g * 16:(g + 1) * 16], reads=[("UFs", b)])
            if nph == 5:
                P.finish()
                P.emit()
                return nc
            P.barrier()
            P.emit()
        with ExitStack() as es:
            s5_phase(nc, P, es, S, ident, uS16, uG, GZ)
            if nph == 6:
                P.dma("sp", dbg_z, GZ.b[3], reads=[("GZ", "b", 3)])
                P.finish()
                P.emit()
                return nc
            P.barrier()
            P.emit()
        with ExitStack() as es:
            T = TokPipe(nc, es, P, ident, tag="_p5")
            OH = es.enter_context(nc.sbuf_tensor("OH_p5", [128, NCORE], F32))
            P.dma("sp", OH[:, :], onehot, writes=[("OH",)])
            for t in range(TPC // TT):
                t0 = t * TT
                T.load_x_only(x3s[t0:t0 + TT, :])

                def ld_z(w, dst, keys, t0=t0):
                    P.dma("sp", dst[:, 0:8, :], GZ.b[w][:, t0:t0 + TT].rearrange("(h p) t -> p h t", p=128),
                          reads=[("GZ", "b", w)], writes=keys)
                _select_window(P, T, OH, 8, ld_z)
                T.load_ln(lng[4, :], lnb[4, :])

                def cons_glu(st, c0, b):
                    k = T.ot % 2
                    T.ot += 1
                    sg = T.SG[:, k, 0:256]
                    xs = T.X[:, st, c0:c0 + 256]
                    P.op("act", lambda e: e.activation(out=sg, in_=T.PG[b][:, 0:256], func=AF.Sigmoid),
                         reads=[("ps", "G", b)], writes=[("SG", k)])
                    P.op("dve", lambda e: e.tensor_tensor(out=sg, in0=sg, in1=T.PY[b][:, 0:256], op=ALU.mult),
                         reads=[("SG", k), ("ps", "Y", b)], writes=[("SG", k)])
                    P.op("dve", lambda e: e.scalar_tensor_tensor(out=xs, in0=sg, scalar=1.0 / ALPHA, in1=xs,
                                                                 op0=ALU.mult, op1=ALU.add),
                         reads=[("SG", k), ("X", st)], writes=[("X", st)])
                T.lin_tm(8, wgo, D, cons_glu, extra_w=wgg)
                T.layernorm(eps2)
                T.ffn(ffw[3][0], ffw[3][1], ffw[3][2], lng[5, :], lnb[5, :])
                T.store_x(out[t0:t0 + TT, :])
            P.finish()
            P.emit()
    return nc


def l4_params(inp, core):
    gs = slice(core * NG, (core + 1) * NG)
    two = lambda a: np.ascontiguousarray(np.concatenate([a, a], 0).astype(np.float32))
    m = dict(l4_consts())
    del m["ident"]
    m["lam_re2"] = two(inp["s5_lambda_re"][0][gs].T)
    m["lam_im2"] = two(inp["s5_lambda_im"][0][gs].T)
    m["logdt2"] = np.ascontiguousarray(np.broadcast_to(inp["s5_log_dt"][0][gs][None, :], (128, NG)).astype(np.float32))
    m["br2"] = two(inp["s5_b_re"][0][gs].transpose(1, 0, 2))
    m["bi2"] = two(inp["s5_b_im"][0][gs].transpose(1, 0, 2))
    m["cr2"] = two(inp["s5_c_re"][0][gs].transpose(2, 0, 1))
    m["ci2"] = two(inp["s5_c_im"][0][gs].transpose(2, 0, 1))
    m["d2"] = np.ascontiguousarray(np.broadcast_to(inp["s5_d"][0][core * 128:(core + 1) * 128][None, :], (128, 128)).astype(np.float32))
    return m


def fused_inputs(inp):
    ca = np.ascontiguousarray
    x = inp["x"][0]
    consts = l2_consts()
    consts["ident_f"] = np.eye(128, dtype=np.float32)
    shared = {
        "w0g": inp["ffn1_w_gate"][0], "w0u": inp["ffn1_w_up"][0], "w0d": inp["ffn1_w_down"][0],
        "w1g": inp["ffn2_w_gate"][0], "w1u": inp["ffn2_w_up"][0], "w1d": inp["ffn2_w_down"][0],
        "w2g": inp["ffn1_w_gate"][1], "w2u": inp["ffn1_w_up"][1], "w2d": inp["ffn1_w_down"][1],
        "w3g": inp["ffn2_w_gate"][1], "w3u": inp["ffn2_w_up"][1], "w3d": inp["ffn2_w_down"][1],
        "lng": ca(inp["ln_gain"].reshape(6, D)), "lnb": ca(inp["ln_bias"].reshape(6, D)),
        "wo": inp["attn_w_out"][0], "wgo": inp["s5_w_glu_out"][0], "wgg": inp["s5_w_glu_gate"][0],
    }
    shared.update(consts)
    win = inp["attn_w_in"][0]
    maps = []
    for c in range(NCORE):
        m = dict(shared)
        m["x"] = ca(x[c * TPC:(c + 1) * TPC])
        cols = np.concatenate([np.arange(128) + off + 128 * c for off in (0, 1024, 2048, 3080, 4104, 5128)]
                              + [np.array([3072 + c])])
        m["winc"] = ca(win[:, cols])
        m["wsic"] = ca(inp["s5_w_in"][0][:, 128 * c:128 * (c + 1)])
        oh = np.zeros((128, NCORE), np.float32)
        oh[:, c] = 1.0
        m["onehot"] = oh
        m["nbf"] = np.full((128, 1), inp["attn_b_f"][0, c], np.float32)
        m.update(l4_params(inp, c))
        maps.append(m)
    return maps


def kernel(**inputs):
    inp = {k: np.asarray(v) for k, v in inputs.items()}
    maps = fused_inputs(inp)
    res = _run(_get("fused", build_fused), maps)
    out = np.concatenate([np.asarray(r["out"]) for r in res], axis=0)
    return out.reshape(1, SEQ, D).astype(np.float32)
```

```python
from contextlib import ExitStack
import math
import numpy as np
import ml_dtypes

import concourse.bass as bass
import concourse.mybir as mybir
from concourse.bass_utils import run_bass_kernel_spmd

F32 = mybir.dt.float32
BF16 = mybir.dt.bfloat16
ALU = mybir.AluOpType
AF = mybir.ActivationFunctionType

D = 2048
SEQ = 16384
NCORE = 8
TPC = SEQ // NCORE
DFF = 5632
NFC = DFF // 128
HD = 128
ATT_IN = 6152
S5W = 1024
ALPHA = 4.0 ** 0.25
LN_EPS = 1e-5
TT = 512
NST = TT // 128


class Prog:
    ENGS = ("pe", "act", "dve", "pool", "sp")

    def __init__(self, nc, es, ndma=6):
        self.nc = nc
        self.ops = {e: [] for e in self.ENGS}
        self.semh = {}
        self.cnt = {e: 0 for e in self.ENGS}
        for e in self.ENGS:
            self.semh["e:" + e] = es.enter_context(nc.semaphore("se_" + e))
        self.nd = ndma
        self.dcnt = {}
        self.dnext = {"sp": 0, "pool": 0}
        for q in ("sp", "pool"):
            for k in range(ndma):
                self.semh[f"d:{q}:{k}"] = es.enter_context(nc.semaphore(f"sd_{q}{k}"))
                self.dcnt[(q, k)] = 0
        self.known = {e: {} for e in self.ENGS}
        self.res = {}
        self.semh["e:cc"] = es.enter_context(nc.semaphore("se_cc"))
        self.ccn = 0

    def collective(self, groups, src, dst, reads=(), writes=()):
        waits = self._collect("pool", reads, writes)
        self.ccn += 1
        ev = ("e:cc", self.ccn)
        self.ops["pool"].append((waits, (lambda e, g=groups, a=src, b=dst: e.collective_compute(
            "AllGather", ALU.bypass, replica_groups=g, ins=[a], outs=[b])), "e:cc", 1))
        self._commit(ev, reads, writes)

    def barrier(self):
        allw = []
        for e in self.ENGS:
            if self.cnt[e]:
                allw.append(("e:" + e, self.cnt[e]))
        for (q, k), c in self.dcnt.items():
            if c:
                allw.append((f"d:{q}:{k}", c * 16))
        for e in self.ENGS:
            kn = self.known[e]
            waits = []
            for sk, v in allw:
                if sk == "e:" + e:
                    continue
                if kn.get(sk, 0) < v:
                    kn[sk] = v
                    waits.append((sk, v))
            self.ops[e].append((waits, None, None, 0))

    def _collect(self, eng, reads, writes):
        deps = {}
        own = "e:" + eng

        def add(ev, same_ok):
            if ev is None:
                return
            sk, v = ev
            if sk == own and not same_ok:
                return
            if deps.get(sk, 0) < v:
                deps[sk] = v

        for r in reads:
            st = self.res.get(r)
            if st is not None and st[0] is not None:
                add(st[0], True)
        for w in writes:
            st = self.res.get(w)
            if st is not None:
                add(st[0], False)
                for ev in st[1].values():
                    add(ev, False)
        waits = []
        kn = self.known[eng]
        for sk, v in deps.items():
            if kn.get(sk, 0) < v:
                kn[sk] = v
                waits.append((sk, v))
        return waits

    def _commit(self, ev, reads, writes):
        sk = ev[0]
        for r in reads:
            st = self.res.get(r)
            if st is None:
                st = [None, {}]
                self.res[r] = st
            old = st[1].get(sk)
            if old is None or old[1] < ev[1]:
                st[1][sk] = ev
        for w in writes:
            self.res[w] = [ev, {}]

    def op(self, eng, fn, reads=(), writes=(), inc=True):
        waits = self._collect(eng, reads, writes)
        if inc:
            self.cnt[eng] += 1
            ev = ("e:" + eng, self.cnt[eng])
        else:
            ev = ("e:" + eng, self.cnt[eng] + 1)
        self.ops[eng].append((waits, fn, ("e:" + eng) if inc else None, 1))
        self._commit(ev, reads, writes)

    def dma(self, q, out, in_, reads=(), writes=()):
        k = self.dnext[q] % self.nd
        self.dnext[q] += 1
        sk = f"d:{q}:{k}"
        waits = self._collect(q, reads, writes)
        prev = self.dcnt[(q, k)] * 16
        if prev and self.known[q].get(sk, 0) < prev:
            self.known[q][sk] = prev
            waits.append((sk, prev))
        self.dcnt[(q, k)] += 1
        ev = (sk, self.dcnt[(q, k)] * 16)
        self.ops[q].append((waits, (lambda e, o=out, i=in_: e.dma_start(out=o, in_=i)), sk, 16))
        self._commit(ev, reads, writes)

    def finish(self):
        waits = []
        for e in self.ENGS:
            if self.cnt[e]:
                waits.append(("e:" + e, self.cnt[e]))
        for (q, k), c in self.dcnt.items():
            if c:
                waits.append((f"d:{q}:{k}", c * 16))
        if self.ccn:
            waits.append(("e:cc", self.ccn))
        self.ops["sp"].append((waits, None, None, 0))

    def emit(self):
        nc = self.nc
        with nc.Block() as block:
            def mk(name):
                def body(e):
                    for waits, fn, sk, n in self.ops[name]:
                        for (wsk, v) in waits:
                            e.wait_ge(self.semh[wsk], v)
                        if fn is None:
                            continue
                        ins = fn(e)
                        if sk is not None:
                            ins.then_inc(self.semh[sk], n)
                return body
            block.tensor(mk("pe"))
            block.scalar(mk("act"))
            block.vector(mk("dve"))
            block.gpsimd(mk("pool"))
            block.sync(mk("sp"))
        self.ops = {e: [] for e in self.ENGS}


def mm(out, lhsT, rhs, start, stop):
    return lambda e: e.matmul(out, lhsT, rhs, start=start, stop=stop)


class TokPipe:
    def __init__(self, nc, es, P, ident_dram, tag=""):
        self.nc, self.P = nc, P
        sb = lambda n, s, d: es.enter_context(nc.sbuf_tensor(n + tag, s, d))
        ps = lambda n: es.enter_context(nc.psum_tensor(n + tag, [128, 512], F32))
        self.X = sb("X", [128, NST, D], F32)
        self.XT = sb("XT", [128, 16, TT], BF16)
        self.HT = sb("HT", [128, NFC, TT], BF16)
        self.WR = sb("WR", [128, 3, 11264], BF16)
        self.G = sb("G", [128, D], F32)
        self.B = sb("B", [128, D], F32)
        self.XB = sb("XB", [128, 2, D], BF16)
        self.SG = sb("SG", [128, 2, TT], F32)
        self.OT = sb("OT", [128, 2, TT], BF16)
        self.OF = sb("OF", [128, 2, TT], F32)
        self.stats = sb("stats", [128, NST, 24], F32)
        self.mv = sb("mv", [128, NST, 2], F32)
        self.rstd = sb("rstd", [128, NST, 1], F32)
        self.nmr = sb("nmr", [128, NST, 1], F32)
        self.ident = sb("ident_sb", [128, 128], BF16)
        self.PG = [ps("PG0"), ps("PG1")]
        self.PU = [ps("PU0"), ps("PU1")]
        self.PY = [ps("PY0"), ps("PY1")]
        self.PT = [ps("PT0"), ps("PT1")]
        self.ring = 0
        self.ot = 0
        P.dma("sp", self.ident[:, :], ident_dram, writes=[("ident",)])

    def ring_next(self):
        s = self.ring % 3
        self.ring += 1
        return s

    def load_x(self, x_rows):
        P = self.P
        P.dma("sp", self.X[:, :, :], x_rows.rearrange("(s p) d -> p s d", p=128),
              writes=[("X", st) for st in range(NST)])
        for st in range(NST):
            self.make_xt(st)

    def load_x_only(self, x_rows):
        P = self.P
        P.dma("sp", self.X[:, :, :], x_rows.rearrange("(s p) d -> p s d", p=128),
              writes=[("X", st) for st in range(NST)])

    def load_xt(self, xt_rows, nk):
        P = self.P
        P.dma("sp", self.XT[:, 0:nk, :], xt_rows.rearrange("(k p) t -> p k t", p=128),
              writes=[("XT", st) for st in range(NST)])

    def store_x(self, out_rows):
        P = self.P
        P.dma("sp", out_rows.rearrange("(s p) d -> p s d", p=128), self.X[:, :, :],
              reads=[("X", st) for st in range(NST)])

    def make_xt(self, st):
        P = self.P
        b = st % 2
        X, XB, XT, PT, ident = self.X, self.XB, self.XT, self.PT, self.ident
        P.op("act", lambda e: e.activation(out=XB[:, b, :], in_=X[:, st, :], func=AF.Copy),
             reads=[("X", st)], writes=[("XB", b)])
        for kg in range(4):
            pb = (st * 4 + kg) % 2
            for j in range(4):
                kc = kg * 4 + j
                P.op("pe", mm(PT[pb][:, j * 128:(j + 1) * 128], XB[:, b, kc * 128:(kc + 1) * 128],
                              ident[:, :], True, True),
                     reads=[("XB", b), ("ident",)], writes=[("ps", "T", pb)], inc=(j == 3))
            src = PT[pb][:, :].rearrange("p (j t) -> p j t", j=4)
            dst = XT[:, kg * 4:(kg + 1) * 4, st * 128:(st + 1) * 128]
            P.op("dve", lambda e, s=src, d=dst: e.tensor_copy(out=d, in_=s),
                 reads=[("ps", "T", pb)], writes=[("XT", st)])

    def load_ln(self, g_row, b_row):
        P = self.P
        P.dma("sp", self.G[:, :], g_row.partition_broadcast(128), writes=[("G",)])
        P.dma("sp", self.B[:, :], b_row.partition_broadcast(128), writes=[("B",)])

    def layernorm(self, eps, make_xt=True):
        P = self.P
        X, G, B = self.X, self.G, self.B
        stats, mv, rstd, nmr = self.stats, self.mv, self.rstd, self.nmr
        for st in range(NST):
            for q in range(4):
                P.op("dve", lambda e, st=st, q=q: e.bn_stats(out=stats[:, st, q * 6:(q + 1) * 6],
                                                             in_=X[:, st, q * 512:(q + 1) * 512]),
                     reads=[("X", st)], writes=[("stats", st)])
            P.op("dve", lambda e, st=st: e.bn_aggr(out=mv[:, st, :], in_=stats[:, st, :]),
                 reads=[("stats", st)], writes=[("mv", st)])
            P.op("dve", lambda e, st=st: e.tensor_scalar(out=rstd[:, st, :], in0=mv[:, st, 1:2],
                                                         scalar1=eps, scalar2=None, op0=ALU.add),
                 reads=[("mv", st)], writes=[("rstd", st)])
            P.op("act", lambda e, st=st: e.activation(out=rstd[:, st, :], in_=rstd[:, st, :], func=AF.Ln),
                 reads=[("rstd", st)], writes=[("rstd", st)])
            P.op("act", lambda e, st=st: e.activation(out=rstd[:, st, :], in_=rstd[:, st, :], func=AF.Exp,
                                                      scale=-0.5),
                 reads=[("rstd", st)], writes=[("rstd", st)])
            P.op("dve", lambda e, st=st: e.scalar_tensor_tensor(out=nmr[:, st, :], in0=mv[:, st, 0:1],
                                                                scalar=-1.0, in1=rstd[:, st, :],
                                                                op0=ALU.mult, op1=ALU.mult),
                 reads=[("mv", st), ("rstd", st)], writes=[("nmr", st)])
            P.op("act", lambda e, st=st: e.activation(out=X[:, st, :], in_=X[:, st, :], func=AF.Identity,
                                                      bias=nmr[:, st, :], scale=rstd[:, st, :]),
                 reads=[("X", st), ("rstd", st), ("nmr", st)], writes=[("X", st)])
            P.op("dve", lambda e, st=st: e.tensor_tensor(out=X[:, st, :], in0=X[:, st, :], in1=G[:, :],
                                                         op=ALU.mult),
                 reads=[("X", st), ("G",)], writes=[("X", st)])
            P.op("pool", lambda e, st=st: e.tensor_tensor(out=X[:, st, :], in0=X[:, st, :], in1=B[:, :],
                                                          op=ALU.add),
                 reads=[("X", st), ("B",)], writes=[("X", st)])
            if make_xt:
                self.make_xt(st)

    def ffn(self, wg, wu, wd, g_row, b_row):
        P = self.P
        XT, HT, WR, SG, X = self.XT, self.HT, self.WR, self.SG, self.X
        PG, PU, PY = self.PG, self.PU, self.PY
        self.load_ln(g_row, b_row)
        wgv = wg.rearrange("(k p) n -> p k n", p=128)
        wuv = wu.rearrange("(k p) n -> p k n", p=128)
        wdv = wd.rearrange("(k p) n -> p k n", p=128)
        xt_keys = [("XT", st) for st in range(NST)]
        for fg in range(NFC // 2):
            s = self.ring_next()
            gv = WR[:, s, 0:4096].rearrange("p (k n) -> p k n", k=16)
            uv = WR[:, s, 4096:8192].rearrange("p (k n) -> p k n", k=16)
            P.dma("pool", gv, wgv[:, :, fg * 256:(fg + 1) * 256], writes=[("WR", s, "a")])
            P.dma("pool", uv, wuv[:, :, fg * 256:(fg + 1) * 256], writes=[("WR", s, "b")])
            for fc in range(2):
                ch = fg * 2 + fc
                b = ch % 2
                for kc in range(16):
                    P.op("pe", mm(PG[b][:, :], gv[:, kc, fc * 128:(fc + 1) * 128], XT[:, kc, :],
                                  kc == 0, kc == 15),
                         reads=[("WR", s, "a")] + xt_keys, writes=[("ps", "G", b)], inc=(kc == 15))
                for kc in range(16):
                    P.op("pe", mm(PU[b][:, :], uv[:, kc, fc * 128:(fc + 1) * 128], XT[:, kc, :],
                                  kc == 0, kc == 15),
                         reads=[("WR", s, "b")] + xt_keys, writes=[("ps", "U", b)], inc=(kc == 15))
                P.op("act", lambda e, b=b: e.activation(out=SG[:, b, :], in_=PG[b][:, :], func=AF.Silu),
                     reads=[("ps", "G", b)], writes=[("SG", b)])
                P.op("dve", lambda e, b=b, ch=ch: e.tensor_tensor(out=HT[:, ch, :], in0=SG[:, b, :],
                                                                  in1=PU[b][:, :], op=ALU.mult),
                     reads=[("SG", b), ("ps", "U", b)], writes=[("HT", ch)])
        coef = 0.5 / ALPHA
        ht_keys = [("HT", ch) for ch in range(NFC)]
        it = 0
        for dp in range(D // 256):
            s = self.ring_next()
            dv = WR[:, s, 0:NFC * 256].rearrange("p (k n) -> p k n", k=NFC)
            P.dma("pool", dv, wdv[:, :, dp * 256:(dp + 1) * 256],
                  writes=[("WR", s, "a"), ("WR", s, "b")])
            for st in range(NST):
                b = it % 2
                it += 1
                for fc in range(NFC):
                    P.op("pe", mm(PY[b][:, 0:256], HT[:, fc, st * 128:(st + 1) * 128], dv[:, fc, :],
                                  fc == 0, fc == NFC - 1),
                         reads=[("WR", s, "a"), ("WR", s, "b")] + (ht_keys if fc == 0 else []),
                         writes=[("ps", "Y", b)], inc=(fc == NFC - 1))
                xs = X[:, st, dp * 256:(dp + 1) * 256]
                P.op("dve", lambda e, b=b, xs=xs: e.scalar_tensor_tensor(out=xs, in0=PY[b][:, 0:256],
                                                                         scalar=coef, in1=xs,
                                                                         op0=ALU.mult, op1=ALU.add),
                     reads=[("ps", "Y", b), ("X", st)], writes=[("X", st)])
        self.layernorm(LN_EPS / (ALPHA * ALPHA))

    def lin_tm(self, nk, w, ncols, consumer, extra_w=None):
        P = self.P
        XT, WR, PY, PG = self.XT, self.WR, self.PY, self.PG
        wv = w.rearrange("(k p) n -> p k n", p=128)
        wv2 = extra_w.rearrange("(k p) n -> p k n", p=128) if extra_w is not None else None
        xt_keys = [("XT", st) for st in range(NST)]
        it = 0
        for dp in range(ncols // 256):
            s = self.ring_next()
            dv = WR[:, s, 0:nk * 256].rearrange("p (k n) -> p k n", k=nk)
            P.dma("pool", dv, wv[:, :, dp * 256:(dp + 1) * 256], writes=[("WR", s, "a")])
            if wv2 is not None:
                dv2 = WR[:, s, 5632:5632 + nk * 256].rearrange("p (k n) -> p k n", k=nk)
                P.dma("pool", dv2, wv2[:, :, dp * 256:(dp + 1) * 256], writes=[("WR", s, "b")])
            for st in range(NST):
                b = it % 2
                it += 1
                for kc in range(nk):
                    P.op("pe", mm(PY[b][:, 0:256], XT[:, kc, st * 128:(st + 1) * 128], dv[:, kc, :],
                                  kc == 0, kc == nk - 1),
                         reads=[("WR", s, "a")] + xt_keys, writes=[("ps", "Y", b)], inc=(kc == nk - 1))
                if wv2 is not None:
                    for kc in range(nk):
                        P.op("pe", mm(PG[b][:, 0:256], XT[:, kc, st * 128:(st + 1) * 128], dv2[:, kc, :],
                                      kc == 0, kc == nk - 1),
                             reads=[("WR", s, "b")] + xt_keys, writes=[("ps", "G", b)],
                             inc=(kc == nk - 1))
                    consumer(st, dp * 256, b)
                else:
                    consumer(st, dp * 256, b)

    def lin_fm(self, w, pieces, consumer):
        P = self.P
        XT, WR, PG = self.XT, self.WR, self.PG
        wv = w.rearrange("(k p) n -> p k n", p=128)
        xt_keys = [("XT", st) for st in range(NST)]
        it = 0
        for (c0, ncol) in pieces:
            s = self.ring_next()
            dv = WR[:, s, 0:16 * ncol].rearrange("p (k n) -> p k n", k=16)
            P.dma("pool", dv, wv[:, :, c0:c0 + ncol], writes=[("WR", s, "a")])
            off = 0
            while off < ncol:
                wdt = min(128, ncol - off)
                b = it % 2
                it += 1
                for kc in range(16):
                    P.op("pe", mm(PG[b][0:wdt, :], dv[:, kc, off:off + wdt], XT[:, kc, :],
                                  kc == 0, kc == 15),
                         reads=[("WR", s, "a")] + xt_keys, writes=[("ps", "G", b)], inc=(kc == 15))
                consumer(c0 + off, wdt, b)
                off += wdt


def _new_nc():
    return bass.Bass("TRN2", target_bir_lowering=False)


def _ident_np():
    return np.eye(128, dtype=np.float32).astype(ml_dtypes.bfloat16)


def build_l1(ntok=TPC):
    nc = _new_nc()
    dt = lambda n, s, d, k: nc.dram_tensor(n, s, d, kind=k).ap()
    x = dt("x", [ntok, D], F32, "ExternalInput")
    wg = dt("wg", [D, DFF], F32, "ExternalInput")
    wu = dt("wu", [D, DFF], F32, "ExternalInput")
    wd = dt("wd", [DFF, D], F32, "ExternalInput")
    lng = dt("lng", [1, D], F32, "ExternalInput")
    lnb = dt("lnb", [1, D], F32, "ExternalInput")
    win = dt("win", [D, ATT_IN], F32, "ExternalInput")
    ident = dt("ident", [128, 128], BF16, "ExternalInput")
    x1 = dt("x1", [ntok, D], F32, "ExternalOutput")
    projT = dt("projT", [6144, ntok], BF16, "ExternalOutput")
    fT = dt("fT", [8, ntok], F32, "ExternalOutput")
    with ExitStack() as es:
        P = Prog(nc, es)
        T = TokPipe(nc, es, P, ident)
        pieces = [(c, 256) for c in range(0, 3072, 256)] + [(3072, 8)] + \
                 [(c, 256) for c in range(3080, 6152, 256)]
        for t in range(ntok // TT):
            t0 = t * TT
            T.load_x(x[t0:t0 + TT, :])
            T.ffn(wg, wu, wd, lng[0, :], lnb[0, :])
            T.store_x(x1[t0:t0 + TT, :])

            def cons(c0, wdt, b, t0=t0):
                k = T.ot % 2
                T.ot += 1
                if wdt == 8:
                    P.op("act", lambda e: e.activation(out=T.OF[0:8, k, :], in_=T.PG[b][0:8, :], func=AF.Copy),
                         reads=[("ps", "G", b)], writes=[("OF", k)])
                    P.dma("sp", fT[:, t0:t0 + TT], T.OF[0:8, k, :], reads=[("OF", k)])
                else:
                    r0 = c0 if c0 < 3072 else c0 - 8
                    P.op("act", lambda e: e.activation(out=T.OT[0:wdt, k, :], in_=T.PG[b][0:wdt, :],
                                                       func=AF.Copy),
                         reads=[("ps", "G", b)], writes=[("OT", k)])
                    P.dma("sp", projT[r0:r0 + wdt, t0:t0 + TT], T.OT[0:wdt, k, :], reads=[("OT", k)])
            T.lin_fm(win, pieces, cons)
        P.finish()
        P.emit()
    return nc


def build_l3(ntok=TPC):
    nc = _new_nc()
    dt = lambda n, s, d, k="ExternalInput": nc.dram_tensor(n, s, d, kind=k).ap()
    x1 = dt("x1", [ntok, D], F32)
    yT = dt("yT", [D, ntok], BF16)
    wo = dt("wo", [D, D], F32)
    w2 = [dt("w2g", [D, DFF], F32), dt("w2u", [D, DFF], F32), dt("w2d", [DFF, D], F32)]
    w3 = [dt("w3g", [D, DFF], F32), dt("w3u", [D, DFF], F32), dt("w3d", [DFF, D], F32)]
    lng = dt("lng", [3, D], F32)
    lnb = dt("lnb", [3, D], F32)
    wsi = dt("wsi", [D, S5W], F32)
    ident = dt("ident", [128, 128], BF16)
    x3 = dt("x3", [ntok, D], F32, "ExternalOutput")
    u = dt("u", [ntok, S5W], F32, "ExternalOutput")
    with ExitStack() as es:
        P = Prog(nc, es)
        T = TokPipe(nc, es, P, ident)
        for t in range(ntok // TT):
            t0 = t * TT
            T.load_x_only(x1[t0:t0 + TT, :])
            T.load_xt(yT[:, t0:t0 + TT], 16)
            T.load_ln(lng[0, :], lnb[0, :])

            def cons_res(st, c0, b):
                xs = T.X[:, st, c0:c0 + 256]
                P.op("dve", lambda e: e.scalar_tensor_tensor(out=xs, in0=T.PY[b][:, 0:256], scalar=1.0 / ALPHA,
                                                             in1=xs, op0=ALU.mult, op1=ALU.add),
                     reads=[("ps", "Y", b), ("X", st)], writes=[("X", st)])
            T.lin_tm(16, wo, D, cons_res)
            T.layernorm(LN_EPS / (ALPHA * ALPHA))
            T.ffn(w2[0], w2[1], w2[2], lng[1, :], lnb[1, :])
            T.ffn(w3[0], w3[1], w3[2], lng[2, :], lnb[2, :])
            T.store_x(x3[t0:t0 + TT, :])

            def cons_u(st, c0, b, t0=t0):
                k = T.ot % 2
                T.ot += 1
                P.op("act", lambda e: e.activation(out=T.OF[:, k, 0:256], in_=T.PY[b][:, 0:256], func=AF.Copy),
                     reads=[("ps", "Y", b)], writes=[("OF", k)])
                P.dma("sp", u[t0 + st * 128:t0 + (st + 1) * 128, c0:c0 + 256], T.OF[:, k, 0:256],
                      reads=[("OF", k)])
            T.lin_tm(16, wsi, S5W, cons_u)
        P.finish()
        P.emit()
    return nc


def build_l5(ntok=TPC):
    nc = _new_nc()
    dt = lambda n, s, d, k="ExternalInput": nc.dram_tensor(n, s, d, kind=k).ap()
    x3 = dt("x3", [ntok, D], F32)
    zT = dt("zT", [S5W, ntok], BF16)
    wgo = dt("wgo", [S5W, D], F32)
    wgg = dt("wgg", [S5W, D], F32)
    w4 = [dt("w4g", [D, DFF], F32), dt("w4u", [D, DFF], F32), dt("w4d", [DFF, D], F32)]
    lng = dt("lng", [2, D], F32)
    lnb = dt("lnb", [2, D], F32)
    ident = dt("ident", [128, 128], BF16)
    out = dt("out", [ntok, D], F32, "ExternalOutput")
    with ExitStack() as es:
        P = Prog(nc, es)
        T = TokPipe(nc, es, P, ident)
        for t in range(ntok // TT):
            t0 = t * TT
            T.load_x_only(x3[t0:t0 + TT, :])
            T.load_xt(zT[:, t0:t0 + TT], 8)
            T.load_ln(lng[0, :], lnb[0, :])

            def cons_glu(st, c0, b):
                k = T.ot % 2
                T.ot += 1
                sg = T.SG[:, k, 0:256]
                xs = T.X[:, st, c0:c0 + 256]
                P.op("act", lambda e: e.activation(out=sg, in_=T.PG[b][:, 0:256], func=AF.Sigmoid),
                     reads=[("ps", "G", b)], writes=[("SG", k)])
                P.op("dve", lambda e: e.tensor_tensor(out=sg, in0=sg, in1=T.PY[b][:, 0:256], op=ALU.mult),
                     reads=[("SG", k), ("ps", "Y", b)], writes=[("SG", k)])
                P.op("dve", lambda e: e.scalar_tensor_tensor(out=xs, in0=sg, scalar=1.0 / ALPHA, in1=xs,
                                                             op0=ALU.mult, op1=ALU.add),
                     reads=[("SG", k), ("X", st)], writes=[("X", st)])
            T.lin_tm(8, wgo, D, cons_glu, extra_w=wgg)
            T.layernorm(LN_EPS / (ALPHA * ALPHA))
            T.ffn(w4[0], w4[1], w4[2], lng[1, :], lnb[1, :])
            T.store_x(out[t0:t0 + TT, :])
        P.finish()
        P.emit()
    return nc


DIL = ((1, 16), (4, 4), (16, 1))
NEG = -30000.0


def l2_consts():
    p = np.arange(128)[:, None]
    q = np.arange(512)[None, :]
    fm = np.zeros((128, 4, 512), np.float32)
    for jj in range(4):
        fm[:, jj, :] = np.where(jj * 128 + p > q, NEG, 0.0)
    q1 = np.arange(128)[None, :]
    dm = np.zeros((128, 256), np.float32)
    dm[:, 0:128] = np.where(p < q1, NEG, 0.0)
    dm[:, 128:256] = np.where(p > q1, NEG, 0.0)
    uincl = (p <= q1).astype(np.float32)
    lstrict = (p < q1).astype(np.float32)
    bf = ml_dtypes.bfloat16
    return {
        "ident": _ident_np(), "ones_bf": np.ones((128, 128), bf), "fmask": fm.astype(bf),
        "dmask": dm.astype(bf), "uincl": uincl, "lstrict": lstrict, "ones_f": np.ones((128, 128), np.float32),
    }


def build_l2(seq=SEQ):
    nc = _new_nc()
    dt = lambda n, s, d, k="ExternalInput": nc.dram_tensor(n, s, d, kind=k).ap()
    nblk = seq // 128
    qf = dt("qf", [128, seq], BF16)
    kf = dt("kf", [128, seq], BF16)
    vf = dt("vf", [seq, 128], BF16)
    f2T = dt("f2T", [128, nblk], F32)
    nbf = dt("nbf", [128, 1], F32)
    qd = dt("qd", [128, seq], BF16)
    kd = dt("kd", [128, seq], BF16)
    vdp = [dt(f"vd{d}", [seq, 128], BF16) for (d, _) in DIL]
    c_ident = dt("ident", [128, 128], BF16)
    c_ones = dt("ones_bf", [128, 128], BF16)
    c_fm = dt("fmask", [128, 4, 512], BF16)
    c_dm = dt("dmask", [128, 256], BF16)
    c_ui = dt("uincl", [128, 128], F32)
    c_ls = dt("lstrict", [128, 128], F32)
    c_of = dt("ones_f", [128, 128], F32)
    yf = dt("yf", [128, seq], BF16, "ExternalOutput")
    yd = dt("yd", [128, seq], BF16, "ExternalOutput")
    scr = nc.dram_tensor("scr", [6, seq], BF16).ap()
    scale = HD ** -0.5
    with ExitStack() as es:
        P = Prog(nc, es)
        sb = lambda n, s, d: es.enter_context(nc.sbuf_tensor(n, s, d))
        ps = lambda n: es.enter_context(nc.psum_tensor(n, [128, 512], F32))
        BIG = [sb(f"BIG{i}", [128, seq], BF16) for i in range(4)]
        ident = sb("identb", [128, 128], BF16)
        ones = sb("onesb", [128, 128], BF16)
        FM = sb("FM", [128, 4, 512], BF16)
        DM = sb("DM", [128, 256], BF16)
        UI = sb("UI", [128, 128], F32)
        LS = sb("LS", [128, 128], F32)
        OFc = sb("OFc", [128, 128], F32)
        NB = sb("NB", [128, 1], F32)
        LF = sb("LF", [128, nblk], F32)
        RB = sb("RB", [128, 128], F32)
        C = sb("C", [128, 128], F32)
        R1 = sb("R1", [128, 128], F32)
        HI = sb("HI", [128, 6, 128], BF16)
        QF = sb("QF", [128, 2, 512], BF16)
        PTt = sb("PTt", [128, 3, 512], BF16)
        RL = sb("RL", [128, 512], F32)
        YO = sb("YO", [128, 2, 512], BF16)
        VU = sb("VU", [128, 4, 2, 128], BF16)
        ACC = sb("ACC", [128, 2, 2048], F32)
        YW = sb("YW", [128, 2048], BF16)
        PS = [ps("PS0"), ps("PS1")]
        PO = [ps("PO0"), ps("PO1")]
        PL = [ps("PL0"), ps("PL1")]
        for (t, src, key) in ((ident, c_ident, "ident"), (ones, c_ones, "ones"), (FM, c_fm, "FM"),
                              (DM, c_dm, "DM"), (UI, c_ui, "UI"), (LS, c_ls, "LS"), (OFc, c_of, "OFc"),
                              (NB, nbf, "NB"), (LF, f2T, "LF")):
            idx = (slice(None),) * len(t.shape)
            P.dma("sp", t[idx], src, writes=[(key,)])
        P.op("dve", lambda e: e.tensor_scalar(out=NB[:, :], in0=NB[:, :], scalar1=-1.0, scalar2=None, op0=ALU.mult),
             reads=[("NB",)], writes=[("NB",)])
        KT, VK, AK, AQ = BIG[0], BIG[1], BIG[2], BIG[3]
        P.dma("sp", KT[:, :], kf, writes=[("BIG", 0)])
        P.dma("sp", VK[:, :].rearrange("p (b d) -> p b d", d=128), vf.rearrange("(b p) d -> p b d", p=128),
              writes=[("BIG", 1)])
        P.op("act", lambda e: e.activation(out=LF[:, :], in_=LF[:, :], func=AF.Exp, bias=NB[:, :], scale=-1.0),
             reads=[("LF",), ("NB",)], writes=[("LF",)])
        P.op("act", lambda e: e.activation(out=LF[:, :], in_=LF[:, :], func=AF.Ln, bias=OFc[:, 0:1], scale=1.0),
             reads=[("LF",), ("OFc",)], writes=[("LF",)])
        P.op("dve", lambda e: e.tensor_scalar(out=LF[:, :], in0=LF[:, :], scalar1=-1.0, scalar2=None, op0=ALU.mult),
             reads=[("LF",)], writes=[("LF",)])
        P.op("pe", mm(PS[0][0:nblk, 0:128], LF[:, :], OFc[:, :], True, True),
             reads=[("LF",), ("OFc",)], writes=[("ps", "S", 0)])
        P.op("dve", lambda e: e.tensor_copy(out=RB[0:nblk, :], in_=PS[0][0:nblk, 0:128]),
             reads=[("ps", "S", 0)], writes=[("RB",)])
        P.op("pe", mm(PS[1][0:nblk, 0:128], LF[:, :], UI[:, :], True, False),
             reads=[("LF",), ("UI",)], writes=[("ps", "S", 1)], inc=False)
        P.op("pe", mm(PS[1][0:nblk, 0:128], LS[0:nblk, 0:nblk], RB[0:nblk, :], False, True),
             reads=[("RB",), ("LS",)], writes=[("ps", "S", 1)])
        P.op("dve", lambda e: e.tensor_scalar(out=C[0:nblk, :], in0=PS[1][0:nblk, 0:128], scalar1=1.0 / scale,
                                              scalar2=None, op0=ALU.mult),
             reads=[("ps", "S", 1)], writes=[("C",)])
        cur = C
        for i in range(3):
            P.op("dve", lambda e, i=i, cur=cur: e.tensor_copy(out=HI[0:nblk, 3 + i, :], in_=cur[0:nblk, :]),
                 reads=[("C",), ("R1",)], writes=[("HI", 3 + i)])
            P.op("dve", lambda e, i=i: e.tensor_scalar(out=HI[0:nblk, i, :], in0=HI[0:nblk, 3 + i, :],
                                                       scalar1=-1.0, scalar2=None, op0=ALU.mult),
                 reads=[("HI", 3 + i)], writes=[("HI", i)])
            if i < 2:
                P.op("dve", lambda e, i=i, cur=cur: e.tensor_tensor(out=R1[0:nblk, :], in0=cur[0:nblk, :],
                                                                    in1=HI[0:nblk, 3 + i, :], op=ALU.subtract),
                     reads=[("C",), ("R1",), ("HI", 3 + i)], writes=[("R1",)])
                cur = R1
        for i in range(6):
            P.dma("sp", scr[i, :].rearrange("(p j) -> p j", j=128), HI[0:nblk, i, :],
                  reads=[("HI", i)], writes=[("scr", i)])
        P.op("pool", lambda e: e.memset(AK[0:6, :], 1.0), writes=[("BIG", 2)])
        P.op("pool", lambda e: e.memset(AQ[0:6, :], 1.0), writes=[("BIG", 3)])
        P.dma("sp", AK[0:3, :], scr[0:3, :], reads=[("scr", i) for i in range(3)], writes=[("BIG", 2)])
        P.dma("sp", AQ[3:6, :], scr[3:6, :], reads=[("scr", 3 + i) for i in range(3)], writes=[("BIG", 3)])
        pt = 0
        for i in range(seq // 512):
            qb = i % 2
            P.dma("sp", QF[:, qb, :], qf[:, i * 512:(i + 1) * 512], writes=[("QF", qb)])
            nj = 4 * i + 4
            for j in range(nj):
                sbk = j % 2
                diag = j >= 4 * i
                P.op("pe", mm(PS[sbk][:, :], KT[:, j * 128:(j + 1) * 128], QF[:, qb, :], True, False),
                     reads=[("BIG", 0), ("QF", qb)], writes=[("ps", "S", sbk)], inc=False)
                P.op("pe", mm(PS[sbk][:, :], AK[0:6, j * 128:(j + 1) * 128], AQ[0:6, i * 512:(i + 1) * 512],
                              False, not diag),
                     reads=[("BIG", 2), ("BIG", 3)], writes=[("ps", "S", sbk)], inc=not diag)
                if diag:
                    P.op("pe", mm(PS[sbk][:, :], ident[:, :], FM[:, j - 4 * i, :], False, True),
                         reads=[("ident",), ("FM",)], writes=[("ps", "S", sbk)])
                pk = pt % 3
                pt += 1
                P.op("act", lambda e, sbk=sbk, pk=pk: e.activation(out=PTt[:, pk, :], in_=PS[sbk][:, :],
                                                                   func=AF.Exp, scale=scale),
                     reads=[("ps", "S", sbk)], writes=[("PT", pk)])
                P.op("pe", mm(PO[qb][:, :], VK[:, j * 128:(j + 1) * 128], PTt[:, pk, :], j == 0, j == nj - 1),
                     reads=[("BIG", 1), ("PT", pk)], writes=[("ps", "O", qb)], inc=False)
                P.op("pe", mm(PL[qb][:, :], ones[:, :], PTt[:, pk, :], j == 0, j == nj - 1),
                     reads=[("ones",), ("PT", pk)], writes=[("ps", "L", qb)])
            P.op("dve", lambda e, qb=qb: e.reciprocal(out=RL[:, :], in_=PL[qb][:, :]),
                 reads=[("ps", "L", qb)], writes=[("RL",)])
            P.op("dve", lambda e, qb=qb: e.tensor_tensor(out=YO[:, qb, :], in0=RL[:, :], in1=PO[qb][:, :],
                                                         op=ALU.mult),
                 reads=[("RL",), ("ps", "O", qb)], writes=[("YO", qb)])
            P.dma("sp", yf[:, i * 512:(i + 1) * 512], YO[:, qb, :], reads=[("YO", qb)])
        QD, KD = BIG[0], BIG[1]
        P.dma("sp", QD[:, :], qd, writes=[("BIG", 0)])
        P.dma("sp", KD[:, :], kd, writes=[("BIG", 1)])
        un = 0
        for w in range(seq // 2048):
            for bi, (d, nb) in enumerate(DIL):
                nbt = seq // (128 * d)
                qv = QD[:, :].rearrange("p (l r) -> p r l", r=d)
                kv = KD[:, :].rearrange("p (l r) -> p r l", r=d)
                ao = ACC[:, 0, :].rearrange("p (l r) -> p r l", r=d)
                al = ACC[:, 1, :].rearrange("p (l r) -> p r l", r=d)
                for r in range(d):
                    for bl in range(nb):
                        Bg = nb * w + bl
                        hp = Bg > 0
                        k = un % 4
                        sbk = un % 2
                        un += 1
                        row0 = (r * nbt + Bg) * 128
                        if hp:
                            P.dma("sp", VU[:, k, :, :],
                                  vdp[bi][row0 - 128:row0 + 128, :].rearrange("(t p) e -> p t e", p=128),
                                  writes=[("VU", k)])
                        else:
                            P.dma("sp", VU[:, k, 1, :], vdp[bi][row0:row0 + 128, :], writes=[("VU", k)])
                        qs = qv[:, r, Bg * 128:(Bg + 1) * 128]
                        lo = 0 if hp else 128
                        P.op("pe", mm(PS[sbk][:, lo:256], ident[:, :], DM[:, lo:256], True, False),
                             reads=[("ident",), ("DM",)], writes=[("ps", "S", sbk)], inc=False)
                        if hp:
                            P.op("pe", mm(PS[sbk][:, 0:128], kv[:, r, (Bg - 1) * 128:Bg * 128], qs, False, False),
                                 reads=[("BIG", 0), ("BIG", 1)], writes=[("ps", "S", sbk)], inc=False)
                        P.op("pe", mm(PS[sbk][:, 128:256], kv[:, r, Bg * 128:(Bg + 1) * 128], qs, False, True),
                             reads=[("BIG", 0), ("BIG", 1)], writes=[("ps", "S", sbk)])
                        pk = pt % 3
                        pt += 1
                        P.op("act", lambda e, sbk=sbk, pk=pk, lo=lo: e.activation(
                            out=PTt[:, pk, lo:256], in_=PS[sbk][:, lo:256], func=AF.Exp, scale=scale),
                             reads=[("ps", "S", sbk)], writes=[("PT", pk)])
                        if hp:
                            P.op("pe", mm(PO[sbk][:, 0:128], VU[:, k, 0, :], PTt[:, pk, 0:128], True, False),
                                 reads=[("VU", k), ("PT", pk)], writes=[("ps", "O", sbk)], inc=False)
                        P.op("pe", mm(PO[sbk][:, 0:128], VU[:, k, 1, :], PTt[:, pk, 128:256], not hp, True),
                             reads=[("VU", k), ("PT", pk)], writes=[("ps", "O", sbk)], inc=False)
                        if hp:
                            P.op("pe", mm(PL[sbk][:, 0:128], ones[:, :], PTt[:, pk, 0:128], True, False),
                                 reads=[("ones",), ("PT", pk)], writes=[("ps", "L", sbk)], inc=False)
                        P.op("pe", mm(PL[sbk][:, 0:128], ones[:, :], PTt[:, pk, 128:256], not hp, True),
                             reads=[("ones",), ("PT", pk)], writes=[("ps", "L", sbk)])
                        do = ao[:, r, bl * 128:(bl + 1) * 128]
                        dl = al[:, r, bl * 128:(bl + 1) * 128]
                        if bi == 0:
                            P.op("dve", lambda e, do=do, sbk=sbk: e.tensor_copy(out=do, in_=PO[sbk][:, 0:128]),
                                 reads=[("ps", "O", sbk)], writes=[("ACC",)])
                            P.op("dve", lambda e, dl=dl, sbk=sbk: e.tensor_copy(out=dl, in_=PL[sbk][:, 0:128]),
                                 reads=[("ps", "L", sbk)], writes=[("ACC",)])
                        else:
                            P.op("dve", lambda e, do=do, sbk=sbk: e.tensor_tensor(out=do, in0=do, in1=PO[sbk][:, 0:128],
                                                                                  op=ALU.add),
                                 reads=[("ps", "O", sbk), ("ACC",)], writes=[("ACC",)])
                            P.op("dve", lambda e, dl=dl, sbk=sbk: e.tensor_tensor(out=dl, in0=dl, in1=PL[sbk][:, 0:128],
                                                                                  op=ALU.add),
                                 reads=[("ps", "L", sbk), ("ACC",)], writes=[("ACC",)])
            P.op("dve", lambda e: e.reciprocal(out=ACC[:, 1, :], in_=ACC[:, 1, :]),
                 reads=[("ACC",)], writes=[("ACC",)])
            P.op("dve", lambda e: e.tensor_tensor(out=YW[:, :], in0=ACC[:, 0, :], in1=ACC[:, 1, :], op=ALU.mult),
                 reads=[("ACC",)], writes=[("YW",)])
            P.dma("sp", yd[:, w * 2048:(w + 1) * 2048], YW[:, :], reads=[("YW",)])
        P.finish()
        P.emit()
    return nc


def attn_phase(nc, P, es, A, scr, fsc, aug, GYF, GYD):
    seq = SEQ
    nblk = seq // 128
    qf, kf, vfT, qd, kd, vdT = scr
    nbf = A["nbf"]
    c_ident, c_ones, c_fm, c_dm = A["ident"], A["ones_bf"], A["fmask"], A["dmask"]
    c_ui, c_ls, c_of, c_idf = A["uincl"], A["lstrict"], A["ones_f"], A["ident_f"]
    scale = HD ** -0.5
    if True:
        sb = lambda n, s, d: es.enter_context(nc.sbuf_tensor(n + "_p2b", s, d))
        ps = lambda n: es.enter_context(nc.psum_tensor(n + "_p2b", [128, 512], F32))
        BIG = [sb(f"BIG{i}", [128, seq], BF16) for i in range(4)]
        ident = sb("identb", [128, 128], BF16)
        ones = sb("onesb", [128, 128], BF16)
        FM = sb("FM", [128, 4, 512], BF16)
        DM = sb("DM", [128, 256], BF16)
        UI = sb("UI", [128, 128], F32)
        LS = sb("LS", [128, 128], F32)
        OFc = sb("OFc", [128, 128], F32)
        NB = sb("NB", [128, 1], F32)
        LF = sb("LF", [128, nblk], F32)
        LFR = sb("LFR", [128, 128], F32)
        IDF = sb("IDF", [128, 128], F32)
        PVb = [ps("PV0"), ps("PV1")]
        RB = sb("RB", [128, 128], F32)
        C = sb("C", [128, 128], F32)
        R1 = sb("R1", [128, 128], F32)
        HI = sb("HI", [128, 6, 128], BF16)
        QF = sb("QF", [128, 2, 512], BF16)
        PTt = sb("PTt", [128, 3, 512], BF16)
        RL = sb("RL", [128, 512], F32)
        YO = sb("YO", [128, 2, 512], BF16)
        VU = sb("VU", [128, 4, 2, 128], BF16)
        ACC = sb("ACC", [128, 2, 2048], F32)
        YW = sb("YW", [128, 2048], BF16)
        PS = [ps("PS0"), ps("PS1")]
        PO = [ps("PO0"), ps("PO1")]
        PL = [ps("PL0"), ps("PL1")]
        for (t, src, key) in ((ident, c_ident, "ident"), (ones, c_ones, "ones"), (FM, c_fm, "FM"),
                              (DM, c_dm, "DM"), (UI, c_ui, "UI"), (LS, c_ls, "LS"), (OFc, c_of, "OFc"),
                              (NB, nbf, "NB"), (IDF, c_idf, "IDF"),
                              (LFR, fsc.rearrange("o (p j) -> (o p) j", j=128), "LFR")):
            idx = (slice(None),) * len(t.shape)
            P.dma("sp", t[idx], src, writes=[(key,)])
        P.op("dve", lambda e: e.tensor_scalar(out=NB[:, :], in0=NB[:, :], scalar1=-1.0, scalar2=None, op0=ALU.mult),
             reads=[("NB",)], writes=[("NB",)])
        KT, VK, AK, AQ = BIG[0], BIG[1], BIG[2], BIG[3]
        P.dma("sp", KT[:, :], kf, writes=[("BIG", 0)])
        P.dma("sp", AQ[:, :], vfT, writes=[("BIG", 3)])
        for bq in range(seq // 512):
            pv = bq % 2
            for j in range(4):
                blk = bq * 4 + j
                P.op("pe", mm(PVb[pv][:, j * 128:(j + 1) * 128], AQ[:, blk * 128:(blk + 1) * 128], ident[:, :],
                              True, True),
                     reads=[("BIG", 3), ("ident",)], writes=[("ps", "V", pv)], inc=(j == 3))
            P.op("act", lambda e, pv=pv, bq=bq: e.activation(out=VK[:, bq * 512:(bq + 1) * 512], in_=PVb[pv][:, :],
                                                            func=AF.Copy),
                 reads=[("ps", "V", pv)], writes=[("BIG", 1)])
        P.op("act", lambda e: e.activation(out=LFR[:, :], in_=LFR[:, :], func=AF.Exp, bias=NB[:, :], scale=-1.0),
             reads=[("LFR",), ("NB",)], writes=[("LFR",)])
        P.op("act", lambda e: e.activation(out=LFR[:, :], in_=LFR[:, :], func=AF.Ln, bias=OFc[:, 0:1], scale=1.0),
             reads=[("LFR",), ("OFc",)], writes=[("LFR",)])
        P.op("dve", lambda e: e.tensor_scalar(out=LFR[:, :], in0=LFR[:, :], scalar1=-1.0, scalar2=None, op0=ALU.mult),
             reads=[("LFR",)], writes=[("LFR",)])
        P.op("pe", mm(PS[0][:, 0:128], LFR[:, :], IDF[:, :], True, True),
             reads=[("LFR",), ("IDF",)], writes=[("ps", "S", 0)])
        P.op("dve", lambda e: e.tensor_copy(out=LF[:, :], in_=PS[0][:, 0:128]),
             reads=[("ps", "S", 0)], writes=[("LF",)])
        P.op("pe", mm(PS[0][0:nblk, 0:128], LF[:, :], OFc[:, :], True, True),
             reads=[("LF",), ("OFc",)], writes=[("ps", "S", 0)])
        P.op("dve", lambda e: e.tensor_copy(out=RB[0:nblk, :], in_=PS[0][0:nblk, 0:128]),
             reads=[("ps", "S", 0)], writes=[("RB",)])
        P.op("pe", mm(PS[1][0:nblk, 0:128], LF[:, :], UI[:, :], True, False),
             reads=[("LF",), ("UI",)], writes=[("ps", "S", 1)], inc=False)
        P.op("pe", mm(PS[1][0:nblk, 0:128], LS[0:nblk, 0:nblk], RB[0:nblk, :], False, True),
             reads=[("RB",), ("LS",)], writes=[("ps", "S", 1)])
        P.op("dve", lambda e: e.tensor_scalar(out=C[0:nblk, :], in0=PS[1][0:nblk, 0:128], scalar1=1.0 / scale,
                                              scalar2=None, op0=ALU.mult),
             reads=[("ps", "S", 1)], writes=[("C",)])
        cur = C
        for i in range(3):
            P.op("dve", lambda e, i=i, cur=cur: e.tensor_copy(out=HI[0:nblk, 3 + i, :], in_=cur[0:nblk, :]),
                 reads=[("C",), ("R1",)], writes=[("HI", 3 + i)])
            P.op("dve", lambda e, i=i: e.tensor_scalar(out=HI[0:nblk, i, :], in0=HI[0:nblk, 3 + i, :],
                                                       scalar1=-1.0, scalar2=None, op0=ALU.mult),
                 reads=[("HI", 3 + i)], writes=[("HI", i)])
            if i < 2:
                P.op("dve", lambda e, i=i, cur=cur: e.tensor_tensor(out=R1[0:nblk, :], in0=cur[0:nblk, :],
                                                                    in1=HI[0:nblk, 3 + i, :], op=ALU.subtract),
                     reads=[("C",), ("R1",), ("HI", 3 + i)], writes=[("R1",)])
                cur = R1
        for i in range(6):
            P.dma("sp", aug[i, :].rearrange("(p j) -> p j", j=128), HI[0:nblk, i, :],
                  reads=[("HI", i)], writes=[("scr", i)])
        P.op("pool", lambda e: e.memset(AK[0:6, :], 1.0), writes=[("BIG", 2)])
        P.op("pool", lambda e: e.memset(AQ[0:6, :], 1.0), writes=[("BIG", 3)])
        P.dma("sp", AK[0:3, :], aug[0:3, :], reads=[("scr", i) for i in range(3)], writes=[("BIG", 2)])
        P.dma("sp", AQ[3:6, :], aug[3:6, :], reads=[("scr", 3 + i) for i in range(3)], writes=[("BIG", 3)])
        pairs = [(i, j) for i in range(seq // 512) for j in range(4 * i + 4)]
        pt = 0

        def fox_scores(n):
            i, j = pairs[n]
            qb, sbk = i % 2, n % 2
            if j == 0:
                P.dma("sp", QF[:, qb, :], qf[:, i * 512:(i + 1) * 512], writes=[("QF", qb)])
            diag = j >= 4 * i
            P.op("pe", mm(PS[sbk][:, :], KT[:, j * 128:(j + 1) * 128], QF[:, qb, :], True, False),
                 reads=[("BIG", 0), ("QF", qb)], writes=[("ps", "S", sbk)], inc=False)
            P.op("pe", mm(PS[sbk][:, :], AK[0:6, j * 128:(j + 1) * 128], AQ[0:6, i * 512:(i + 1) * 512],
                          False, not diag),
                 reads=[("BIG", 2), ("BIG", 3)], writes=[("ps", "S", sbk)], inc=not diag)
            if diag:
                P.op("pe", mm(PS[sbk][:, :], ident[:, :], FM[:, j - 4 * i, :], False, True),
                     reads=[("ident",), ("FM",)], writes=[("ps", "S", sbk)])

        fox_scores(0)
        for n, (i, j) in enumerate(pairs):
            qb, sbk = i % 2, n % 2
            nj = 4 * i + 4
            if n + 1 < len(pairs):
                fox_scores(n + 1)
            pk = pt % 3
            pt += 1
            P.op("act", lambda e, sbk=sbk, pk=pk: e.activation(out=PTt[:, pk, :], in_=PS[sbk][:, :],
                                                               func=AF.Exp, scale=scale),
                 reads=[("ps", "S", sbk)], writes=[("PT", pk)])
            P.op("pe", mm(PO[qb][:, :], VK[:, j * 128:(j + 1) * 128], PTt[:, pk, :], j == 0, j == nj - 1),
                 reads=[("BIG", 1), ("PT", pk)], writes=[("ps", "O", qb)], inc=False)
            P.op("pe", mm(PL[qb][:, :], ones[:, :], PTt[:, pk, :], j == 0, j == nj - 1),
                 reads=[("ones",), ("PT", pk)], writes=[("ps", "L", qb)])
            if j == nj - 1:
                P.op("dve", lambda e, qb=qb: e.reciprocal(out=RL[:, :], in_=PL[qb][:, :]),
                     reads=[("ps", "L", qb)], writes=[("RL",)])
                P.op("dve", lambda e, qb=qb: e.tensor_tensor(out=YO[:, qb, :], in0=RL[:, :], in1=PO[qb][:, :],
                                                             op=ALU.mult),
                     reads=[("RL",), ("ps", "O", qb)], writes=[("YO", qb)])
                wdx = i // 4
                P.dma("sp", GYF.src[wdx][:, (i % 4) * 512:(i % 4 + 1) * 512], YO[:, qb, :], reads=[("YO", qb)],
                      writes=[("GYF", "s", wdx, i % 4)])
                if i % 4 == 3:
                    P.collective(G4, GYF.src[wdx], GYF.a[wdx], reads=[("GYF", "s", wdx, k4) for k4 in range(4)],
                                 writes=[("GYF", "a", wdx)])
                    if wdx > 0:
                        GYF.s2(wdx - 1)
        GYF.s2(seq // 2048 - 1)
        QD, KD, VD = BIG[0], BIG[1], BIG[2]
        P.dma("sp", QD[:, :], qd, writes=[("BIG", 0)])
        P.dma("sp", KD[:, :], kd, writes=[("BIG", 1)])
        P.dma("sp", VD[:, :], vdT, writes=[("BIG", 2)])
        units = []
        for w in range(seq // 2048):
            for bi, (d, nb) in enumerate(DIL):
                for r in range(d):
                    for bl in range(nb):
                        units.append((w, bi, d, nb, r, bl))

        def dil_front(un):
            w, bi, d, nb, r, bl = units[un]
            qv = QD[:, :].rearrange("p (l r) -> p r l", r=d)
            kv = KD[:, :].rearrange("p (l r) -> p r l", r=d)
            vv = VD[:, :].rearrange("p (l r) -> p r l", r=d)
            Bg = nb * w + bl
            hp = Bg > 0
            k, sbk = un % 4, un % 2
            if hp:
                P.op("pe", mm(PVb[sbk][:, 0:128], vv[:, r, (Bg - 1) * 128:Bg * 128], ident[:, :], True, True),
                     reads=[("BIG", 2), ("ident",)], writes=[("ps", "V", sbk)], inc=False)
            P.op("pe", mm(PVb[sbk][:, 128:256], vv[:, r, Bg * 128:(Bg + 1) * 128], ident[:, :], True, True),
                 reads=[("BIG", 2), ("ident",)], writes=[("ps", "V", sbk)])
            vlo = 0 if hp else 1
            P.op("act", lambda e, k=k, sbk=sbk, vlo=vlo: e.activation(
                out=VU[:, k, vlo:2, :], in_=PVb[sbk][:, vlo * 128:256].rearrange("p (t e) -> p t e", e=128),
                func=AF.Copy),
                 reads=[("ps", "V", sbk)], writes=[("VU", k)])
            qs = qv[:, r, Bg * 128:(Bg + 1) * 128]
            lo = 0 if hp else 128
            P.op("pe", mm(PS[sbk][:, lo:256], ident[:, :], DM[:, lo:256], True, False),
                 reads=[("ident",), ("DM",)], writes=[("ps", "S", sbk)], inc=False)
            if hp:
                P.op("pe", mm(PS[sbk][:, 0:128], kv[:, r, (Bg - 1) * 128:Bg * 128], qs, False, False),
                     reads=[("BIG", 0), ("BIG", 1)], writes=[("ps", "S", sbk)], inc=False)
            P.op("pe", mm(PS[sbk][:, 128:256], kv[:, r, Bg * 128:(Bg + 1) * 128], qs, False, True),
                 reads=[("BIG", 0), ("BIG", 1)], writes=[("ps", "S", sbk)])

        dil_front(0)
        for un, (w, bi, d, nb, r, bl) in enumerate(units):
            if un + 1 < len(units):
                dil_front(un + 1)
            Bg = nb * w + bl
            hp = Bg > 0
            k, sbk = un % 4, un % 2
            lo = 0 if hp else 128
            ao = ACC[:, 0, :].rearrange("p (l r) -> p r l", r=d)
            al = ACC[:, 1, :].rearrange("p (l r) -> p r l", r=d)
            pk = pt % 3
            pt += 1
            P.op("act", lambda e, sbk=sbk, pk=pk, lo=lo: e.activation(
                out=PTt[:, pk, lo:256], in_=PS[sbk][:, lo:256], func=AF.Exp, scale=scale),
                 reads=[("ps", "S", sbk)], writes=[("PT", pk)])
            if hp:
                P.op("pe", mm(PO[sbk][:, 0:128], VU[:, k, 0, :], PTt[:, pk, 0:128], True, False),
                     reads=[("VU", k), ("PT", pk)], writes=[("ps", "O", sbk)], inc=False)
            P.op("pe", mm(PO[sbk][:, 0:128], VU[:, k, 1, :], PTt[:, pk, 128:256], not hp, True),
                 reads=[("VU", k), ("PT", pk)], writes=[("ps", "O", sbk)], inc=False)
            if hp:
                P.op("pe", mm(PL[sbk][:, 0:128], ones[:, :], PTt[:, pk, 0:128], True, False),
                     reads=[("ones",), ("PT", pk)], writes=[("ps", "L", sbk)], inc=False)
            P.op("pe", mm(PL[sbk][:, 0:128], ones[:, :], PTt[:, pk, 128:256], not hp, True),
                 reads=[("ones",), ("PT", pk)], writes=[("ps", "L", sbk)])
            do = ao[:, r, bl * 128:(bl + 1) * 128]
            dl = al[:, r, bl * 128:(bl + 1) * 128]
            if bi == 0:
                P.op("dve", lambda e, do=do, sbk=sbk: e.tensor_copy(out=do, in_=PO[sbk][:, 0:128]),
                     reads=[("ps", "O", sbk)], writes=[("ACC",)])
                P.op("dve", lambda e, dl=dl, sbk=sbk: e.tensor_copy(out=dl, in_=PL[sbk][:, 0:128]),
                     reads=[("ps", "L", sbk)], writes=[("ACC",)])
            else:
                P.op("dve", lambda e, do=do, sbk=sbk: e.tensor_tensor(out=do, in0=do, in1=PO[sbk][:, 0:128],
                                                                      op=ALU.add),
                     reads=[("ps", "O", sbk), ("ACC",)], writes=[("ACC",)])
                P.op("dve", lambda e, dl=dl, sbk=sbk: e.tensor_tensor(out=dl, in0=dl, in1=PL[sbk][:, 0:128],
                                                                      op=ALU.add),
                     reads=[("ps", "L", sbk), ("ACC",)], writes=[("ACC",)])
            last_in_window = (un + 1 == len(units)) or (units[un + 1][0] != w)
            if last_in_window:
                P.op("dve", lambda e: e.reciprocal(out=ACC[:, 1, :], in_=ACC[:, 1, :]),
                     reads=[("ACC",)], writes=[("ACC",)])
                P.op("dve", lambda e: e.tensor_tensor(out=YW[:, :], in0=ACC[:, 0, :], in1=ACC[:, 1, :], op=ALU.mult),
                     reads=[("ACC",)], writes=[("YW",)])
                P.dma("sp", GYD.src[w], YW[:, :], reads=[("YW",)], writes=[("GYD", "s", w)])
                GYD.s1(w)
                if w > 0:
                    GYD.s2(w - 1)
        GYD.s2(seq // 2048 - 1)


def dil_perm(seq, d):
    nbt = seq // (128 * d)
    r = np.arange(d)[:, None, None]
    B = np.arange(nbt)[None, :, None]
    p = np.arange(128)[None, None, :]
    return (r + d * (128 * B + p)).reshape(-1)


NG = 8
TWO_PI = 2.0 * math.pi


def l4_consts():
    p = np.arange(128)[:, None]
    t = np.arange(128)[None, :]
    m0 = (np.arange(128) < 64).astype(np.float32)[:, None]
    mask4 = np.broadcast_to((t >= p).astype(np.float32)[:, None, :], (128, 4, 128)).copy()
    ti = np.broadcast_to(np.arange(130, dtype=np.float32)[None, :], (128, 130)).copy()
    tir = np.broadcast_to((127.0 - np.arange(128, dtype=np.float32))[None, :], (128, 128)).copy()
    sg = np.concatenate([m0, 1.0 - m0, -m0, -(1.0 - m0), np.full_like(m0, 0.5 * math.pi)], 1)
    return {"mask4": mask4, "ti": ti, "tir": tir, "sg": sg, "ident": _ident_np()}


def build_l4(seq=SEQ):
    nc = _new_nc()
    dt = lambda n, s, d, k="ExternalInput": nc.dram_tensor(n, s, d, kind=k).ap()
    nch = seq // 128
    uS = dt("uS", [128, nch, 128], F32)
    uG = dt("uG", [NG, 128, nch, 16], F32)
    lr_in = dt("lam_re2", [128, NG], F32)
    li_in = dt("lam_im2", [128, NG], F32)
    ldt_in = dt("logdt2", [128, NG], F32)
    br_in = dt("br2", [128, NG, 16], F32)
    bi_in = dt("bi2", [128, NG, 16], F32)
    cr_in = dt("cr2", [128, NG, 16], F32)
    ci_in = dt("ci2", [128, NG, 16], F32)
    d_in = dt("d2", [128, 128], F32)
    c_mask = dt("mask4", [128, 4, 128], F32)
    c_ti = dt("ti", [128, 130], F32)
    c_tir = dt("tir", [128, 128], F32)
    c_sg = dt("sg", [128, 5], F32)
    c_ident = dt("ident", [128, 128], BF16)
    zout = dt("zout", [NG, 128, nch, 16], BF16, "ExternalOutput")
    with ExitStack() as es:
        P = Prog(nc, es)
        sb = lambda n, s, d: es.enter_context(nc.sbuf_tensor(n, s, d))
        ps = lambda n: es.enter_context(nc.psum_tensor(n, [128, 512], F32))
        U16 = sb("U16", [128, nch, 128], BF16)
        UF = sb("UF", [128, 1, nch, 16], F32)
        TS = sb("TS", [128, 16, 16, 128], BF16)
        CF = sb("CF", [128, 1, 16, 129], BF16)
        BFc = sb("BFc", [128, 16, 128], BF16)
        WF = sb("WF", [128, 16, 128], BF16)
        M1 = sb("M1", [128, 16, 128], BF16)
        YS = sb("YS", [128, 1, nch, 16], F32)
        ZS = sb("ZS", [128, 1, nch, 16], BF16)
        G1 = sb("G1", [128, nch * 16], F32)
        VST = sb("VST", [128, NG, nch], F32)
        VI = sb("VI", [64, NG, nch], F32)
        SC = [[sb(f"SC{a}{b}", [64, NG, nch], F32) for b in range(2)] for a in range(2)]
        XP = sb("XP", [128, NG, nch], BF16)
        XPI = sb("XPI", [64, NG, nch], BF16)
        LR = sb("LR", [128, NG], F32)
        LI = sb("LI", [128, NG], F32)
        DTt = sb("DTt", [128, NG], F32)
        LDR = sb("LDR", [128, NG], F32)
        LDI = sb("LDI", [128, NG], F32)
        NLDR = sb("NLDR", [128, NG], F32)
        SM = sb("SM", [128, 12, NG], F32)
        BR = sb("BR", [128, NG, 16], F32)
        BI = sb("BI", [128, NG, 16], F32)
        CR = sb("CR", [128, NG, 16], F32)
        CI = sb("CI", [128, NG, 16], F32)
        BBR = sb("BBR", [128, NG, 16], F32)
        BBI = sb("BBI", [128, NG, 16], F32)
        CA = sb("CA", [128, NG, 16], F32)
        CB = sb("CB", [128, NG, 16], F32)
        BA = sb("BA", [128, NG, 16], F32)
        BB = sb("BB", [128, NG, 16], F32)
        D2 = sb("D2", [128, 128], F32)
        MASK = sb("MASK", [128, 4, 128], F32)
        TI = sb("TI", [128, 130], F32)
        TIR = sb("TIR", [128, 128], F32)
        SGN = sb("SGN", [128, 5], F32)
        ident = sb("identb", [128, 128], BF16)
        TB = sb("TB", [128, 6, 130], F32)
        AW = sb("AW", [128, 2, NG], F32)
        WW = sb("WW", [64, 2, 3, NG], F32)
        PA = [ps(f"PA{i}") for i in range(4)]
        PYb = [ps("PYa"), ps("PYb")]
        PV = ps("PV")
        loads = ((LR, lr_in, "LR"), (LI, li_in, "LI"), (DTt, ldt_in, "DT"), (BR, br_in, "BR"), (BI, bi_in, "BI"),
                 (CR, cr_in, "CR"), (CI, ci_in, "CI"), (D2, d_in, "D2"), (MASK, c_mask, "MASK"), (TI, c_ti, "TI"),
                 (TIR, c_tir, "TIR"), (SGN, c_sg, "SGN"), (ident, c_ident, "ident"))
        for (t, src, key) in loads:
            idx = (slice(None),) * len(t.shape)
            P.dma("sp", t[idx], src, writes=[(key,)])
        P.dma("pool", U16[:, :, :], uS, writes=[("U16",)])
        m0, m1, nm0, nm1 = SGN[:, 0:1], SGN[:, 1:2], SGN[:, 2:3], SGN[:, 3:4]

        def dve(fn, reads, writes):
            P.op("dve", fn, reads=reads, writes=writes)

        def tt(out, a, b, op, reads, writes, eng="dve"):
            P.op(eng, lambda e: e.tensor_tensor(out=out, in0=a, in1=b, op=op), reads=reads, writes=writes)

        def tsc(out, a, s1, op0, reads, writes, s2=None, op1=None, eng="dve"):
            if op1 is None:
                P.op(eng, lambda e: e.tensor_scalar(out=out, in0=a, scalar1=s1, scalar2=None, op0=op0),
                     reads=reads, writes=writes)
            else:
                P.op(eng, lambda e: e.tensor_scalar(out=out, in0=a, scalar1=s1, scalar2=s2, op0=op0, op1=op1),
                     reads=reads, writes=writes)

        def stt(out, a, s, b, op0, op1, reads, writes, eng="dve"):
            P.op(eng, lambda e: e.scalar_tensor_tensor(out=out, in0=a, scalar=s, in1=b, op0=op0, op1=op1),
                 reads=reads, writes=writes)

        def act(out, a, func, reads, writes, bias=None, scale=None):
            kw = {}
            if bias is not None:
                kw["bias"] = bias
            if scale is not None:
                kw["scale"] = scale
            P.op("act", lambda e: e.activation(out=out, in_=a, func=func, **kw), reads=reads, writes=writes)

        def cp(out, a, reads, writes, eng="dve"):
            P.op(eng, lambda e: e.tensor_copy(out=out, in_=a), reads=reads, writes=writes)

        RI = sb("RI", [128, 130], mybir.dt.int32)
        RF = sb("RF", [128, 2, 130], F32)

        def reduce_angle(arg, shift, out, rk, n):
            t, tf = RF[:, 0, 0:n], RF[:, 1, 0:n]
            ti = RI[:, 0:n]
            tsc(t, arg, 1.0 / TWO_PI, ALU.mult, rk, [("RF", 0)], s2=0.5 + shift, op1=ALU.add)
            cp(ti, t, [("RF", 0)], [("RI",)])
            cp(tf, ti, [("RI",)], [("RF", 1)])
            tt(t, t, tf, ALU.subtract, [("RF", 0), ("RF", 1)], [("RF", 0)])
            tsc(t, t, -0.5, ALU.add, [("RF", 0)], [("RF", 0)], s2=TWO_PI, op1=ALU.mult)
            tsc(tf, t, -math.pi, ALU.is_lt, [("RF", 0)], [("RF", 1)])
            stt(t, tf, TWO_PI, t, ALU.mult, ALU.add, [("RF", 0), ("RF", 1)], [("RF", 0)])
            tsc(tf, t, math.pi, ALU.is_gt, [("RF", 0)], [("RF", 1)])
            stt(out, tf, -TWO_PI, t, ALU.mult, ALU.add, [("RF", 0), ("RF", 1)], [("RF", 0)])

        def sincos(arg, sin_out, cos_out, tmp, rk, wk_s, wk_c, tk):
            n = arg.shape[-1]
            reduce_angle(arg, 0.0, RF[:, 0, 0:n], rk, n)
            act(sin_out, RF[:, 0, 0:n], AF.Sin, [("RF", 0)], wk_s)
            tsc(RF[:, 1, 0:n], RF[:, 0, 0:n], -1.0, ALU.mult, [("RF", 0)], [("RF", 1)])
            tt(RF[:, 1, 0:n], RF[:, 0, 0:n], RF[:, 1, 0:n], ALU.max, [("RF", 0), ("RF", 1)], [("RF", 1)])
            act(cos_out, RF[:, 1, 0:n], AF.Sin, [("RF", 1), ("SGN",)], wk_c, bias=SGN[:, 4:5], scale=-1.0)

        S = lambda i: SM[:, i, :]
        act(DTt[:, :], DTt[:, :], AF.Exp, [("DT",)], [("DT",)])
        tt(LDR[:, :], LR[:, :], DTt[:, :], ALU.mult, [("LR",), ("DT",)], [("LDR",)])
        tt(LDI[:, :], LI[:, :], DTt[:, :], ALU.mult, [("LI",), ("DT",)], [("LDI",)])
        tsc(NLDR[:, :], LDR[:, :], -1.0, ALU.mult, [("LDR",)], [("NLDR",)])
        tsc(S(5), LDI[:, :], TWO_PI, ALU.add, [("LDI",)], [("S", 5)])
        sincos(S(5), S(0), S(1), S(6), [("S", 5)], [("S", 0)], [("S", 1)], ("S", 6))
        act(S(2), LDR[:, :], AF.Exp, [("LDR",)], [("S", 2)])
        tt(S(3), S(2), S(1), ALU.mult, [("S", 2), ("S", 1)], [("S", 3)])
        tsc(S(3), S(3), -1.0, ALU.add, [("S", 3)], [("S", 3)])
        tt(S(4), S(2), S(0), ALU.mult, [("S", 2), ("S", 0)], [("S", 4)])
        tt(S(5), LR[:, :], LR[:, :], ALU.mult, [("LR",)], [("S", 5)])
        tt(S(6), LI[:, :], LI[:, :], ALU.mult, [("LI",)], [("S", 6)])
        tt(S(5), S(5), S(6), ALU.add, [("S", 5), ("S", 6)], [("S", 5)])
        dve(lambda e: e.reciprocal(out=S(5), in_=S(5)), [("S", 5)], [("S", 5)])
        tt(S(7), S(3), LR[:, :], ALU.mult, [("S", 3), ("LR",)], [("S", 7)])
        tt(S(6), S(4), LI[:, :], ALU.mult, [("S", 4), ("LI",)], [("S", 6)])
        tt(S(7), S(7), S(6), ALU.add, [("S", 7), ("S", 6)], [("S", 7)])
        tt(S(7), S(7), S(5), ALU.mult, [("S", 7), ("S", 5)], [("S", 7)])
        tt(S(8), S(4), LR[:, :], ALU.mult, [("S", 4), ("LR",)], [("S", 8)])
        tt(S(6), S(3), LI[:, :], ALU.mult, [("S", 3), ("LI",)], [("S", 6)])
        tt(S(8), S(8), S(6), ALU.subtract, [("S", 8), ("S", 6)], [("S", 8)])
        tt(S(8), S(8), S(5), ALU.mult, [("S", 8), ("S", 5)], [("S", 8)])
        tsc(S(9), S(8), -1.0, ALU.mult, [("S", 8)], [("S", 9)])
        for g in range(NG):
            tsc(BBR[:, g, :], BR[:, g, :], SM[:, 7, g:g + 1], ALU.mult, [("BR",), ("S", 7)], [("BBR", g)])
            stt(BBR[:, g, :], BI[:, g, :], SM[:, 9, g:g + 1], BBR[:, g, :], ALU.mult, ALU.add,
                [("BI",), ("S", 9), ("BBR", g)], [("BBR", g)])
            tsc(BBI[:, g, :], BI[:, g, :], SM[:, 7, g:g + 1], ALU.mult, [("BI",), ("S", 7)], [("BBI", g)])
            stt(BBI[:, g, :], BR[:, g, :], SM[:, 8, g:g + 1], BBI[:, g, :], ALU.mult, ALU.add,
                [("BR",), ("S", 8), ("BBI", g)], [("BBI", g)])
        tsc(CA[:, :, :], CR[:, :, :], m0, ALU.mult, [("CR",), ("SGN",)], [("CA",)])
        stt(CA[:, :, :], CI[:, :, :], nm1, CA[:, :, :], ALU.mult, ALU.add, [("CI",), ("SGN",), ("CA",)], [("CA",)])
        tsc(CB[:, :, :], CI[:, :, :], nm0, ALU.mult, [("CI",), ("SGN",)], [("CB",)])
        stt(CB[:, :, :], CR[:, :, :], nm1, CB[:, :, :], ALU.mult, ALU.add, [("CR",), ("SGN",), ("CB",)], [("CB",)])
        bbk = [("BBR", g) for g in range(NG)] + [("BBI", g) for g in range(NG)]
        tsc(BA[:, :, :], BBR[:, :, :], m0, ALU.mult, bbk + [("SGN",)], [("BA",)])
        stt(BA[:, :, :], BBI[:, :, :], m1, BA[:, :, :], ALU.mult, ALU.add, bbk + [("SGN",), ("BA",)], [("BA",)])
        tsc(BB[:, :, :], BBI[:, :, :], nm0, ALU.mult, bbk + [("SGN",)], [("BB",)])
        stt(BB[:, :, :], BBR[:, :, :], m1, BB[:, :, :], ALU.mult, ALU.add, bbk + [("SGN",), ("BB",)], [("BB",)])

        def table(g, tvec, n, neg):
            tk = [("TB", i) for i in range(6)]
            tsc(TB[:, 0, 0:n], tvec, LDI[:, g:g + 1], ALU.mult, [("LDI",), ("TI",), ("TIR",)], [tk[0]])
            tsc(TB[:, 0, 0:n], TB[:, 0, 0:n], TWO_PI, ALU.add, [tk[0]], [tk[0]])
            sincos(TB[:, 0, 0:n], TB[:, 1, 0:n], TB[:, 2, 0:n], TB[:, 3, 0:n], [tk[0]], [tk[1]], [tk[2]], tk[3])
            act(TB[:, 3, 0:n], tvec, AF.Exp, [("TI",), ("TIR",), ("LDR",), ("NLDR",), tk[3]], [tk[3]],
                scale=(NLDR if neg else LDR)[:, g:g + 1])
            tt(TB[:, 4, 0:n], TB[:, 3, 0:n], TB[:, 2, 0:n], ALU.mult, [tk[3], tk[2]], [tk[4]])
            if neg:
                stt(TB[:, 5, 0:n], TB[:, 3, 0:n], -1.0, TB[:, 1, 0:n], ALU.mult, ALU.mult, [tk[3], tk[1]], [tk[5]])
            else:
                tt(TB[:, 5, 0:n], TB[:, 3, 0:n], TB[:, 1, 0:n], ALU.mult, [tk[3], tk[1]], [tk[5]])

        pa = 0
        for g in range(NG):
            table(g, TI[:, 128:129], 1, False)
            tsc(AW[:, 0, g:g + 1], TB[:, 4, 0:1], 1.0, ALU.mult, [("TB", 4)], [("AW",)])
            tsc(AW[:, 1, g:g + 1], TB[:, 5, 0:1], 1.0, ALU.mult, [("TB", 5)], [("AW",)])
            table(g, TIR[:, :], 128, False)
            for hp in range(16):
                tsc(G1[:, 0:128], TB[:, 4, 0:128], BA[:, g, hp:hp + 1], ALU.mult, [("TB", 4), ("BA",)], [("G1",)])
                stt(WF[:, hp, :], TB[:, 5, 0:128], BB[:, g, hp:hp + 1], G1[:, 0:128], ALU.mult, ALU.add,
                    [("TB", 5), ("BB",), ("G1",)], [("WF",)])
            for q in range(4):
                bk = pa % 4
                pa += 1
                for j in range(4):
                    P.op("pe", mm(PA[bk][:, j * 128:(j + 1) * 128], WF[:, q * 4 + j, :], ident[:, :], True, True),
                         reads=[("WF",), ("ident",)], writes=[("ps", "A", bk)], inc=(j == 3))
                act(M1[:, q * 4:(q + 1) * 4, :], PA[bk][:, :].rearrange("p (j t) -> p j t", j=4), AF.Copy,
                    [("ps", "A", bk)], [("M1",)])
            for hp in range(16):
                P.op("pe", mm(PV[:, 0:nch], M1[:, hp, :], U16[:, :, g * 16 + hp], hp == 0, hp == 15),
                     reads=[("M1",), ("U16",)], writes=[("ps", "V")], inc=(hp == 15))
            act(VST[:, g, :], PV[:, 0:nch], AF.Copy, [("ps", "V")], [("VST",)])
        P.dma("sp", VI[:, :, :], VST[64:128, :, :], reads=[("VST",)], writes=[("VI",)])
        cp(SC[0][0][:, :, :], VST[0:64, :, :], [("VST",)], [("SC", 0, 0)])
        cp(SC[0][1][:, :, :], VI[:, :, :], [("VI",)], [("SC", 0, 1)], eng="pool")
        tsc(WW[:, 0, 0, :], AW[0:64, 0, :], 1.0, ALU.mult, [("AW",)], [("WW", 0)])
        tsc(WW[:, 0, 1, :], AW[0:64, 1, :], 1.0, ALU.mult, [("AW",)], [("WW", 0)])
        tsc(WW[:, 0, 2, :], AW[0:64, 1, :], -1.0, ALU.mult, [("AW",)], [("WW", 0)])
        cur = 0
        sh = 1
        while sh < nch:
            nx = 1 - cur
            re, im = SC[cur]
            nre, nim = SC[nx]
            for g in range(NG):
                wr, wi, nwi = WW[:, cur, 0, g:g + 1], WW[:, cur, 1, g:g + 1], WW[:, cur, 2, g:g + 1]
                rk = [("SC", cur, 0), ("SC", cur, 1), ("WW", cur)]
                cp(nre[:, g, 0:sh], re[:, g, 0:sh], rk, [("SC", nx, 0)])
                stt(nre[:, g, sh:nch], re[:, g, 0:nch - sh], wr, re[:, g, sh:nch], ALU.mult, ALU.add, rk, [("SC", nx, 0)])
                stt(nre[:, g, sh:nch], im[:, g, 0:nch - sh], nwi, nre[:, g, sh:nch], ALU.mult, ALU.add,
                    rk + [("SC", nx, 0)], [("SC", nx, 0)])
                cp(nim[:, g, 0:sh], im[:, g, 0:sh], rk, [("SC", nx, 1)], eng="pool")
                stt(nim[:, g, sh:nch], im[:, g, 0:nch - sh], wr, im[:, g, sh:nch], ALU.mult, ALU.add, rk,
                    [("SC", nx, 1)])
                stt(nim[:, g, sh:nch], re[:, g, 0:nch - sh], wi, nim[:, g, sh:nch], ALU.mult, ALU.add,
                    rk + [("SC", nx, 1)], [("SC", nx, 1)])
            wk = [("WW", cur)]
            tt(WW[:, nx, 0, :], WW[:, cur, 0, :], WW[:, cur, 0, :], ALU.mult, wk, [("WW", nx)])
            tt(WW[:, nx, 2, :], WW[:, cur, 1, :], WW[:, cur, 1, :], ALU.mult, wk, [("WW", nx)])
            tt(WW[:, nx, 0, :], WW[:, nx, 0, :], WW[:, nx, 2, :], ALU.subtract, [("WW", nx)], [("WW", nx)])
            tt(WW[:, nx, 1, :], WW[:, cur, 0, :], WW[:, cur, 1, :], ALU.mult, wk, [("WW", nx)])
            tsc(WW[:, nx, 1, :], WW[:, nx, 1, :], 2.0, ALU.mult, [("WW", nx)], [("WW", nx)])
            tsc(WW[:, nx, 2, :], WW[:, nx, 1, :], -1.0, ALU.mult, [("WW", nx)], [("WW", nx)])
            cur = nx
            sh *= 2
        re, im = SC[cur]
        P.op("pool", lambda e: e.memset(XP[:, :, 0:1], 0.0), writes=[("XP",)])
        P.op("pool", lambda e: e.memset(XPI[:, :, 0:1], 0.0), writes=[("XPI",)])
        if nch > 1:
            P.op("dve", lambda e: e.tensor_copy(out=XP[0:64, :, 1:nch], in_=re[:, :, 0:nch - 1]),
                 reads=[("SC", cur, 0)], writes=[("XP",)])
            P.op("dve", lambda e: e.tensor_copy(out=XPI[:, :, 1:nch], in_=im[:, :, 0:nch - 1]),
                 reads=[("SC", cur, 1)], writes=[("XPI",)])
        P.dma("sp", XP[64:128, :, :], XPI[:, :, :], reads=[("XPI",)], writes=[("XP",)])
        gsc = 2.0 * math.sqrt(2.0 / math.pi)
        py = 0
        for g in range(NG):
            ub = 0
            P.dma("sp", UF[:, ub, :, :], uG[g], writes=[("UF", ub)])
            table(g, TI[:, 0:129], 129, False)
            for h in range(16):
                tsc(G1[:, 0:129], TB[:, 4, 0:129], CA[:, g, h:h + 1], ALU.mult, [("TB", 4), ("CA",)], [("G1",)])
                stt(CF[:, 0, h, :], TB[:, 5, 0:129], CB[:, g, h:h + 1], G1[:, 0:129], ALU.mult, ALU.add,
                    [("TB", 5), ("CB",), ("G1",)], [("CF", 0)])
            table(g, TI[:, 0:128], 128, True)
            for hp in range(16):
                tsc(G1[:, 0:128], TB[:, 4, 0:128], BA[:, g, hp:hp + 1], ALU.mult, [("TB", 4), ("BA",)], [("G1",)])
                stt(BFc[:, hp, :], TB[:, 5, 0:128], BB[:, g, hp:hp + 1], G1[:, 0:128], ALU.mult, ALU.add,
                    [("TB", 5), ("BB",), ("G1",)], [("BFc",)])
            for hp in range(16):
                for q in range(4):
                    bk = pa % 4
                    pa += 1
                    P.op("pe", mm(PA[bk][:, :], BFc[:, hp, :], CF[:, 0, q * 4:(q + 1) * 4, 0:128], True, True),
                         reads=[("BFc",), ("CF", 0)], writes=[("ps", "A", bk)])
                    tt(TS[:, hp, q * 4:(q + 1) * 4, :], PA[bk][:, :].rearrange("p (j t) -> p j t", j=4),
                       MASK[:, :, :], ALU.mult, [("ps", "A", bk), ("MASK",)], [("TS",)])
            for q in range(4):
                yb = py % 2
                py += 1
                for j in range(4):
                    h = q * 4 + j
                    o = PYb[yb][:, j * 128:j * 128 + nch]
                    for hp in range(16):
                        P.op("pe", mm(o, TS[:, hp, h, :], U16[:, :, g * 16 + hp], hp == 0, False),
                             reads=[("TS",), ("U16",)], writes=[("ps", "Y", yb)], inc=False)
                    P.op("pe", mm(o, CF[:, 0, h, 1:129], XP[:, g, :], False, True),
                         reads=[("CF", 0), ("XP",)], writes=[("ps", "Y", yb)], inc=(j == 3))
                for j in range(4):
                    h = q * 4 + j
                    stt(YS[:, ub, :, h], UF[:, ub, :, h], D2[:, g * 16 + h:g * 16 + h + 1],
                        PYb[yb][:, j * 128:j * 128 + nch], ALU.mult, ALU.add,
                        [("UF", ub), ("D2",), ("ps", "Y", yb)], [("YS", ub)])
            yv = YS[:, ub, :, :].rearrange("p c h -> p (c h)")
            zv = ZS[:, ub, :, :].rearrange("p c h -> p (c h)")
            n = nch * 16
            tt(G1[:, 0:n], yv, yv, ALU.mult, [("YS", ub)], [("G1",)], eng="pool")
            tsc(G1[:, 0:n], G1[:, 0:n], 0.044715, ALU.mult, [("G1",)], [("G1",)], s2=1.0, op1=ALU.add, eng="pool")
            tt(G1[:, 0:n], G1[:, 0:n], yv, ALU.mult, [("G1",), ("YS", ub)], [("G1",)], eng="pool")
            act(G1[:, 0:n], G1[:, 0:n], AF.Sigmoid, [("G1",)], [("G1",)], scale=gsc)
            tt(zv, G1[:, 0:n], yv, ALU.mult, [("G1",), ("YS", ub)], [("ZS", ub)], eng="pool")
            P.dma("sp", zout[g], ZS[:, ub, :, :], reads=[("ZS", ub)])
        P.finish()
        P.emit()
    return nc


def s5_phase(nc, P, es, S, c_ident, uS16, uG, GZ):
    seq = SEQ
    nch = seq // 128
    lr_in, li_in, ldt_in = S["lam_re2"], S["lam_im2"], S["logdt2"]
    br_in, bi_in, cr_in, ci_in, d_in = S["br2"], S["bi2"], S["cr2"], S["ci2"], S["d2"]
    c_mask, c_ti, c_tir, c_sg = S["mask4"], S["ti"], S["tir"], S["sg"]
    if True:
        sb = lambda n, s, d: es.enter_context(nc.sbuf_tensor(n + "_p4b", s, d))
        ps = lambda n: es.enter_context(nc.psum_tensor(n + "_p4b", [128, 512], F32))
        U16 = sb("U16", [128, nch, 128], BF16)
        UF = sb("UF", [128, 1, nch, 16], F32)
        TS = sb("TS", [128, 16, 16, 128], BF16)
        CF = sb("CF", [128, 1, 16, 129], BF16)
        BFc = sb("BFc", [128, 16, 128], BF16)
        WF = BFc
        M1 = CF[:, 0, :, 0:128]
        YS = sb("YS", [128, 1, nch, 16], F32)
        ZALL = sb("ZALL", [128, nch, 128], BF16)
        ZT = sb("ZT", [128, 2, 512], BF16)
        G1 = sb("G1", [128, nch * 16], F32)
        VST = sb("VST", [128, NG, nch], F32)
        VI = sb("VI", [64, NG, nch], F32)
        SC = [[sb(f"SC{a}{b}", [64, NG, nch], F32) for b in range(2)] for a in range(2)]
        XP = sb("XP", [128, NG, nch], BF16)
        XPI = sb("XPI", [64, NG, nch], BF16)
        LR = sb("LR", [128, NG], F32)
        LI = sb("LI", [128, NG], F32)
        DTt = sb("DTt", [128, NG], F32)
        LDR = sb("LDR", [128, NG], F32)
        LDI = sb("LDI", [128, NG], F32)
        NLDR = sb("NLDR", [128, NG], F32)
        SM = sb("SM", [128, 12, NG], F32)
        BR = sb("BR", [128, NG, 16], F32)
        BI = sb("BI", [128, NG, 16], F32)
        CR = sb("CR", [128, NG, 16], F32)
        CI = sb("CI", [128, NG, 16], F32)
        BBR = sb("BBR", [128, NG, 16], F32)
        BBI = sb("BBI", [128, NG, 16], F32)
        CA = sb("CA", [128, NG, 16], F32)
        CB = sb("CB", [128, NG, 16], F32)
        BA = sb("BA", [128, NG, 16], F32)
        BB = sb("BB", [128, NG, 16], F32)
        D2 = sb("D2", [128, 128], F32)
        MASK = sb("MASK", [128, 4, 128], F32)
        TI = sb("TI", [128, 130], F32)
        TIR = sb("TIR", [128, 128], F32)
        SGN = sb("SGN", [128, 5], F32)
        ident = sb("identb", [128, 128], BF16)
        TB = sb("TB", [128, 6, 130], F32)
        AW = sb("AW", [128, 2, NG], F32)
        WW = sb("WW", [64, 2, 3, NG], F32)
        PA = [ps(f"PA{i}") for i in range(4)]
        PYb = [ps("PYa"), ps("PYb")]
        PV = ps("PV")
        loads = ((LR, lr_in, "LR"), (LI, li_in, "LI"), (DTt, ldt_in, "DT"), (BR, br_in, "BR"), (BI, bi_in, "BI"),
                 (CR, cr_in, "CR"), (CI, ci_in, "CI"), (D2, d_in, "D2"), (MASK, c_mask, "MASK"), (TI, c_ti, "TI"),
                 (TIR, c_tir, "TIR"), (SGN, c_sg, "SGN"), (ident, c_ident, "ident"))
        for (t, src, key) in loads:
            idx = (slice(None),) * len(t.shape)
            P.dma("sp", t[idx], src, writes=[(key,)])
        P.dma("sp", U16[:, :, :], uS16, writes=[("U16",)])
        m0, m1, nm0, nm1 = SGN[:, 0:1], SGN[:, 1:2], SGN[:, 2:3], SGN[:, 3:4]

        def dve(fn, reads, writes):
            P.op("dve", fn, reads=reads, writes=writes)

        def tt(out, a, b, op, reads, writes, eng="dve"):
            P.op(eng, lambda e: e.tensor_tensor(out=out, in0=a, in1=b, op=op), reads=reads, writes=writes)

        def tsc(out, a, s1, op0, reads, writes, s2=None, op1=None, eng="dve"):
            if op1 is None:
                P.op(eng, lambda e: e.tensor_scalar(out=out, in0=a, scalar1=s1, scalar2=None, op0=op0),
                     reads=reads, writes=writes)
            else:
                P.op(eng, lambda e: e.tensor_scalar(out=out, in0=a, scalar1=s1, scalar2=s2, op0=op0, op1=op1),
                     reads=reads, writes=writes)

        def stt(out, a, s, b, op0, op1, reads, writes, eng="dve"):
            P.op(eng, lambda e: e.scalar_tensor_tensor(out=out, in0=a, scalar=s, in1=b, op0=op0, op1=op1),
                 reads=reads, writes=writes)

        def act(out, a, func, reads, writes, bias=None, scale=None):
            kw = {}
            if bias is not None:
                kw["bias"] = bias
            if scale is not None:
                kw["scale"] = scale
            P.op("act", lambda e: e.activation(out=out, in_=a, func=func, **kw), reads=reads, writes=writes)

        def cp(out, a, reads, writes, eng="dve"):
            P.op(eng, lambda e: e.tensor_copy(out=out, in_=a), reads=reads, writes=writes)

        RI = sb("RI", [128, 130], mybir.dt.int32)
        RF = sb("RF", [128, 2, 130], F32)

        def reduce_angle(arg, shift, out, rk, n):
            t, tf = RF[:, 0, 0:n], RF[:, 1, 0:n]
            ti = RI[:, 0:n]
            tsc(t, arg, 1.0 / TWO_PI, ALU.mult, rk, [("RF", 0)], s2=0.5 + shift, op1=ALU.add)
            cp(ti, t, [("RF", 0)], [("RI",)])
            cp(tf, ti, [("RI",)], [("RF", 1)])
            tt(t, t, tf, ALU.subtract, [("RF", 0), ("RF", 1)], [("RF", 0)])
            tsc(t, t, -0.5, ALU.add, [("RF", 0)], [("RF", 0)], s2=TWO_PI, op1=ALU.mult)
            tsc(tf, t, -math.pi, ALU.is_lt, [("RF", 0)], [("RF", 1)])
            stt(t, tf, TWO_PI, t, ALU.mult, ALU.add, [("RF", 0), ("RF", 1)], [("RF", 0)])
            tsc(tf, t, math.pi, ALU.is_gt, [("RF", 0)], [("RF", 1)])
            stt(out, tf, -TWO_PI, t, ALU.mult, ALU.add, [("RF", 0), ("RF", 1)], [("RF", 0)])

        def sincos(arg, sin_out, cos_out, tmp, rk, wk_s, wk_c, tk):
            n = arg.shape[-1]
            reduce_angle(arg, 0.0, RF[:, 0, 0:n], rk, n)
            act(sin_out, RF[:, 0, 0:n], AF.Sin, [("RF", 0)], wk_s)
            tsc(RF[:, 1, 0:n], RF[:, 0, 0:n], -1.0, ALU.mult, [("RF", 0)], [("RF", 1)])
            tt(RF[:, 1, 0:n], RF[:, 0, 0:n], RF[:, 1, 0:n], ALU.max, [("RF", 0), ("RF", 1)], [("RF", 1)])
            act(cos_out, RF[:, 1, 0:n], AF.Sin, [("RF", 1), ("SGN",)], wk_c, bias=SGN[:, 4:5], scale=-1.0)

        S = lambda i: SM[:, i, :]
        act(DTt[:, :], DTt[:, :], AF.Exp, [("DT",)], [("DT",)])
        tt(LDR[:, :], LR[:, :], DTt[:, :], ALU.mult, [("LR",), ("DT",)], [("LDR",)])
        tt(LDI[:, :], LI[:, :], DTt[:, :], ALU.mult, [("LI",), ("DT",)], [("LDI",)])
        tsc(NLDR[:, :], LDR[:, :], -1.0, ALU.mult, [("LDR",)], [("NLDR",)])
        tsc(S(5), LDI[:, :], TWO_PI, ALU.add, [("LDI",)], [("S", 5)])
        sincos(S(5), S(0), S(1), S(6), [("S", 5)], [("S", 0)], [("S", 1)], ("S", 6))
        act(S(2), LDR[:, :], AF.Exp, [("LDR",)], [("S", 2)])
        tt(S(3), S(2), S(1), ALU.mult, [("S", 2), ("S", 1)], [("S", 3)])
        tsc(S(3), S(3), -1.0, ALU.add, [("S", 3)], [("S", 3)])
        tt(S(4), S(2), S(0), ALU.mult, [("S", 2), ("S", 0)], [("S", 4)])
        tt(S(5), LR[:, :], LR[:, :], ALU.mult, [("LR",)], [("S", 5)])
        tt(S(6), LI[:, :], LI[:, :], ALU.mult, [("LI",)], [("S", 6)])
        tt(S(5), S(5), S(6), ALU.add, [("S", 5), ("S", 6)], [("S", 5)])
        dve(lambda e: e.reciprocal(out=S(5), in_=S(5)), [("S", 5)], [("S", 5)])
        tt(S(7), S(3), LR[:, :], ALU.mult, [("S", 3), ("LR",)], [("S", 7)])
        tt(S(6), S(4), LI[:, :], ALU.mult, [("S", 4), ("LI",)], [("S", 6)])
        tt(S(7), S(7), S(6), ALU.add, [("S", 7), ("S", 6)], [("S", 7)])
        tt(S(7), S(7), S(5), ALU.mult, [("S", 7), ("S", 5)], [("S", 7)])
        tt(S(8), S(4), LR[:, :], ALU.mult, [("S", 4), ("LR",)], [("S", 8)])
        tt(S(6), S(3), LI[:, :], ALU.mult, [("S", 3), ("LI",)], [("S", 6)])
        tt(S(8), S(8), S(6), ALU.subtract, [("S", 8), ("S", 6)], [("S", 8)])
        tt(S(8), S(8), S(5), ALU.mult, [("S", 8), ("S", 5)], [("S", 8)])
        tsc(S(9), S(8), -1.0, ALU.mult, [("S", 8)], [("S", 9)])
        for g in range(NG):
            tsc(BBR[:, g, :], BR[:, g, :], SM[:, 7, g:g + 1], ALU.mult, [("BR",), ("S", 7)], [("BBR", g)])
            stt(BBR[:, g, :], BI[:, g, :], SM[:, 9, g:g + 1], BBR[:, g, :], ALU.mult, ALU.add,
                [("BI",), ("S", 9), ("BBR", g)], [("BBR", g)])
            tsc(BBI[:, g, :], BI[:, g, :], SM[:, 7, g:g + 1], ALU.mult, [("BI",), ("S", 7)], [("BBI", g)])
            stt(BBI[:, g, :], BR[:, g, :], SM[:, 8, g:g + 1], BBI[:, g, :], ALU.mult, ALU.add,
                [("BR",), ("S", 8), ("BBI", g)], [("BBI", g)])
        tsc(CA[:, :, :], CR[:, :, :], m0, ALU.mult, [("CR",), ("SGN",)], [("CA",)])
        stt(CA[:, :, :], CI[:, :, :], nm1, CA[:, :, :], ALU.mult, ALU.add, [("CI",), ("SGN",), ("CA",)], [("CA",)])
        tsc(CB[:, :, :], CI[:, :, :], nm0, ALU.mult, [("CI",), ("SGN",)], [("CB",)])
        stt(CB[:, :, :], CR[:, :, :], nm1, CB[:, :, :], ALU.mult, ALU.add, [("CR",), ("SGN",), ("CB",)], [("CB",)])
        bbk = [("BBR", g) for g in range(NG)] + [("BBI", g) for g in range(NG)]
        tsc(BA[:, :, :], BBR[:, :, :], m0, ALU.mult, bbk + [("SGN",)], [("BA",)])
        stt(BA[:, :, :], BBI[:, :, :], m1, BA[:, :, :], ALU.mult, ALU.add, bbk + [("SGN",), ("BA",)], [("BA",)])
        tsc(BB[:, :, :], BBI[:, :, :], nm0, ALU.mult, bbk + [("SGN",)], [("BB",)])
        stt(BB[:, :, :], BBR[:, :, :], m1, BB[:, :, :], ALU.mult, ALU.add, bbk + [("SGN",), ("BB",)], [("BB",)])

        def table(g, tvec, n, neg):
            tk = [("TB", i) for i in range(6)]
            tsc(TB[:, 0, 0:n], tvec, LDI[:, g:g + 1], ALU.mult, [("LDI",), ("TI",), ("TIR",)], [tk[0]])
            tsc(TB[:, 0, 0:n], TB[:, 0, 0:n], TWO_PI, ALU.add, [tk[0]], [tk[0]])
            sincos(TB[:, 0, 0:n], TB[:, 1, 0:n], TB[:, 2, 0:n], TB[:, 3, 0:n], [tk[0]], [tk[1]], [tk[2]], tk[3])
            act(TB[:, 3, 0:n], tvec, AF.Exp, [("TI",), ("TIR",), ("LDR",), ("NLDR",), tk[3]], [tk[3]],
                scale=(NLDR if neg else LDR)[:, g:g + 1])
            tt(TB[:, 4, 0:n], TB[:, 3, 0:n], TB[:, 2, 0:n], ALU.mult, [tk[3], tk[2]], [tk[4]])
            if neg:
                stt(TB[:, 5, 0:n], TB[:, 3, 0:n], -1.0, TB[:, 1, 0:n], ALU.mult, ALU.mult, [tk[3], tk[1]], [tk[5]])
            else:
                tt(TB[:, 5, 0:n], TB[:, 3, 0:n], TB[:, 1, 0:n], ALU.mult, [tk[3], tk[1]], [tk[5]])

        pa = 0
        for g in range(NG):
            table(g, TI[:, 128:129], 1, False)
            tsc(AW[:, 0, g:g + 1], TB[:, 4, 0:1], 1.0, ALU.mult, [("TB", 4)], [("AW",)])
            tsc(AW[:, 1, g:g + 1], TB[:, 5, 0:1], 1.0, ALU.mult, [("TB", 5)], [("AW",)])
            table(g, TIR[:, :], 128, False)
            for hp in range(16):
                tsc(G1[:, 0:128], TB[:, 4, 0:128], BA[:, g, hp:hp + 1], ALU.mult, [("TB", 4), ("BA",)], [("G1",)])
                stt(WF[:, hp, :], TB[:, 5, 0:128], BB[:, g, hp:hp + 1], G1[:, 0:128], ALU.mult, ALU.add,
                    [("TB", 5), ("BB",), ("G1",)], [("BFc",)])
            for q in range(4):
                bk = pa % 4
                pa += 1
                for j in range(4):
                    P.op("pe", mm(PA[bk][:, j * 128:(j + 1) * 128], WF[:, q * 4 + j, :], ident[:, :], True, True),
                         reads=[("BFc",), ("ident",)], writes=[("ps", "A", bk)], inc=(j == 3))
                act(M1[:, q * 4:(q + 1) * 4, :], PA[bk][:, :].rearrange("p (j t) -> p j t", j=4), AF.Copy,
                    [("ps", "A", bk)], [("CF", 0)])
            for hp in range(16):
                P.op("pe", mm(PV[:, 0:nch], M1[:, hp, :], U16[:, :, g * 16 + hp], hp == 0, hp == 15),
                     reads=[("CF", 0), ("U16",)], writes=[("ps", "V")], inc=(hp == 15))
            act(VST[:, g, :], PV[:, 0:nch], AF.Copy, [("ps", "V")], [("VST",)])
        P.dma("sp", VI[:, :, :], VST[64:128, :, :], reads=[("VST",)], writes=[("VI",)])
        cp(SC[0][0][:, :, :], VST[0:64, :, :], [("VST",)], [("SC", 0, 0)])
        cp(SC[0][1][:, :, :], VI[:, :, :], [("VI",)], [("SC", 0, 1)], eng="pool")
        tsc(WW[:, 0, 0, :], AW[0:64, 0, :], 1.0, ALU.mult, [("AW",)], [("WW", 0)])
        tsc(WW[:, 0, 1, :], AW[0:64, 1, :], 1.0, ALU.mult, [("AW",)], [("WW", 0)])
        tsc(WW[:, 0, 2, :], AW[0:64, 1, :], -1.0, ALU.mult, [("AW",)], [("WW", 0)])
        cur = 0
        sh = 1
        while sh < nch:
            nx = 1 - cur
            re, im = SC[cur]
            nre, nim = SC[nx]
            for g in range(NG):
                wr, wi, nwi = WW[:, cur, 0, g:g + 1], WW[:, cur, 1, g:g + 1], WW[:, cur, 2, g:g + 1]
                rk = [("SC", cur, 0), ("SC", cur, 1), ("WW", cur)]
                cp(nre[:, g, 0:sh], re[:, g, 0:sh], rk, [("SC", nx, 0)])
                stt(nre[:, g, sh:nch], re[:, g, 0:nch - sh], wr, re[:, g, sh:nch], ALU.mult, ALU.add, rk, [("SC", nx, 0)])
                stt(nre[:, g, sh:nch], im[:, g, 0:nch - sh], nwi, nre[:, g, sh:nch], ALU.mult, ALU.add,
                    rk + [("SC", nx, 0)], [("SC", nx, 0)])
                cp(nim[:, g, 0:sh], im[:, g, 0:sh], rk, [("SC", nx, 1)], eng="pool")
                stt(nim[:, g, sh:nch], im[:, g, 0:nch - sh], wr, im[:, g, sh:nch], ALU.mult, ALU.add, rk,
                    [("SC", nx, 1)])
                stt(nim[:, g, sh:nch], re[:, g, 0:nch - sh], wi, nim[:, g, sh:nch], ALU.mult, ALU.add,
                    rk + [("SC", nx, 1)], [("SC", nx, 1)])
            wk = [("WW", cur)]
            tt(WW[:, nx, 0, :], WW[:, cur, 0, :], WW[:, cur, 0, :], ALU.mult, wk, [("WW", nx)])
            tt(WW[:, nx, 2, :], WW[:, cur, 1, :], WW[:, cur, 1, :], ALU.mult, wk, [("WW", nx)])
            tt(WW[:, nx, 0, :], WW[:, nx, 0, :], WW[:, nx, 2, :], ALU.subtract, [("WW", nx)], [("WW", nx)])
            tt(WW[:, nx, 1, :], WW[:, cur, 0, :], WW[:, cur, 1, :], ALU.mult, wk, [("WW", nx)])
            tsc(WW[:, nx, 1, :], WW[:, nx, 1, :], 2.0, ALU.mult, [("WW", nx)], [("WW", nx)])
            tsc(WW[:, nx, 2, :], WW[:, nx, 1, :], -1.0, ALU.mult, [("WW", nx)], [("WW", nx)])
            cur = nx
            sh *= 2
        re, im = SC[cur]
        P.op("pool", lambda e: e.memset(XP[:, :, 0:1], 0.0), writes=[("XP",)])
        P.op("pool", lambda e: e.memset(XPI[:, :, 0:1], 0.0), writes=[("XPI",)])
        if nch > 1:
            P.op("dve", lambda e: e.tensor_copy(out=XP[0:64, :, 1:nch], in_=re[:, :, 0:nch - 1]),
                 reads=[("SC", cur, 0)], writes=[("XP",)])
            P.op("dve", lambda e: e.tensor_copy(out=XPI[:, :, 1:nch], in_=im[:, :, 0:nch - 1]),
                 reads=[("SC", cur, 1)], writes=[("XPI",)])
        P.dma("sp", XP[64:128, :, :], XPI[:, :, :], reads=[("XPI",)], writes=[("XP",)])
        gsc = 2.0 * math.sqrt(2.0 / math.pi)
        py = 0
        for g in range(NG):
            ub = 0
            P.dma("sp", UF[:, ub, :, :], uG[g * 128:(g + 1) * 128, :].rearrange("s (c h) -> s c h", h=16),
                  writes=[("UF", ub)])
            table(g, TI[:, 0:129], 129, False)
            for h in range(16):
                tsc(G1[:, 0:129], TB[:, 4, 0:129], CA[:, g, h:h + 1], ALU.mult, [("TB", 4), ("CA",)], [("G1",)])
                stt(CF[:, 0, h, :], TB[:, 5, 0:129], CB[:, g, h:h + 1], G1[:, 0:129], ALU.mult, ALU.add,
                    [("TB", 5), ("CB",), ("G1",)], [("CF", 0)])
            table(g, TI[:, 0:128], 128, True)
            for hp in range(16):
                tsc(G1[:, 0:128], TB[:, 4, 0:128], BA[:, g, hp:hp + 1], ALU.mult, [("TB", 4), ("BA",)], [("G1",)])
                stt(BFc[:, hp, :], TB[:, 5, 0:128], BB[:, g, hp:hp + 1], G1[:, 0:128], ALU.mult, ALU.add,
                    [("TB", 5), ("BB",), ("G1",)], [("BFc",)])
            for hp in range(16):
                for q in range(4):
                    bk = pa % 4
                    pa += 1
                    P.op("pe", mm(PA[bk][:, :], BFc[:, hp, :], CF[:, 0, q * 4:(q + 1) * 4, 0:128], True, True),
                         reads=[("BFc",), ("CF", 0)], writes=[("ps", "A", bk)])
                    tt(TS[:, hp, q * 4:(q + 1) * 4, :], PA[bk][:, :].rearrange("p (j t) -> p j t", j=4),
                       MASK[:, :, :], ALU.mult, [("ps", "A", bk), ("MASK",)], [("TS",)])
            for q in range(4):
                yb = py % 2
                py += 1
                for j in range(4):
                    h = q * 4 + j
                    o = PYb[yb][:, j * 128:j * 128 + nch]
                    for hp in range(16):
                        P.op("pe", mm(o, TS[:, hp, h, :], U16[:, :, g * 16 + hp], hp == 0, False),
                             reads=[("TS",), ("U16",)], writes=[("ps", "Y", yb)], inc=False)
                    P.op("pe", mm(o, CF[:, 0, h, 1:129], XP[:, g, :], False, True),
                         reads=[("CF", 0), ("XP",)], writes=[("ps", "Y", yb)], inc=(j == 3))
                for j in range(4):
                    h = q * 4 + j
                    stt(YS[:, ub, :, h], UF[:, ub, :, h], D2[:, g * 16 + h:g * 16 + h + 1],
                        PYb[yb][:, j * 128:j * 128 + nch], ALU.mult, ALU.add,
                        [("UF", ub), ("D2",), ("ps", "Y", yb)], [("YS", ub)])
            yv = YS[:, ub, :, :]
            g3 = G1[:, 0:nch * 16].rearrange("p (c h) -> p c h", h=16)
            zv = ZALL[:, :, g * 16:(g + 1) * 16]
            tt(g3, yv, yv, ALU.mult, [("YS", ub)], [("G1",)], eng="pool")
            tsc(g3, g3, 0.044715, ALU.mult, [("G1",)], [("G1",)], s2=1.0, op1=ALU.add, eng="pool")
            tt(g3, g3, yv, ALU.mult, [("G1",), ("YS", ub)], [("G1",)], eng="pool")
            act(g3, g3, AF.Sigmoid, [("G1",)], [("G1",)], scale=gsc)
            tt(zv, g3, yv, ALU.mult, [("G1",), ("YS", ub)], [("ZALL",)], eng="pool")
        for c4 in range(nch // 4):
            bk = pa % 4
            pa += 1
            zb = c4 % 2
            for j in range(4):
                c = c4 * 4 + j
                P.op("pe", mm(PA[bk][:, j * 128:(j + 1) * 128], ZALL[:, c, :], ident[:, :], True, True),
                     reads=[("ZALL",), ("ident",)], writes=[("ps", "A", bk)], inc=(j == 3))
            act(ZT[:, zb, :], PA[bk][:, :], AF.Copy, [("ps", "A", bk)], [("ZT", zb)])
            w = c4 // 4
            P.dma("sp", GZ.src[w][:, (c4 % 4) * 512:(c4 % 4 + 1) * 512], ZT[:, zb, :], reads=[("ZT", zb)],
                  writes=[("GZ", "s", w, c4 % 4)])
            if c4 % 4 == 3:
                P.collective(G4, GZ.src[w], GZ.a[w], reads=[("GZ", "s", w, k4) for k4 in range(4)],
                             writes=[("GZ", "a", w)])
                if w > 0:
                    GZ.s2(w - 1)
        GZ.s2(nch // 16 - 1)


def l4_inputs(u_c, inp, core, seq=SEQ):
    nch = seq // 128
    gs = slice(core * NG, (core + 1) * NG)
    two = lambda a: np.ascontiguousarray(np.concatenate([a, a], 0).astype(np.float32))
    m = dict(l4_consts())
    u3 = u_c.reshape(nch, 128, 128)
    m["uS"] = np.ascontiguousarray(u3.transpose(1, 0, 2))
    m["uG"] = np.ascontiguousarray(u3.reshape(nch, 128, NG, 16).transpose(2, 1, 0, 3))
    m["lam_re2"] = two(inp["s5_lambda_re"][0][gs].T)
    m["lam_im2"] = two(inp["s5_lambda_im"][0][gs].T)
    m["logdt2"] = np.ascontiguousarray(np.broadcast_to(inp["s5_log_dt"][0][gs][None, :], (128, NG)).astype(np.float32))
    m["br2"] = two(inp["s5_b_re"][0][gs].transpose(1, 0, 2))
    m["bi2"] = two(inp["s5_b_im"][0][gs].transpose(1, 0, 2))
    m["cr2"] = two(inp["s5_c_re"][0][gs].transpose(2, 0, 1))
    m["ci2"] = two(inp["s5_c_im"][0][gs].transpose(2, 0, 1))
    m["d2"] = np.ascontiguousarray(np.broadcast_to(inp["s5_d"][0][core * 128:(core + 1) * 128][None, :], (128, 128)).astype(np.float32))
    return m


def l4_unpack(zout, seq=SEQ):
    nch = seq // 128
    return np.asarray(zout).transpose(2, 1, 0, 3).reshape(seq, 128)


_CACHE = {}


def _get(name, builder):
    if name not in _CACHE:
        _CACHE[name] = builder()
    return _CACHE[name]


def _run(nc, in_maps):
    res = run_bass_kernel_spmd(nc, in_maps, core_ids=list(range(NCORE)))
    return res.results


def kernel_unfused(**inputs):
    inp = {k: np.asarray(v) for k, v in inputs.items()}
    x = inp["x"][0]
    ident = _ident_np()
    cs = lambda c: slice(c * TPC, (c + 1) * TPC)
    ca = np.ascontiguousarray
    maps = [{"x": ca(x[cs(c)]), "wg": inp["ffn1_w_gate"][0], "wu": inp["ffn1_w_up"][0], "wd": inp["ffn1_w_down"][0],
             "lng": ca(inp["ln_gain"][0, 0][None]), "lnb": ca(inp["ln_bias"][0, 0][None]),
             "win": inp["attn_w_in"][0], "ident": ident} for c in range(NCORE)]
    r1 = _run(_get("l1", build_l1), maps)
    x1 = [r["x1"] for r in r1]
    projT = np.concatenate([np.asarray(r["projT"]) for r in r1], axis=1)
    fT = np.concatenate([np.asarray(r["fT"]) for r in r1], axis=1)
    del r1, maps
    consts = l2_consts()
    perms = {d: dil_perm(SEQ, d) for (d, _) in DIL}
    maps = []
    for c in range(NCORE):
        m = dict(consts)
        hs = slice(128 * c, 128 * (c + 1))
        m["qf"] = ca(projT[0:1024][hs])
        m["kf"] = ca(projT[1024:2048][hs])
        m["vf"] = ca(projT[2048:3072][hs].T)
        m["f2T"] = ca(fT[c].reshape(SEQ // 128, 128).T)
        m["nbf"] = np.full((128, 1), inp["attn_b_f"][0, c], np.float32)
        m["qd"] = ca(projT[3072:4096][hs])
        m["kd"] = ca(projT[4096:5120][hs])
        vdt = ca(projT[5120:6144][hs].T)
        for (d, _) in DIL:
            m[f"vd{d}"] = ca(vdt[perms[d]])
        maps.append(m)
    r2 = _run(_get("l2", build_l2), maps)
    yT = np.concatenate([np.asarray(r["yf"]) for r in r2] + [np.asarray(r["yd"]) for r in r2], axis=0)
    del r2, maps, projT
    lng3 = ca(np.stack([inp["ln_gain"][0, 1], inp["ln_gain"][0, 2], inp["ln_gain"][1, 0]]))
    lnb3 = ca(np.stack([inp["ln_bias"][0, 1], inp["ln_bias"][0, 2], inp["ln_bias"][1, 0]]))
    maps = [{"x1": x1[c], "yT": ca(yT[:, cs(c)]), "wo": inp["attn_w_out"][0],
             "w2g": inp["ffn2_w_gate"][0], "w2u": inp["ffn2_w_up"][0], "w2d": inp["ffn2_w_down"][0],
             "w3g": inp["ffn1_w_gate"][1], "w3u": inp["ffn1_w_up"][1], "w3d": inp["ffn1_w_down"][1],
             "lng": lng3, "lnb": lnb3, "wsi": inp["s5_w_in"][0], "ident": ident} for c in range(NCORE)]
    r3 = _run(_get("l3", build_l3), maps)
    x3 = [r["x3"] for r in r3]
    u = np.concatenate([np.asarray(r["u"]) for r in r3], axis=0)
    del r3, maps, x1, yT
    maps = [l4_inputs(ca(u[:, 128 * c:128 * (c + 1)]), inp, c) for c in range(NCORE)]
    r4 = _run(_get("l4", build_l4), maps)
    z = np.concatenate([l4_unpack(r["zout"]) for r in r4], axis=1)
    zT = ca(z.T)
    del r4, maps, u, z
    lng5 = ca(np.stack([inp["ln_gain"][1, 1], inp["ln_gain"][1, 2]]))
    lnb5 = ca(np.stack([inp["ln_bias"][1, 1], inp["ln_bias"][1, 2]]))
    maps = [{"x3": x3[c], "zT": ca(zT[:, cs(c)]), "wgo": inp["s5_w_glu_out"][0], "wgg": inp["s5_w_glu_gate"][0],
             "w4g": inp["ffn2_w_gate"][1], "w4u": inp["ffn2_w_up"][1], "w4d": inp["ffn2_w_down"][1],
             "lng": lng5, "lnb": lnb5, "ident": ident} for c in range(NCORE)]
    r5 = _run(_get("l5", build_l5), maps)
    out = np.concatenate([np.asarray(r["out"]) for r in r5], axis=0)
    return out.reshape(1, SEQ, D).astype(np.float32)


G4 = [[0, 1, 2, 3], [4, 5, 6, 7]]
G2 = [[0, 4], [1, 5], [2, 6], [3, 7]]


class Gather:
    def __init__(self, nc, P, name, rows, cols, dtype, n):
        self.P, self.name = P, name
        self.src = nc.dram_tensor(name + "_s", [n, rows, cols], dtype).ap()
        self.a = nc.dram_tensor(name + "_a", [n, 4 * rows, cols], dtype).ap()
        self.b = nc.dram_tensor(name + "_b", [n, 8 * rows, cols], dtype).ap()

    def s1(self, i):
        self.P.collective(G4, self.src[i], self.a[i], reads=[(self.name, "s", i)], writes=[(self.name, "a", i)])

    def s2(self, i):
        self.P.collective(G2, self.a[i], self.b[i], reads=[(self.name, "a", i)], writes=[(self.name, "b", i)])


def _xt_to_gather(P, T, GXo, t):
    for q in range(4):
        i = t * 4 + q
        P.dma("sp", GXo.src[i].rearrange("(j p) t -> p j t", p=128), T.XT[:, 4 * q:4 * q + 4, :],
              reads=[("XT", st) for st in range(NST)], writes=[(GXo.name, "s", i)])
        GXo.s1(i)
    if t > 0:
        for q in range(4):
            GXo.s2((t - 1) * 4 + q)


def _select_window(P, T, OH, nk, loader):
    xk = [("XT", st) for st in range(NST)]
    for w in range(NCORE):
        s0 = (w % 2) * 16
        keys = [("HT", s0 + k) for k in range(nk)]
        loader(w, T.HT[:, s0:s0 + nk, :], keys)
        src = T.HT[:, s0:s0 + nk, :]
        dst = T.XT[:, 0:nk, :]
        if w == 0:
            P.op("dve", lambda e, src=src, dst=dst, w=w: e.tensor_scalar(out=dst, in0=src, scalar1=OH[:, w:w + 1],
                                                                      scalar2=None, op0=ALU.mult),
                 reads=keys + [("OH",)], writes=xk)
        else:
            P.op("dve", lambda e, src=src, dst=dst, w=w: e.scalar_tensor_tensor(out=dst, in0=src, scalar=OH[:, w:w + 1],
                                                                             in1=dst, op0=ALU.mult, op1=ALU.add),
                 reads=keys + [("OH",)] + xk, writes=xk)


def build_fused(nph=7, dbg=False):
    nc = _new_nc()
    dt = lambda n, s, d, k="ExternalInput": nc.dram_tensor(n, s, d, kind=k).ap()
    it_ = lambda n, s, d: nc.dram_tensor(n, s, d).ap()
    x = dt("x", [TPC, D], F32)
    nff = 1 if nph < 4 else (3 if nph < 7 else 4)
    ffw = [[dt(f"w{i}g", [D, DFF], F32), dt(f"w{i}u", [D, DFF], F32), dt(f"w{i}d", [DFF, D], F32)] for i in range(nff)]
    lng = dt("lng", [6, D], F32)
    lnb = dt("lnb", [6, D], F32)
    winc = dt("winc", [D, 769], F32)
    wo = dt("wo", [D, D], F32)
    wsic = dt("wsic", [D, 128], F32)
    wgo = dt("wgo", [S5W, D], F32)
    wgg = dt("wgg", [S5W, D], F32)
    onehot = dt("onehot", [128, NCORE], F32)
    A = {"nbf": dt("nbf", [128, 1], F32)}
    for n_, shp, d_ in (("ident", [128, 128], BF16), ("ones_bf", [128, 128], BF16), ("fmask", [128, 4, 512], BF16),
                        ("dmask", [128, 256], BF16), ("uincl", [128, 128], F32), ("lstrict", [128, 128], F32),
                        ("ones_f", [128, 128], F32), ("ident_f", [128, 128], F32)):
        A[n_] = dt(n_, shp, d_)
    S = {}
    for n_, shp in (("lam_re2", [128, NG]), ("lam_im2", [128, NG]), ("logdt2", [128, NG]), ("br2", [128, NG, 16]),
                    ("bi2", [128, NG, 16]), ("cr2", [128, NG, 16]), ("ci2", [128, NG, 16]), ("d2", [128, 128]),
                    ("mask4", [128, 4, 128]), ("ti", [128, 130]), ("tir", [128, 128]), ("sg", [128, 5])):
        S[n_] = dt(n_, shp, F32)
    out = dt("out", [TPC, D], F32, "ExternalOutput")
    ident = A["ident"]
    if dbg:
        it_ = lambda n, s, d: nc.dram_tensor(n, s, d, kind="ExternalOutput").ap()
    x1s = it_("x1s", [TPC, D], F32)
    x3s = it_("x3s", [TPC, D], F32)
    scr = [it_(f"scr{i}", [128, SEQ], BF16) for i in range(6)]
    fsc = it_("fsc", [1, SEQ], F32)
    dbg_g = it_("dbg_g", [8 * 512, TT], BF16)
    dbg_y = it_("dbg_y", [2, 8 * 128, 2048], BF16)
    dbg_z = it_("dbg_z", [8 * 128, 2048], BF16)
    it_ = lambda n, s, d: nc.dram_tensor(n, s, d).ap()
    aug = it_("augscr", [6, SEQ], BF16)
    uS16 = it_("uS16", [128, SEQ // 128, 128], BF16)
    uG = it_("uG", [NG * 128, (SEQ // 128) * 16], F32)
    nch = SEQ // 128
    eps2 = LN_EPS / (ALPHA * ALPHA)
    with ExitStack() as es0:
        P = Prog(nc, es0)
        GX = Gather(nc, P, "GX", 512, TT, BF16, 16)
        GYF = Gather(nc, P, "GYF", 128, 2048, BF16, 8)
        GYD = Gather(nc, P, "GYD", 128, 2048, BF16, 8)
        GZ = Gather(nc, P, "GZ", 128, 2048, BF16, 8)
        with ExitStack() as es:
            T = TokPipe(nc, es, P, ident, tag="_p1")
            for t in range(TPC // TT):
                t0 = t * TT
                T.load_x(x[t0:t0 + TT, :])
                T.ffn(ffw[0][0], ffw[0][1], ffw[0][2], lng[0, :], lnb[0, :])
                T.store_x(x1s[t0:t0 + TT, :])
                _xt_to_gather(P, T, GX, t)
            if nph == 1:
                P.dma("sp", dbg_g, GX.b[5], reads=[("GX", "b", 5)])
                P.finish()
                P.emit()
                return nc
            P.barrier()
            P.emit()
        with ExitStack() as es:
            sb = lambda n, s, d: es.enter_context(nc.sbuf_tensor(n + "_p2a", s, d))
            ps = lambda n: es.enter_context(nc.psum_tensor(n + "_p2a", [128, 512], F32))
            XT2 = sb("XT2", [128, 2, 16, TT], BF16)
            W = sb("Wip", [128, 16, 769], BF16)
            OT = sb("OT", [128, 4, TT], BF16)
            OF = sb("OF", [1, 2, TT], F32)
            PG = [ps(f"PG{i}") for i in range(4)]
            P.dma("pool", W[:, :, :], winc.rearrange("(k p) n -> p k n", p=128), writes=[("Wip",)])
            for q in range(4):
                GX.s2((TPC // TT - 1) * 4 + q)
            it = 0
            oc = 0
            for t in range(TPC // TT):
                for r in range(NCORE):
                    b = it % 2
                    it += 1
                    for q in range(4):
                        i = t * 4 + q
                        P.dma("sp", XT2[:, b, 4 * q:4 * q + 4, :],
                              GX.b[i][r * 512:(r + 1) * 512, :].rearrange("(j p) t -> p j t", p=128),
                              reads=[("GX", "b", i)], writes=[("XT2", b, q)])
                    tok0 = r * TPC + t * TT
                    xk = [("XT2", b, q) for q in range(4)]
                    for ci in range(7):
                        wdt = 128 if ci < 6 else 1
                        k = oc % 4
                        oc += 1
                        for kc in range(16):
                            P.op("pe", mm(PG[k][0:wdt, :], W[:, kc, ci * 128:ci * 128 + wdt], XT2[:, b, kc, :],
                                          kc == 0, kc == 15),
                                 reads=[("Wip",)] + xk, writes=[("ps", "Gp", k)], inc=(kc == 15))
                        if ci < 6:
                            P.op("act", lambda e, k=k: e.activation(out=OT[:, k, :], in_=PG[k][:, :], func=AF.Copy),
                                 reads=[("ps", "Gp", k)], writes=[("OTp", k)])
                            P.dma("pool", scr[ci][:, tok0:tok0 + TT], OT[:, k, :], reads=[("OTp", k)])
                        else:
                            k2 = k % 2
                            P.op("act", lambda e, k=k, k2=k2: e.activation(out=OF[0:1, k2, :], in_=PG[k][0:1, :],
                                                                          func=AF.Copy),
                                 reads=[("ps", "Gp", k)], writes=[("OFp", k2)])
                            P.dma("pool", fsc[0:1, tok0:tok0 + TT], OF[0:1, k2, :], reads=[("OFp", k2)])
            if nph == 2:
                P.finish()
                P.emit()
                return nc
            P.barrier()
            P.emit()
        with ExitStack() as es:
            attn_phase(nc, P, es, A, scr, fsc, aug, GYF, GYD)
            if nph == 3:
                P.dma("sp", dbg_y[0], GYF.b[0], reads=[("GYF", "b", 0)])
                P.dma("sp", dbg_y[1], GYD.b[0], reads=[("GYD", "b", 0)])
                P.finish()
                P.emit()
                return nc
            P.barrier()
            P.emit()
        with ExitStack() as es:
            T = TokPipe(nc, es, P, ident, tag="_p3")
            OH = es.enter_context(nc.sbuf_tensor("OH_p3", [128, NCORE], F32))
            P.dma("sp", OH[:, :], onehot, writes=[("OH",)])
            for t in range(TPC // TT):
                t0 = t * TT
                T.load_x_only(x1s[t0:t0 + TT, :])

                def ld_y(w, dst, keys, t0=t0):
                    P.dma("sp", dst[:, 0:8, :], GYF.b[w][:, t0:t0 + TT].rearrange("(h p) t -> p h t", p=128),
                          reads=[("GYF", "b", w)], writes=keys[0:8])
                    P.dma("sp", dst[:, 8:16, :], GYD.b[w][:, t0:t0 + TT].rearrange("(h p) t -> p h t", p=128),
                          reads=[("GYD", "b", w)], writes=keys[8:16])
                _select_window(P, T, OH, 16, ld_y)
                T.load_ln(lng[1, :], lnb[1, :])

                def cons_res(st, c0, b):
                    xs = T.X[:, st, c0:c0 + 256]
                    P.op("dve", lambda e: e.scalar_tensor_tensor(out=xs, in0=T.PY[b][:, 0:256], scalar=1.0 / ALPHA,
                                                                 in1=xs, op0=ALU.mult, op1=ALU.add),
                         reads=[("ps", "Y", b), ("X", st)], writes=[("X", st)])
                T.lin_tm(16, wo, D, cons_res)
                T.layernorm(eps2)
                T.ffn(ffw[1][0], ffw[1][1], ffw[1][2], lng[2, :], lnb[2, :])
                T.ffn(ffw[2][0], ffw[2][1], ffw[2][2], lng[3, :], lnb[3, :])
                T.store_x(x3s[t0:t0 + TT, :])
                _xt_to_gather(P, T, GX, t)
            if nph == 4:
                P.finish()
                P.emit()
                return nc
            P.barrier()
            P.emit()
        with ExitStack() as es:
            sb = lambda n, s, d: es.enter_context(nc.sbuf_tensor(n + "_p4a", s, d))
            ps = lambda n: es.enter_context(nc.psum_tensor(n + "_p4a", [128, 512], F32))
            XT2 = sb("XT2", [128, 2, 16, TT], BF16)
            W = sb("Wsi", [128, 16, 128], BF16)
            UB = sb("UB", [128, 2, 4, 128], BF16)
            UF = sb("UF", [128, 2, 4, 128], F32)
            PG = [ps(f"PG{i}") for i in range(2)]
            P.dma("pool", W[:, :, :], wsic.rearrange("(k p) n -> p k n", p=128), writes=[("Wsi",)])
            for q in range(4):
                GX.s2((TPC // TT - 1) * 4 + q)
            it = 0
            for t in range(TPC // TT):
                for r in range(NCORE):
                    b = it % 2
                    it += 1
                    for q in range(4):
                        i = t * 4 + q
                        P.dma("sp", XT2[:, b, 4 * q:4 * q + 4, :],
                              GX.b[i][r * 512:(r + 1) * 512, :].rearrange("(j p) t -> p j t", p=128),
                              reads=[("GX", "b", i)], writes=[("XT2", b, q)])
                    c0 = (r * TPC + t * TT) // 128
                    xk = [("XT2", b, q) for q in range(4)]
                    for st in range(4):
                        for kc in range(16):
                            P.op("pe", mm(PG[b][:, st * 128:(st + 1) * 128], XT2[:, b, kc, st * 128:(st + 1) * 128],
                                          W[:, kc, :], kc == 0, kc == 15),
                                 reads=[("Wsi",)] + xk, writes=[("ps", "Gu", b)], inc=(st == 3 and kc == 15))
                    pv = PG[b][:, :].rearrange("p (c h) -> p c h", c=4)
                    P.op("dve", lambda e, b=b, pv=pv: e.tensor_copy(out=UF[:, b, :, :], in_=pv),
                         reads=[("ps", "Gu", b)], writes=[("UFs", b)])
                    P.op("act", lambda e, b=b: e.activation(out=UB[:, b, :, :], in_=UF[:, b, :, :], func=AF.Copy),
                         reads=[("UFs", b)], writes=[("UB", b)])
                    P.dma("pool", uS16[:, c0:c0 + 4, :], UB[:, b, :, :], reads=[("UB", b)])
                    for g in range(NG):
                        P.dma("pool", uG[g * 128:(g + 1) * 128, c0 * 16:(c0 + 4) * 16].rearrange("s (c h) -> s c h", h=16),
                              UF[:, b, :, g * 16:(g + 1) * 16], reads=[("UFs", b)])
            if nph == 5:
                P.finish()
                P.emit()
                return nc
            P.barrier()
            P.emit()
        with ExitStack() as es:
            s5_phase(nc, P, es, S, ident, uS16, uG, GZ)
            if nph == 6:
                P.dma("sp", dbg_z, GZ.b[3], reads=[("GZ", "b", 3)])
                P.finish()
                P.emit()
                return nc
            P.barrier()
            P.emit()
        with ExitStack() as es:
            T = TokPipe(nc, es, P, ident, tag="_p5")
            OH = es.enter_context(nc.sbuf_tensor("OH_p5", [128, NCORE], F32))
            P.dma("sp", OH[:, :], onehot, writes=[("OH",)])
            for t in range(TPC // TT):
                t0 = t * TT
                T.load_x_only(x3s[t0:t0 + TT, :])

                def ld_z(w, dst, keys, t0=t0):
                    P.dma("sp", dst[:, 0:8, :], GZ.b[w][:, t0:t0 + TT].rearrange("(h p) t -> p h t", p=128),
                          reads=[("GZ", "b", w)], writes=keys)
                _select_window(P, T, OH, 8, ld_z)
                T.load_ln(lng[4, :], lnb[4, :])

                def cons_glu(st, c0, b):
                    k = T.ot % 2
                    T.ot += 1
                    sg = T.SG[:, k, 0:256]
                    xs = T.X[:, st, c0:c0 + 256]
                    P.op("act", lambda e: e.activation(out=sg, in_=T.PG[b][:, 0:256], func=AF.Sigmoid),
                         reads=[("ps", "G", b)], writes=[("SG", k)])
                    P.op("dve", lambda e: e.tensor_tensor(out=sg, in0=sg, in1=T.PY[b][:, 0:256], op=ALU.mult),
                         reads=[("SG", k), ("ps", "Y", b)], writes=[("SG", k)])
                    P.op("dve", lambda e: e.scalar_tensor_tensor(out=xs, in0=sg, scalar=1.0 / ALPHA, in1=xs,
                                                                 op0=ALU.mult, op1=ALU.add),
                         reads=[("SG", k), ("X", st)], writes=[("X", st)])
                T.lin_tm(8, wgo, D, cons_glu, extra_w=wgg)
                T.layernorm(eps2)
                T.ffn(ffw[3][0], ffw[3][1], ffw[3][2], lng[5, :], lnb[5, :])
                T.store_x(out[t0:t0 + TT, :])
            P.finish()
            P.emit()
    return nc


def l4_params(inp, core):
    gs = slice(core * NG, (core + 1) * NG)
    two = lambda a: np.ascontiguousarray(np.concatenate([a, a], 0).astype(np.float32))
    m = dict(l4_consts())
    del m["ident"]
    m["lam_re2"] = two(inp["s5_lambda_re"][0][gs].T)
    m["lam_im2"] = two(inp["s5_lambda_im"][0][gs].T)
    m["logdt2"] = np.ascontiguousarray(np.broadcast_to(inp["s5_log_dt"][0][gs][None, :], (128, NG)).astype(np.float32))
    m["br2"] = two(inp["s5_b_re"][0][gs].transpose(1, 0, 2))
    m["bi2"] = two(inp["s5_b_im"][0][gs].transpose(1, 0, 2))
    m["cr2"] = two(inp["s5_c_re"][0][gs].transpose(2, 0, 1))
    m["ci2"] = two(inp["s5_c_im"][0][gs].transpose(2, 0, 1))
    m["d2"] = np.ascontiguousarray(np.broadcast_to(inp["s5_d"][0][core * 128:(core + 1) * 128][None, :], (128, 128)).astype(np.float32))
    return m


def fused_inputs(inp):
    ca = np.ascontiguousarray
    x = inp["x"][0]
    consts = l2_consts()
    consts["ident_f"] = np.eye(128, dtype=np.float32)
    shared = {
        "w0g": inp["ffn1_w_gate"][0], "w0u": inp["ffn1_w_up"][0], "w0d": inp["ffn1_w_down"][0],
        "w1g": inp["ffn2_w_gate"][0], "w1u": inp["ffn2_w_up"][0], "w1d": inp["ffn2_w_down"][0],
        "w2g": inp["ffn1_w_gate"][1], "w2u": inp["ffn1_w_up"][1], "w2d": inp["ffn1_w_down"][1],
        "w3g": inp["ffn2_w_gate"][1], "w3u": inp["ffn2_w_up"][1], "w3d": inp["ffn2_w_down"][1],
        "lng": ca(inp["ln_gain"].reshape(6, D)), "lnb": ca(inp["ln_bias"].reshape(6, D)),
        "wo": inp["attn_w_out"][0], "wgo": inp["s5_w_glu_out"][0], "wgg": inp["s5_w_glu_gate"][0],
    }
    shared.update(consts)
    win = inp["attn_w_in"][0]
    maps = []
    for c in range(NCORE):
        m = dict(shared)
        m["x"] = ca(x[c * TPC:(c + 1) * TPC])
        cols = np.concatenate([np.arange(128) + off + 128 * c for off in (0, 1024, 2048, 3080, 4104, 5128)]
                              + [np.array([3072 + c])])
        m["winc"] = ca(win[:, cols])
        m["wsic"] = ca(inp["s5_w_in"][0][:, 128 * c:128 * (c + 1)])
        oh = np.zeros((128, NCORE), np.float32)
        oh[:, c] = 1.0
        m["onehot"] = oh
        m["nbf"] = np.full((128, 1), inp["attn_b_f"][0, c], np.float32)
        m.update(l4_params(inp, c))
        maps.append(m)
    return maps


def kernel(**inputs):
    inp = {k: np.asarray(v) for k, v in inputs.items()}
    maps = fused_inputs(inp)
    res = _run(_get("fused", build_fused), maps)
    out = np.concatenate([np.asarray(r["out"]) for r in res], axis=0)
    return out.reshape(1, SEQ, D).astype(np.float32)
```
